# Optimizing a Trainium2 kernel written in Bass

```python
import jax, jax.numpy as jnp
from jax import lax
import numpy as np

D_MODEL = 1024
BATCH = 16
SEQ = 256
DEPTH = 1
DEC_BATCH = 2
DEC_SEQ = 2048
PAST_LEN = 256

GRID_W = 64
HEAD_SIZE = 64
D_RWKV = D_MODEL // 2
N_HEADS_RWKV = D_RWKV // HEAD_SIZE
D_CONV = D_MODEL // 2
CONV_WIDTH = 3
LORA_DECAY = 64
LORA_ICLR = 64
D_FF = 2816
N_DIR = 2
N_MOD = 9
N_NORMS = 6
EPS_RMS = 1e-6
EPS_GN = 64e-5
HALF_STEP = 0.5
D_IN = 4 * D_RWKV + 3 * D_CONV + 2 * D_MODEL

kernel_name = "bidir_rwkv7_shortconv_macaron_prefix_dit_step"


def rms_norm(x, g):
    xf = x.astype(jnp.float32)
    y = xf * lax.rsqrt(jnp.mean(xf * xf, -1, keepdims=True) + EPS_RMS)
    return (y * g.astype(jnp.float32)).astype(x.dtype)


def modulate(h, shift, scale):
    return h * (1 + scale) + shift


def swiglu(h, w13, w2):
    gt, up = jnp.split(h @ w13, 2, -1)
    return (jax.nn.silu(gt) * up) @ w2


def token_shift(h, direction):
    if direction == 0:
        return jnp.pad(h[:, :-1], ((0, 0), (1, 0), (0, 0)))
    return jnp.pad(h[:, 1:], ((0, 0), (0, 1), (0, 0)))


def conv3_centred(u, w, b, axis):
    n = u.shape[axis]
    pad = [(0, 0)] * u.ndim
    pad[axis] = (1, 1)
    up = jnp.pad(u, pad)
    left = lax.slice_in_dim(up, 0, n, axis=axis)
    mid = lax.slice_in_dim(up, 1, n + 1, axis=axis)
    right = lax.slice_in_dim(up, 2, n + 2, axis=axis)
    return left * w[0] + mid * w[1] + right * w[2] + b


def rwkv7_scan(r, w, k, v, kk, a, s0, reverse):
    def step(S, inp):
        r_t, w_t, k_t, v_t, kk_t, a_t = inp
        sa = jnp.einsum('bhij,bhj->bhi', S, -kk_t)
        S = (S * w_t[:, :, None, :]
             + sa[..., None] * (kk_t * a_t)[:, :, None, :]
             + v_t[..., None] * k_t[:, :, None, :])
        return S, jnp.einsum('bhij,bhj->bhi', S, r_t)
    xs = tuple(jnp.moveaxis(t, 1, 0) for t in (r, w, k, v, kk, a))
    s_final, y = lax.scan(step, s0, xs, reverse=reverse)
    return jnp.moveaxis(y, 0, 1), s_final


def token_mixer(h, s0, p, latent):
    B, T, _ = h.shape
    H, N = N_HEADS_RWKV, HEAD_SIZE
    f32 = jnp.float32
    sizes = (D_RWKV,) * 4 + (D_CONV,) * 3 + (D_MODEL,) * 2
    idx = [int(i) for i in np.cumsum(sizes)[:-1]]
    r, k, v, g, cb, cc, xc, gate_a, gate_b = jnp.split(h @ p['w_in'], idx, -1)

    def heads(t):
        return t.astype(f32).reshape(B, T, H, N)
    r_h, k_h, v_h = heads(r), heads(k), heads(v)
    kk = k_h * p['k_k'].astype(f32).reshape(H, N)
    kk = kk * lax.rsqrt(jnp.sum(kk * kk, -1, keepdims=True) + 1e-12)
    k_a = p['k_a'].astype(f32).reshape(H, N)
    r_k = p['r_k'].astype(f32)
    mu = p['mu_shift']
    ys, finals, bonus = [], [], []
    for d in range(N_DIR):
        sh = token_shift(h, d) - h
        xw = h + mu[d, 0] * sh
        xa = h + mu[d, 1] * sh
        w_log = -jax.nn.softplus(-(p['decay_w0'][d] + jnp.tanh(xw @ p['decay_w1'][d]) @ p['decay_w2'][d]).astype(f32)) - 0.5
        decay = heads(jnp.exp(-jnp.exp(w_log)))
        a_d = heads(jax.nn.sigmoid((p['iclr_a0'][d] + (xa @ p['iclr_a1'][d]) @ p['iclr_a2'][d]).astype(f32)))
        k_d = k_h * (1 + (a_d - 1) * k_a)
        y_d, s_d = rwkv7_scan(r_h, decay, k_d, v_h, kk, a_d, s0[:, d].astype(f32), reverse=(d == 1))
        ys.append(y_d)
        finals.append(s_d)
        bonus.append(jnp.sum(r_h * k_d * r_k, -1, keepdims=True))
    y = ys[0] + ys[1]
    mean = jnp.mean(y, -1, keepdims=True)
    var = jnp.var(y, -1, keepdims=True)
    y = ((y - mean) * lax.rsqrt(var + EPS_GN)).reshape(B, T, D_RWKV)
    y = y * p['gn_gain'].astype(f32) + p['gn_bias'].astype(f32)
    y = y + ((bonus[0] + bonus[1]) * v_h).reshape(B, T, D_RWKV)
    y = y * jax.nn.sigmoid(g.astype(f32))
    y_a = y.astype(h.dtype) @ p['w_branch_a']
    new_state = jnp.stack(finals, 1)

    u = cc * xc
    if latent:
        rows = T // GRID_W
        conv = conv3_centred(u.reshape(B, rows, GRID_W, D_CONV), p['conv_w'], p['conv_b'], 2).reshape(B, T, D_CONV)
    else:
        conv = conv3_centred(u, p['conv_w'], p['conv_b'], 1)
    y_b = (cb * conv) @ p['w_branch_b']

    merged = jax.nn.sigmoid(gate_a) * y_a + jax.nn.sigmoid(gate_b) * y_b
    return merged @ p['w_out'], new_state


def trunk_layer(x, mod, s0, p, latent):
    m = jnp.split(mod, N_MOD, -1)
    gn = p['norm_g']
    h = modulate(rms_norm(x, gn[0]), m[0], m[1])
    x = x + HALF_STEP * m[2] * rms_norm(swiglu(h, p['ffn1_w13'], p['ffn1_w2']), gn[1])
    h = modulate(rms_norm(x, gn[2]), m[3], m[4])
    out, s_fin = token_mixer(h, s0, p, latent)
    x = x + m[5] * rms_norm(out, gn[3])
    h = modulate(rms_norm(x, gn[4]), m[6], m[7])
    x = x + HALF_STEP * m[8] * rms_norm(swiglu(h, p['ffn2_w13'], p['ffn2_w2']), gn[5])
    return x, s_fin


def setup_inputs(seed: int = 0) -> dict:
    key = jax.random.key(seed)
    ks = jax.random.split(key, 32)
    nrm = lambda k, shape, s: jax.random.normal(k, shape, jnp.float32) * s
    L, D = DEPTH, D_MODEL
    return {
        'x_prompt': nrm(ks[0], (BATCH, SEQ, D), 1.0),
        'x_sample': nrm(ks[1], (DEC_BATCH, DEC_SEQ, D), 1.0),
        'c': nrm(ks[2], (DEC_BATCH, D), 1.0),
        'state_rwkv': nrm(ks[3], (DEC_BATCH, L, N_DIR, N_HEADS_RWKV, HEAD_SIZE, HEAD_SIZE), 0.5),
        'c_ctx': nrm(ks[4], (D,), 1.0),
        'w_mod': nrm(ks[5], (L, D, N_MOD * D), 0.5 * D ** -0.5),
        'b_mod': nrm(ks[6], (L, N_MOD * D), 0.02),
        'norm_g': 1.0 + nrm(ks[7], (L, N_NORMS, D), 0.02),
        'ffn1_w13': nrm(ks[8], (L, D, 2 * D_FF), D ** -0.5),
        'ffn1_w2': nrm(ks[9], (L, D_FF, D), D_FF ** -0.5),
        'ffn2_w13': nrm(ks[10], (L, D, 2 * D_FF), D ** -0.5),
        'ffn2_w2': nrm(ks[11], (L, D_FF, D), D_FF ** -0.5),
        'w_in': nrm(ks[12], (L, D, D_IN), D ** -0.5),
        'mu_shift': jax.random.uniform(ks[13], (L, N_DIR, 2, D), jnp.float32),
        'decay_w0': 1.0 + nrm(ks[14], (L, N_DIR, D_RWKV), 0.5),
        'decay_w1': nrm(ks[15], (L, N_DIR, D, LORA_DECAY), 0.3 * D ** -0.5),
        'decay_w2': nrm(ks[16], (L, N_DIR, LORA_DECAY, D_RWKV), 0.3 * LORA_DECAY ** -0.5),
        'iclr_a0': nrm(ks[17], (L, N_DIR, D_RWKV), 0.3),
        'iclr_a1': nrm(ks[18], (L, N_DIR, D, LORA_ICLR), 0.3 * D ** -0.5),
        'iclr_a2': nrm(ks[19], (L, N_DIR, LORA_ICLR, D_RWKV), 0.3 * LORA_ICLR ** -0.5),
        'k_k': 0.85 + nrm(ks[20], (L, D_RWKV), 0.05),
        'k_a': 1.0 + nrm(ks[21], (L, D_RWKV), 0.05),
        'r_k': nrm(ks[22], (L, N_HEADS_RWKV, HEAD_SIZE), 0.1),
        'gn_gain': 1.0 + nrm(ks[23], (L, D_RWKV), 0.02),
        'gn_bias': nrm(ks[24], (L, D_RWKV), 0.02),
        'conv_w': nrm(ks[25], (L, CONV_WIDTH, D_CONV), CONV_WIDTH ** -0.5),
        'conv_b': nrm(ks[26], (L, D_CONV), 0.02),
        'w_branch_a': nrm(ks[27], (L, D_RWKV, D), D_RWKV ** -0.5),
        'w_branch_b': nrm(ks[28], (L, D_CONV, D), D_CONV ** -0.5),
        'w_out': nrm(ks[29], (L, D, D), D ** -0.5),
    }


def reference(x_prompt, x_sample, c, state_rwkv, c_ctx, w_mod, b_mod, norm_g,
              ffn1_w13, ffn1_w2, ffn2_w13, ffn2_w2, w_in, mu_shift,
              decay_w0, decay_w1, decay_w2, iclr_a0, iclr_a1, iclr_a2,
              k_k, k_a, r_k, gn_gain, gn_bias, conv_w, conv_b,
              w_branch_a, w_branch_b, w_out):
    y_prompt, y_sample = x_prompt, x_sample
    ctx_states = []
    for l in range(DEPTH):
        p = {
            'norm_g': norm_g[l], 'ffn1_w13': ffn1_w13[l], 'ffn1_w2': ffn1_w2[l],
            'ffn2_w13': ffn2_w13[l], 'ffn2_w2': ffn2_w2[l], 'w_in': w_in[l],
            'mu_shift': mu_shift[l], 'decay_w0': decay_w0[l], 'decay_w1': decay_w1[l],
            'decay_w2': decay_w2[l], 'iclr_a0': iclr_a0[l], 'iclr_a1': iclr_a1[l],
            'iclr_a2': iclr_a2[l], 'k_k': k_k[l], 'k_a': k_a[l], 'r_k': r_k[l],
            'gn_gain': gn_gain[l], 'gn_bias': gn_bias[l], 'conv_w': conv_w[l],
            'conv_b': conv_b[l], 'w_branch_a': w_branch_a[l], 'w_branch_b': w_branch_b[l],
            'w_out': w_out[l],
        }
        mod_ctx = (jax.nn.silu(c_ctx)[None] @ w_mod[l] + b_mod[l])[:, None, :]
        mod_lat = (jax.nn.silu(c) @ w_mod[l] + b_mod[l])[:, None, :]
        s0_ctx = jnp.zeros((y_prompt.shape[0], N_DIR, N_HEADS_RWKV, HEAD_SIZE, HEAD_SIZE), jnp.float32)
        y_prompt, s_ctx = trunk_layer(y_prompt, mod_ctx, s0_ctx, p, latent=False)
        ctx_states.append(s_ctx)
        y_sample, _ = trunk_layer(y_sample, mod_lat, state_rwkv[:, l], p, latent=True)
    new_state_rwkv = jnp.stack(ctx_states, 1)
    return (y_prompt, y_sample, new_state_rwkv)
```

```python
import contextlib
import numpy as np
import concourse.bass as bass
import concourse.mybir as mybir
from concourse.bass_utils import run_bass_kernel_spmd

F32 = mybir.dt.float32
BF16 = mybir.dt.bfloat16
ALU = mybir.AluOpType
AF = mybir.ActivationFunctionType

D = 1024
DFF = 2816
import os
SUB = int(os.environ.get('KSUB', '99'))
KT = int(os.environ.get('KT', '1000000'))
KTC = [0]
MARK = {}
TAPS = [t for t in os.environ.get('KTAPS', '').split(',') if t]
KV = int(os.environ.get('KV', '0'))
N = 512
C = 64
NCH = N // C
C0 = float(np.exp(-0.5))


class _Op:
    __slots__ = ("idx", "eng", "fn", "deps", "chan", "chanpos", "needs_inc", "inc_count", "engpos")

    def __init__(self, idx, eng, fn, deps, chan):
        self.idx = idx
        self.eng = eng
        self.fn = fn
        self.deps = deps
        self.chan = chan
        self.chanpos = None
        self.needs_inc = False
        self.inc_count = None
        self.engpos = None


class MK:
    ENGS = ("pe", "act", "dve", "pool", "sp")

    def __init__(self, nc):
        self.nc = nc
        self.ops = []
        self.last_writer = {}
        self.readers = {}
        self.chan_count = {}

    def add(self, eng, fn, reads=(), writes=(), chan=None):
        idx = len(self.ops)
        deps = set()
        writes = list(writes)
        if chan is not None:
            writes.append(("__chan__", chan))
        for r in reads:
            w = self.last_writer.get(r)
            if w is not None:
                deps.add(w)
        for w in writes:
            lw = self.last_writer.get(w)
            if lw is not None:
                deps.add(lw)
            deps.update(self.readers.get(w, ()))
        op = _Op(idx, eng, fn, deps, chan)
        if chan is not None:
            op.chanpos = self.chan_count.get(chan, 0)
            self.chan_count[chan] = op.chanpos + 1
        self.ops.append(op)
        for r in reads:
            self.readers.setdefault(r, []).append(idx)
        for w in writes:
            self.last_writer[w] = idx
            self.readers[w] = []
        return idx

    def pe(self, fn, reads=(), writes=()):
        return self.add("pe", fn, reads, writes)

    def act(self, fn, reads=(), writes=()):
        return self.add("act", fn, reads, writes)

    def dve(self, fn, reads=(), writes=()):
        return self.add("dve", fn, reads, writes)

    def pool(self, fn, reads=(), writes=()):
        return self.add("pool", fn, reads, writes)

    def dma(self, eng, chan, fn, reads=(), writes=()):
        return self.add(eng, fn, reads, writes, chan=chan)

    def emit(self):
        nc = self.nc
        ops = self.ops
        per_eng = {e: [] for e in self.ENGS}
        for op in ops:
            op.engpos = len(per_eng[op.eng])
            per_eng[op.eng].append(op)

        def need_sem(op, d):
            if d.chan is not None:
                return True
            if d.eng != op.eng:
                return True
            if op.eng == "pe":
                return False
            return (op.engpos - d.engpos) <= 2

        for op in ops:
            for di in op.deps:
                d = ops[di]
                if d.chan is None and need_sem(op, d):
                    d.needs_inc = True
        cnt = {e: 0 for e in self.ENGS}
        for op in ops:
            if op.chan is None and op.needs_inc:
                cnt[op.eng] += 1
                op.inc_count = cnt[op.eng]
        chans = sorted(self.chan_count.keys())
        with contextlib.ExitStack() as st:
            esem = {e: st.enter_context(nc.semaphore("s_" + e)) for e in self.ENGS}
            csem = {c: st.enter_context(nc.semaphore("c_" + str(c))) for c in chans}
            block = st.enter_context(nc.Block())

            def run_engine(ename, eobj):
                waited = {}

                def wait(key, sem, val):
                    if waited.get(key, 0) >= val:
                        return
                    waited[key] = val
                    eobj.wait_ge(sem, val)

                for op in per_eng[ename]:
                    for di in sorted(op.deps):
                        d = ops[di]
                        if not need_sem(op, d):
                            continue
                        if d.chan is not None:
                            wait(("c", d.chan), csem[d.chan], 16 * (d.chanpos + 1))
                        else:
                            wait(("e", d.eng), esem[d.eng], d.inc_count)
                    ins = op.fn(eobj)
                    if op.chan is not None:
                        ins.then_inc(csem[op.chan], 16)
                    elif op.needs_inc:
                        ins.then_inc(esem[op.eng], 1)
                if ename == "sp":
                    for c in chans:
                        wait(("c", c), csem[c], 16 * self.chan_count[c])

            @block.tensor
            def _(e):
                run_engine("pe", e)

            @block.scalar
            def _(e):
                run_engine("act", e)

            @block.vector
            def _(e):
                run_engine("dve", e)

            @block.gpsimd
            def _(e):
                run_engine("pool", e)

            @block.sync
            def _(e):
                run_engine("sp", e)
        return {e: len(v) for e, v in per_eng.items()}


def _colize(v):
    v = np.asarray(v, np.float32).reshape(-1, 128)
    return np.ascontiguousarray(v.T)


COLS = {}
_off = 0
for _name, _w in [("cv", 16), ("bmod", 72), ("ng", 48), ("mu", 80), ("w0", 20), ("a0", 20), ("kk", 4), ("ka", 4),
                  ("rk", 4), ("gng", 4), ("gnb", 4), ("cw", 12), ("cb", 4), ("coef", 8), ("num", 8)]:
    COLS[_name] = _off
    _off += _w
NCOL = _off

CONSTS = {}
_off = 0
for _name, _w in [("ident", 128), ("bones", 128), ("maskG", 512), ("maskZ", 128), ("cmask", 512), ("id64", 64), ("bones64", 128)]:
    CONSTS[_name] = _off
    _off += _w
NCONST = _off


def _make_consts():
    cst = np.zeros((128, NCONST), np.float32)
    cst[:, CONSTS["ident"]:CONSTS["ident"] + 128] = np.eye(128, dtype=np.float32)
    bo = np.zeros((128, 128), np.float32)
    bo[:64, :64] = 1
    bo[64:, 64:] = 1
    cst[:, CONSTS["bones"]:CONSTS["bones"] + 128] = bo
    cst[:, CONSTS["bones64"]:CONSTS["bones64"] + 128] = bo / 64.0
    s = (np.arange(128) % 64)[:, None]
    t = np.arange(64)[None, :]
    mg = np.zeros((128, 2, 256), np.float32)
    for blk in range(2):
        mg[:, 0, blk * 128:blk * 128 + 64] = (t > s)
        mg[:, 0, blk * 128 + 64:blk * 128 + 128] = (t >= s)
        mg[:, 1, blk * 128:blk * 128 + 64] = (t < s)
        mg[:, 1, blk * 128 + 64:blk * 128 + 128] = (t <= s)
    cst[:, CONSTS["maskG"]:CONSTS["maskG"] + 512] = mg.reshape(128, 512)
    mz = np.zeros((128, 2, 64), np.float32)
    mz[:, 0, :] = (t < s)
    mz[:, 1, :] = (t > s)
    cst[:, CONSTS["maskZ"]:CONSTS["maskZ"] + 128] = mz.reshape(128, 128)
    cm = np.ones((128, 512), np.float32)
    cm[:, ::64] = 0
    cst[:, CONSTS["cmask"]:CONSTS["cmask"] + 512] = cm
    i64 = np.zeros((128, 64), np.float32)
    i64[np.arange(128), np.arange(128) % 64] = 1
    cst[:, CONSTS["id64"]:CONSTS["id64"] + 64] = i64
    return cst


def build_program(limit=10 ** 9, dbg_spec=None, mlimit=10 ** 9):
    nc = bass.Bass("TRN2", target_bir_lowering=False)
    stage = [0]

    def go():
        stage[0] += 1
        return stage[0] <= limit
    mstage = [0]

    def mgo():
        mstage[0] += 1
        return mstage[0] <= mlimit

    def din(name, shape):
        return nc.dram_tensor(name, list(shape), F32, kind="ExternalInput").ap()

    xg = din("xg", [5, N, D])
    colsd = din("cols", [128, NCOL])
    cstd = din("consts", [128, NCONST])
    std = din("st", [5, 4, 128, 64])
    w_mod = din("w_mod", [D, 9 * D])
    f1w13 = din("f1w13", [D, 2 * DFF])
    f1w2 = din("f1w2", [DFF, D])
    f2w13 = din("f2w13", [D, 2 * DFF])
    f2w2 = din("f2w2", [DFF, D])
    w_in = din("w_in", [D, 5632])
    w1c = din("w1c", [5, D, 128])
    w2c = din("w2c", [5, 128, 512])
    wba = din("wba", [512, D])
    wbb = din("wbb", [512, D])
    wout = din("wout", [D, D])
    yout = nc.dram_tensor("y", [2 * N, D], F32, kind="ExternalOutput").ap()
    nsout = nc.dram_tensor("ns", [2, 2, 8, 64, 64], F32, kind="ExternalOutput").ap()

    with contextlib.ExitStack() as stk:
        def sb(name, shape, dt=F32):
            return stk.enter_context(nc.sbuf_tensor(name, list(shape), dt))

        def ps(name, shape):
            return stk.enter_context(nc.psum_tensor(name, list(shape), F32))

        mk = MK(nc)
        cols = sb("cols_t", [128, NCOL])
        cst = sb("cst_t", [128, NCONST])
        onesb = sb("onesb", [128, 128], BF16)
        scb = sb("scb", [128, 8, 2], BF16)
        modT = sb("modT", [128, 72, 2])
        mods = sb("mods", [128, 2, 6, 8])
        xT = sb("xT", [128, 8, N])
        rstd = sb("rstd", [128, N])
        lnt = sb("lnt", [128, N])
        bdum = sb("bdum", [128, 1])
        NSLOT = 3
        slots = [sb("slot%d" % i, [128, 4096], BF16) for i in range(NSLOT)]
        hlast = sb("hlast", [128, 3, 8])
        hbF = sb("hbF", [128, 8])
        hbB = sb("hbB", [128, 8])
        hbA = sb("hbA", [128, 8])
        endst = sb("endst", [128, 3, 4, 64])
        Mst = sb("Mst", [128, 4, 64])
        Mtmp = sb("Mtmp", [128, 4, 64])
        SCR = 30720
        scr = sb("scr", [128, SCR])

        class Pool:
            def __init__(self):
                self.off = 0

            def f32(self, n):
                a = scr[:, self.off:self.off + n]
                self.off += n
                assert self.off <= SCR, self.off
                return a

            def bf16(self, n):
                m = (n + 1) // 2
                a = scr[:, self.off:self.off + m].bitcast(BF16)
                self.off += m
                assert self.off <= SCR, self.off
                return a

        psA = ps("psA", [128, 2, 512])
        psB = ps("psB", [128, 2, 512])
        psS = ps("psS", [128, 512])
        psC = ps("psC", [128, 2, 512])
        psT = ps("psT", [128, 512])

        def tap(name, ap2d, width, keys):
            if name not in TAPS:
                return
            dt_ = nc.dram_tensor("dbg_" + name, [128, width], F32, kind="ExternalOutput").ap()
            mk.dma("sp", "yout", lambda e: e.dma_start(out=dt_, in_=ap2d), reads=keys)

        cnum = lambda i: cols[:, COLS["num"] + i:COLS["num"] + i + 1]
        ccol = lambda name, i: cols[:, COLS[name] + i:COLS[name] + i + 1]
        ident = cst[:, CONSTS["ident"]:CONSTS["ident"] + 128]
        bones = cst[:, CONSTS["bones"]:CONSTS["bones"] + 128]
        bones64 = cst[:, CONSTS["bones64"]:CONSTS["bones64"] + 128]
        id64 = cst[:, CONSTS["id64"]:CONSTS["id64"] + 64]
        cmask = cst[:, CONSTS["cmask"]:CONSTS["cmask"] + 512]

        def maskG(d):
            o = CONSTS["maskG"] + d * 256
            return cst[:, o:o + 256]

        def maskZ(d):
            o = CONSTS["maskZ"] + d * 64
            return cst[:, o:o + 64]

        evac_rr = [0]

        def evac(out, in_, reads, writes):
            evac_rr[0] ^= 1
            if evac_rr[0]:
                mk.act(lambda e: e.copy(out=out, in_=in_), reads=reads, writes=writes)
            else:
                mk.dve(lambda e: e.tensor_copy(out=out, in_=in_), reads=reads, writes=writes)

        slot_rr = [0]

        def load_slab(pieces):
            s = slot_rr[0] % NSLOT
            slot_rr[0] += 1
            sl = slots[s]
            key = "W%d" % s
            for i, (vf, dap) in enumerate(pieces):
                mk.dma("pool", "w%d" % s, lambda e, vf=vf, dap=dap: e.dma_start(out=vf(sl), in_=dap), writes=[key])
            return sl, key

        def mm_group(out_ap, out_key, terms):
            n = len(terms)
            for i, (l, r, keys) in enumerate(terms):
                mk.pe(lambda e, l=l, r=r, i=i: e.matmul(out_ap, lhsT=l, rhs=r, start=(i == 0), stop=(i == n - 1)),
                      reads=keys, writes=(out_key if isinstance(out_key, list) else [out_key]))

        barrier_n = [0]

        def barrier():
            barrier_n[0] += 1
            mk.dve(lambda e: e.memset(bdum[:], 0.0), reads=["bdum"], writes=["SCR"])

        R = lambda *k: ["SCR"] + list(k)

        mk.dma("sp", "c0", lambda e: e.dma_start(out=cols[:], in_=colsd), writes=["cols"])
        mk.dma("sp", "c0", lambda e: e.dma_start(out=cst[:], in_=cstd), writes=["cst"])
        mk.dve(lambda e: e.memset(onesb[:], 1.0 / 1024.0), writes=["onesb"])
        cv0 = COLS["cv"]
        mk.act(lambda e: e.activation(out=scb[:].rearrange("p k j -> p (k j)"), in_=cols[:, cv0:cv0 + 16], func=AF.Silu),
               reads=["cols"], writes=["scb"])
        for s in range(18):
            sl, key = load_slab([(lambda t: t[:, 0:4096].rearrange("p (k n) -> p k n", k=8),
                                 w_mod[:, s * 512:(s + 1) * 512].rearrange("(k p) n -> p k n", p=128))])
            slv = sl[:, 0:4096].rearrange("p (k n) -> p k n", k=8)
            for tt in range(4):
                mm_group(psS[:, tt * 2:tt * 2 + 2], "psS",
                         [(slv[:, k, tt * 128:(tt + 1) * 128], scb[:, k, :], [key, "scb"]) for k in range(8)])
            for j in range(2):
                b0 = COLS["bmod"] + s * 4
                mk.dve(lambda e, s=s, j=j, b0=b0: e.tensor_tensor(
                    out=modT[:, s * 4:s * 4 + 4, j], in0=psS[:, 0:8].rearrange("p (t j) -> p t j", j=2)[:, :, j],
                    in1=cols[:, b0:b0 + 4], op=ALU.add), reads=["psS", "cols"], writes=["modT"])
        ng = lambda i: cols[:, COLS["ng"] + i * 8:COLS["ng"] + i * 8 + 8]
        m_ = lambda i, j: modT[:, i * 8:(i + 1) * 8, j]
        for j in range(2):
            for (dst, mi, gi, half) in [(0, 1, 0, None), (1, 2, 1, 0.5), (2, 4, 2, None), (3, 5, 3, 1.0), (4, 7, 4, None), (5, 8, 5, 0.5)]:
                if half is None:
                    mk.dve(lambda e, j=j, dst=dst, mi=mi, gi=gi: e.scalar_tensor_tensor(
                        out=mods[:, j, dst, :], in0=m_(mi, j), scalar=1.0, in1=ng(gi), op0=ALU.add, op1=ALU.mult),
                        reads=["modT", "cols"], writes=["mods"])
                else:
                    mk.dve(lambda e, j=j, dst=dst, mi=mi, gi=gi, half=half: e.scalar_tensor_tensor(
                        out=mods[:, j, dst, :], in0=m_(mi, j), scalar=half, in1=ng(gi), op0=ALU.mult, op1=ALU.mult),
                        reads=["modT", "cols"], writes=["mods"])

        def rms_rstd(src, src_key, sq, eps_idx):
            for k in range(8):
                if k % 2 == 0:
                    mk.act(lambda e, k=k: e.activation(out=sq[:, k, :], in_=src[:, k, :], func=AF.Square),
                           reads=R(src_key), writes=["sq%d" % k])
                else:
                    mk.dve(lambda e, k=k: e.tensor_tensor(out=sq[:, k, :], in0=src[:, k, :], in1=src[:, k, :], op=ALU.mult),
                           reads=R(src_key), writes=["sq%d" % k])
            mm_group(psS[:], "psS", [(onesb[:], sq[:, k, :], ["onesb", "sq%d" % k, "SCR"]) for k in range(8)])
            mk.act(lambda e: e.activation(out=lnt[:], in_=psS[:], func=AF.Ln, bias=cnum(eps_idx), scale=1.0),
                   reads=["psS", "cols"], writes=["lnt"])
            mk.act(lambda e: e.activation(out=rstd[:], in_=lnt[:], func=AF.Exp, scale=-0.5), reads=["lnt"], writes=["rstd"])

        def prenorm(j, ai, bi, hT, sq, tmp):
            rms_rstd(xT, "xT", sq, 0)
            for k in range(8):
                mk.dve(lambda e, k=k: e.scalar_tensor_tensor(out=tmp[:, k % 2, :], in0=xT[:, k, :], scalar=mods[:, j, ai, k:k + 1],
                                                             in1=rstd[:], op0=ALU.mult, op1=ALU.mult),
                       reads=R("xT", "mods", "rstd"), writes=["ptmp%d" % (k % 2)])
                mk.act(lambda e, k=k: e.activation(out=hT[:, k, :], in_=tmp[:, k % 2, :], func=AF.Identity,
                                                   bias=modT[:, bi * 8 + k, j:j + 1], scale=1.0),
                       reads=R("ptmp%d" % (k % 2), "modT"), writes=["hT%d" % k])

        def postnorm_residual(j, gi, oT, sq, tmp):
            rms_rstd(oT, "oT", sq, 0)
            for k in range(8):
                mk.dve(lambda e, k=k: e.scalar_tensor_tensor(out=tmp[:, k % 2, :], in0=oT[:, k, :], scalar=mods[:, j, gi, k:k + 1],
                                                             in1=rstd[:], op0=ALU.mult, op1=ALU.mult),
                       reads=R("oT", "mods", "rstd"), writes=["ptmp%d" % (k % 2)])
                mk.dve(lambda e, k=k: e.tensor_tensor(out=xT[:, k, :], in0=xT[:, k, :], in1=tmp[:, k % 2, :], op=ALU.add),
                       reads=R("ptmp%d" % (k % 2), "xT"), writes=["xT"])

        def ffn(w13, w2, hT, hid, oT, sgt):
            for s in range(11):
                sl, key = load_slab([
                    (lambda t: t[:, 0:4096].rearrange("p (k n) -> p k n", k=8)[:, :, 0:256],
                     w13[:, s * 256:(s + 1) * 256].rearrange("(k p) n -> p k n", p=128)),
                    (lambda t: t[:, 0:4096].rearrange("p (k n) -> p k n", k=8)[:, :, 256:512],
                     w13[:, DFF + s * 256:DFF + (s + 1) * 256].rearrange("(k p) n -> p k n", p=128))])
                slv = sl[:, 0:4096].rearrange("p (k n) -> p k n", k=8)
                for jj in range(2):
                    jt = 2 * s + jj
                    b = jt % 2
                    mm_group(psA[:, b, :], "psA%d" % b,
                             [(slv[:, k, jj * 128:(jj + 1) * 128], hT[:, k, :], [key, "hT%d" % k, "SCR"]) for k in range(8)])
                    mm_group(psB[:, b, :], KB(b),
                             [(slv[:, k, 256 + jj * 128:256 + (jj + 1) * 128], hT[:, k, :], [key, "hT%d" % k, "SCR"]) for k in range(8)])
                    mk.act(lambda e, b=b: e.activation(out=sgt[:, b, :], in_=psA[:, b, :], func=AF.Silu),
                           reads=R("psA%d" % b), writes=["sgt%d" % b])
                    mk.dve(lambda e, b=b, jt=jt: e.tensor_tensor(out=hid[:, jt, :], in0=sgt[:, b, :], in1=psB[:, b, :], op=ALU.mult),
                           reads=R("sgt%d" % b, *KB(b)), writes=["hid%d" % jt])
            for i in range(8):
                sl, key = load_slab([(lambda t: t[:, 0:2816].rearrange("p (j n) -> p j n", j=22),
                                     w2[:, i * 128:(i + 1) * 128].rearrange("(j p) n -> p j n", p=128))])
                slv = sl[:, 0:2816].rearrange("p (j n) -> p j n", j=22)
                b = i % 2
                mm_group(psA[:, b, :], "psA%d" % b, [(slv[:, jt, :], hid[:, jt, :], [key, "hid%d" % jt, "SCR"]) for jt in range(22)])
                evac(oT[:, i, :], psA[:, b, :], R("psA%d" % b), ["oT"])

        def KB(b):
            return ["psB0", "psB0b"] if b == 0 else ["psB1"]

        def mixer(kind, j, hT, aux_idx):
            pool = Pool()
            pool.off = 2048
            full = kind != "aux"
            nseq = 2 if kind == "prompt" else 1
            L = N // nseq
            cps = NCH // nseq
            rkv = pool.f32(12 * N).rearrange("p (t n) -> p t n", t=12)
            kk = pool.f32(4 * N).rearrange("p (t n) -> p t n", t=4)
            yT = pool.f32(4 * N).rearrange("p (t n) -> p t n", t=4)
            asum = pool.f32(4 * N).rearrange("p (t n) -> p t n", t=4)
            lh = pool.bf16(2 * N).rearrange("p (d n) -> p d n", d=2)
            w2b = pool.bf16(2 * 512).rearrange("p (d n) -> p d n", d=2)
            base_d = pool.off
            w1b = pool.bf16(2 * 1024).rearrange("p (d k n) -> p d k n", d=2, k=8)
            w1s = pool.bf16(2 * 1024).rearrange("p (d k n) -> p d k n", d=2, k=8)
            hk = ["hT%d" % k for k in range(8)]
            for s in range(3):
                sl, key = load_slab([(lambda t: t[:, 0:4096].rearrange("p (k n) -> p k n", k=8),
                                     w_in[:, s * 512:(s + 1) * 512].rearrange("(k p) n -> p k n", p=128))])
                slv = sl[:, 0:4096].rearrange("p (k n) -> p k n", k=8)
                for tt in range(4):
                    b = tt % 2
                    mm_group(psA[:, b, :], "psA%d" % b,
                             [(slv[:, k, tt * 128:(tt + 1) * 128], hT[:, k, :], [key, "hT%d" % k, "SCR"]) for k in range(8)])
                    evac(rkv[:, s * 4 + tt, :], psA[:, b, :], R("psA%d" % b), ["rkv%d" % (s * 4 + tt)])
            if not mgo():
                return
            sqk = pool.f32(2 * N).rearrange("p (b n) -> p b n", b=2)
            for pr in range(4):
                b = pr % 2
                mk.dve(lambda e, pr=pr: e.tensor_scalar(out=kk[:, pr, :], in0=rkv[:, 4 + pr, :], scalar1=ccol("kk", pr), scalar2=None,
                                                        op0=ALU.mult), reads=R("rkv%d" % (4 + pr), "cols"), writes=["kk%d" % pr])
                mk.dve(lambda e, pr=pr, b=b: e.tensor_tensor(out=sqk[:, b, :], in0=kk[:, pr, :], in1=kk[:, pr, :], op=ALU.mult),
                       reads=R("kk%d" % pr), writes=["sqk%d" % b])
                mm_group(psB[:, b, :], KB(b), [(bones, sqk[:, b, :], ["cst", "sqk%d" % b, "SCR"])])
                mk.act(lambda e, b=b: e.activation(out=sqk[:, b, :], in_=psB[:, b, :], func=AF.Ln, bias=cnum(1), scale=1.0),
                       reads=R("cols", *KB(b)), writes=["sqk%d" % b])
                mk.act(lambda e, b=b: e.activation(out=sqk[:, b, :], in_=sqk[:, b, :], func=AF.Exp, scale=-0.5),
                       reads=R("sqk%d" % b), writes=["sqk%d" % b])
                mk.dve(lambda e, pr=pr, b=b: e.tensor_tensor(out=kk[:, pr, :], in0=kk[:, pr, :], in1=sqk[:, b, :], op=ALU.mult),
                       reads=R("sqk%d" % b, "kk%d" % pr), writes=["kk%d" % pr])
            if not mgo():
                return
            dslots = [(0, 0), (1, 1)] if full else [(0, 2 + aux_idx)]
            sh = pool.bf16(8 * N).rearrange("p (k n) -> p k n", k=8)
            for di, (dt_, ds) in enumerate(dslots):
                mk.dma("pool", "wl", lambda e, di=di, ds=ds: e.dma_start(out=w1b[:, di, :, :], in_=w1c[ds].rearrange("(k p) n -> p k n", p=128)),
                       reads=["SCR"], writes=["w1b%d" % di])
                mk.dma("pool", "wl", lambda e, di=di, ds=ds: e.dma_start(out=w2b[:, di, :], in_=w2c[ds]), reads=["SCR"], writes=["w2b%d" % di])
                for x in range(2):
                    for k in range(8):
                        mc = COLS["mu"] + ds * 16 + x * 8 + k
                        mk.dve(lambda e, di=di, x=x, k=k, mc=mc: e.tensor_scalar(
                            out=w1s[:, di, k, x * 64:(x + 1) * 64], in0=w1b[:, di, k, x * 64:(x + 1) * 64],
                            scalar1=cols[:, mc:mc + 1], scalar2=None, op0=ALU.mult),
                            reads=R("w1b%d" % di, "cols"), writes=["w1s%d" % di])
                if dt_ == 0:
                    mk.dve(lambda e: e.tensor_tensor(out=sh[:, :, 1:N], in0=hT[:, :, 0:N - 1], in1=hT[:, :, 1:N], op=ALU.subtract),
                           reads=R(*hk), writes=["sh"])
                    for sq_ in range(nseq):
                        col = sq_ * L
                        if kind == "prompt" or (kind == "aux" and aux_idx == 0):
                            mk.dve(lambda e, col=col: e.tensor_scalar(out=sh[:, :, col], in0=hT[:, :, col], scalar1=-1.0, scalar2=None,
                                                                      op0=ALU.mult), reads=R("sh", *hk), writes=["sh"])
                        else:
                            hb = hbF if kind == "own" else hbA
                            hbk = "hbF" if kind == "own" else "hbA"
                            mk.dve(lambda e, col=col, hb=hb: e.tensor_tensor(out=sh[:, :, col], in0=hb[:, :], in1=hT[:, :, col], op=ALU.subtract),
                                   reads=R("sh", hbk, *hk), writes=["sh"])
                else:
                    mk.dve(lambda e: e.tensor_tensor(out=sh[:, :, 0:N - 1], in0=hT[:, :, 1:N], in1=hT[:, :, 0:N - 1], op=ALU.subtract),
                           reads=R(*hk), writes=["sh"])
                    for sq_ in range(nseq):
                        col = sq_ * L + L - 1
                        if kind == "prompt":
                            mk.dve(lambda e, col=col: e.tensor_scalar(out=sh[:, :, col], in0=hT[:, :, col], scalar1=-1.0, scalar2=None,
                                                                      op0=ALU.mult), reads=R("sh", *hk), writes=["sh"])
                        else:
                            mk.dve(lambda e, col=col: e.tensor_tensor(out=sh[:, :, col], in0=hbB[:, :], in1=hT[:, :, col], op=ALU.subtract),
                                   reads=R("sh", "hbB", *hk), writes=["sh"])
                b = di % 2
                mm_group(psB[:, b, :], KB(b),
                         [(w1b[:, di, k, :], hT[:, k, :], ["w1b%d" % di, "hT%d" % k, "SCR"]) for k in range(8)] +
                         [(w1s[:, di, k, :], sh[:, k, :], ["w1s%d" % di, "sh", "SCR"]) for k in range(8)])
                mk.act(lambda e, di=di, b=b: e.activation(out=lh[0:64, di, :], in_=psB[0:64, b, :], func=AF.Tanh),
                       reads=R(*KB(b)), writes=["lh%d" % di])
                mk.act(lambda e, di=di, b=b: e.copy(out=lh[64:128, di, :], in_=psB[64:128, b, :]),
                       reads=R(*KB(b)), writes=["lh%d" % di])
            if kind == "aux":
                mk.dve(lambda e: e.tensor_copy(out=hlast[:, aux_idx, :], in_=hT[:, :, N - 1]), reads=R(*hk), writes=["hlast%d" % aux_idx])

            if not mgo():
                return
            v3 = lambda ap: ap.rearrange("p (c n) -> p c n", c=8)
            psG = psA[:].rearrange("p a (h n) -> p (a h) n", h=2)
            psZ = psB[:].rearrange("p a (h n) -> p (a h) n", h=8)
            psZv = lambda a, hh: psZ[:, a * 4 + hh, :]
            ZK = ["psB0", "psB0b", "psB1"]
            psTv = psT[:].rearrange("p (a h n) -> p a h n", a=2, h=4)
            psCv = psC[:].rearrange("p a (h n) -> p a h n", h=8)
            psSv = psS[:].rearrange("p (h n) -> p h n", h=8)
            for di, (dt_, ds) in enumerate(dslots):
                order = list(range(NCH)) if dt_ == 0 else list(range(NCH - 1, -1, -1))
                barrier()
                pool.off = base_d
                AR = pool.f32(4 * 8 * 128).rearrange("p (q c n) -> p q c n", q=4, c=8)
                BK = pool.f32(4 * 8 * 128).rearrange("p (q c n) -> p q c n", q=4, c=8)
                Pend = pool.f32(32).rearrange("p (q c) -> p q c", q=4)
                base_t = pool.off
                sw = pool.f32(2 * N).rearrange("p (q n) -> p q n", q=2)
                av = pool.f32(2 * N).rearrange("p (q n) -> p q n", q=2)
                cs = pool.f32(2 * N).rearrange("p (q n) -> p q n", q=2)
                Lx = pool.f32(2 * N).rearrange("p (q n) -> p q n", q=2)
                Ep = pool.f32(2 * N).rearrange("p (q n) -> p q n", q=2)
                t1 = pool.f32(2 * N).rearrange("p (q n) -> p q n", q=2)
                for hp in range(2):
                    for ql in range(2):
                        pr = 2 * hp + ql
                        b = ql
                        mm_group(psA[:, b, :], "psA%d" % b, [(w2b[0:64, di, pr * 128:(pr + 1) * 128], lh[0:64, di, :], ["w2b%d" % di, "lh%d" % di, "SCR"])])
                        mm_group(psB[:, b, :], KB(b), [(w2b[64:128, di, pr * 128:(pr + 1) * 128], lh[64:128, di, :], ["w2b%d" % di, "lh%d" % di, "SCR"])])
                        mk.act(lambda e, pr=pr, ql=ql, b=b, ds=ds: e.activation(out=sw[:, ql, :], in_=psA[:, b, :], func=AF.Sigmoid,
                                                                               bias=ccol("w0", ds * 4 + pr), scale=1.0),
                               reads=R("psA%d" % b, "cols"), writes=["sw%d" % ql])
                        mk.act(lambda e, pr=pr, ql=ql, b=b, ds=ds: e.activation(out=av[:, ql, :], in_=psB[:, b, :], func=AF.Sigmoid,
                                                                               bias=ccol("a0", ds * 4 + pr), scale=1.0),
                               reads=R("cols", *KB(b)), writes=["av%d" % ql])
                        if full:
                            if di == 0:
                                mk.dve(lambda e, pr=pr, ql=ql: e.tensor_copy(out=asum[:, pr, :], in_=av[:, ql, :]), reads=R("av%d" % ql), writes=["asum%d" % pr])
                            else:
                                mk.dve(lambda e, pr=pr, ql=ql: e.tensor_tensor(out=asum[:, pr, :], in0=asum[:, pr, :], in1=av[:, ql, :], op=ALU.add),
                                       reads=R("av%d" % ql, "asum%d" % pr), writes=["asum%d" % pr])
                        mk.dve(lambda e, ql=ql: e.tensor_tensor_scan(out=cs[:, ql, :], data0=cmask, data1=sw[:, ql, :], initial=0.0,
                                                                     op0=ALU.mult, op1=ALU.add), reads=R("sw%d" % ql, "cst"), writes=["cs%d" % ql])
                        if dt_ == 0:
                            mk.dve(lambda e, ql=ql: e.tensor_tensor(out=Lx[:, ql, :], in0=cs[:, ql, :], in1=sw[:, ql, :], op=ALU.subtract),
                                   reads=R("cs%d" % ql, "sw%d" % ql), writes=["Lx%d" % ql])
                        else:
                            mk.dve(lambda e, ql=ql: e.tensor_tensor(out=v3(Lx[:, ql, :]), in0=v3(cs[:, ql, :])[:, :, 63:64].to_broadcast([128, 8, 64]),
                                                                    in1=v3(cs[:, ql, :]), op=ALU.subtract),
                                   reads=R("cs%d" % ql), writes=["Lx%d" % ql])
                            mk.dve(lambda e, ql=ql: e.tensor_tensor(out=cs[:, ql, :], in0=Lx[:, ql, :], in1=sw[:, ql, :], op=ALU.add),
                                   reads=R("Lx%d" % ql, "sw%d" % ql), writes=["cs%d" % ql])
                        mk.act(lambda e, ql=ql: e.activation(out=Ep[:, ql, :], in_=cs[:, ql, :], func=AF.Exp, scale=-C0), reads=R("cs%d" % ql), writes=["Ep%d" % ql])
                        mk.act(lambda e, ql=ql: e.activation(out=Lx[:, ql, :], in_=Lx[:, ql, :], func=AF.Exp, scale=-C0), reads=R("Lx%d" % ql), writes=["Lx%d" % ql])
                        mk.act(lambda e, ql=ql: e.activation(out=cs[:, ql, :], in_=cs[:, ql, :], func=AF.Exp, scale=C0), reads=R("cs%d" % ql, "Ep%d" % ql), writes=["cs%d" % ql])
                        pcol = 63 if dt_ == 0 else 0
                        mk.dve(lambda e, pr=pr, ql=ql, pcol=pcol: e.tensor_copy(out=Pend[:, pr, :], in_=v3(Ep[:, ql, :])[:, :, pcol]), reads=R("Ep%d" % ql), writes=["Pend"])
                        mk.dve(lambda e, pr=pr, ql=ql: e.scalar_tensor_tensor(out=AR[:, pr, :, 0:64], in0=v3(kk[:, pr, :]), scalar=-1.0, in1=v3(Lx[:, ql, :]),
                                                                              op0=ALU.mult, op1=ALU.mult), reads=R("kk%d" % pr, "Lx%d" % ql), writes=["AR%d" % pr])
                        mk.dve(lambda e, pr=pr, ql=ql: e.tensor_tensor(out=AR[:, pr, :, 64:128], in0=v3(rkv[:, pr, :]), in1=v3(Ep[:, ql, :]), op=ALU.mult),
                               reads=R("rkv%d" % pr, "Ep%d" % ql), writes=["AR%d" % pr])
                        mk.dve(lambda e, pr=pr, ql=ql: e.tensor_tensor(out=t1[:, ql, :], in0=kk[:, pr, :], in1=av[:, ql, :], op=ALU.mult),
                               reads=R("kk%d" % pr, "av%d" % ql), writes=["t1%d" % ql])
                        mk.dve(lambda e, pr=pr, ql=ql: e.tensor_tensor(out=BK[:, pr, :, 0:64], in0=v3(t1[:, ql, :]), in1=v3(cs[:, ql, :]), op=ALU.mult),
                               reads=R("t1%d" % ql, "cs%d" % ql), writes=["BK%d" % pr])
                        mk.dve(lambda e, pr=pr, ql=ql: e.tensor_scalar(out=t1[:, ql, :], in0=av[:, ql, :], scalar1=cnum(4), scalar2=ccol("ka", pr),
                                                                       op0=ALU.subtract, op1=ALU.mult), reads=R("av%d" % ql, "cols", "t1%d" % ql), writes=["t1%d" % ql])
                        mk.dve(lambda e, pr=pr, ql=ql: e.scalar_tensor_tensor(out=t1[:, ql, :], in0=t1[:, ql, :], scalar=1.0, in1=rkv[:, 4 + pr, :],
                                                                              op0=ALU.add, op1=ALU.mult), reads=R("t1%d" % ql, "rkv%d" % (4 + pr)), writes=["t1%d" % ql])
                        mk.dve(lambda e, pr=pr, ql=ql: e.tensor_tensor(out=BK[:, pr, :, 64:128], in0=v3(t1[:, ql, :]), in1=v3(cs[:, ql, :]), op=ALU.mult),
                               reads=R("t1%d" % ql, "cs%d" % ql), writes=["BK%d" % pr])
                if not mgo():
                    return
                barrier()
                pool.off = base_t
                Gm = [pool.f32(4 * 256).rearrange("p (h n) -> p h n", h=4) for _ in range(2)]
                ZZ = [pool.f32(2 * 4 * 64).rearrange("p (a h n) -> p a h n", a=2, h=4) for _ in range(2)]
                Qt = [pool.f32(4 * 64).rearrange("p (h n) -> p h n", h=4) for _ in range(2)]
                TOK = [pool.f32(3 * 4 * 64).rearrange("p (a h n) -> p a h n", a=3, h=4) for _ in range(2)]
                Wsb = pool.f32(4 * 64).rearrange("p (h n) -> p h n", h=4)
                Usb = pool.f32(4 * 64).rearrange("p (h n) -> p h n", h=4)
                unit = [0]

                def heads():
                    for q in range(4):
                        for e_ in range(2):
                            yield q, 64 * e_

                def tseries_stages(c, dt_=dt_):
                    u = unit[0] % 2
                    unit[0] += 1
                    G, Z2, Q, TK = Gm[u], ZZ[u], Qt[u], TOK[u]
                    gk, zk, qk, tk = "Gm%d" % u, "ZZ%d" % u, "Q%d" % u, "TOK%d" % u
                    stages = []

                    def st_g():
                        for q, fo in heads():
                            mm_group(psG[fo:fo + 64, q, 0:128], "psA%d" % (q // 2),
                                     [(BK[fo:fo + 64, q, c, 0:64], AR[fo:fo + 64, q, c, :], ["BK%d" % q, "AR%d" % q, "SCR"])])
                            mm_group(psG[fo:fo + 64, q, 128:256], "psA%d" % (q // 2),
                                     [(BK[fo:fo + 64, q, c, 64:128], AR[fo:fo + 64, q, c, :], ["BK%d" % q, "AR%d" % q, "SCR"])])
                            mm_group(psZv(0, q)[fo:fo + 64, :], "psB0",
                                     [(AR[fo:fo + 64, q, c, 0:64], BK[fo:fo + 64, q, c, 0:64], ["BK%d" % q, "AR%d" % q, "SCR"])])
                        mk.dve(lambda e: e.tensor_tensor(out=G[:], in0=psG, in1=maskG(dt_).unsqueeze(1).to_broadcast([128, 4, 256]), op=ALU.mult),
                               reads=R("psA0", "psA1", "cst"), writes=[gk])
                        mk.dve(lambda e: e.tensor_tensor(out=Z2[:, 1, :, :], in0=psZ[:, 0:4, :], in1=maskZ(dt_).unsqueeze(1).to_broadcast([128, 4, 64]),
                                                         op=ALU.mult), reads=R("psB0", "cst"), writes=[zk])
                        mk.act(lambda e: e.copy(out=Z2[:, 0, :, :], in_=G[:, :, 0:64]), reads=R(gk), writes=[zk])
                        mk.dve(lambda e: e.tensor_tensor(out=Q[:], in0=G[:, :, 0:64], in1=id64.unsqueeze(1).to_broadcast([128, 4, 64]), op=ALU.add),
                               reads=R(gk, "cst"), writes=[qk])
                    stages.append(st_g)

                    def mk_burst(lev):
                        def st():
                            for q, fo in heads():
                                idb = ident[fo:fo + 64, fo:fo + 64]
                                if lev <= 4:
                                    mm_group(psZv(1, q)[fo:fo + 64, :], "psB0b", [(Z2[fo:fo + 64, 1, q, :], Z2[fo:fo + 64, 0, q, :], [zk, "SCR"])])
                                mm_group(psZv(2, q)[fo:fo + 64, :], "psB1", [(Z2[fo:fo + 64, 0, q, :], Z2[fo:fo + 64, 1, q, :], [zk, "SCR"])])
                                if lev >= 2:
                                    mm_group(psZv(0, q)[fo:fo + 64, :], "psB0", [(idb, Q[fo:fo + 64, q, :], ["cst", qk, "SCR"]),
                                                                                 (Z2[fo:fo + 64, 1, q, :], Q[fo:fo + 64, q, :], [zk, qk, "SCR"])])
                            if lev >= 2:
                                mk.act(lambda e: e.copy(out=Q[:], in_=psZ[:, 0:4, :]), reads=R("psB0"), writes=[qk])
                            if lev <= 4:
                                mk.act(lambda e: e.copy(out=Z2[:, 0, :, :], in_=psZ[:, 4:8, :]), reads=R("psB0b"), writes=[zk])
                            mk.act(lambda e: e.copy(out=Z2[:, 1, :, :], in_=psZ[:, 8:12, :]), reads=R("psB1"), writes=[zk])
                        return st
                    for lev in range(1, 6):
                        stages.append(mk_burst(lev))

                    def st_last():
                        for q, fo in heads():
                            idb = ident[fo:fo + 64, fo:fo + 64]
                            mm_group(psZv(0, q)[fo:fo + 64, :], "psB0", [(idb, Q[fo:fo + 64, q, :], ["cst", qk, "SCR"]),
                                                                         (Z2[fo:fo + 64, 1, q, :], Q[fo:fo + 64, q, :], [zk, qk, "SCR"])])
                        mk.act(lambda e: e.copy(out=Q[:], in_=psZ[:, 0:4, :]), reads=R("psB0"), writes=[qk])
                        for q, fo in heads():
                            idb = ident[fo:fo + 64, fo:fo + 64]
                            mm_group(psTv[fo:fo + 64, 0, q, :], "psT", [(BK[fo:fo + 64, q, c, 0:64], idb, ["BK%d" % q, "cst", "SCR"])])
                            mm_group(psTv[fo:fo + 64, 1, q, :], "psT", [(BK[fo:fo + 64, q, c, 64:128], idb, ["BK%d" % q, "cst", "SCR"])])
                            mm_group(psCv[fo:fo + 64, 1, 4 + q, :], "psCv", [(rkv[fo:fo + 64, 8 + q, c * 64:(c + 1) * 64], idb,
                                                                             ["rkv%d" % (8 + q), "cst", "SCR"])])
                        mk.act(lambda e: e.copy(out=TK[:, 0:2, :, :], in_=psTv), reads=R("psT"), writes=[tk])
                        mk.dve(lambda e: e.tensor_copy(out=TK[:, 2, :, :], in_=psCv[:, 1, 4:8, :]), reads=R("psCv"), writes=[tk])
                    stages.append(st_last)
                    return stages, (G, Q, TK, gk, qk, tk)

                def chain_stages(c, bufs, dt_=dt_, di=di, order=order):
                    G, Q, TK, gk, qk, tk = bufs
                    seq = c // cps
                    pos = order.index(c) % cps
                    stages = []

                    def st_w():
                        if pos == 0:
                            if kind == "prompt":
                                mk.dve(lambda e: e.memset(Mst[:], 0.0), reads=R(), writes=["Mst"])
                            elif kind == "aux":
                                if aux_idx == 0:
                                    mk.dve(lambda e: e.tensor_copy(out=Mst[:], in_=stt[:, 2, :, :]), reads=R("stt"), writes=["Mst"])
                                else:
                                    cc_ = COLS["coef"] + (aux_idx - 1)
                                    mk.dve(lambda e: e.scalar_tensor_tensor(out=Mst[:], in0=endst[:, aux_idx - 1, :, :], scalar=cols[:, cc_:cc_ + 1],
                                                                            in1=stt[:, 2 + aux_idx, :, :], op0=ALU.mult, op1=ALU.add),
                                           reads=R("stt", "end%d" % (aux_idx - 1), "cols"), writes=["Mst"])
                            else:
                                if dt_ == 0:
                                    mk.dve(lambda e: e.tensor_copy(out=Mst[:], in_=stt[:, 0, :, :]), reads=R("stt"), writes=["Mst"])
                                    for a in range(3):
                                        cc_ = COLS["coef"] + 2 + a
                                        mk.dve(lambda e, a=a, cc_=cc_: e.scalar_tensor_tensor(out=Mst[:], in0=endst[:, a, :, :], scalar=cols[:, cc_:cc_ + 1],
                                                                                          in1=Mst[:], op0=ALU.mult, op1=ALU.add),
                                               reads=R("end%d" % a, "cols", "Mst"), writes=["Mst"])
                                else:
                                    cc_ = COLS["coef"] + 5
                                    mk.dve(lambda e: e.scalar_tensor_tensor(out=Mst[:], in0=endst[:, 2, :, :], scalar=cols[:, cc_:cc_ + 1],
                                                                            in1=stt[:, 1, :, :], op0=ALU.mult, op1=ALU.add),
                                           reads=R("stt", "end2", "cols"), writes=["Mst"])
                        for q, fo in heads():
                            mm_group(psCv[fo:fo + 64, 0, q, :], "psC0w",
                                     [(AR[fo:fo + 64, q, c, 0:64], Mst[fo:fo + 64, q, :], ["AR%d" % q, "Mst", "SCR"]),
                                      (G[fo:fo + 64, q, 128:192], TK[fo:fo + 64, 2, q, :], [gk, tk, "SCR"])])
                        mk.act(lambda e: e.copy(out=Wsb[:], in_=psCv[:, 0, 0:4, :]), reads=R("psC0w"), writes=["Wsb"])
                    stages.append(st_w)

                    def st_u():
                        for q, fo in heads():
                            mm_group(psCv[fo:fo + 64, 0, 4 + q, :], "psC0u", [(Q[fo:fo + 64, q, :], Wsb[fo:fo + 64, q, :], [qk, "Wsb", "SCR"])])
                        mk.dve(lambda e: e.tensor_copy(out=Usb[:], in_=psCv[:, 0, 4:8, :]), reads=R("psC0u"), writes=["Usb"])
                    stages.append(st_u)

                    def st_ym():
                        for q, fo in heads():
                            if full and KV != 5:
                                mm_group(psSv[fo:fo + 64, q, :], "psS",
                                         [(Mst[fo:fo + 64, q, :], AR[fo:fo + 64, q, c, 64:128], ["Mst", "AR%d" % q, "SCR"]),
                                          (Usb[fo:fo + 64, q, :], G[fo:fo + 64, q, 64:128], ["Usb", gk, "SCR"]),
                                          (TK[fo:fo + 64, 2, q, :], G[fo:fo + 64, q, 192:256], [tk, gk, "SCR"])])
                            mm_group(psSv[fo:fo + 64, 4 + q, :], "psS",
                                     [(TK[fo:fo + 64, 0, q, :], Usb[fo:fo + 64, q, :], [tk, "Usb", "SCR"]),
                                      (TK[fo:fo + 64, 1, q, :], TK[fo:fo + 64, 2, q, :], [tk, "SCR"])])
                        if full and KV != 6:
                            ydst = yT[:, :, c * 64:(c + 1) * 64]
                            if di == 0 and KV != 9:
                                mk.dve(lambda e: e.tensor_copy(out=ydst, in_=psSv[:, 0:4, :]), reads=R("psS"), writes=["yT"])
                            elif di == 0 and KV == 8:
                                mk.act(lambda e: e.copy(out=Wsb[:], in_=psSv[:, 0:4, :]), reads=R("psS", "Wsb"), writes=["Wsb"])
                                mk.dve(lambda e: e.tensor_copy(out=ydst, in_=Wsb[:]), reads=R("Wsb"), writes=["yT"])
                            elif di == 0:
                                mk.act(lambda e: e.copy(out=ydst, in_=psSv[:, 0:4, :]), reads=R("psS"), writes=["yT"])
                            else:
                                mk.act(lambda e: e.copy(out=Wsb[:], in_=psSv[:, 0:4, :]), reads=R("psS", "Wsb"), writes=["Wsb"])
                                mk.dve(lambda e: e.tensor_tensor(out=ydst, in0=ydst, in1=Wsb[:], op=ALU.add), reads=R("Wsb", "yT"), writes=["yT"])
                        mk.dve(lambda e: e.tensor_tensor(out=Mtmp[:], in0=Mst[:], in1=psSv[:, 4:8, :], op=ALU.add),
                               reads=R("psS", "Mst"), writes=["Mtmp"])
                        mk.dve(lambda e: e.tensor_tensor(out=Mst[:], in0=Mtmp[:], in1=Pend[:, :, c:c + 1].to_broadcast([128, 4, 64]), op=ALU.mult),
                               reads=R("Mtmp", "Pend"), writes=["Mst"])
                        if pos == cps - 1:
                            if kind == "aux":
                                mk.act(lambda e: e.copy(out=endst[:, aux_idx, :, :], in_=Mst[:]), reads=R("Mst"), writes=["end%d" % aux_idx])
                            elif kind == "prompt":
                                for q in range(4):
                                    mm_group(psT[0:64, q * 128:(q + 1) * 128], "psT", [(Mst[:, q, :], ident, ["Mst", "cst", "SCR"])])
                                mk.act(lambda e: e.copy(out=nsb[0:64, seq, di, :, :], in_=psT[0:64, :].rearrange("p (a n) -> p a n", a=4)),
                                       reads=R("psT"), writes=["nsb"])
                    stages.append(st_ym)
                    return stages

                prev = None
                for idx_c in range(len(order) + 1):
                    A, bufsA = ([], None)
                    if idx_c < len(order):
                        A, bufsA = tseries_stages(order[idx_c])
                    Bs = []
                    if prev is not None:
                        Bs = chain_stages(prev[0], prev[1])
                    for i in range(max(len(A), len(Bs))):
                        if i < len(A):
                            KTC[0] += 1
                            if KTC[0] <= KT:
                                A[i]()
                        if i < len(Bs):
                            KTC[0] += 1
                            if KTC[0] <= KT:
                                Bs[i]()
                    prev = (order[idx_c], bufsA) if idx_c < len(order) else None
            if not full:
                return
            tap("yT_" + kind, yT.rearrange("p q n -> p (q n)"), 4 * N, R("yT"))
            tap("rkv_" + kind, rkv.rearrange("p q n -> p (q n)"), 12 * N, R(*["rkv%d" % i for i in range(12)]))
            barrier()
            MARK[kind] = len(mk.ops)
            pool.off = base_d
            bv = pool.f32(4 * N).rearrange("p (q n) -> p q n", q=4)
            tA = pool.f32(2 * N).rearrange("p (q n) -> p q n", q=2)
            tB = pool.f32(2 * N).rearrange("p (q n) -> p q n", q=2)
            for pr in range(4):
                b = pr % 2
                mk.dve(lambda e, pr=pr, b=b: e.tensor_scalar(out=tA[:, b, :], in0=asum[:, pr, :], scalar1=cnum(5), scalar2=ccol("ka", pr), op0=ALU.subtract, op1=ALU.mult),
                       reads=R("asum%d" % pr, "cols", "tA%d" % b), writes=["tA%d" % b])
                mk.dve(lambda e, pr=pr, b=b: e.scalar_tensor_tensor(out=tA[:, b, :], in0=tA[:, b, :], scalar=2.0, in1=rkv[:, 4 + pr, :], op0=ALU.add, op1=ALU.mult),
                       reads=R("tA%d" % b, "rkv%d" % (4 + pr)), writes=["tA%d" % b])
                mk.dve(lambda e, pr=pr, b=b: e.scalar_tensor_tensor(out=tA[:, b, :], in0=tA[:, b, :], scalar=ccol("rk", pr), in1=rkv[:, pr, :], op0=ALU.mult, op1=ALU.mult),
                       reads=R("tA%d" % b, "rkv%d" % pr, "cols"), writes=["tA%d" % b])
                mm_group(psA[:, b, :], "psA%d" % b, [(bones, tA[:, b, :], ["cst", "tA%d" % b, "SCR"])])
                mk.dve(lambda e, pr=pr, b=b: e.tensor_tensor(out=bv[:, pr, :], in0=psA[:, b, :], in1=rkv[:, 8 + pr, :], op=ALU.mult),
                       reads=R("psA%d" % b, "rkv%d" % (8 + pr)), writes=["bv%d" % pr])
                mm_group(psB[:, b, :], KB(b), [(bones64, yT[:, pr, :], ["cst", "yT", "SCR"])])
                mk.act(lambda e, b=b: e.copy(out=tB[:, b, :], in_=psB[:, b, :]), reads=R("tB%d" % b, *KB(b)), writes=["tB%d" % b])
                mk.dve(lambda e, pr=pr, b=b: e.tensor_tensor(out=yT[:, pr, :], in0=yT[:, pr, :], in1=tB[:, b, :], op=ALU.subtract), reads=R("tB%d" % b, "yT"), writes=["yT"])
                mk.act(lambda e, pr=pr, b=b: e.activation(out=tA[:, b, :], in_=yT[:, pr, :], func=AF.Square), reads=R("yT", "tA%d" % b), writes=["tA%d" % b])
                mm_group(psA[:, b, :], "psA%d" % b, [(bones64, tA[:, b, :], ["cst", "tA%d" % b, "SCR"])])
                mk.act(lambda e, b=b: e.activation(out=tB[:, b, :], in_=psA[:, b, :], func=AF.Ln, bias=cnum(2), scale=1.0), reads=R("cols", "tB%d" % b, "psA%d" % b), writes=["tB%d" % b])
                mk.act(lambda e, b=b: e.activation(out=tB[:, b, :], in_=tB[:, b, :], func=AF.Exp, scale=-0.5), reads=R("tB%d" % b), writes=["tB%d" % b])
                mk.dve(lambda e, pr=pr, b=b: e.scalar_tensor_tensor(out=yT[:, pr, :], in0=yT[:, pr, :], scalar=ccol("gng", pr), in1=tB[:, b, :], op0=ALU.mult, op1=ALU.mult),
                       reads=R("tB%d" % b, "yT", "cols"), writes=["yT"])
                mk.dve(lambda e, pr=pr: e.scalar_tensor_tensor(out=yT[:, pr, :], in0=yT[:, pr, :], scalar=ccol("gnb", pr), in1=bv[:, pr, :], op0=ALU.add, op1=ALU.add),
                       reads=R("bv%d" % pr, "yT", "cols"), writes=["yT"])
            tap("yn_" + kind, yT.rearrange("p q n -> p (q n)"), 4 * N, R("yT"))
            barrier()
            pool.off = 2048
            yA = pool.f32(8 * N).rearrange("p (q n) -> p q n", q=8)
            pool.off = 12288
            yB = pool.f32(8 * N).rearrange("p (q n) -> p q n", q=8)
            tA2 = pool.f32(2 * N).rearrange("p (q n) -> p q n", q=2)
            tB2 = pool.f32(2 * N).rearrange("p (q n) -> p q n", q=2)
            yaT = pool.bf16(4 * N).rearrange("p (q n) -> p q n", q=4)
            ybT = pool.bf16(4 * N).rearrange("p (q n) -> p q n", q=4)
            cbT = pool.bf16(4 * N).rearrange("p (q n) -> p q n", q=4)
            ccT = pool.f32(4 * N).rearrange("p (q n) -> p q n", q=4)
            mgT = pool.bf16(8 * N).rearrange("p (q n) -> p q n", q=8)
            rl = 64 if kind == "own" else L
            r3 = lambda ap: ap.rearrange("p (r n) -> p r n", n=rl)

            def branch(wmat, srcT, srckey, dst, dstkey):
                for half in range(2):
                    slw, keyw = load_slab([(lambda t: t[:, 0:2048].rearrange("p (k n) -> p k n", k=4),
                                            wmat[:, half * 512:(half + 1) * 512].rearrange("(k p) n -> p k n", p=128))])
                    slwv = slw[:, 0:2048].rearrange("p (k n) -> p k n", k=4)
                    for tt in range(4):
                        o = half * 4 + tt
                        b = tt % 2
                        mm_group(psB[:, b, :], KB(b), [(slwv[:, k, tt * 128:(tt + 1) * 128], srcT[:, k, :], [keyw, srckey % k, "SCR"]) for k in range(4)])
                        evac(dst[:, o, :], psB[:, b, :], R(*KB(b)), [dstkey % o])

            for s in range(3, 11):
                sl, key = load_slab([(lambda t: t[:, 0:4096].rearrange("p (k n) -> p k n", k=8),
                                     w_in[:, s * 512:(s + 1) * 512].rearrange("(k p) n -> p k n", p=128))])
                slv = sl[:, 0:4096].rearrange("p (k n) -> p k n", k=8)
                for tt in range(4):
                    b = tt % 2
                    mm_group(psA[:, b, :], "psA%d" % b,
                             [(slv[:, k, tt * 128:(tt + 1) * 128], hT[:, k, :], [key, "hT%d" % k, "SCR"]) for k in range(8)])
                    pa = psA[:, b, :]
                    pk = "psA%d" % b
                    tb = tt % 2
                    if s == 3:
                        mk.act(lambda e, pa=pa, tb=tb: e.activation(out=tA2[:, tb, :], in_=pa, func=AF.Sigmoid), reads=R(pk, "tA2%d" % tb), writes=["tA2%d" % tb])
                        mk.dve(lambda e, tt=tt, tb=tb: e.tensor_tensor(out=yaT[:, tt, :], in0=yT[:, tt, :], in1=tA2[:, tb, :], op=ALU.mult),
                               reads=R("yT", "tA2%d" % tb), writes=["yaT%d" % tt])
                    elif s == 4:
                        mk.act(lambda e, tt=tt, pa=pa: e.copy(out=cbT[:, tt, :], in_=pa), reads=R(pk), writes=["cbT%d" % tt])
                    elif s == 5:
                        mk.act(lambda e, tt=tt, pa=pa: e.copy(out=ccT[:, tt, :], in_=pa), reads=R(pk), writes=["ccT%d" % tt])
                    elif s == 6:
                        u = tA2[:, tb, :]
                        uk = "tA2%d" % tb
                        acc = tB2[:, tb, :]
                        ak = "tB2%d" % tb
                        mk.dve(lambda e, tt=tt, pa=pa, u=u: e.tensor_tensor(out=u, in0=ccT[:, tt, :], in1=pa, op=ALU.mult), reads=R(pk, "ccT%d" % tt, uk), writes=[uk])
                        mk.dve(lambda e, tt=tt, u=u, acc=acc: e.tensor_scalar(out=acc, in0=u, scalar1=ccol("cw", 4 + tt), scalar2=ccol("cb", tt), op0=ALU.mult, op1=ALU.add),
                               reads=R(uk, "cols", ak), writes=[ak])
                        mk.dve(lambda e, tt=tt, u=u, acc=acc: e.scalar_tensor_tensor(out=r3(acc)[:, :, 1:rl], in0=r3(u)[:, :, 0:rl - 1], scalar=ccol("cw", tt),
                                                                                    in1=r3(acc)[:, :, 1:rl], op0=ALU.mult, op1=ALU.add), reads=R(uk, ak, "cols"), writes=[ak])
                        mk.dve(lambda e, tt=tt, u=u, acc=acc: e.scalar_tensor_tensor(out=r3(acc)[:, :, 0:rl - 1], in0=r3(u)[:, :, 1:rl], scalar=ccol("cw", 8 + tt),
                                                                                    in1=r3(acc)[:, :, 0:rl - 1], op0=ALU.mult, op1=ALU.add), reads=R(uk, ak, "cols"), writes=[ak])
                        mk.dve(lambda e, tt=tt, acc=acc: e.tensor_tensor(out=ybT[:, tt, :], in0=cbT[:, tt, :], in1=acc, op=ALU.mult), reads=R(ak, "cbT%d" % tt), writes=["ybT%d" % tt])
                    else:
                        gi = (s - 7) * 4 + tt
                        sg = tA2[:, tb, :]
                        sk = "tA2%d" % tb
                        mk.act(lambda e, pa=pa, sg=sg: e.activation(out=sg, in_=pa, func=AF.Sigmoid), reads=R(pk, sk), writes=[sk])
                        if gi < 8:
                            mk.dve(lambda e, gi=gi, sg=sg: e.tensor_tensor(out=yA[:, gi, :], in0=yA[:, gi, :], in1=sg, op=ALU.mult), reads=R(sk, "yA%d" % gi), writes=["yA%d" % gi])
                        else:
                            g2 = gi - 8
                            mk.dve(lambda e, g2=g2, sg=sg: e.tensor_tensor(out=sg, in0=sg, in1=yB[:, g2, :], op=ALU.mult), reads=R(sk, "yB%d" % g2), writes=[sk])
                            mk.dve(lambda e, g2=g2, sg=sg: e.tensor_tensor(out=mgT[:, g2, :], in0=yA[:, g2, :], in1=sg, op=ALU.add), reads=R(sk, "yA%d" % g2), writes=["mgT%d" % g2])
                if s == 3:
                    barrier()
                    branch(wba, yaT, "yaT%d", yA, "yA%d")
                if s == 6:
                    branch(wbb, ybT, "ybT%d", yB, "yB%d")
            oT = pool.f32(8 * N).rearrange("p (k n) -> p k n", k=8)
            pool.off = 6144
            sq = pool.bf16(8 * N).rearrange("p (k n) -> p k n", k=8)
            tmp = pool.f32(2 * N).rearrange("p (k n) -> p k n", k=2)
            for half in range(2):
                sl, key = load_slab([(lambda t: t[:, 0:4096].rearrange("p (k n) -> p k n", k=8),
                                     wout[:, half * 512:(half + 1) * 512].rearrange("(k p) n -> p k n", p=128))])
                slv = sl[:, 0:4096].rearrange("p (k n) -> p k n", k=8)
                for tt in range(4):
                    i = half * 4 + tt
                    b = tt % 2
                    mm_group(psA[:, b, :], "psA%d" % b, [(slv[:, k, tt * 128:(tt + 1) * 128], mgT[:, k, :], [key, "mgT%d" % k, "SCR"]) for k in range(8)])
                    evac(oT[:, i, :], psA[:, b, :], R("psA%d" % b), ["oT"])
            tap("mo_" + kind, oT.rearrange("p q n -> p (q n)"), 8 * N, R("oT"))
            postnorm_residual(j, 3, oT, sq, tmp)

        stt = sb("stt", [128, 5, 4, 64])
        nsb = sb("nsb", [64, 2, 2, 4, 128])
        mk.dma("sp", "c0", lambda e: e.dma_start(out=stt[:], in_=std.rearrange("a q p v -> p a q v")), writes=["stt"])

        def load_x(g):
            pool = Pool()
            xin = pool.f32(4 * D).rearrange("p (t n) -> p t n", t=4)
            for tt in range(4):
                mk.dma("sp", "xin", lambda e, tt=tt: e.dma_start(out=xin[:, tt, :], in_=xg[g, tt * 128:(tt + 1) * 128, :]), reads=["SCR"], writes=["xin%d" % tt])
            for k in range(8):
                b = k % 2
                for tt in range(4):
                    mm_group(psA[:, b, tt * 128:(tt + 1) * 128], "psA%d" % b, [(xin[:, tt, k * 128:(k + 1) * 128], ident, ["xin%d" % tt, "cst", "SCR"])])
                evac(xT[:, k, :], psA[:, b, :], R("psA%d" % b), ["xT"])

        def store_y(gout):
            pool = Pool()
            yo = pool.f32(4 * D).rearrange("p (t n) -> p t n", t=4)
            for tt in range(4):
                for k in range(8):
                    b = k % 2
                    mm_group(psA[:, b, 0:128], "psA%d" % b, [(xT[:, k, tt * 128:(tt + 1) * 128], ident, ["xT", "cst", "SCR"])])
                    evac(yo[:, tt, k * 128:(k + 1) * 128], psA[:, b, 0:128], R("psA%d" % b), ["yo%d" % tt])
                mk.dma("sp", "yout", lambda e, tt=tt: e.dma_start(out=yout[gout * N + tt * 128:gout * N + (tt + 1) * 128, :], in_=yo[:, tt, :]), reads=R("yo%d" % tt))

        def ffn_phase(j, ai, bi, gi, w13, w2):
            pool = Pool()
            hT = pool.bf16(8 * N).rearrange("p (k n) -> p k n", k=8)
            sq = pool.bf16(8 * N).rearrange("p (k n) -> p k n", k=8)
            hid = pool.bf16(22 * N).rearrange("p (k n) -> p k n", k=22)
            oT = pool.f32(8 * N).rearrange("p (k n) -> p k n", k=8)
            tmp = pool.f32(2 * N).rearrange("p (k n) -> p k n", k=2)
            sgt = pool.f32(2 * N).rearrange("p (k n) -> p k n", k=2)
            prenorm(j, ai, bi, hT, sq, tmp)
            ffn(w13, w2, hT, hid, oT, sgt)
            postnorm_residual(j, gi, oT, sq, tmp)

        def mixer_phase(kind, j, aux_idx):
            pool = Pool()
            hT = pool.bf16(8 * N).rearrange("p (k n) -> p k n", k=8)
            tail = Pool()
            tail.off = SCR - (8 * N // 2 + 2 * N)
            sq = tail.bf16(8 * N).rearrange("p (k n) -> p k n", k=8)
            tmp = tail.f32(2 * N).rearrange("p (k n) -> p k n", k=2)
            prenorm(j, 2, 3, hT, sq, tmp)
            barrier()
            mixer(kind, j, hT, aux_idx)

        def chain_prep_aux(a):
            cc_ = COLS["coef"] + (a - 1)
            mk.dve(lambda e: e.tensor_scalar(out=hbA[:], in0=hlast[:, a - 1, :], scalar1=cols[:, cc_:cc_ + 1], scalar2=None, op0=ALU.mult),
                   reads=["hlast%d" % (a - 1), "cols"], writes=["hbA"])

        def chain_prep_own():
            c2 = COLS["coef"] + 2
            mk.dve(lambda e: e.tensor_scalar(out=hbF[:], in0=hlast[:, 0, :], scalar1=cols[:, c2:c2 + 1], scalar2=None, op0=ALU.mult),
                   reads=["hlast0", "cols"], writes=["hbF"])
            for a in (1, 2):
                mk.dve(lambda e, a=a: e.scalar_tensor_tensor(out=hbF[:], in0=hlast[:, a, :], scalar=cols[:, c2 + a:c2 + a + 1], in1=hbF[:], op0=ALU.mult, op1=ALU.add),
                       reads=["hlast%d" % a, "cols", "hbF"], writes=["hbF"])
            mk.dve(lambda e: e.tensor_scalar(out=hbB[:], in0=hlast[:, 2, :], scalar1=cols[:, c2 + 3:c2 + 4], scalar2=None, op0=ALU.mult),
                   reads=["hlast2", "cols"], writes=["hbB"])

        for a in range(3):
            if go():
                barrier()
                load_x(2 + a)
            if go():
                barrier()
                ffn_phase(1, 0, 0, 1, f1w13, f1w2)
            if go():
                barrier()
                if a > 0:
                    chain_prep_aux(a)
                mixer_phase("aux", 1, a)
        for (g, kind, j) in [(1, "own", 1), (0, "prompt", 0)]:
            if go():
                barrier()
                load_x(g)
            if go():
                barrier()
                ffn_phase(j, 0, 0, 1, f1w13, f1w2)
            if go():
                barrier()
                if kind == "own":
                    chain_prep_own()
                mixer_phase(kind, j, None)
            if go():
                barrier()
                ffn_phase(j, 4, 6, 5, f2w13, f2w2)
            if go():
                barrier()
                store_y(1 if kind == "own" else 0)
        if dbg_spec is not None:
            barrier()
            dbg_spec(mk, nc, locals())
        for sq_ in range(2):
            for dd in range(2):
                mk.dma("sp", "yout", lambda e, sq_=sq_, dd=dd: e.dma_start(out=nsout[sq_, dd].rearrange("(q e) v k -> v q e k", e=2),
                                                                           in_=nsb[:, sq_, dd, :, :].rearrange("p q (e k) -> p q e k", e=2)), reads=["nsb"])
        stats = mk.emit()
    return nc, stats


_CACHE = {}


def _prep_inputs(inp):
    f = lambda a: np.ascontiguousarray(np.asarray(a, np.float32))
    x_prompt, x_sample = f(inp["x_prompt"]), f(inp["x_sample"])
    c, state, c_ctx = f(inp["c"]), f(inp["state_rwkv"]), f(inp["c_ctx"])
    mu = f(inp["mu_shift"])[0]
    w1 = [np.concatenate([f(inp["decay_w1"])[0, d], f(inp["iclr_a1"])[0, d]], axis=1) for d in range(2)]
    w2 = [np.concatenate([f(inp["decay_w2"])[0, d], f(inp["iclr_a2"])[0, d]], axis=0) for d in range(2)]
    dw0, ia0 = f(inp["decay_w0"])[0], f(inp["iclr_a0"])[0]
    shared = {
        "w_mod": f(inp["w_mod"])[0], "f1w13": f(inp["ffn1_w13"])[0], "f1w2": f(inp["ffn1_w2"])[0],
        "f2w13": f(inp["ffn2_w13"])[0], "f2w2": f(inp["ffn2_w2"])[0], "w_in": f(inp["w_in"])[0],
        "wba": f(inp["w_branch_a"])[0], "wbb": f(inp["w_branch_b"])[0], "wout": f(inp["w_out"])[0],
        "consts": _make_consts(),
    }
    in_maps = []
    for core in range(8):
        b, s = core // 4, core % 4
        if s == 0:
            aux = [(3, 1), (2, 1), (1, 1)]
            cont = (1.0, 1.0)
            selF = (0.0, 0.0, 0.0)
            selB = 1.0
            init = ["F", "0", "B", "0", "0"]
        elif s == 1:
            aux = [(0, 0), (3, 1), (2, 1)]
            cont = (0.0, 1.0)
            selF = (1.0, 0.0, 0.0)
            selB = 1.0
            init = ["0", "0", "F", "B", "0"]
        elif s == 2:
            aux = [(0, 0), (1, 0), (3, 1)]
            cont = (1.0, 0.0)
            selF = (0.0, 1.0, 0.0)
            selB = 1.0
            init = ["0", "0", "F", "0", "B"]
        else:
            aux = [(0, 0), (1, 0), (2, 0)]
            cont = (1.0, 1.0)
            selF = (0.0, 0.0, 1.0)
            selB = 0.0
            init = ["0", "B", "F", "0", "0"]
        xgr = np.empty((5, N, D), np.float32)
        xgr[0] = x_prompt[2 * core:2 * core + 2].reshape(N, D)
        xgr[1] = x_sample[b, s * N:(s + 1) * N]
        for a, (seg, dr) in enumerate(aux):
            xs = x_sample[b, seg * N:(seg + 1) * N]
            xgr[2 + a] = xs[::-1] if dr == 1 else xs
        dsl = [0, 1] + [dr for (_, dr) in aux]
        cols = np.zeros((128, NCOL), np.float32)
        cvs = [c_ctx, c[b]]
        for k in range(8):
            for j in range(2):
                cols[:, COLS["cv"] + k * 2 + j] = cvs[j][k * 128:(k + 1) * 128]
        cols[:, COLS["bmod"]:COLS["bmod"] + 72] = _colize(inp["b_mod"][0])
        cols[:, COLS["ng"]:COLS["ng"] + 48] = _colize(np.asarray(inp["norm_g"][0]).reshape(-1))
        for ds, dr in enumerate(dsl):
            for x in range(2):
                cols[:, COLS["mu"] + ds * 16 + x * 8:COLS["mu"] + ds * 16 + x * 8 + 8] = _colize(mu[dr, x])
            cols[:, COLS["w0"] + ds * 4:COLS["w0"] + ds * 4 + 4] = _colize(dw0[dr])
            cols[:, COLS["a0"] + ds * 4:COLS["a0"] + ds * 4 + 4] = _colize(ia0[dr])
        cols[:, COLS["kk"]:COLS["kk"] + 4] = _colize(inp["k_k"][0])
        cols[:, COLS["ka"]:COLS["ka"] + 4] = _colize(inp["k_a"][0])
        cols[:, COLS["rk"]:COLS["rk"] + 4] = _colize(np.asarray(inp["r_k"][0]).reshape(-1))
        cols[:, COLS["gng"]:COLS["gng"] + 4] = _colize(inp["gn_gain"][0])
        cols[:, COLS["gnb"]:COLS["gnb"] + 4] = _colize(inp["gn_bias"][0])
        cols[:, COLS["cw"]:COLS["cw"] + 12] = _colize(np.asarray(inp["conv_w"][0]).reshape(-1))
        cols[:, COLS["cb"]:COLS["cb"] + 4] = _colize(inp["conv_b"][0])
        cols[:, COLS["coef"]:COLS["coef"] + 6] = np.array([cont[0], cont[1], selF[0], selF[1], selF[2], selB], np.float32)[None, :]
        cols[:, COLS["num"]:COLS["num"] + 6] = np.array([1e-6, 1e-12, 64e-5, 0.0, 1.0, 2.0], np.float32)[None, :]
        def mlay(d):
            S = state[b, 0, d]
            return np.ascontiguousarray(S.transpose(0, 2, 1).reshape(4, 128, 64))
        stv = np.zeros((5, 4, 128, 64), np.float32)
        for i, t in enumerate(init):
            if t == "F":
                stv[i] = mlay(0)
            elif t == "B":
                stv[i] = mlay(1)
        m = dict(shared)
        m.update({"xg": xgr, "cols": cols, "st": stv,
                  "w1c": np.ascontiguousarray(np.stack([w1[d] for d in dsl])),
                  "w2c": np.ascontiguousarray(np.stack([w2[d] for d in dsl]))})
        in_maps.append(m)
    return in_maps


def kernel(**inputs):
    if "nc" not in _CACHE:
        _CACHE["nc"], _CACHE["stats"] = build_program()
    nc = _CACHE["nc"]
    in_maps = _prep_inputs(inputs)
    res = run_bass_kernel_spmd(nc, in_maps, core_ids=list(range(8)))
    y_prompt = np.empty((16, 256, D), np.float32)
    y_sample = np.empty((2, 2048, D), np.float32)
    new_state = np.empty((16, 1, 2, 8, 64, 64), np.float32)
    for core in range(8):
        r = res.results[core]
        b, s = core // 4, core % 4
        y = np.asarray(r["y"], np.float32)
        y_prompt[2 * core:2 * core + 2] = y[0:N].reshape(2, 256, D)
        y_sample[b, s * N:(s + 1) * N] = y[N:2 * N]
        new_state[2 * core:2 * core + 2, 0] = np.asarray(r["ns"], np.float32)
    return (y_prompt, y_sample, new_state)
```

```python
import contextlib
import numpy as np
import concourse.bass as bass
import concourse.mybir as mybir
from concourse.bass_utils import run_bass_kernel_spmd

F32 = mybir.dt.float32
BF16 = mybir.dt.bfloat16
ALU = mybir.AluOpType
AF = mybir.ActivationFunctionType

D = 1024
DFF = 2816
import os
SUB = int(os.environ.get('KSUB', '99'))
KT = int(os.environ.get('KT', '1000000'))
KTC = [0]
MARK = {}
TAPS = [t for t in os.environ.get('KTAPS', '').split(',') if t]
KV = int(os.environ.get('KV', '0'))
N = 512
C = 64
NCH = N // C
C0 = float(np.exp(-0.5))


class _Op:
    __slots__ = ("idx", "eng", "fn", "deps", "chan", "chanpos", "needs_inc", "inc_count", "engpos")

    def __init__(self, idx, eng, fn, deps, chan):
        self.idx = idx
        self.eng = eng
        self.fn = fn
        self.deps = deps
        self.chan = chan
        self.chanpos = None
        self.needs_inc = False
        self.inc_count = None
        self.engpos = None


class MK:
    ENGS = ("pe", "act", "dve", "pool", "sp")

    def __init__(self, nc):
        self.nc = nc
        self.ops = []
        self.last_writer = {}
        self.readers = {}
        self.chan_count = {}

    def add(self, eng, fn, reads=(), writes=(), chan=None):
        idx = len(self.ops)
        deps = set()
        writes = list(writes)
        if chan is not None:
            writes.append(("__chan__", chan))
        for r in reads:
            w = self.last_writer.get(r)
            if w is not None:
                deps.add(w)
        for w in writes:
            lw = self.last_writer.get(w)
            if lw is not None:
                deps.add(lw)
            deps.update(self.readers.get(w, ()))
        op = _Op(idx, eng, fn, deps, chan)
        if chan is not None:
            op.chanpos = self.chan_count.get(chan, 0)
            self.chan_count[chan] = op.chanpos + 1
        self.ops.append(op)
        for r in reads:
            self.readers.setdefault(r, []).append(idx)
        for w in writes:
            self.last_writer[w] = idx
            self.readers[w] = []
        return idx

    def pe(self, fn, reads=(), writes=()):
        return self.add("pe", fn, reads, writes)

    def act(self, fn, reads=(), writes=()):
        return self.add("act", fn, reads, writes)

    def dve(self, fn, reads=(), writes=()):
        return self.add("dve", fn, reads, writes)

    def pool(self, fn, reads=(), writes=()):
        return self.add("pool", fn, reads, writes)

    def dma(self, eng, chan, fn, reads=(), writes=()):
        return self.add(eng, fn, reads, writes, chan=chan)

    def emit(self):
        nc = self.nc
        ops = self.ops
        per_eng = {e: [] for e in self.ENGS}
        for op in ops:
            op.engpos = len(per_eng[op.eng])
            per_eng[op.eng].append(op)

        def need_sem(op, d):
            if d.chan is not None:
                return True
            if d.eng != op.eng:
                return True
            if op.eng == "pe":
                return False
            return (op.engpos - d.engpos) <= 2

        for op in ops:
            for di in op.deps:
                d = ops[di]
                if d.chan is None and need_sem(op, d):
                    d.needs_inc = True
        cnt = {e: 0 for e in self.ENGS}
        for op in ops:
            if op.chan is None and op.needs_inc:
                cnt[op.eng] += 1
                op.inc_count = cnt[op.eng]
        chans = sorted(self.chan_count.keys())
        with contextlib.ExitStack() as st:
            esem = {e: st.enter_context(nc.semaphore("s_" + e)) for e in self.ENGS}
            csem = {c: st.enter_context(nc.semaphore("c_" + str(c))) for c in chans}
            block = st.enter_context(nc.Block())

            def run_engine(ename, eobj):
                waited = {}

                def wait(key, sem, val):
                    if waited.get(key, 0) >= val:
                        return
                    waited[key] = val
                    eobj.wait_ge(sem, val)

                for op in per_eng[ename]:
                    for di in sorted(op.deps):
                        d = ops[di]
                        if not need_sem(op, d):
                            continue
                        if d.chan is not None:
                            wait(("c", d.chan), csem[d.chan], 16 * (d.chanpos + 1))
                        else:
                            wait(("e", d.eng), esem[d.eng], d.inc_count)
                    ins = op.fn(eobj)
                    if op.chan is not None:
                        ins.then_inc(csem[op.chan], 16)
                    elif op.needs_inc:
                        ins.then_inc(esem[op.eng], 1)
                if ename == "sp":
                    for c in chans:
                        wait(("c", c), csem[c], 16 * self.chan_count[c])

            @block.tensor
            def _(e):
                run_engine("pe", e)

            @block.scalar
            def _(e):
                run_engine("act", e)

            @block.vector
            def _(e):
                run_engine("dve", e)

            @block.gpsimd
            def _(e):
                run_engine("pool", e)

            @block.sync
            def _(e):
                run_engine("sp", e)
        return {e: len(v) for e, v in per_eng.items()}


def _colize(v):
    v = np.asarray(v, np.float32).reshape(-1, 128)
    return np.ascontiguousarray(v.T)


COLS = {}
_off = 0
for _name, _w in [("cv", 16), ("bmod", 72), ("ng", 48), ("mu", 80), ("w0", 20), ("a0", 20), ("kk", 4), ("ka", 4),
                  ("rk", 4), ("gng", 4), ("gnb", 4), ("cw", 12), ("cb", 4), ("coef", 8), ("num", 8)]:
    COLS[_name] = _off
    _off += _w
NCOL = _off

CONSTS = {}
_off = 0
for _name, _w in [("ident", 128), ("bones", 128), ("maskG", 512), ("maskZ", 128), ("cmask", 512), ("id64", 64), ("bones64", 128)]:
    CONSTS[_name] = _off
    _off += _w
NCONST = _off


def _make_consts():
    cst = np.zeros((128, NCONST), np.float32)
    cst[:, CONSTS["ident"]:CONSTS["ident"] + 128] = np.eye(128, dtype=np.float32)
    bo = np.zeros((128, 128), np.float32)
    bo[:64, :64] = 1
    bo[64:, 64:] = 1
    cst[:, CONSTS["bones"]:CONSTS["bones"] + 128] = bo
    cst[:, CONSTS["bones64"]:CONSTS["bones64"] + 128] = bo / 64.0
    s = (np.arange(128) % 64)[:, None]
    t = np.arange(64)[None, :]
    mg = np.zeros((128, 2, 256), np.float32)
    for blk in range(2):
        mg[:, 0, blk * 128:blk * 128 + 64] = (t > s)
        mg[:, 0, blk * 128 + 64:blk * 128 + 128] = (t >= s)
        mg[:, 1, blk * 128:blk * 128 + 64] = (t < s)
        mg[:, 1, blk * 128 + 64:blk * 128 + 128] = (t <= s)
    cst[:, CONSTS["maskG"]:CONSTS["maskG"] + 512] = mg.reshape(128, 512)
    mz = np.zeros((128, 2, 64), np.float32)
    mz[:, 0, :] = (t < s)
    mz[:, 1, :] = (t > s)
    cst[:, CONSTS["maskZ"]:CONSTS["maskZ"] + 128] = mz.reshape(128, 128)
    cm = np.ones((128, 512), np.float32)
    cm[:, ::64] = 0
    cst[:, CONSTS["cmask"]:CONSTS["cmask"] + 512] = cm
    i64 = np.zeros((128, 64), np.float32)
    i64[np.arange(128), np.arange(128) % 64] = 1
    cst[:, CONSTS["id64"]:CONSTS["id64"] + 64] = i64
    return cst


def build_program(limit=10 ** 9, dbg_spec=None, mlimit=10 ** 9):
    nc = bass.Bass("TRN2", target_bir_lowering=False)
    stage = [0]

    def go():
        stage[0] += 1
        return stage[0] <= limit
    mstage = [0]

    def mgo():
        mstage[0] += 1
        return mstage[0] <= mlimit

    def din(name, shape):
        return nc.dram_tensor(name, list(shape), F32, kind="ExternalInput").ap()

    xg = din("xg", [5, N, D])
    colsd = din("cols", [128, NCOL])
    cstd = din("consts", [128, NCONST])
    std = din("st", [5, 4, 128, 64])
    w_mod = din("w_mod", [D, 9 * D])
    f1w13 = din("f1w13", [D, 2 * DFF])
    f1w2 = din("f1w2", [DFF, D])
    f2w13 = din("f2w13", [D, 2 * DFF])
    f2w2 = din("f2w2", [DFF, D])
    w_in = din("w_in", [D, 5632])
    w1c = din("w1c", [5, D, 128])
    w2c = din("w2c", [5, 128, 512])
    wba = din("wba", [512, D])
    wbb = din("wbb", [512, D])
    wout = din("wout", [D, D])
    yout = nc.dram_tensor("y", [2 * N, D], F32, kind="ExternalOutput").ap()
    nsout = nc.dram_tensor("ns", [2, 2, 8, 64, 64], F32, kind="ExternalOutput").ap()

    with contextlib.ExitStack() as stk:
        def sb(name, shape, dt=F32):
            return stk.enter_context(nc.sbuf_tensor(name, list(shape), dt))

        def ps(name, shape):
            return stk.enter_context(nc.psum_tensor(name, list(shape), F32))

        mk = MK(nc)
        cols = sb("cols_t", [128, NCOL])
        cst = sb("cst_t", [128, NCONST])
        onesb = sb("onesb", [128, 128], BF16)
        identb = sb("identb", [128, 128], BF16)
        scb = sb("scb", [128, 8, 2], BF16)
        modT = sb("modT", [128, 72, 2])
        mods = sb("mods", [128, 2, 6, 8])
        xT = sb("xT", [128, 8, N])
        rstd = sb("rstd", [128, N])
        lnt = sb("lnt", [128, N])
        bdum = sb("bdum", [128, 1])
        NSLOT = 3
        slots = [sb("slot%d" % i, [128, 4096], BF16) for i in range(NSLOT)]
        hlast = sb("hlast", [128, 3, 8])
        hbF = sb("hbF", [128, 8])
        hbB = sb("hbB", [128, 8])
        hbA = sb("hbA", [128, 8])
        endst = sb("endst", [128, 3, 4, 64])
        Mst = sb("Mst", [128, 4, 64])
        Mtmp = sb("Mtmp", [128, 4, 64])
        SCR = 30720
        scr = sb("scr", [128, SCR])

        class Pool:
            def __init__(self):
                self.off = 0

            def f32(self, n):
                a = scr[:, self.off:self.off + n]
                self.off += n
                assert self.off <= SCR, self.off
                return a

            def bf16(self, n):
                m = (n + 1) // 2
                a = scr[:, self.off:self.off + m].bitcast(BF16)
                self.off += m
                assert self.off <= SCR, self.off
                return a

        psA = ps("psA", [128, 2, 512])
        psB = ps("psB", [128, 2, 512])
        psS = ps("psS", [128, 512])
        psC = ps("psC", [128, 2, 512])
        psT = ps("psT", [128, 512])

        def tap(name, ap2d, width, keys):
            if name not in TAPS:
                return
            dt_ = nc.dram_tensor("dbg_" + name, [128, width], F32, kind="ExternalOutput").ap()
            mk.dma("sp", "yout", lambda e: e.dma_start(out=dt_, in_=ap2d), reads=keys)

        cnum = lambda i: cols[:, COLS["num"] + i:COLS["num"] + i + 1]
        ccol = lambda name, i: cols[:, COLS[name] + i:COLS[name] + i + 1]
        ident = cst[:, CONSTS["ident"]:CONSTS["ident"] + 128]
        bones = cst[:, CONSTS["bones"]:CONSTS["bones"] + 128]
        bones64 = cst[:, CONSTS["bones64"]:CONSTS["bones64"] + 128]
        id64 = cst[:, CONSTS["id64"]:CONSTS["id64"] + 64]
        cmask = cst[:, CONSTS["cmask"]:CONSTS["cmask"] + 512]

        def maskG(d):
            o = CONSTS["maskG"] + d * 256
            return cst[:, o:o + 256]

        def maskZ(d):
            o = CONSTS["maskZ"] + d * 64
            return cst[:, o:o + 64]

        evac_rr = [0]

        def evac(out, in_, reads, writes):
            evac_rr[0] ^= 1
            if evac_rr[0]:
                mk.act(lambda e: e.copy(out=out, in_=in_), reads=reads, writes=writes)
            else:
                mk.dve(lambda e: e.tensor_copy(out=out, in_=in_), reads=reads, writes=writes)

        slot_rr = [0]

        def load_slab(pieces):
            s = slot_rr[0] % NSLOT
            slot_rr[0] += 1
            sl = slots[s]
            key = "W%d" % s
            for i, (vf, dap) in enumerate(pieces):
                mk.dma("pool", "w%d" % s, lambda e, vf=vf, dap=dap: e.dma_start(out=vf(sl), in_=dap), writes=[key])
            return sl, key

        def mm_group(out_ap, out_key, terms):
            n = len(terms)
            for i, (l, r, keys) in enumerate(terms):
                mk.pe(lambda e, l=l, r=r, i=i: e.matmul(out_ap, lhsT=l, rhs=r, start=(i == 0), stop=(i == n - 1)),
                      reads=keys, writes=(out_key if isinstance(out_key, list) else [out_key]))

        barrier_n = [0]

        def barrier():
            barrier_n[0] += 1
            mk.dve(lambda e: e.memset(bdum[:], 0.0), reads=["bdum"], writes=["SCR"])

        R = lambda *k: ["SCR"] + list(k)

        mk.dma("sp", "c0", lambda e: e.dma_start(out=cols[:], in_=colsd), writes=["cols"])
        mk.dma("sp", "c0", lambda e: e.dma_start(out=cst[:], in_=cstd), writes=["cst"])
        mk.dve(lambda e: e.memset(onesb[:], 1.0 / 1024.0), writes=["onesb"])
        mk.dve(lambda e: e.tensor_copy(out=identb[:], in_=cst[:, CONSTS["ident"]:CONSTS["ident"] + 128]), reads=["cst"], writes=["cst2"])
        cv0 = COLS["cv"]
        mk.act(lambda e: e.activation(out=scb[:].rearrange("p k j -> p (k j)"), in_=cols[:, cv0:cv0 + 16], func=AF.Silu),
               reads=["cols"], writes=["scb"])
        for s in range(18):
            sl, key = load_slab([(lambda t: t[:, 0:4096].rearrange("p (k n) -> p k n", k=8),
                                 w_mod[:, s * 512:(s + 1) * 512].rearrange("(k p) n -> p k n", p=128))])
            slv = sl[:, 0:4096].rearrange("p (k n) -> p k n", k=8)
            for tt in range(4):
                mm_group(psS[:, tt * 2:tt * 2 + 2], "psS",
                         [(slv[:, k, tt * 128:(tt + 1) * 128], scb[:, k, :], [key, "scb"]) for k in range(8)])
            for j in range(2):
                b0 = COLS["bmod"] + s * 4
                mk.dve(lambda e, s=s, j=j, b0=b0: e.tensor_tensor(
                    out=modT[:, s * 4:s * 4 + 4, j], in0=psS[:, 0:8].rearrange("p (t j) -> p t j", j=2)[:, :, j],
                    in1=cols[:, b0:b0 + 4], op=ALU.add), reads=["psS", "cols"], writes=["modT"])
        ng = lambda i: cols[:, COLS["ng"] + i * 8:COLS["ng"] + i * 8 + 8]
        m_ = lambda i, j: modT[:, i * 8:(i + 1) * 8, j]
        for j in range(2):
            for (dst, mi, gi, half) in [(0, 1, 0, None), (1, 2, 1, 0.5), (2, 4, 2, None), (3, 5, 3, 1.0), (4, 7, 4, None), (5, 8, 5, 0.5)]:
                if half is None:
                    mk.dve(lambda e, j=j, dst=dst, mi=mi, gi=gi: e.scalar_tensor_tensor(
                        out=mods[:, j, dst, :], in0=m_(mi, j), scalar=1.0, in1=ng(gi), op0=ALU.add, op1=ALU.mult),
                        reads=["modT", "cols"], writes=["mods"])
                else:
                    mk.dve(lambda e, j=j, dst=dst, mi=mi, gi=gi, half=half: e.scalar_tensor_tensor(
                        out=mods[:, j, dst, :], in0=m_(mi, j), scalar=half, in1=ng(gi), op0=ALU.mult, op1=ALU.mult),
                        reads=["modT", "cols"], writes=["mods"])

        def rms_rstd(src, src_key, sq, eps_idx):
            for k in range(8):
                if k % 2 == 0:
                    mk.act(lambda e, k=k: e.activation(out=sq[:, k, :], in_=src[:, k, :], func=AF.Square),
                           reads=R(src_key), writes=["sq%d" % k])
                else:
                    mk.dve(lambda e, k=k: e.tensor_tensor(out=sq[:, k, :], in0=src[:, k, :], in1=src[:, k, :], op=ALU.mult),
                           reads=R(src_key), writes=["sq%d" % k])
            mm_group(psS[:], "psS", [(onesb[:], sq[:, k, :], ["onesb", "sq%d" % k, "SCR"]) for k in range(8)])
            mk.act(lambda e: e.activation(out=lnt[:], in_=psS[:], func=AF.Ln, bias=cnum(eps_idx), scale=1.0),
                   reads=["psS", "cols"], writes=["lnt"])
            mk.act(lambda e: e.activation(out=rstd[:], in_=lnt[:], func=AF.Exp, scale=-0.5), reads=["lnt"], writes=["rstd"])

        def prenorm(j, ai, bi, hT, sq, tmp):
            rms_rstd(xT, "xT", sq, 0)
            for k in range(8):
                mk.dve(lambda e, k=k: e.scalar_tensor_tensor(out=tmp[:, k % 2, :], in0=xT[:, k, :], scalar=mods[:, j, ai, k:k + 1],
                                                             in1=rstd[:], op0=ALU.mult, op1=ALU.mult),
                       reads=R("xT", "mods", "rstd"), writes=["ptmp%d" % (k % 2)])
                mk.act(lambda e, k=k: e.activation(out=hT[:, k, :], in_=tmp[:, k % 2, :], func=AF.Identity,
                                                   bias=modT[:, bi * 8 + k, j:j + 1], scale=1.0),
                       reads=R("ptmp%d" % (k % 2), "modT"), writes=["hT%d" % k])

        def postnorm_residual(j, gi, oT, sq, tmp):
            rms_rstd(oT, "oT", sq, 0)
            for k in range(8):
                mk.dve(lambda e, k=k: e.scalar_tensor_tensor(out=tmp[:, k % 2, :], in0=oT[:, k, :], scalar=mods[:, j, gi, k:k + 1],
                                                             in1=rstd[:], op0=ALU.mult, op1=ALU.mult),
                       reads=R("oT", "mods", "rstd"), writes=["ptmp%d" % (k % 2)])
                mk.dve(lambda e, k=k: e.tensor_tensor(out=xT[:, k, :], in0=xT[:, k, :], in1=tmp[:, k % 2, :], op=ALU.add),
                       reads=R("ptmp%d" % (k % 2), "xT"), writes=["xT"])

        def ffn(w13, w2, hT, hid, oT, sgt):
            for s in range(11):
                sl, key = load_slab([
                    (lambda t: t[:, 0:4096].rearrange("p (k n) -> p k n", k=8)[:, :, 0:256],
                     w13[:, s * 256:(s + 1) * 256].rearrange("(k p) n -> p k n", p=128)),
                    (lambda t: t[:, 0:4096].rearrange("p (k n) -> p k n", k=8)[:, :, 256:512],
                     w13[:, DFF + s * 256:DFF + (s + 1) * 256].rearrange("(k p) n -> p k n", p=128))])
                slv = sl[:, 0:4096].rearrange("p (k n) -> p k n", k=8)
                for jj in range(2):
                    jt = 2 * s + jj
                    b = jt % 2
                    mm_group(psA[:, b, :], "psA%d" % b,
                             [(slv[:, k, jj * 128:(jj + 1) * 128], hT[:, k, :], [key, "hT%d" % k, "SCR"]) for k in range(8)])
                    mm_group(psB[:, b, :], KB(b),
                             [(slv[:, k, 256 + jj * 128:256 + (jj + 1) * 128], hT[:, k, :], [key, "hT%d" % k, "SCR"]) for k in range(8)])
                    mk.act(lambda e, b=b: e.activation(out=sgt[:, b, :], in_=psA[:, b, :], func=AF.Silu),
                           reads=R("psA%d" % b), writes=["sgt%d" % b])
                    mk.dve(lambda e, b=b, jt=jt: e.tensor_tensor(out=hid[:, jt, :], in0=sgt[:, b, :], in1=psB[:, b, :], op=ALU.mult),
                           reads=R("sgt%d" % b, *KB(b)), writes=["hid%d" % jt])
            for i in range(8):
                sl, key = load_slab([(lambda t: t[:, 0:2816].rearrange("p (j n) -> p j n", j=22),
                                     w2[:, i * 128:(i + 1) * 128].rearrange("(j p) n -> p j n", p=128))])
                slv = sl[:, 0:2816].rearrange("p (j n) -> p j n", j=22)
                b = i % 2
                mm_group(psA[:, b, :], "psA%d" % b, [(slv[:, jt, :], hid[:, jt, :], [key, "hid%d" % jt, "SCR"]) for jt in range(22)])
                evac(oT[:, i, :], psA[:, b, :], R("psA%d" % b), ["oT"])

        def KB(b):
            return ["psB0", "psB0b"] if b == 0 else ["psB1"]

        def mixer(kind, j, hT, aux_idx):
            pool = Pool()
            pool.off = 2048
            full = kind != "aux"
            nseq = 2 if kind == "prompt" else 1
            L = N // nseq
            cps = NCH // nseq
            rkv = pool.f32(12 * N).rearrange("p (t n) -> p t n", t=12)
            kk = pool.f32(4 * N).rearrange("p (t n) -> p t n", t=4)
            yT = pool.f32(4 * N).rearrange("p (t n) -> p t n", t=4)
            asum = pool.f32(4 * N).rearrange("p (t n) -> p t n", t=4)
            lh = pool.bf16(2 * N).rearrange("p (d n) -> p d n", d=2)
            w2b = pool.bf16(2 * 512).rearrange("p (d n) -> p d n", d=2)
            base_d = pool.off
            w1b = pool.bf16(2 * 1024).rearrange("p (d k n) -> p d k n", d=2, k=8)
            w1s = pool.bf16(2 * 1024).rearrange("p (d k n) -> p d k n", d=2, k=8)
            hk = ["hT%d" % k for k in range(8)]
            for s in range(3):
                sl, key = load_slab([(lambda t: t[:, 0:4096].rearrange("p (k n) -> p k n", k=8),
                                     w_in[:, s * 512:(s + 1) * 512].rearrange("(k p) n -> p k n", p=128))])
                slv = sl[:, 0:4096].rearrange("p (k n) -> p k n", k=8)
                for tt in range(4):
                    b = tt % 2
                    mm_group(psA[:, b, :], "psA%d" % b,
                             [(slv[:, k, tt * 128:(tt + 1) * 128], hT[:, k, :], [key, "hT%d" % k, "SCR"]) for k in range(8)])
                    evac(rkv[:, s * 4 + tt, :], psA[:, b, :], R("psA%d" % b), ["rkv%d" % (s * 4 + tt)])
            if not mgo():
                return
            sqk = pool.f32(2 * N).rearrange("p (b n) -> p b n", b=2)
            for pr in range(4):
                b = pr % 2
                mk.dve(lambda e, pr=pr: e.tensor_scalar(out=kk[:, pr, :], in0=rkv[:, 4 + pr, :], scalar1=ccol("kk", pr), scalar2=None,
                                                        op0=ALU.mult), reads=R("rkv%d" % (4 + pr), "cols"), writes=["kk%d" % pr])
                mk.dve(lambda e, pr=pr, b=b: e.tensor_tensor(out=sqk[:, b, :], in0=kk[:, pr, :], in1=kk[:, pr, :], op=ALU.mult),
                       reads=R("kk%d" % pr), writes=["sqk%d" % b])
                mm_group(psB[:, b, :], KB(b), [(bones, sqk[:, b, :], ["cst", "sqk%d" % b, "SCR"])])
                mk.act(lambda e, b=b: e.activation(out=sqk[:, b, :], in_=psB[:, b, :], func=AF.Ln, bias=cnum(1), scale=1.0),
                       reads=R("cols", *KB(b)), writes=["sqk%d" % b])
                mk.act(lambda e, b=b: e.activation(out=sqk[:, b, :], in_=sqk[:, b, :], func=AF.Exp, scale=-0.5),
                       reads=R("sqk%d" % b), writes=["sqk%d" % b])
                mk.dve(lambda e, pr=pr, b=b: e.tensor_tensor(out=kk[:, pr, :], in0=kk[:, pr, :], in1=sqk[:, b, :], op=ALU.mult),
                       reads=R("sqk%d" % b, "kk%d" % pr), writes=["kk%d" % pr])
            if not mgo():
                return
            dslots = [(0, 0), (1, 1)] if full else [(0, 2 + aux_idx)]
            sh = pool.bf16(8 * N).rearrange("p (k n) -> p k n", k=8)
            for di, (dt_, ds) in enumerate(dslots):
                mk.dma("pool", "wl", lambda e, di=di, ds=ds: e.dma_start(out=w1b[:, di, :, :], in_=w1c[ds].rearrange("(k p) n -> p k n", p=128)),
                       reads=["SCR"], writes=["w1b%d" % di])
                mk.dma("pool", "wl", lambda e, di=di, ds=ds: e.dma_start(out=w2b[:, di, :], in_=w2c[ds]), reads=["SCR"], writes=["w2b%d" % di])
                for x in range(2):
                    for k in range(8):
                        mc = COLS["mu"] + ds * 16 + x * 8 + k
                        mk.dve(lambda e, di=di, x=x, k=k, mc=mc: e.tensor_scalar(
                            out=w1s[:, di, k, x * 64:(x + 1) * 64], in0=w1b[:, di, k, x * 64:(x + 1) * 64],
                            scalar1=cols[:, mc:mc + 1], scalar2=None, op0=ALU.mult),
                            reads=R("w1b%d" % di, "cols"), writes=["w1s%d" % di])
                if dt_ == 0:
                    mk.dve(lambda e: e.tensor_tensor(out=sh[:, :, 1:N], in0=hT[:, :, 0:N - 1], in1=hT[:, :, 1:N], op=ALU.subtract),
                           reads=R(*hk), writes=["sh"])
                    for sq_ in range(nseq):
                        col = sq_ * L
                        if kind == "prompt" or (kind == "aux" and aux_idx == 0):
                            mk.dve(lambda e, col=col: e.tensor_scalar(out=sh[:, :, col], in0=hT[:, :, col], scalar1=-1.0, scalar2=None,
                                                                      op0=ALU.mult), reads=R("sh", *hk), writes=["sh"])
                        else:
                            hb = hbF if kind == "own" else hbA
                            hbk = "hbF" if kind == "own" else "hbA"
                            mk.dve(lambda e, col=col, hb=hb: e.tensor_tensor(out=sh[:, :, col], in0=hb[:, :], in1=hT[:, :, col], op=ALU.subtract),
                                   reads=R("sh", hbk, *hk), writes=["sh"])
                else:
                    mk.dve(lambda e: e.tensor_tensor(out=sh[:, :, 0:N - 1], in0=hT[:, :, 1:N], in1=hT[:, :, 0:N - 1], op=ALU.subtract),
                           reads=R(*hk), writes=["sh"])
                    for sq_ in range(nseq):
                        col = sq_ * L + L - 1
                        if kind == "prompt":
                            mk.dve(lambda e, col=col: e.tensor_scalar(out=sh[:, :, col], in0=hT[:, :, col], scalar1=-1.0, scalar2=None,
                                                                      op0=ALU.mult), reads=R("sh", *hk), writes=["sh"])
                        else:
                            mk.dve(lambda e, col=col: e.tensor_tensor(out=sh[:, :, col], in0=hbB[:, :], in1=hT[:, :, col], op=ALU.subtract),
                                   reads=R("sh", "hbB", *hk), writes=["sh"])
                b = di % 2
                mm_group(psB[:, b, :], KB(b),
                         [(w1b[:, di, k, :], hT[:, k, :], ["w1b%d" % di, "hT%d" % k, "SCR"]) for k in range(8)] +
                         [(w1s[:, di, k, :], sh[:, k, :], ["w1s%d" % di, "sh", "SCR"]) for k in range(8)])
                mk.act(lambda e, di=di, b=b: e.activation(out=lh[0:64, di, :], in_=psB[0:64, b, :], func=AF.Tanh),
                       reads=R(*KB(b)), writes=["lh%d" % di])
                mk.act(lambda e, di=di, b=b: e.copy(out=lh[64:128, di, :], in_=psB[64:128, b, :]),
                       reads=R(*KB(b)), writes=["lh%d" % di])
            if kind == "aux":
                mk.dve(lambda e: e.tensor_copy(out=hlast[:, aux_idx, :], in_=hT[:, :, N - 1]), reads=R(*hk), writes=["hlast%d" % aux_idx])

            if not mgo():
                return
            v3 = lambda ap: ap.rearrange("p (c n) -> p c n", c=8)
            psG = psA[:].rearrange("p a (h n) -> p (a h) n", h=2)
            psZ = psB[:].rearrange("p a (h n) -> p (a h) n", h=8)
            psZv = lambda a, hh: psZ[:, a * 4 + hh, :]
            ZK = ["psB0", "psB0b", "psB1"]
            psTv = psT[:].rearrange("p (a h n) -> p a h n", a=2, h=4)
            psCv = psC[:].rearrange("p a (h n) -> p a h n", h=8)
            psSv = psS[:].rearrange("p (h n) -> p h n", h=8)
            for di, (dt_, ds) in enumerate(dslots):
                order = list(range(NCH)) if dt_ == 0 else list(range(NCH - 1, -1, -1))
                barrier()
                pool.off = base_d
                AR = pool.f32(4 * 8 * 128).rearrange("p (q c n) -> p q c n", q=4, c=8)
                BK = pool.f32(4 * 8 * 128).rearrange("p (q c n) -> p q c n", q=4, c=8)
                Pend = pool.f32(32).rearrange("p (q c) -> p q c", q=4)
                base_t = pool.off
                sw = pool.f32(2 * N).rearrange("p (q n) -> p q n", q=2)
                av = pool.f32(2 * N).rearrange("p (q n) -> p q n", q=2)
                cs = pool.f32(2 * N).rearrange("p (q n) -> p q n", q=2)
                Lx = pool.f32(2 * N).rearrange("p (q n) -> p q n", q=2)
                Ep = pool.f32(2 * N).rearrange("p (q n) -> p q n", q=2)
                t1 = pool.f32(2 * N).rearrange("p (q n) -> p q n", q=2)
                for hp in range(2):
                    for ql in range(2):
                        pr = 2 * hp + ql
                        b = ql
                        mm_group(psA[:, b, :], "psA%d" % b, [(w2b[0:64, di, pr * 128:(pr + 1) * 128], lh[0:64, di, :], ["w2b%d" % di, "lh%d" % di, "SCR"])])
                        mm_group(psB[:, b, :], KB(b), [(w2b[64:128, di, pr * 128:(pr + 1) * 128], lh[64:128, di, :], ["w2b%d" % di, "lh%d" % di, "SCR"])])
                        mk.act(lambda e, pr=pr, ql=ql, b=b, ds=ds: e.activation(out=sw[:, ql, :], in_=psA[:, b, :], func=AF.Sigmoid,
                                                                               bias=ccol("w0", ds * 4 + pr), scale=1.0),
                               reads=R("psA%d" % b, "cols"), writes=["sw%d" % ql])
                        mk.act(lambda e, pr=pr, ql=ql, b=b, ds=ds: e.activation(out=av[:, ql, :], in_=psB[:, b, :], func=AF.Sigmoid,
                                                                               bias=ccol("a0", ds * 4 + pr), scale=1.0),
                               reads=R("cols", *KB(b)), writes=["av%d" % ql])
                        if full:
                            if di == 0:
                                mk.dve(lambda e, pr=pr, ql=ql: e.tensor_copy(out=asum[:, pr, :], in_=av[:, ql, :]), reads=R("av%d" % ql), writes=["asum%d" % pr])
                            else:
                                mk.dve(lambda e, pr=pr, ql=ql: e.tensor_tensor(out=asum[:, pr, :], in0=asum[:, pr, :], in1=av[:, ql, :], op=ALU.add),
                                       reads=R("av%d" % ql, "asum%d" % pr), writes=["asum%d" % pr])
                        mk.dve(lambda e, ql=ql: e.tensor_tensor_scan(out=cs[:, ql, :], data0=cmask, data1=sw[:, ql, :], initial=0.0,
                                                                     op0=ALU.mult, op1=ALU.add), reads=R("sw%d" % ql, "cst"), writes=["cs%d" % ql])
                        if dt_ == 0:
                            mk.dve(lambda e, ql=ql: e.tensor_tensor(out=Lx[:, ql, :], in0=cs[:, ql, :], in1=sw[:, ql, :], op=ALU.subtract),
                                   reads=R("cs%d" % ql, "sw%d" % ql), writes=["Lx%d" % ql])
                        else:
                            mk.dve(lambda e, ql=ql: e.tensor_tensor(out=v3(Lx[:, ql, :]), in0=v3(cs[:, ql, :])[:, :, 63:64].to_broadcast([128, 8, 64]),
                                                                    in1=v3(cs[:, ql, :]), op=ALU.subtract),
                                   reads=R("cs%d" % ql), writes=["Lx%d" % ql])
                            mk.dve(lambda e, ql=ql: e.tensor_tensor(out=cs[:, ql, :], in0=Lx[:, ql, :], in1=sw[:, ql, :], op=ALU.add),
                                   reads=R("Lx%d" % ql, "sw%d" % ql), writes=["cs%d" % ql])
                        mk.act(lambda e, ql=ql: e.activation(out=Ep[:, ql, :], in_=cs[:, ql, :], func=AF.Exp, scale=-C0), reads=R("cs%d" % ql), writes=["Ep%d" % ql])
                        mk.act(lambda e, ql=ql: e.activation(out=Lx[:, ql, :], in_=Lx[:, ql, :], func=AF.Exp, scale=-C0), reads=R("Lx%d" % ql), writes=["Lx%d" % ql])
                        mk.act(lambda e, ql=ql: e.activation(out=cs[:, ql, :], in_=cs[:, ql, :], func=AF.Exp, scale=C0), reads=R("cs%d" % ql, "Ep%d" % ql), writes=["cs%d" % ql])
                        pcol = 63 if dt_ == 0 else 0
                        mk.dve(lambda e, pr=pr, ql=ql, pcol=pcol: e.tensor_copy(out=Pend[:, pr, :], in_=v3(Ep[:, ql, :])[:, :, pcol]), reads=R("Ep%d" % ql), writes=["Pend"])
                        mk.dve(lambda e, pr=pr, ql=ql: e.scalar_tensor_tensor(out=AR[:, pr, :, 0:64], in0=v3(kk[:, pr, :]), scalar=-1.0, in1=v3(Lx[:, ql, :]),
                                                                              op0=ALU.mult, op1=ALU.mult), reads=R("kk%d" % pr, "Lx%d" % ql), writes=["AR%d" % pr])
                        mk.dve(lambda e, pr=pr, ql=ql: e.tensor_tensor(out=AR[:, pr, :, 64:128], in0=v3(rkv[:, pr, :]), in1=v3(Ep[:, ql, :]), op=ALU.mult),
                               reads=R("rkv%d" % pr, "Ep%d" % ql), writes=["AR%d" % pr])
                        mk.dve(lambda e, pr=pr, ql=ql: e.tensor_tensor(out=t1[:, ql, :], in0=kk[:, pr, :], in1=av[:, ql, :], op=ALU.mult),
                               reads=R("kk%d" % pr, "av%d" % ql), writes=["t1%d" % ql])
                        mk.dve(lambda e, pr=pr, ql=ql: e.tensor_tensor(out=BK[:, pr, :, 0:64], in0=v3(t1[:, ql, :]), in1=v3(cs[:, ql, :]), op=ALU.mult),
                               reads=R("t1%d" % ql, "cs%d" % ql), writes=["BK%d" % pr])
                        mk.dve(lambda e, pr=pr, ql=ql: e.tensor_scalar(out=t1[:, ql, :], in0=av[:, ql, :], scalar1=cnum(4), scalar2=ccol("ka", pr),
                                                                       op0=ALU.subtract, op1=ALU.mult), reads=R("av%d" % ql, "cols", "t1%d" % ql), writes=["t1%d" % ql])
                        mk.dve(lambda e, pr=pr, ql=ql: e.scalar_tensor_tensor(out=t1[:, ql, :], in0=t1[:, ql, :], scalar=1.0, in1=rkv[:, 4 + pr, :],
                                                                              op0=ALU.add, op1=ALU.mult), reads=R("t1%d" % ql, "rkv%d" % (4 + pr)), writes=["t1%d" % ql])
                        mk.dve(lambda e, pr=pr, ql=ql: e.tensor_tensor(out=BK[:, pr, :, 64:128], in0=v3(t1[:, ql, :]), in1=v3(cs[:, ql, :]), op=ALU.mult),
                               reads=R("t1%d" % ql, "cs%d" % ql), writes=["BK%d" % pr])
                if not mgo():
                    return
                barrier()
                pool.off = base_t
                Gm = [pool.f32(4 * 256).rearrange("p (h n) -> p h n", h=4) for _ in range(2)]
                ZZ = [pool.bf16(2 * 4 * 64).rearrange("p (a h n) -> p a h n", a=2, h=4) for _ in range(2)]
                Qt = [pool.bf16(4 * 64).rearrange("p (h n) -> p h n", h=4) for _ in range(2)]
                TOK = [pool.f32(3 * 4 * 64).rearrange("p (a h n) -> p a h n", a=3, h=4) for _ in range(2)]
                Wsb = pool.bf16(4 * 64).rearrange("p (h n) -> p h n", h=4)
                Usb = pool.f32(4 * 64).rearrange("p (h n) -> p h n", h=4)
                Ytmp = pool.f32(4 * 64).rearrange("p (h n) -> p h n", h=4)
                unit = [0]

                def heads():
                    for q in range(4):
                        for e_ in range(2):
                            yield q, 64 * e_

                def tseries_stages(c, dt_=dt_):
                    u = unit[0] % 2
                    unit[0] += 1
                    G, Z2, Q, TK = Gm[u], ZZ[u], Qt[u], TOK[u]
                    gk, zk, qk, tk = "Gm%d" % u, "ZZ%d" % u, "Q%d" % u, "TOK%d" % u
                    stages = []

                    def st_g():
                        for q, fo in heads():
                            mm_group(psG[fo:fo + 64, q, 0:128], "psA%d" % (q // 2),
                                     [(BK[fo:fo + 64, q, c, 0:64], AR[fo:fo + 64, q, c, :], ["BK%d" % q, "AR%d" % q, "SCR"])])
                            mm_group(psG[fo:fo + 64, q, 128:256], "psA%d" % (q // 2),
                                     [(BK[fo:fo + 64, q, c, 64:128], AR[fo:fo + 64, q, c, :], ["BK%d" % q, "AR%d" % q, "SCR"])])
                            mm_group(psZv(0, q)[fo:fo + 64, :], "psB0",
                                     [(AR[fo:fo + 64, q, c, 0:64], BK[fo:fo + 64, q, c, 0:64], ["BK%d" % q, "AR%d" % q, "SCR"])])
                        mk.dve(lambda e: e.tensor_tensor(out=G[:], in0=psG, in1=maskG(dt_).unsqueeze(1).to_broadcast([128, 4, 256]), op=ALU.mult),
                               reads=R("psA0", "psA1", "cst"), writes=[gk])
                        mk.dve(lambda e: e.tensor_tensor(out=Z2[:, 1, :, :], in0=psZ[:, 0:4, :], in1=maskZ(dt_).unsqueeze(1).to_broadcast([128, 4, 64]),
                                                         op=ALU.mult), reads=R("psB0", "cst"), writes=[zk])
                        mk.act(lambda e: e.copy(out=Z2[:, 0, :, :], in_=G[:, :, 0:64]), reads=R(gk), writes=[zk])
                        mk.dve(lambda e: e.tensor_tensor(out=Q[:], in0=G[:, :, 0:64], in1=id64.unsqueeze(1).to_broadcast([128, 4, 64]), op=ALU.add),
                               reads=R(gk, "cst"), writes=[qk])
                    stages.append(st_g)

                    def mk_burst(lev):
                        def st():
                            for q, fo in heads():
                                idb = identb[fo:fo + 64, fo:fo + 64]
                                if lev <= 4:
                                    mm_group(psZv(1, q)[fo:fo + 64, :], "psB0b", [(Z2[fo:fo + 64, 1, q, :], Z2[fo:fo + 64, 0, q, :], [zk, "SCR"])])
                                mm_group(psZv(2, q)[fo:fo + 64, :], "psB1", [(Z2[fo:fo + 64, 0, q, :], Z2[fo:fo + 64, 1, q, :], [zk, "SCR"])])
                                if lev >= 2:
                                    mm_group(psZv(0, q)[fo:fo + 64, :], "psB0", [(idb, Q[fo:fo + 64, q, :], ["cst", qk, "SCR"]),
                                                                                 (Z2[fo:fo + 64, 1, q, :], Q[fo:fo + 64, q, :], [zk, qk, "SCR"])])
                            if lev >= 2:
                                mk.act(lambda e: e.copy(out=Q[:], in_=psZ[:, 0:4, :]), reads=R("psB0"), writes=[qk])
                            if lev <= 4:
                                mk.act(lambda e: e.copy(out=Z2[:, 0, :, :], in_=psZ[:, 4:8, :]), reads=R("psB0b"), writes=[zk])
                            mk.act(lambda e: e.copy(out=Z2[:, 1, :, :], in_=psZ[:, 8:12, :]), reads=R("psB1"), writes=[zk])
                        return st
                    for lev in range(1, 6):
                        stages.append(mk_burst(lev))

                    def st_last():
                        for q, fo in heads():
                            idb = identb[fo:fo + 64, fo:fo + 64]
                            mm_group(psZv(0, q)[fo:fo + 64, :], "psB0", [(idb, Q[fo:fo + 64, q, :], ["cst", qk, "SCR"]),
                                                                         (Z2[fo:fo + 64, 1, q, :], Q[fo:fo + 64, q, :], [zk, qk, "SCR"])])
                        mk.act(lambda e: e.copy(out=Q[:], in_=psZ[:, 0:4, :]), reads=R("psB0"), writes=[qk])
                        for q, fo in heads():
                            idb = ident[fo:fo + 64, fo:fo + 64]
                            mm_group(psTv[fo:fo + 64, 0, q, :], "psT", [(BK[fo:fo + 64, q, c, 0:64], idb, ["BK%d" % q, "cst", "SCR"])])
                            mm_group(psTv[fo:fo + 64, 1, q, :], "psT", [(BK[fo:fo + 64, q, c, 64:128], idb, ["BK%d" % q, "cst", "SCR"])])
                            mm_group(psCv[fo:fo + 64, 1, 4 + q, :], "psCv", [(rkv[fo:fo + 64, 8 + q, c * 64:(c + 1) * 64], idb,
                                                                             ["rkv%d" % (8 + q), "cst", "SCR"])])
                        mk.act(lambda e: e.copy(out=TK[:, 0:2, :, :], in_=psTv), reads=R("psT"), writes=[tk])
                        mk.dve(lambda e: e.tensor_copy(out=TK[:, 2, :, :], in_=psCv[:, 1, 4:8, :]), reads=R("psCv"), writes=[tk])
                    stages.append(st_last)
                    return stages, (G, Q, TK, gk, qk, tk)

                def chain_stages(c, bufs, dt_=dt_, di=di, order=order):
                    G, Q, TK, gk, qk, tk = bufs
                    seq = c // cps
                    pos = order.index(c) % cps
                    stages = []

                    def st_w():
                        if pos == 0:
                            if kind == "prompt":
                                mk.dve(lambda e: e.memset(Mst[:], 0.0), reads=R(), writes=["Mst"])
                            elif kind == "aux":
                                if aux_idx == 0:
                                    mk.dve(lambda e: e.tensor_copy(out=Mst[:], in_=stt[:, 2, :, :]), reads=R("stt"), writes=["Mst"])
                                else:
                                    cc_ = COLS["coef"] + (aux_idx - 1)
                                    mk.dve(lambda e: e.scalar_tensor_tensor(out=Mst[:], in0=endst[:, aux_idx - 1, :, :], scalar=cols[:, cc_:cc_ + 1],
                                                                            in1=stt[:, 2 + aux_idx, :, :], op0=ALU.mult, op1=ALU.add),
                                           reads=R("stt", "end%d" % (aux_idx - 1), "cols"), writes=["Mst"])
                            else:
                                if dt_ == 0:
                                    mk.dve(lambda e: e.tensor_copy(out=Mst[:], in_=stt[:, 0, :, :]), reads=R("stt"), writes=["Mst"])
                                    for a in range(3):
                                        cc_ = COLS["coef"] + 2 + a
                                        mk.dve(lambda e, a=a, cc_=cc_: e.scalar_tensor_tensor(out=Mst[:], in0=endst[:, a, :, :], scalar=cols[:, cc_:cc_ + 1],
                                                                                          in1=Mst[:], op0=ALU.mult, op1=ALU.add),
                                               reads=R("end%d" % a, "cols", "Mst"), writes=["Mst"])
                                else:
                                    cc_ = COLS["coef"] + 5
                                    mk.dve(lambda e: e.scalar_tensor_tensor(out=Mst[:], in0=endst[:, 2, :, :], scalar=cols[:, cc_:cc_ + 1],
                                                                            in1=stt[:, 1, :, :], op0=ALU.mult, op1=ALU.add),
                                           reads=R("stt", "end2", "cols"), writes=["Mst"])
                        for q, fo in heads():
                            mm_group(psCv[fo:fo + 64, 0, q, :], "psC0w",
                                     [(AR[fo:fo + 64, q, c, 0:64], Mst[fo:fo + 64, q, :], ["AR%d" % q, "Mst", "SCR"]),
                                      (G[fo:fo + 64, q, 128:192], TK[fo:fo + 64, 2, q, :], [gk, tk, "SCR"])])
                        mk.act(lambda e: e.copy(out=Wsb[:], in_=psCv[:, 0, 0:4, :]), reads=R("psC0w"), writes=["Wsb"])
                    stages.append(st_w)

                    def st_u():
                        for q, fo in heads():
                            mm_group(psCv[fo:fo + 64, 0, 4 + q, :], "psC0u", [(Q[fo:fo + 64, q, :], Wsb[fo:fo + 64, q, :], [qk, "Wsb", "SCR"])])
                        mk.dve(lambda e: e.tensor_copy(out=Usb[:], in_=psCv[:, 0, 4:8, :]), reads=R("psC0u"), writes=["Usb"])
                    stages.append(st_u)

                    def st_ym():
                        for q, fo in heads():
                            if full and KV != 5:
                                mm_group(psSv[fo:fo + 64, q, :], "psS",
                                         [(Mst[fo:fo + 64, q, :], AR[fo:fo + 64, q, c, 64:128], ["Mst", "AR%d" % q, "SCR"]),
                                          (Usb[fo:fo + 64, q, :], G[fo:fo + 64, q, 64:128], ["Usb", gk, "SCR"]),
                                          (TK[fo:fo + 64, 2, q, :], G[fo:fo + 64, q, 192:256], [tk, gk, "SCR"])])
                            mm_group(psSv[fo:fo + 64, 4 + q, :], "psS",
                                     [(TK[fo:fo + 64, 0, q, :], Usb[fo:fo + 64, q, :], [tk, "Usb", "SCR"]),
                                      (TK[fo:fo + 64, 1, q, :], TK[fo:fo + 64, 2, q, :], [tk, "SCR"])])
                        if full and KV != 6:
                            ydst = yT[:, :, c * 64:(c + 1) * 64]
                            if di == 0 and KV != 9:
                                mk.dve(lambda e: e.tensor_copy(out=ydst, in_=psSv[:, 0:4, :]), reads=R("psS"), writes=["yT"])
                            elif di == 0 and KV == 8:
                                mk.act(lambda e: e.copy(out=Wsb[:], in_=psSv[:, 0:4, :]), reads=R("psS", "Wsb"), writes=["Wsb"])
                                mk.dve(lambda e: e.tensor_copy(out=ydst, in_=Wsb[:]), reads=R("Wsb"), writes=["yT"])
                            elif di == 0:
                                mk.act(lambda e: e.copy(out=ydst, in_=psSv[:, 0:4, :]), reads=R("psS"), writes=["yT"])
                            else:
                                mk.act(lambda e: e.copy(out=Ytmp[:], in_=psSv[:, 0:4, :]), reads=R("psS", "Ytmp"), writes=["Ytmp"])
                                mk.dve(lambda e: e.tensor_tensor(out=ydst, in0=ydst, in1=Ytmp[:], op=ALU.add), reads=R("Ytmp", "yT"), writes=["yT"])
                        mk.dve(lambda e: e.tensor_tensor(out=Mtmp[:], in0=Mst[:], in1=psSv[:, 4:8, :], op=ALU.add),
                               reads=R("psS", "Mst"), writes=["Mtmp"])
                        mk.dve(lambda e: e.tensor_tensor(out=Mst[:], in0=Mtmp[:], in1=Pend[:, :, c:c + 1].to_broadcast([128, 4, 64]), op=ALU.mult),
                               reads=R("Mtmp", "Pend"), writes=["Mst"])
                        if pos == cps - 1:
                            if kind == "aux":
                                mk.act(lambda e: e.copy(out=endst[:, aux_idx, :, :], in_=Mst[:]), reads=R("Mst"), writes=["end%d" % aux_idx])
                            elif kind == "prompt":
                                for q in range(4):
                                    mm_group(psT[0:64, q * 128:(q + 1) * 128], "psT", [(Mst[:, q, :], ident, ["Mst", "cst", "SCR"])])
                                mk.act(lambda e: e.copy(out=nsb[0:64, seq, di, :, :], in_=psT[0:64, :].rearrange("p (a n) -> p a n", a=4)),
                                       reads=R("psT"), writes=["nsb"])
                    stages.append(st_ym)
                    return stages

                prev = None
                for idx_c in range(len(order) + 1):
                    A, bufsA = ([], None)
                    if idx_c < len(order):
                        A, bufsA = tseries_stages(order[idx_c])
                    Bs = []
                    if prev is not None:
                        Bs = chain_stages(prev[0], prev[1])
                    for i in range(max(len(A), len(Bs))):
                        if i < len(A):
                            KTC[0] += 1
                            if KTC[0] <= KT:
                                A[i]()
                        if i < len(Bs):
                            KTC[0] += 1
                            if KTC[0] <= KT:
                                Bs[i]()
                    prev = (order[idx_c], bufsA) if idx_c < len(order) else None
            if not full:
                return
            tap("yT_" + kind, yT.rearrange("p q n -> p (q n)"), 4 * N, R("yT"))
            tap("rkv_" + kind, rkv.rearrange("p q n -> p (q n)"), 12 * N, R(*["rkv%d" % i for i in range(12)]))
            barrier()
            MARK[kind] = len(mk.ops)
            pool.off = base_d
            bv = pool.f32(4 * N).rearrange("p (q n) -> p q n", q=4)
            tA = pool.f32(2 * N).rearrange("p (q n) -> p q n", q=2)
            tB = pool.f32(2 * N).rearrange("p (q n) -> p q n", q=2)
            for pr in range(4):
                b = pr % 2
                mk.dve(lambda e, pr=pr, b=b: e.tensor_scalar(out=tA[:, b, :], in0=asum[:, pr, :], scalar1=cnum(5), scalar2=ccol("ka", pr), op0=ALU.subtract, op1=ALU.mult),
                       reads=R("asum%d" % pr, "cols", "tA%d" % b), writes=["tA%d" % b])
                mk.dve(lambda e, pr=pr, b=b: e.scalar_tensor_tensor(out=tA[:, b, :], in0=tA[:, b, :], scalar=2.0, in1=rkv[:, 4 + pr, :], op0=ALU.add, op1=ALU.mult),
                       reads=R("tA%d" % b, "rkv%d" % (4 + pr)), writes=["tA%d" % b])
                mk.dve(lambda e, pr=pr, b=b: e.scalar_tensor_tensor(out=tA[:, b, :], in0=tA[:, b, :], scalar=ccol("rk", pr), in1=rkv[:, pr, :], op0=ALU.mult, op1=ALU.mult),
                       reads=R("tA%d" % b, "rkv%d" % pr, "cols"), writes=["tA%d" % b])
                mm_group(psA[:, b, :], "psA%d" % b, [(bones, tA[:, b, :], ["cst", "tA%d" % b, "SCR"])])
                mk.dve(lambda e, pr=pr, b=b: e.tensor_tensor(out=bv[:, pr, :], in0=psA[:, b, :], in1=rkv[:, 8 + pr, :], op=ALU.mult),
                       reads=R("psA%d" % b, "rkv%d" % (8 + pr)), writes=["bv%d" % pr])
                mm_group(psB[:, b, :], KB(b), [(bones64, yT[:, pr, :], ["cst", "yT", "SCR"])])
                mk.act(lambda e, b=b: e.copy(out=tB[:, b, :], in_=psB[:, b, :]), reads=R("tB%d" % b, *KB(b)), writes=["tB%d" % b])
                mk.dve(lambda e, pr=pr, b=b: e.tensor_tensor(out=yT[:, pr, :], in0=yT[:, pr, :], in1=tB[:, b, :], op=ALU.subtract), reads=R("tB%d" % b, "yT"), writes=["yT"])
                mk.act(lambda e, pr=pr, b=b: e.activation(out=tA[:, b, :], in_=yT[:, pr, :], func=AF.Square), reads=R("yT", "tA%d" % b), writes=["tA%d" % b])
                mm_group(psA[:, b, :], "psA%d" % b, [(bones64, tA[:, b, :], ["cst", "tA%d" % b, "SCR"])])
                mk.act(lambda e, b=b: e.activation(out=tB[:, b, :], in_=psA[:, b, :], func=AF.Ln, bias=cnum(2), scale=1.0), reads=R("cols", "tB%d" % b, "psA%d" % b), writes=["tB%d" % b])
                mk.act(lambda e, b=b: e.activation(out=tB[:, b, :], in_=tB[:, b, :], func=AF.Exp, scale=-0.5), reads=R("tB%d" % b), writes=["tB%d" % b])
                mk.dve(lambda e, pr=pr, b=b: e.scalar_tensor_tensor(out=yT[:, pr, :], in0=yT[:, pr, :], scalar=ccol("gng", pr), in1=tB[:, b, :], op0=ALU.mult, op1=ALU.mult),
                       reads=R("tB%d" % b, "yT", "cols"), writes=["yT"])
                mk.dve(lambda e, pr=pr: e.scalar_tensor_tensor(out=yT[:, pr, :], in0=yT[:, pr, :], scalar=ccol("gnb", pr), in1=bv[:, pr, :], op0=ALU.add, op1=ALU.add),
                       reads=R("bv%d" % pr, "yT", "cols"), writes=["yT"])
            tap("yn_" + kind, yT.rearrange("p q n -> p (q n)"), 4 * N, R("yT"))
            barrier()
            pool.off = 2048
            yA = pool.f32(8 * N).rearrange("p (q n) -> p q n", q=8)
            pool.off = 12288
            yB = pool.f32(8 * N).rearrange("p (q n) -> p q n", q=8)
            tA2 = pool.f32(2 * N).rearrange("p (q n) -> p q n", q=2)
            tB2 = pool.f32(2 * N).rearrange("p (q n) -> p q n", q=2)
            yaT = pool.bf16(4 * N).rearrange("p (q n) -> p q n", q=4)
            ybT = pool.bf16(4 * N).rearrange("p (q n) -> p q n", q=4)
            cbT = pool.bf16(4 * N).rearrange("p (q n) -> p q n", q=4)
            ccT = pool.f32(4 * N).rearrange("p (q n) -> p q n", q=4)
            mgT = pool.bf16(8 * N).rearrange("p (q n) -> p q n", q=8)
            rl = 64 if kind == "own" else L
            r3 = lambda ap: ap.rearrange("p (r n) -> p r n", n=rl)

            def branch(wmat, srcT, srckey, dst, dstkey):
                for half in range(2):
                    slw, keyw = load_slab([(lambda t: t[:, 0:2048].rearrange("p (k n) -> p k n", k=4),
                                            wmat[:, half * 512:(half + 1) * 512].rearrange("(k p) n -> p k n", p=128))])
                    slwv = slw[:, 0:2048].rearrange("p (k n) -> p k n", k=4)
                    for tt in range(4):
                        o = half * 4 + tt
                        b = tt % 2
                        mm_group(psB[:, b, :], KB(b), [(slwv[:, k, tt * 128:(tt + 1) * 128], srcT[:, k, :], [keyw, srckey % k, "SCR"]) for k in range(4)])
                        evac(dst[:, o, :], psB[:, b, :], R(*KB(b)), [dstkey % o])

            for s in range(3, 11):
                sl, key = load_slab([(lambda t: t[:, 0:4096].rearrange("p (k n) -> p k n", k=8),
                                     w_in[:, s * 512:(s + 1) * 512].rearrange("(k p) n -> p k n", p=128))])
                slv = sl[:, 0:4096].rearrange("p (k n) -> p k n", k=8)
                for tt in range(4):
                    b = tt % 2
                    mm_group(psA[:, b, :], "psA%d" % b,
                             [(slv[:, k, tt * 128:(tt + 1) * 128], hT[:, k, :], [key, "hT%d" % k, "SCR"]) for k in range(8)])
                    pa = psA[:, b, :]
                    pk = "psA%d" % b
                    tb = tt % 2
                    if s == 3:
                        mk.act(lambda e, pa=pa, tb=tb: e.activation(out=tA2[:, tb, :], in_=pa, func=AF.Sigmoid), reads=R(pk, "tA2%d" % tb), writes=["tA2%d" % tb])
                        mk.dve(lambda e, tt=tt, tb=tb: e.tensor_tensor(out=yaT[:, tt, :], in0=yT[:, tt, :], in1=tA2[:, tb, :], op=ALU.mult),
                               reads=R("yT", "tA2%d" % tb), writes=["yaT%d" % tt])
                    elif s == 4:
                        mk.act(lambda e, tt=tt, pa=pa: e.copy(out=cbT[:, tt, :], in_=pa), reads=R(pk), writes=["cbT%d" % tt])
                    elif s == 5:
                        mk.act(lambda e, tt=tt, pa=pa: e.copy(out=ccT[:, tt, :], in_=pa), reads=R(pk), writes=["ccT%d" % tt])
                    elif s == 6:
                        u = tA2[:, tb, :]
                        uk = "tA2%d" % tb
                        acc = tB2[:, tb, :]
                        ak = "tB2%d" % tb
                        mk.dve(lambda e, tt=tt, pa=pa, u=u: e.tensor_tensor(out=u, in0=ccT[:, tt, :], in1=pa, op=ALU.mult), reads=R(pk, "ccT%d" % tt, uk), writes=[uk])
                        mk.dve(lambda e, tt=tt, u=u, acc=acc: e.tensor_scalar(out=acc, in0=u, scalar1=ccol("cw", 4 + tt), scalar2=ccol("cb", tt), op0=ALU.mult, op1=ALU.add),
                               reads=R(uk, "cols", ak), writes=[ak])
                        mk.dve(lambda e, tt=tt, u=u, acc=acc: e.scalar_tensor_tensor(out=r3(acc)[:, :, 1:rl], in0=r3(u)[:, :, 0:rl - 1], scalar=ccol("cw", tt),
                                                                                    in1=r3(acc)[:, :, 1:rl], op0=ALU.mult, op1=ALU.add), reads=R(uk, ak, "cols"), writes=[ak])
                        mk.dve(lambda e, tt=tt, u=u, acc=acc: e.scalar_tensor_tensor(out=r3(acc)[:, :, 0:rl - 1], in0=r3(u)[:, :, 1:rl], scalar=ccol("cw", 8 + tt),
                                                                                    in1=r3(acc)[:, :, 0:rl - 1], op0=ALU.mult, op1=ALU.add), reads=R(uk, ak, "cols"), writes=[ak])
                        mk.dve(lambda e, tt=tt, acc=acc: e.tensor_tensor(out=ybT[:, tt, :], in0=cbT[:, tt, :], in1=acc, op=ALU.mult), reads=R(ak, "cbT%d" % tt), writes=["ybT%d" % tt])
                    else:
                        gi = (s - 7) * 4 + tt
                        sg = tA2[:, tb, :]
                        sk = "tA2%d" % tb
                        mk.act(lambda e, pa=pa, sg=sg: e.activation(out=sg, in_=pa, func=AF.Sigmoid), reads=R(pk, sk), writes=[sk])
                        if gi < 8:
                            mk.dve(lambda e, gi=gi, sg=sg: e.tensor_tensor(out=yA[:, gi, :], in0=yA[:, gi, :], in1=sg, op=ALU.mult), reads=R(sk, "yA%d" % gi), writes=["yA%d" % gi])
                        else:
                            g2 = gi - 8
                            mk.dve(lambda e, g2=g2, sg=sg: e.tensor_tensor(out=sg, in0=sg, in1=yB[:, g2, :], op=ALU.mult), reads=R(sk, "yB%d" % g2), writes=[sk])
                            mk.dve(lambda e, g2=g2, sg=sg: e.tensor_tensor(out=mgT[:, g2, :], in0=yA[:, g2, :], in1=sg, op=ALU.add), reads=R(sk, "yA%d" % g2), writes=["mgT%d" % g2])
                if s == 3:
                    barrier()
                    branch(wba, yaT, "yaT%d", yA, "yA%d")
                if s == 6:
                    branch(wbb, ybT, "ybT%d", yB, "yB%d")
            oT = pool.f32(8 * N).rearrange("p (k n) -> p k n", k=8)
            pool.off = 6144
            sq = pool.bf16(8 * N).rearrange("p (k n) -> p k n", k=8)
            tmp = pool.f32(2 * N).rearrange("p (k n) -> p k n", k=2)
            for half in range(2):
                sl, key = load_slab([(lambda t: t[:, 0:4096].rearrange("p (k n) -> p k n", k=8),
                                     wout[:, half * 512:(half + 1) * 512].rearrange("(k p) n -> p k n", p=128))])
                slv = sl[:, 0:4096].rearrange("p (k n) -> p k n", k=8)
                for tt in range(4):
                    i = half * 4 + tt
                    b = tt % 2
                    mm_group(psA[:, b, :], "psA%d" % b, [(slv[:, k, tt * 128:(tt + 1) * 128], mgT[:, k, :], [key, "mgT%d" % k, "SCR"]) for k in range(8)])
                    evac(oT[:, i, :], psA[:, b, :], R("psA%d" % b), ["oT"])
            tap("mo_" + kind, oT.rearrange("p q n -> p (q n)"), 8 * N, R("oT"))
            postnorm_residual(j, 3, oT, sq, tmp)

        stt = sb("stt", [128, 5, 4, 64])
        nsb = sb("nsb", [64, 2, 2, 4, 128])
        mk.dma("sp", "c0", lambda e: e.dma_start(out=stt[:], in_=std.rearrange("a q p v -> p a q v")), writes=["stt"])

        def load_x(g):
            pool = Pool()
            xin = pool.f32(4 * D).rearrange("p (t n) -> p t n", t=4)
            for tt in range(4):
                mk.dma("sp", "xin", lambda e, tt=tt: e.dma_start(out=xin[:, tt, :], in_=xg[g, tt * 128:(tt + 1) * 128, :]), reads=["SCR"], writes=["xin%d" % tt])
            for k in range(8):
                b = k % 2
                for tt in range(4):
                    mm_group(psA[:, b, tt * 128:(tt + 1) * 128], "psA%d" % b, [(xin[:, tt, k * 128:(k + 1) * 128], ident, ["xin%d" % tt, "cst", "SCR"])])
                evac(xT[:, k, :], psA[:, b, :], R("psA%d" % b), ["xT"])

        def store_y(gout):
            pool = Pool()
            yo = pool.f32(4 * D).rearrange("p (t n) -> p t n", t=4)
            for tt in range(4):
                for k in range(8):
                    b = k % 2
                    mm_group(psA[:, b, 0:128], "psA%d" % b, [(xT[:, k, tt * 128:(tt + 1) * 128], ident, ["xT", "cst", "SCR"])])
                    evac(yo[:, tt, k * 128:(k + 1) * 128], psA[:, b, 0:128], R("psA%d" % b), ["yo%d" % tt])
                mk.dma("sp", "yout", lambda e, tt=tt: e.dma_start(out=yout[gout * N + tt * 128:gout * N + (tt + 1) * 128, :], in_=yo[:, tt, :]), reads=R("yo%d" % tt))

        def ffn_phase(j, ai, bi, gi, w13, w2):
            pool = Pool()
            hT = pool.bf16(8 * N).rearrange("p (k n) -> p k n", k=8)
            sq = pool.bf16(8 * N).rearrange("p (k n) -> p k n", k=8)
            hid = pool.bf16(22 * N).rearrange("p (k n) -> p k n", k=22)
            oT = pool.f32(8 * N).rearrange("p (k n) -> p k n", k=8)
            tmp = pool.f32(2 * N).rearrange("p (k n) -> p k n", k=2)
            sgt = pool.f32(2 * N).rearrange("p (k n) -> p k n", k=2)
            prenorm(j, ai, bi, hT, sq, tmp)
            ffn(w13, w2, hT, hid, oT, sgt)
            postnorm_residual(j, gi, oT, sq, tmp)

        def mixer_phase(kind, j, aux_idx):
            pool = Pool()
            hT = pool.bf16(8 * N).rearrange("p (k n) -> p k n", k=8)
            tail = Pool()
            tail.off = SCR - (8 * N // 2 + 2 * N)
            sq = tail.bf16(8 * N).rearrange("p (k n) -> p k n", k=8)
            tmp = tail.f32(2 * N).rearrange("p (k n) -> p k n", k=2)
            prenorm(j, 2, 3, hT, sq, tmp)
            barrier()
            mixer(kind, j, hT, aux_idx)

        def chain_prep_aux(a):
            cc_ = COLS["coef"] + (a - 1)
            mk.dve(lambda e: e.tensor_scalar(out=hbA[:], in0=hlast[:, a - 1, :], scalar1=cols[:, cc_:cc_ + 1], scalar2=None, op0=ALU.mult),
                   reads=["hlast%d" % (a - 1), "cols"], writes=["hbA"])

        def chain_prep_own():
            c2 = COLS["coef"] + 2
            mk.dve(lambda e: e.tensor_scalar(out=hbF[:], in0=hlast[:, 0, :], scalar1=cols[:, c2:c2 + 1], scalar2=None, op0=ALU.mult),
                   reads=["hlast0", "cols"], writes=["hbF"])
            for a in (1, 2):
                mk.dve(lambda e, a=a: e.scalar_tensor_tensor(out=hbF[:], in0=hlast[:, a, :], scalar=cols[:, c2 + a:c2 + a + 1], in1=hbF[:], op0=ALU.mult, op1=ALU.add),
                       reads=["hlast%d" % a, "cols", "hbF"], writes=["hbF"])
            mk.dve(lambda e: e.tensor_scalar(out=hbB[:], in0=hlast[:, 2, :], scalar1=cols[:, c2 + 3:c2 + 4], scalar2=None, op0=ALU.mult),
                   reads=["hlast2", "cols"], writes=["hbB"])

        for a in range(3):
            if go():
                barrier()
                load_x(2 + a)
            if go():
                barrier()
                ffn_phase(1, 0, 0, 1, f1w13, f1w2)
            if go():
                barrier()
                if a > 0:
                    chain_prep_aux(a)
                mixer_phase("aux", 1, a)
        for (g, kind, j) in [(1, "own", 1), (0, "prompt", 0)]:
            if go():
                barrier()
                load_x(g)
            if go():
                barrier()
                ffn_phase(j, 0, 0, 1, f1w13, f1w2)
            if go():
                barrier()
                if kind == "own":
                    chain_prep_own()
                mixer_phase(kind, j, None)
            if go():
                barrier()
                ffn_phase(j, 4, 6, 5, f2w13, f2w2)
            if go():
                barrier()
                store_y(1 if kind == "own" else 0)
        if dbg_spec is not None:
            barrier()
            dbg_spec(mk, nc, locals())
        for sq_ in range(2):
            for dd in range(2):
                mk.dma("sp", "yout", lambda e, sq_=sq_, dd=dd: e.dma_start(out=nsout[sq_, dd].rearrange("(q e) v k -> v q e k", e=2),
                                                                           in_=nsb[:, sq_, dd, :, :].rearrange("p q (e k) -> p q e k", e=2)), reads=["nsb"])
        stats = mk.emit()
    return nc, stats


_CACHE = {}


def _prep_inputs(inp):
    f = lambda a: np.ascontiguousarray(np.asarray(a, np.float32))
    x_prompt, x_sample = f(inp["x_prompt"]), f(inp["x_sample"])
    c, state, c_ctx = f(inp["c"]), f(inp["state_rwkv"]), f(inp["c_ctx"])
    mu = f(inp["mu_shift"])[0]
    w1 = [np.concatenate([f(inp["decay_w1"])[0, d], f(inp["iclr_a1"])[0, d]], axis=1) for d in range(2)]
    w2 = [np.concatenate([f(inp["decay_w2"])[0, d], f(inp["iclr_a2"])[0, d]], axis=0) for d in range(2)]
    dw0, ia0 = f(inp["decay_w0"])[0], f(inp["iclr_a0"])[0]
    shared = {
        "w_mod": f(inp["w_mod"])[0], "f1w13": f(inp["ffn1_w13"])[0], "f1w2": f(inp["ffn1_w2"])[0],
        "f2w13": f(inp["ffn2_w13"])[0], "f2w2": f(inp["ffn2_w2"])[0], "w_in": f(inp["w_in"])[0],
        "wba": f(inp["w_branch_a"])[0], "wbb": f(inp["w_branch_b"])[0], "wout": f(inp["w_out"])[0],
        "consts": _make_consts(),
    }
    in_maps = []
    for core in range(8):
        b, s = core // 4, core % 4
        if s == 0:
            aux = [(3, 1), (2, 1), (1, 1)]
            cont = (1.0, 1.0)
            selF = (0.0, 0.0, 0.0)
            selB = 1.0
            init = ["F", "0", "B", "0", "0"]
        elif s == 1:
            aux = [(0, 0), (3, 1), (2, 1)]
            cont = (0.0, 1.0)
            selF = (1.0, 0.0, 0.0)
            selB = 1.0
            init = ["0", "0", "F", "B", "0"]
        elif s == 2:
            aux = [(0, 0), (1, 0), (3, 1)]
            cont = (1.0, 0.0)
            selF = (0.0, 1.0, 0.0)
            selB = 1.0
            init = ["0", "0", "F", "0", "B"]
        else:
            aux = [(0, 0), (1, 0), (2, 0)]
            cont = (1.0, 1.0)
            selF = (0.0, 0.0, 1.0)
            selB = 0.0
            init = ["0", "B", "F", "0", "0"]
        xgr = np.empty((5, N, D), np.float32)
        xgr[0] = x_prompt[2 * core:2 * core + 2].reshape(N, D)
        xgr[1] = x_sample[b, s * N:(s + 1) * N]
        for a, (seg, dr) in enumerate(aux):
            xs = x_sample[b, seg * N:(seg + 1) * N]
            xgr[2 + a] = xs[::-1] if dr == 1 else xs
        dsl = [0, 1] + [dr for (_, dr) in aux]
        cols = np.zeros((128, NCOL), np.float32)
        cvs = [c_ctx, c[b]]
        for k in range(8):
            for j in range(2):
                cols[:, COLS["cv"] + k * 2 + j] = cvs[j][k * 128:(k + 1) * 128]
        cols[:, COLS["bmod"]:COLS["bmod"] + 72] = _colize(inp["b_mod"][0])
        cols[:, COLS["ng"]:COLS["ng"] + 48] = _colize(np.asarray(inp["norm_g"][0]).reshape(-1))
        for ds, dr in enumerate(dsl):
            for x in range(2):
                cols[:, COLS["mu"] + ds * 16 + x * 8:COLS["mu"] + ds * 16 + x * 8 + 8] = _colize(mu[dr, x])
            cols[:, COLS["w0"] + ds * 4:COLS["w0"] + ds * 4 + 4] = _colize(dw0[dr])
            cols[:, COLS["a0"] + ds * 4:COLS["a0"] + ds * 4 + 4] = _colize(ia0[dr])
        cols[:, COLS["kk"]:COLS["kk"] + 4] = _colize(inp["k_k"][0])
        cols[:, COLS["ka"]:COLS["ka"] + 4] = _colize(inp["k_a"][0])
        cols[:, COLS["rk"]:COLS["rk"] + 4] = _colize(np.asarray(inp["r_k"][0]).reshape(-1))
        cols[:, COLS["gng"]:COLS["gng"] + 4] = _colize(inp["gn_gain"][0])
        cols[:, COLS["gnb"]:COLS["gnb"] + 4] = _colize(inp["gn_bias"][0])
        cols[:, COLS["cw"]:COLS["cw"] + 12] = _colize(np.asarray(inp["conv_w"][0]).reshape(-1))
        cols[:, COLS["cb"]:COLS["cb"] + 4] = _colize(inp["conv_b"][0])
        cols[:, COLS["coef"]:COLS["coef"] + 6] = np.array([cont[0], cont[1], selF[0], selF[1], selF[2], selB], np.float32)[None, :]
        cols[:, COLS["num"]:COLS["num"] + 6] = np.array([1e-6, 1e-12, 64e-5, 0.0, 1.0, 2.0], np.float32)[None, :]
        def mlay(d):
            S = state[b, 0, d]
            return np.ascontiguousarray(S.transpose(0, 2, 1).reshape(4, 128, 64))
        stv = np.zeros((5, 4, 128, 64), np.float32)
        for i, t in enumerate(init):
            if t == "F":
                stv[i] = mlay(0)
            elif t == "B":
                stv[i] = mlay(1)
        m = dict(shared)
        m.update({"xg": xgr, "cols": cols, "st": stv,
                  "w1c": np.ascontiguousarray(np.stack([w1[d] for d in dsl])),
                  "w2c": np.ascontiguousarray(np.stack([w2[d] for d in dsl]))})
        in_maps.append(m)
    return in_maps


def kernel(**inputs):
    if "nc" not in _CACHE:
        _CACHE["nc"], _CACHE["stats"] = build_program()
    nc = _CACHE["nc"]
    in_maps = _prep_inputs(inputs)
    res = run_bass_kernel_spmd(nc, in_maps, core_ids=list(range(8)))
    y_prompt = np.empty((16, 256, D), np.float32)
    y_sample = np.empty((2, 2048, D), np.float32)
    new_state = np.empty((16, 1, 2, 8, 64, 64), np.float32)
    for core in range(8):
        r = res.results[core]
        b, s = core // 4, core % 4
        y = np.asarray(r["y"], np.float32)
        y_prompt[2 * core:2 * core + 2] = y[0:N].reshape(2, 256, D)
        y_sample[b, s * N:(s + 1) * N] = y[N:2 * N]
        new_state[2 * core:2 * core + 2, 0] = np.asarray(r["ns"], np.float32)
    return (y_prompt, y_sample, new_state)
```

```python
import contextlib
import numpy as np
import concourse.bass as bass
import concourse.mybir as mybir
from concourse.bass_utils import run_bass_kernel_spmd

F32 = mybir.dt.float32
BF16 = mybir.dt.bfloat16
ALU = mybir.AluOpType
AF = mybir.ActivationFunctionType

D = 1024
DFF = 2816
import os
SUB = int(os.environ.get('KSUB', '99'))
KT = int(os.environ.get('KT', '1000000'))
KTC = [0]
MARK = {}
TAPS = [t for t in os.environ.get('KTAPS', '').split(',') if t]
KV = int(os.environ.get('KV', '0'))
N = 512
C = 64
NCH = N // C
C0 = float(np.exp(-0.5))


class _Op:
    __slots__ = ("idx", "eng", "fn", "deps", "chan", "chanpos", "needs_inc", "inc_count", "engpos")

    def __init__(self, idx, eng, fn, deps, chan):
        self.idx = idx
        self.eng = eng
        self.fn = fn
        self.deps = deps
        self.chan = chan
        self.chanpos = None
        self.needs_inc = False
        self.inc_count = None
        self.engpos = None


class MK:
    ENGS = ("pe", "act", "dve", "pool", "sp")

    def __init__(self, nc):
        self.nc = nc
        self.ops = []
        self.last_writer = {}
        self.readers = {}
        self.chan_count = {}

    def add(self, eng, fn, reads=(), writes=(), chan=None):
        idx = len(self.ops)
        deps = set()
        writes = list(writes)
        if chan is not None:
            writes.append(("__chan__", chan))
        for r in reads:
            w = self.last_writer.get(r)
            if w is not None:
                deps.add(w)
        for w in writes:
            lw = self.last_writer.get(w)
            if lw is not None:
                deps.add(lw)
            deps.update(self.readers.get(w, ()))
        op = _Op(idx, eng, fn, deps, chan)
        if chan is not None:
            op.chanpos = self.chan_count.get(chan, 0)
            self.chan_count[chan] = op.chanpos + 1
        self.ops.append(op)
        for r in reads:
            self.readers.setdefault(r, []).append(idx)
        for w in writes:
            self.last_writer[w] = idx
            self.readers[w] = []
        return idx

    def pe(self, fn, reads=(), writes=()):
        return self.add("pe", fn, reads, writes)

    def act(self, fn, reads=(), writes=()):
        return self.add("act", fn, reads, writes)

    def dve(self, fn, reads=(), writes=()):
        return self.add("dve", fn, reads, writes)

    def pool(self, fn, reads=(), writes=()):
        return self.add("pool", fn, reads, writes)

    def dma(self, eng, chan, fn, reads=(), writes=()):
        return self.add(eng, fn, reads, writes, chan=chan)

    def emit(self):
        nc = self.nc
        ops = self.ops
        per_eng = {e: [] for e in self.ENGS}
        for op in ops:
            op.engpos = len(per_eng[op.eng])
            per_eng[op.eng].append(op)

        def need_sem(op, d):
            if d.chan is not None:
                return True
            if d.eng != op.eng:
                return True
            if op.eng == "pe":
                return False
            return (op.engpos - d.engpos) <= 2

        for op in ops:
            for di in op.deps:
                d = ops[di]
                if d.chan is None and need_sem(op, d):
                    d.needs_inc = True
        cnt = {e: 0 for e in self.ENGS}
        for op in ops:
            if op.chan is None and op.needs_inc:
                cnt[op.eng] += 1
                op.inc_count = cnt[op.eng]
        chans = sorted(self.chan_count.keys())
        with contextlib.ExitStack() as st:
            esem = {e: st.enter_context(nc.semaphore("s_" + e)) for e in self.ENGS}
            csem = {c: st.enter_context(nc.semaphore("c_" + str(c))) for c in chans}
            block = st.enter_context(nc.Block())

            def run_engine(ename, eobj):
                waited = {}

                def wait(key, sem, val):
                    if waited.get(key, 0) >= val:
                        return
                    waited[key] = val
                    eobj.wait_ge(sem, val)

                for op in per_eng[ename]:
                    for di in sorted(op.deps):
                        d = ops[di]
                        if not need_sem(op, d):
                            continue
                        if d.chan is not None:
                            wait(("c", d.chan), csem[d.chan], 16 * (d.chanpos + 1))
                        else:
                            wait(("e", d.eng), esem[d.eng], d.inc_count)
                    ins = op.fn(eobj)
                    if op.chan is not None:
                        ins.then_inc(csem[op.chan], 16)
                    elif op.needs_inc:
                        ins.then_inc(esem[op.eng], 1)
                if ename == "sp":
                    for c in chans:
                        wait(("c", c), csem[c], 16 * self.chan_count[c])

            @block.tensor
            def _(e):
                run_engine("pe", e)

            @block.scalar
            def _(e):
                run_engine("act", e)

            @block.vector
            def _(e):
                run_engine("dve", e)

            @block.gpsimd
            def _(e):
                run_engine("pool", e)

            @block.sync
            def _(e):
                run_engine("sp", e)
        return {e: len(v) for e, v in per_eng.items()}


def _colize(v):
    v = np.asarray(v, np.float32).reshape(-1, 128)
    return np.ascontiguousarray(v.T)


COLS = {}
_off = 0
for _name, _w in [("cv", 16), ("bmod", 72), ("ng", 48), ("mu", 80), ("w0", 20), ("a0", 20), ("kk", 4), ("ka", 4),
                  ("rk", 4), ("gng", 4), ("gnb", 4), ("cw", 12), ("cb", 4), ("coef", 8), ("num", 8)]:
    COLS[_name] = _off
    _off += _w
NCOL = _off

CONSTS = {}
_off = 0
for _name, _w in [("ident", 128), ("bones", 128), ("maskG", 512), ("maskZ", 128), ("cmask", 512), ("id64", 64), ("bones64", 128)]:
    CONSTS[_name] = _off
    _off += _w
NCONST = _off


def _make_consts():
    cst = np.zeros((128, NCONST), np.float32)
    cst[:, CONSTS["ident"]:CONSTS["ident"] + 128] = np.eye(128, dtype=np.float32)
    bo = np.zeros((128, 128), np.float32)
    bo[:64, :64] = 1
    bo[64:, 64:] = 1
    cst[:, CONSTS["bones"]:CONSTS["bones"] + 128] = bo
    cst[:, CONSTS["bones64"]:CONSTS["bones64"] + 128] = bo / 64.0
    s = (np.arange(128) % 64)[:, None]
    t = np.arange(64)[None, :]
    mg = np.zeros((128, 2, 256), np.float32)
    for blk in range(2):
        mg[:, 0, blk * 128:blk * 128 + 64] = (t > s)
        mg[:, 0, blk * 128 + 64:blk * 128 + 128] = (t >= s)
        mg[:, 1, blk * 128:blk * 128 + 64] = (t < s)
        mg[:, 1, blk * 128 + 64:blk * 128 + 128] = (t <= s)
    cst[:, CONSTS["maskG"]:CONSTS["maskG"] + 512] = mg.reshape(128, 512)
    mz = np.zeros((128, 2, 64), np.float32)
    mz[:, 0, :] = (t < s)
    mz[:, 1, :] = (t > s)
    cst[:, CONSTS["maskZ"]:CONSTS["maskZ"] + 128] = mz.reshape(128, 128)
    cm = np.ones((128, 512), np.float32)
    cm[:, ::64] = 0
    cst[:, CONSTS["cmask"]:CONSTS["cmask"] + 512] = cm
    i64 = np.zeros((128, 64), np.float32)
    i64[np.arange(128), np.arange(128) % 64] = 1
    cst[:, CONSTS["id64"]:CONSTS["id64"] + 64] = i64
    return cst


def build_program(limit=10 ** 9, dbg_spec=None, mlimit=10 ** 9):
    nc = bass.Bass("TRN2", target_bir_lowering=False)
    stage = [0]

    def go():
        stage[0] += 1
        return stage[0] <= limit
    mstage = [0]

    def mgo():
        mstage[0] += 1
        return mstage[0] <= mlimit

    def din(name, shape):
        return nc.dram_tensor(name, list(shape), F32, kind="ExternalInput").ap()

    xg = din("xg", [5, N, D])
    colsd = din("cols", [128, NCOL])
    cstd = din("consts", [128, NCONST])
    std = din("st", [5, 4, 128, 64])
    w_mod = din("w_mod", [D, 9 * D])
    f1w13 = din("f1w13", [D, 2 * DFF])
    f1w2 = din("f1w2", [DFF, D])
    f2w13 = din("f2w13", [D, 2 * DFF])
    f2w2 = din("f2w2", [DFF, D])
    w_in = din("w_in", [D, 5632])
    w1c = din("w1c", [5, D, 128])
    w2c = din("w2c", [5, 128, 512])
    wba = din("wba", [512, D])
    wbb = din("wbb", [512, D])
    wout = din("wout", [D, D])
    yout = nc.dram_tensor("y", [2 * N, D], F32, kind="ExternalOutput").ap()
    nsout = nc.dram_tensor("ns", [2, 2, 8, 64, 64], F32, kind="ExternalOutput").ap()

    with contextlib.ExitStack() as stk:
        def sb(name, shape, dt=F32):
            return stk.enter_context(nc.sbuf_tensor(name, list(shape), dt))

        def ps(name, shape):
            return stk.enter_context(nc.psum_tensor(name, list(shape), F32))

        mk = MK(nc)
        cols = sb("cols_t", [128, NCOL])
        cst = sb("cst_t", [128, NCONST])
        onesb = sb("onesb", [128, 128], BF16)
        identb = sb("identb", [128, 128], BF16)
        scb = sb("scb", [128, 8, 2], BF16)
        modT = sb("modT", [128, 72, 2])
        mods = sb("mods", [128, 2, 6, 8])
        xT = sb("xT", [128, 8, N])
        rstd = sb("rstd", [128, N])
        lnt = sb("lnt", [128, N])
        bdum = sb("bdum", [128, 1])
        NSLOT = 3
        slots = [sb("slot%d" % i, [128, 4096], BF16) for i in range(NSLOT)]
        hlast = sb("hlast", [128, 3, 8])
        hbF = sb("hbF", [128, 8])
        hbB = sb("hbB", [128, 8])
        hbA = sb("hbA", [128, 8])
        endst = sb("endst", [128, 3, 4, 64])
        Mst = sb("Mst", [128, 4, 64])
        Mtmp = sb("Mtmp", [128, 4, 64])
        Mbf = sb("Mbf", [128, 4, 64], BF16)
        SCR = 30720
        scr = sb("scr", [128, SCR])

        class Pool:
            def __init__(self):
                self.off = 0

            def f32(self, n):
                a = scr[:, self.off:self.off + n]
                self.off += n
                assert self.off <= SCR, self.off
                return a

            def bf16(self, n):
                m = (n + 1) // 2
                a = scr[:, self.off:self.off + m].bitcast(BF16)
                self.off += m
                assert self.off <= SCR, self.off
                return a

        psA = ps("psA", [128, 2, 512])
        psB = ps("psB", [128, 2, 512])
        psS = ps("psS", [128, 512])
        psC = ps("psC", [128, 2, 512])
        psT = ps("psT", [128, 512])

        def tap(name, ap2d, width, keys):
            if name not in TAPS:
                return
            dt_ = nc.dram_tensor("dbg_" + name, [128, width], F32, kind="ExternalOutput").ap()
            mk.dma("sp", "yout", lambda e: e.dma_start(out=dt_, in_=ap2d), reads=keys)

        cnum = lambda i: cols[:, COLS["num"] + i:COLS["num"] + i + 1]
        ccol = lambda name, i: cols[:, COLS[name] + i:COLS[name] + i + 1]
        ident = cst[:, CONSTS["ident"]:CONSTS["ident"] + 128]
        bones = cst[:, CONSTS["bones"]:CONSTS["bones"] + 128]
        bones64 = cst[:, CONSTS["bones64"]:CONSTS["bones64"] + 128]
        id64 = cst[:, CONSTS["id64"]:CONSTS["id64"] + 64]
        cmask = cst[:, CONSTS["cmask"]:CONSTS["cmask"] + 512]

        def maskG(d):
            o = CONSTS["maskG"] + d * 256
            return cst[:, o:o + 256]

        def maskZ(d):
            o = CONSTS["maskZ"] + d * 64
            return cst[:, o:o + 64]

        evac_rr = [0]

        def evac(out, in_, reads, writes):
            evac_rr[0] ^= 1
            if evac_rr[0]:
                mk.act(lambda e: e.copy(out=out, in_=in_), reads=reads, writes=writes)
            else:
                mk.dve(lambda e: e.tensor_copy(out=out, in_=in_), reads=reads, writes=writes)

        slot_rr = [0]

        def load_slab(pieces):
            s = slot_rr[0] % NSLOT
            slot_rr[0] += 1
            sl = slots[s]
            key = "W%d" % s
            for i, (vf, dap) in enumerate(pieces):
                mk.dma("pool", "w%d" % s, lambda e, vf=vf, dap=dap: e.dma_start(out=vf(sl), in_=dap), writes=[key])
            return sl, key

        def mm_group(out_ap, out_key, terms):
            n = len(terms)
            for i, (l, r, keys) in enumerate(terms):
                mk.pe(lambda e, l=l, r=r, i=i: e.matmul(out_ap, lhsT=l, rhs=r, start=(i == 0), stop=(i == n - 1)),
                      reads=keys, writes=(out_key if isinstance(out_key, list) else [out_key]))

        barrier_n = [0]

        def barrier():
            barrier_n[0] += 1
            mk.dve(lambda e: e.memset(bdum[:], 0.0), reads=["bdum"], writes=["SCR"])

        R = lambda *k: ["SCR"] + list(k)

        mk.dma("sp", "c0", lambda e: e.dma_start(out=cols[:], in_=colsd), writes=["cols"])
        mk.dma("sp", "c0", lambda e: e.dma_start(out=cst[:], in_=cstd), writes=["cst"])
        mk.dve(lambda e: e.memset(onesb[:], 1.0 / 1024.0), writes=["onesb"])
        mk.dve(lambda e: e.tensor_copy(out=identb[:], in_=cst[:, CONSTS["ident"]:CONSTS["ident"] + 128]), reads=["cst"], writes=["cst2"])
        cv0 = COLS["cv"]
        mk.act(lambda e: e.activation(out=scb[:].rearrange("p k j -> p (k j)"), in_=cols[:, cv0:cv0 + 16], func=AF.Silu),
               reads=["cols"], writes=["scb"])
        for s in range(18):
            sl, key = load_slab([(lambda t: t[:, 0:4096].rearrange("p (k n) -> p k n", k=8),
                                 w_mod[:, s * 512:(s + 1) * 512].rearrange("(k p) n -> p k n", p=128))])
            slv = sl[:, 0:4096].rearrange("p (k n) -> p k n", k=8)
            for tt in range(4):
                mm_group(psS[:, tt * 2:tt * 2 + 2], "psS",
                         [(slv[:, k, tt * 128:(tt + 1) * 128], scb[:, k, :], [key, "scb"]) for k in range(8)])
            for j in range(2):
                b0 = COLS["bmod"] + s * 4
                mk.dve(lambda e, s=s, j=j, b0=b0: e.tensor_tensor(
                    out=modT[:, s * 4:s * 4 + 4, j], in0=psS[:, 0:8].rearrange("p (t j) -> p t j", j=2)[:, :, j],
                    in1=cols[:, b0:b0 + 4], op=ALU.add), reads=["psS", "cols"], writes=["modT"])
        ng = lambda i: cols[:, COLS["ng"] + i * 8:COLS["ng"] + i * 8 + 8]
        m_ = lambda i, j: modT[:, i * 8:(i + 1) * 8, j]
        for j in range(2):
            for (dst, mi, gi, half) in [(0, 1, 0, None), (1, 2, 1, 0.5), (2, 4, 2, None), (3, 5, 3, 1.0), (4, 7, 4, None), (5, 8, 5, 0.5)]:
                if half is None:
                    mk.dve(lambda e, j=j, dst=dst, mi=mi, gi=gi: e.scalar_tensor_tensor(
                        out=mods[:, j, dst, :], in0=m_(mi, j), scalar=1.0, in1=ng(gi), op0=ALU.add, op1=ALU.mult),
                        reads=["modT", "cols"], writes=["mods"])
                else:
                    mk.dve(lambda e, j=j, dst=dst, mi=mi, gi=gi, half=half: e.scalar_tensor_tensor(
                        out=mods[:, j, dst, :], in0=m_(mi, j), scalar=half, in1=ng(gi), op0=ALU.mult, op1=ALU.mult),
                        reads=["modT", "cols"], writes=["mods"])

        def rms_rstd(src, src_key, sq, eps_idx):
            for k in range(8):
                if k % 2 == 0:
                    mk.act(lambda e, k=k: e.activation(out=sq[:, k, :], in_=src[:, k, :], func=AF.Square),
                           reads=R(src_key), writes=["sq%d" % k])
                else:
                    mk.dve(lambda e, k=k: e.tensor_tensor(out=sq[:, k, :], in0=src[:, k, :], in1=src[:, k, :], op=ALU.mult),
                           reads=R(src_key), writes=["sq%d" % k])
            mm_group(psS[:], "psS", [(onesb[:], sq[:, k, :], ["onesb", "sq%d" % k, "SCR"]) for k in range(8)])
            mk.act(lambda e: e.activation(out=lnt[:], in_=psS[:], func=AF.Ln, bias=cnum(eps_idx), scale=1.0),
                   reads=["psS", "cols"], writes=["lnt"])
            mk.act(lambda e: e.activation(out=rstd[:], in_=lnt[:], func=AF.Exp, scale=-0.5), reads=["lnt"], writes=["rstd"])

        def prenorm(j, ai, bi, hT, sq, tmp):
            rms_rstd(xT, "xT", sq, 0)
            for k in range(8):
                mk.dve(lambda e, k=k: e.scalar_tensor_tensor(out=tmp[:, k % 2, :], in0=xT[:, k, :], scalar=mods[:, j, ai, k:k + 1],
                                                             in1=rstd[:], op0=ALU.mult, op1=ALU.mult),
                       reads=R("xT", "mods", "rstd"), writes=["ptmp%d" % (k % 2)])
                mk.act(lambda e, k=k: e.activation(out=hT[:, k, :], in_=tmp[:, k % 2, :], func=AF.Identity,
                                                   bias=modT[:, bi * 8 + k, j:j + 1], scale=1.0),
                       reads=R("ptmp%d" % (k % 2), "modT"), writes=["hT%d" % k])

        def postnorm_residual(j, gi, oT, sq, tmp):
            rms_rstd(oT, "oT", sq, 0)
            for k in range(8):
                mk.dve(lambda e, k=k: e.scalar_tensor_tensor(out=tmp[:, k % 2, :], in0=oT[:, k, :], scalar=mods[:, j, gi, k:k + 1],
                                                             in1=rstd[:], op0=ALU.mult, op1=ALU.mult),
                       reads=R("oT", "mods", "rstd"), writes=["ptmp%d" % (k % 2)])
                mk.dve(lambda e, k=k: e.tensor_tensor(out=xT[:, k, :], in0=xT[:, k, :], in1=tmp[:, k % 2, :], op=ALU.add),
                       reads=R("ptmp%d" % (k % 2), "xT"), writes=["xT"])

        def ffn(w13, w2, hT, hid, oT, sgt):
            for s in range(11):
                sl, key = load_slab([
                    (lambda t: t[:, 0:4096].rearrange("p (k n) -> p k n", k=8)[:, :, 0:256],
                     w13[:, s * 256:(s + 1) * 256].rearrange("(k p) n -> p k n", p=128)),
                    (lambda t: t[:, 0:4096].rearrange("p (k n) -> p k n", k=8)[:, :, 256:512],
                     w13[:, DFF + s * 256:DFF + (s + 1) * 256].rearrange("(k p) n -> p k n", p=128))])
                slv = sl[:, 0:4096].rearrange("p (k n) -> p k n", k=8)
                for jj in range(2):
                    jt = 2 * s + jj
                    b = jt % 2
                    mm_group(psA[:, b, :], "psA%d" % b,
                             [(slv[:, k, jj * 128:(jj + 1) * 128], hT[:, k, :], [key, "hT%d" % k, "SCR"]) for k in range(8)])
                    mm_group(psB[:, b, :], KB(b),
                             [(slv[:, k, 256 + jj * 128:256 + (jj + 1) * 128], hT[:, k, :], [key, "hT%d" % k, "SCR"]) for k in range(8)])
                    mk.act(lambda e, b=b: e.activation(out=sgt[:, b, :], in_=psA[:, b, :], func=AF.Silu),
                           reads=R("psA%d" % b), writes=["sgt%d" % b])
                    mk.dve(lambda e, b=b, jt=jt: e.tensor_tensor(out=hid[:, jt, :], in0=sgt[:, b, :], in1=psB[:, b, :], op=ALU.mult),
                           reads=R("sgt%d" % b, *KB(b)), writes=["hid%d" % jt])
            for i in range(8):
                sl, key = load_slab([(lambda t: t[:, 0:2816].rearrange("p (j n) -> p j n", j=22),
                                     w2[:, i * 128:(i + 1) * 128].rearrange("(j p) n -> p j n", p=128))])
                slv = sl[:, 0:2816].rearrange("p (j n) -> p j n", j=22)
                b = i % 2
                mm_group(psA[:, b, :], "psA%d" % b, [(slv[:, jt, :], hid[:, jt, :], [key, "hid%d" % jt, "SCR"]) for jt in range(22)])
                evac(oT[:, i, :], psA[:, b, :], R("psA%d" % b), ["oT"])

        def KB(b):
            return ["psB0", "psB0b"] if b == 0 else ["psB1"]

        def mixer(kind, j, hT, aux_idx):
            pool = Pool()
            pool.off = 2048
            full = kind != "aux"
            nseq = 2 if kind == "prompt" else 1
            L = N // nseq
            cps = NCH // nseq
            rkv = pool.f32(12 * N).rearrange("p (t n) -> p t n", t=12)
            kk = pool.f32(4 * N).rearrange("p (t n) -> p t n", t=4)
            yT = pool.f32(4 * N).rearrange("p (t n) -> p t n", t=4)
            asum = pool.f32(4 * N).rearrange("p (t n) -> p t n", t=4)
            lh = pool.bf16(2 * N).rearrange("p (d n) -> p d n", d=2)
            w2b = pool.bf16(2 * 512).rearrange("p (d n) -> p d n", d=2)
            vbf = pool.bf16(4 * N).rearrange("p (t n) -> p t n", t=4)
            base_d = pool.off
            w1b = pool.bf16(2 * 1024).rearrange("p (d k n) -> p d k n", d=2, k=8)
            w1s = pool.bf16(2 * 1024).rearrange("p (d k n) -> p d k n", d=2, k=8)
            hk = ["hT%d" % k for k in range(8)]
            for s in range(3):
                sl, key = load_slab([(lambda t: t[:, 0:4096].rearrange("p (k n) -> p k n", k=8),
                                     w_in[:, s * 512:(s + 1) * 512].rearrange("(k p) n -> p k n", p=128))])
                slv = sl[:, 0:4096].rearrange("p (k n) -> p k n", k=8)
                for tt in range(4):
                    b = tt % 2
                    mm_group(psA[:, b, :], "psA%d" % b,
                             [(slv[:, k, tt * 128:(tt + 1) * 128], hT[:, k, :], [key, "hT%d" % k, "SCR"]) for k in range(8)])
                    evac(rkv[:, s * 4 + tt, :], psA[:, b, :], R("psA%d" % b), ["rkv%d" % (s * 4 + tt)])
            for pr in range(4):
                mk.act(lambda e, pr=pr: e.copy(out=vbf[:, pr, :], in_=rkv[:, 8 + pr, :]), reads=R("rkv%d" % (8 + pr)), writes=["vbf%d" % pr])
            if not mgo():
                return
            sqk = pool.f32(2 * N).rearrange("p (b n) -> p b n", b=2)
            for pr in range(4):
                b = pr % 2
                mk.dve(lambda e, pr=pr: e.tensor_scalar(out=kk[:, pr, :], in0=rkv[:, 4 + pr, :], scalar1=ccol("kk", pr), scalar2=None,
                                                        op0=ALU.mult), reads=R("rkv%d" % (4 + pr), "cols"), writes=["kk%d" % pr])
                mk.dve(lambda e, pr=pr, b=b: e.tensor_tensor(out=sqk[:, b, :], in0=kk[:, pr, :], in1=kk[:, pr, :], op=ALU.mult),
                       reads=R("kk%d" % pr), writes=["sqk%d" % b])
                mm_group(psB[:, b, :], KB(b), [(bones, sqk[:, b, :], ["cst", "sqk%d" % b, "SCR"])])
                mk.act(lambda e, b=b: e.activation(out=sqk[:, b, :], in_=psB[:, b, :], func=AF.Ln, bias=cnum(1), scale=1.0),
                       reads=R("cols", *KB(b)), writes=["sqk%d" % b])
                mk.act(lambda e, b=b: e.activation(out=sqk[:, b, :], in_=sqk[:, b, :], func=AF.Exp, scale=-0.5),
                       reads=R("sqk%d" % b), writes=["sqk%d" % b])
                mk.dve(lambda e, pr=pr, b=b: e.tensor_tensor(out=kk[:, pr, :], in0=kk[:, pr, :], in1=sqk[:, b, :], op=ALU.mult),
                       reads=R("sqk%d" % b, "kk%d" % pr), writes=["kk%d" % pr])
            if not mgo():
                return
            dslots = [(0, 0), (1, 1)] if full else [(0, 2 + aux_idx)]
            sh = pool.bf16(8 * N).rearrange("p (k n) -> p k n", k=8)
            for di, (dt_, ds) in enumerate(dslots):
                mk.dma("pool", "wl", lambda e, di=di, ds=ds: e.dma_start(out=w1b[:, di, :, :], in_=w1c[ds].rearrange("(k p) n -> p k n", p=128)),
                       reads=["SCR"], writes=["w1b%d" % di])
                mk.dma("pool", "wl", lambda e, di=di, ds=ds: e.dma_start(out=w2b[:, di, :], in_=w2c[ds]), reads=["SCR"], writes=["w2b%d" % di])
                for x in range(2):
                    for k in range(8):
                        mc = COLS["mu"] + ds * 16 + x * 8 + k
                        mk.dve(lambda e, di=di, x=x, k=k, mc=mc: e.tensor_scalar(
                            out=w1s[:, di, k, x * 64:(x + 1) * 64], in0=w1b[:, di, k, x * 64:(x + 1) * 64],
                            scalar1=cols[:, mc:mc + 1], scalar2=None, op0=ALU.mult),
                            reads=R("w1b%d" % di, "cols"), writes=["w1s%d" % di])
                if dt_ == 0:
                    mk.dve(lambda e: e.tensor_tensor(out=sh[:, :, 1:N], in0=hT[:, :, 0:N - 1], in1=hT[:, :, 1:N], op=ALU.subtract),
                           reads=R(*hk), writes=["sh"])
                    for sq_ in range(nseq):
                        col = sq_ * L
                        if kind == "prompt" or (kind == "aux" and aux_idx == 0):
                            mk.dve(lambda e, col=col: e.tensor_scalar(out=sh[:, :, col], in0=hT[:, :, col], scalar1=-1.0, scalar2=None,
                                                                      op0=ALU.mult), reads=R("sh", *hk), writes=["sh"])
                        else:
                            hb = hbF if kind == "own" else hbA
                            hbk = "hbF" if kind == "own" else "hbA"
                            mk.dve(lambda e, col=col, hb=hb: e.tensor_tensor(out=sh[:, :, col], in0=hb[:, :], in1=hT[:, :, col], op=ALU.subtract),
                                   reads=R("sh", hbk, *hk), writes=["sh"])
                else:
                    mk.dve(lambda e: e.tensor_tensor(out=sh[:, :, 0:N - 1], in0=hT[:, :, 1:N], in1=hT[:, :, 0:N - 1], op=ALU.subtract),
                           reads=R(*hk), writes=["sh"])
                    for sq_ in range(nseq):
                        col = sq_ * L + L - 1
                        if kind == "prompt":
                            mk.dve(lambda e, col=col: e.tensor_scalar(out=sh[:, :, col], in0=hT[:, :, col], scalar1=-1.0, scalar2=None,
                                                                      op0=ALU.mult), reads=R("sh", *hk), writes=["sh"])
                        else:
                            mk.dve(lambda e, col=col: e.tensor_tensor(out=sh[:, :, col], in0=hbB[:, :], in1=hT[:, :, col], op=ALU.subtract),
                                   reads=R("sh", "hbB", *hk), writes=["sh"])
                b = di % 2
                mm_group(psB[:, b, :], KB(b),
                         [(w1b[:, di, k, :], hT[:, k, :], ["w1b%d" % di, "hT%d" % k, "SCR"]) for k in range(8)] +
                         [(w1s[:, di, k, :], sh[:, k, :], ["w1s%d" % di, "sh", "SCR"]) for k in range(8)])
                mk.act(lambda e, di=di, b=b: e.activation(out=lh[0:64, di, :], in_=psB[0:64, b, :], func=AF.Tanh),
                       reads=R(*KB(b)), writes=["lh%d" % di])
                mk.act(lambda e, di=di, b=b: e.copy(out=lh[64:128, di, :], in_=psB[64:128, b, :]),
                       reads=R(*KB(b)), writes=["lh%d" % di])
            if kind == "aux":
                mk.dve(lambda e: e.tensor_copy(out=hlast[:, aux_idx, :], in_=hT[:, :, N - 1]), reads=R(*hk), writes=["hlast%d" % aux_idx])

            if not mgo():
                return
            v3 = lambda ap: ap.rearrange("p (c n) -> p c n", c=8)
            psG = psA[:].rearrange("p a (h n) -> p (a h) n", h=2)
            psZ = psB[:].rearrange("p a (h n) -> p (a h) n", h=8)
            psZv = lambda a, hh: psZ[:, a * 4 + hh, :]
            ZK = ["psB0", "psB0b", "psB1"]
            psTv = psT[:].rearrange("p (a h n) -> p a h n", a=2, h=4)
            psCv = psC[:].rearrange("p a (h n) -> p a h n", h=8)
            psSv = psS[:].rearrange("p (h n) -> p h n", h=8)
            for di, (dt_, ds) in enumerate(dslots):
                order = list(range(NCH)) if dt_ == 0 else list(range(NCH - 1, -1, -1))
                barrier()
                pool.off = base_d
                AR = pool.bf16(4 * 8 * 128).rearrange("p (q c n) -> p q c n", q=4, c=8)
                BK = pool.bf16(4 * 8 * 128).rearrange("p (q c n) -> p q c n", q=4, c=8)
                Pend = pool.f32(32).rearrange("p (q c) -> p q c", q=4)
                base_t = pool.off
                sw = pool.f32(2 * N).rearrange("p (q n) -> p q n", q=2)
                av = pool.f32(2 * N).rearrange("p (q n) -> p q n", q=2)
                cs = pool.f32(2 * N).rearrange("p (q n) -> p q n", q=2)
                Lx = pool.f32(2 * N).rearrange("p (q n) -> p q n", q=2)
                Ep = pool.f32(2 * N).rearrange("p (q n) -> p q n", q=2)
                t1 = pool.f32(2 * N).rearrange("p (q n) -> p q n", q=2)
                for hp in range(2):
                    for ql in range(2):
                        pr = 2 * hp + ql
                        b = ql
                        mm_group(psA[:, b, :], "psA%d" % b, [(w2b[0:64, di, pr * 128:(pr + 1) * 128], lh[0:64, di, :], ["w2b%d" % di, "lh%d" % di, "SCR"])])
                        mm_group(psB[:, b, :], KB(b), [(w2b[64:128, di, pr * 128:(pr + 1) * 128], lh[64:128, di, :], ["w2b%d" % di, "lh%d" % di, "SCR"])])
                        mk.act(lambda e, pr=pr, ql=ql, b=b, ds=ds: e.activation(out=sw[:, ql, :], in_=psA[:, b, :], func=AF.Sigmoid,
                                                                               bias=ccol("w0", ds * 4 + pr), scale=1.0),
                               reads=R("psA%d" % b, "cols"), writes=["sw%d" % ql])
                        mk.act(lambda e, pr=pr, ql=ql, b=b, ds=ds: e.activation(out=av[:, ql, :], in_=psB[:, b, :], func=AF.Sigmoid,
                                                                               bias=ccol("a0", ds * 4 + pr), scale=1.0),
                               reads=R("cols", *KB(b)), writes=["av%d" % ql])
                        if full:
                            if di == 0:
                                mk.dve(lambda e, pr=pr, ql=ql: e.tensor_copy(out=asum[:, pr, :], in_=av[:, ql, :]), reads=R("av%d" % ql), writes=["asum%d" % pr])
                            else:
                                mk.dve(lambda e, pr=pr, ql=ql: e.tensor_tensor(out=asum[:, pr, :], in0=asum[:, pr, :], in1=av[:, ql, :], op=ALU.add),
                                       reads=R("av%d" % ql, "asum%d" % pr), writes=["asum%d" % pr])
                        mk.dve(lambda e, ql=ql: e.tensor_tensor_scan(out=cs[:, ql, :], data0=cmask, data1=sw[:, ql, :], initial=0.0,
                                                                     op0=ALU.mult, op1=ALU.add), reads=R("sw%d" % ql, "cst"), writes=["cs%d" % ql])
                        if dt_ == 0:
                            mk.dve(lambda e, ql=ql: e.tensor_tensor(out=Lx[:, ql, :], in0=cs[:, ql, :], in1=sw[:, ql, :], op=ALU.subtract),
                                   reads=R("cs%d" % ql, "sw%d" % ql), writes=["Lx%d" % ql])
                        else:
                            mk.dve(lambda e, ql=ql: e.tensor_tensor(out=v3(Lx[:, ql, :]), in0=v3(cs[:, ql, :])[:, :, 63:64].to_broadcast([128, 8, 64]),
                                                                    in1=v3(cs[:, ql, :]), op=ALU.subtract),
                                   reads=R("cs%d" % ql), writes=["Lx%d" % ql])
                            mk.dve(lambda e, ql=ql: e.tensor_tensor(out=cs[:, ql, :], in0=Lx[:, ql, :], in1=sw[:, ql, :], op=ALU.add),
                                   reads=R("Lx%d" % ql, "sw%d" % ql), writes=["cs%d" % ql])
                        mk.act(lambda e, ql=ql: e.activation(out=Ep[:, ql, :], in_=cs[:, ql, :], func=AF.Exp, scale=-C0), reads=R("cs%d" % ql), writes=["Ep%d" % ql])
                        mk.act(lambda e, ql=ql: e.activation(out=Lx[:, ql, :], in_=Lx[:, ql, :], func=AF.Exp, scale=-C0), reads=R("Lx%d" % ql), writes=["Lx%d" % ql])
                        mk.act(lambda e, ql=ql: e.activation(out=cs[:, ql, :], in_=cs[:, ql, :], func=AF.Exp, scale=C0), reads=R("cs%d" % ql, "Ep%d" % ql), writes=["cs%d" % ql])
                        pcol = 63 if dt_ == 0 else 0
                        mk.dve(lambda e, pr=pr, ql=ql, pcol=pcol: e.tensor_copy(out=Pend[:, pr, :], in_=v3(Ep[:, ql, :])[:, :, pcol]), reads=R("Ep%d" % ql), writes=["Pend"])
                        mk.dve(lambda e, pr=pr, ql=ql: e.scalar_tensor_tensor(out=AR[:, pr, :, 0:64], in0=v3(kk[:, pr, :]), scalar=-1.0, in1=v3(Lx[:, ql, :]),
                                                                              op0=ALU.mult, op1=ALU.mult), reads=R("kk%d" % pr, "Lx%d" % ql), writes=["AR%d" % pr])
                        mk.dve(lambda e, pr=pr, ql=ql: e.tensor_tensor(out=AR[:, pr, :, 64:128], in0=v3(rkv[:, pr, :]), in1=v3(Ep[:, ql, :]), op=ALU.mult),
                               reads=R("rkv%d" % pr, "Ep%d" % ql), writes=["AR%d" % pr])
                        mk.dve(lambda e, pr=pr, ql=ql: e.tensor_tensor(out=t1[:, ql, :], in0=kk[:, pr, :], in1=av[:, ql, :], op=ALU.mult),
                               reads=R("kk%d" % pr, "av%d" % ql), writes=["t1%d" % ql])
                        mk.dve(lambda e, pr=pr, ql=ql: e.tensor_tensor(out=BK[:, pr, :, 0:64], in0=v3(t1[:, ql, :]), in1=v3(cs[:, ql, :]), op=ALU.mult),
                               reads=R("t1%d" % ql, "cs%d" % ql), writes=["BK%d" % pr])
                        mk.dve(lambda e, pr=pr, ql=ql: e.tensor_scalar(out=t1[:, ql, :], in0=av[:, ql, :], scalar1=cnum(4), scalar2=ccol("ka", pr),
                                                                       op0=ALU.subtract, op1=ALU.mult), reads=R("av%d" % ql, "cols", "t1%d" % ql), writes=["t1%d" % ql])
                        mk.dve(lambda e, pr=pr, ql=ql: e.scalar_tensor_tensor(out=t1[:, ql, :], in0=t1[:, ql, :], scalar=1.0, in1=rkv[:, 4 + pr, :],
                                                                              op0=ALU.add, op1=ALU.mult), reads=R("t1%d" % ql, "rkv%d" % (4 + pr)), writes=["t1%d" % ql])
                        mk.dve(lambda e, pr=pr, ql=ql: e.tensor_tensor(out=BK[:, pr, :, 64:128], in0=v3(t1[:, ql, :]), in1=v3(cs[:, ql, :]), op=ALU.mult),
                               reads=R("t1%d" % ql, "cs%d" % ql), writes=["BK%d" % pr])
                if not mgo():
                    return
                barrier()
                pool.off = base_t
                Gm = [pool.bf16(4 * 256).rearrange("p (h n) -> p h n", h=4) for _ in range(2)]
                ZZ = [pool.bf16(2 * 4 * 64).rearrange("p (a h n) -> p a h n", a=2, h=4) for _ in range(2)]
                Qt = [pool.bf16(4 * 64).rearrange("p (h n) -> p h n", h=4) for _ in range(2)]
                TOK = [pool.bf16(3 * 4 * 64).rearrange("p (a h n) -> p a h n", a=3, h=4) for _ in range(2)]
                Wsb = pool.bf16(4 * 64).rearrange("p (h n) -> p h n", h=4)
                Usb = pool.bf16(4 * 64).rearrange("p (h n) -> p h n", h=4)
                Ytmp = pool.f32(4 * 64).rearrange("p (h n) -> p h n", h=4)
                unit = [0]

                def heads():
                    for q in range(4):
                        for e_ in range(2):
                            yield q, 64 * e_

                def tseries_stages(c, dt_=dt_):
                    u = unit[0] % 2
                    unit[0] += 1
                    G, Z2, Q, TK = Gm[u], ZZ[u], Qt[u], TOK[u]
                    gk, zk, qk, tk = "Gm%d" % u, "ZZ%d" % u, "Q%d" % u, "TOK%d" % u
                    stages = []

                    def st_g():
                        for q, fo in heads():
                            mm_group(psG[fo:fo + 64, q, 0:128], "psA%d" % (q // 2),
                                     [(BK[fo:fo + 64, q, c, 0:64], AR[fo:fo + 64, q, c, :], ["BK%d" % q, "AR%d" % q, "SCR"])])
                            mm_group(psG[fo:fo + 64, q, 128:256], "psA%d" % (q // 2),
                                     [(BK[fo:fo + 64, q, c, 64:128], AR[fo:fo + 64, q, c, :], ["BK%d" % q, "AR%d" % q, "SCR"])])
                            mm_group(psZv(0, q)[fo:fo + 64, :], "psB0",
                                     [(AR[fo:fo + 64, q, c, 0:64], BK[fo:fo + 64, q, c, 0:64], ["BK%d" % q, "AR%d" % q, "SCR"])])
                        mk.dve(lambda e: e.tensor_tensor(out=G[:], in0=psG, in1=maskG(dt_).unsqueeze(1).to_broadcast([128, 4, 256]), op=ALU.mult),
                               reads=R("psA0", "psA1", "cst"), writes=[gk])
                        mk.dve(lambda e: e.tensor_tensor(out=Z2[:, 1, :, :], in0=psZ[:, 0:4, :], in1=maskZ(dt_).unsqueeze(1).to_broadcast([128, 4, 64]),
                                                         op=ALU.mult), reads=R("psB0", "cst"), writes=[zk])
                        mk.act(lambda e: e.copy(out=Z2[:, 0, :, :], in_=G[:, :, 0:64]), reads=R(gk), writes=[zk])
                        mk.dve(lambda e: e.tensor_tensor(out=Q[:], in0=G[:, :, 0:64], in1=id64.unsqueeze(1).to_broadcast([128, 4, 64]), op=ALU.add),
                               reads=R(gk, "cst"), writes=[qk])
                    stages.append(st_g)

                    def mk_burst(lev):
                        def st():
                            for q, fo in heads():
                                idb = identb[fo:fo + 64, fo:fo + 64]
                                if lev <= 4:
                                    mm_group(psZv(1, q)[fo:fo + 64, :], "psB0b", [(Z2[fo:fo + 64, 1, q, :], Z2[fo:fo + 64, 0, q, :], [zk, "SCR"])])
                                mm_group(psZv(2, q)[fo:fo + 64, :], "psB1", [(Z2[fo:fo + 64, 0, q, :], Z2[fo:fo + 64, 1, q, :], [zk, "SCR"])])
                                if lev >= 2:
                                    mm_group(psZv(0, q)[fo:fo + 64, :], "psB0", [(idb, Q[fo:fo + 64, q, :], ["cst", qk, "SCR"]),
                                                                                 (Z2[fo:fo + 64, 1, q, :], Q[fo:fo + 64, q, :], [zk, qk, "SCR"])])
                            if lev >= 2:
                                mk.act(lambda e: e.copy(out=Q[:], in_=psZ[:, 0:4, :]), reads=R("psB0"), writes=[qk])
                            if lev <= 4:
                                mk.act(lambda e: e.copy(out=Z2[:, 0, :, :], in_=psZ[:, 4:8, :]), reads=R("psB0b"), writes=[zk])
                            mk.act(lambda e: e.copy(out=Z2[:, 1, :, :], in_=psZ[:, 8:12, :]), reads=R("psB1"), writes=[zk])
                        return st
                    for lev in range(1, 6):
                        stages.append(mk_burst(lev))

                    def st_last():
                        for q, fo in heads():
                            idb = identb[fo:fo + 64, fo:fo + 64]
                            mm_group(psZv(0, q)[fo:fo + 64, :], "psB0", [(idb, Q[fo:fo + 64, q, :], ["cst", qk, "SCR"]),
                                                                         (Z2[fo:fo + 64, 1, q, :], Q[fo:fo + 64, q, :], [zk, qk, "SCR"])])
                        mk.act(lambda e: e.copy(out=Q[:], in_=psZ[:, 0:4, :]), reads=R("psB0"), writes=[qk])
                        for q, fo in heads():
                            idb = identb[fo:fo + 64, fo:fo + 64]
                            mm_group(psTv[fo:fo + 64, 0, q, :], "psT", [(BK[fo:fo + 64, q, c, 0:64], idb, ["BK%d" % q, "cst", "SCR"])])
                            mm_group(psTv[fo:fo + 64, 1, q, :], "psT", [(BK[fo:fo + 64, q, c, 64:128], idb, ["BK%d" % q, "cst", "SCR"])])
                            mm_group(psCv[fo:fo + 64, 1, 4 + q, :], "psCv", [(vbf[fo:fo + 64, q, c * 64:(c + 1) * 64], idb,
                                                                             ["vbf%d" % q, "cst", "SCR"])])
                        mk.act(lambda e: e.copy(out=TK[:, 0:2, :, :], in_=psTv), reads=R("psT"), writes=[tk])
                        mk.dve(lambda e: e.tensor_copy(out=TK[:, 2, :, :], in_=psCv[:, 1, 4:8, :]), reads=R("psCv"), writes=[tk])
                    stages.append(st_last)
                    return stages, (G, Q, TK, gk, qk, tk)

                def chain_stages(c, bufs, dt_=dt_, di=di, order=order):
                    G, Q, TK, gk, qk, tk = bufs
                    seq = c // cps
                    pos = order.index(c) % cps
                    stages = []

                    def st_w():
                        if pos == 0:
                            if kind == "prompt":
                                mk.dve(lambda e: e.memset(Mst[:], 0.0), reads=R(), writes=["Mst"])
                            elif kind == "aux":
                                if aux_idx == 0:
                                    mk.dve(lambda e: e.tensor_copy(out=Mst[:], in_=stt[:, 2, :, :]), reads=R("stt"), writes=["Mst"])
                                else:
                                    cc_ = COLS["coef"] + (aux_idx - 1)
                                    mk.dve(lambda e: e.scalar_tensor_tensor(out=Mst[:], in0=endst[:, aux_idx - 1, :, :], scalar=cols[:, cc_:cc_ + 1],
                                                                            in1=stt[:, 2 + aux_idx, :, :], op0=ALU.mult, op1=ALU.add),
                                           reads=R("stt", "end%d" % (aux_idx - 1), "cols"), writes=["Mst"])
                            else:
                                if dt_ == 0:
                                    mk.dve(lambda e: e.tensor_copy(out=Mst[:], in_=stt[:, 0, :, :]), reads=R("stt"), writes=["Mst"])
                                    for a in range(3):
                                        cc_ = COLS["coef"] + 2 + a
                                        mk.dve(lambda e, a=a, cc_=cc_: e.scalar_tensor_tensor(out=Mst[:], in0=endst[:, a, :, :], scalar=cols[:, cc_:cc_ + 1],
                                                                                          in1=Mst[:], op0=ALU.mult, op1=ALU.add),
                                               reads=R("end%d" % a, "cols", "Mst"), writes=["Mst"])
                                else:
                                    cc_ = COLS["coef"] + 5
                                    mk.dve(lambda e: e.scalar_tensor_tensor(out=Mst[:], in0=endst[:, 2, :, :], scalar=cols[:, cc_:cc_ + 1],
                                                                            in1=stt[:, 1, :, :], op0=ALU.mult, op1=ALU.add),
                                           reads=R("stt", "end2", "cols"), writes=["Mst"])
                        if pos == 0:
                            mk.act(lambda e: e.copy(out=Mbf[:], in_=Mst[:]), reads=R("Mst"), writes=["Mbf"])
                        for q, fo in heads():
                            mm_group(psCv[fo:fo + 64, 0, q, :], "psC0w",
                                     [(AR[fo:fo + 64, q, c, 0:64], Mbf[fo:fo + 64, q, :], ["AR%d" % q, "Mbf", "SCR"]),
                                      (G[fo:fo + 64, q, 128:192], TK[fo:fo + 64, 2, q, :], [gk, tk, "SCR"])])
                        mk.act(lambda e: e.copy(out=Wsb[:], in_=psCv[:, 0, 0:4, :]), reads=R("psC0w"), writes=["Wsb"])
                    stages.append(st_w)

                    def st_u():
                        for q, fo in heads():
                            mm_group(psCv[fo:fo + 64, 0, 4 + q, :], "psC0u", [(Q[fo:fo + 64, q, :], Wsb[fo:fo + 64, q, :], [qk, "Wsb", "SCR"])])
                        mk.dve(lambda e: e.tensor_copy(out=Usb[:], in_=psCv[:, 0, 4:8, :]), reads=R("psC0u"), writes=["Usb"])
                    stages.append(st_u)

                    def st_ym():
                        for q, fo in heads():
                            if full and KV != 5:
                                mm_group(psSv[fo:fo + 64, q, :], "psS",
                                         [(Mbf[fo:fo + 64, q, :], AR[fo:fo + 64, q, c, 64:128], ["Mbf", "AR%d" % q, "SCR"]),
                                          (Usb[fo:fo + 64, q, :], G[fo:fo + 64, q, 64:128], ["Usb", gk, "SCR"]),
                                          (TK[fo:fo + 64, 2, q, :], G[fo:fo + 64, q, 192:256], [tk, gk, "SCR"])])
                            mm_group(psSv[fo:fo + 64, 4 + q, :], "psS",
                                     [(TK[fo:fo + 64, 0, q, :], Usb[fo:fo + 64, q, :], [tk, "Usb", "SCR"]),
                                      (TK[fo:fo + 64, 1, q, :], TK[fo:fo + 64, 2, q, :], [tk, "SCR"])])
                        if full and KV != 6:
                            ydst = yT[:, :, c * 64:(c + 1) * 64]
                            if di == 0 and KV != 9:
                                mk.dve(lambda e: e.tensor_copy(out=ydst, in_=psSv[:, 0:4, :]), reads=R("psS"), writes=["yT"])
                            elif di == 0 and KV == 8:
                                mk.act(lambda e: e.copy(out=Wsb[:], in_=psSv[:, 0:4, :]), reads=R("psS", "Wsb"), writes=["Wsb"])
                                mk.dve(lambda e: e.tensor_copy(out=ydst, in_=Wsb[:]), reads=R("Wsb"), writes=["yT"])
                            elif di == 0:
                                mk.act(lambda e: e.copy(out=ydst, in_=psSv[:, 0:4, :]), reads=R("psS"), writes=["yT"])
                            else:
                                mk.act(lambda e: e.copy(out=Ytmp[:], in_=psSv[:, 0:4, :]), reads=R("psS", "Ytmp"), writes=["Ytmp"])
                                mk.dve(lambda e: e.tensor_tensor(out=ydst, in0=ydst, in1=Ytmp[:], op=ALU.add), reads=R("Ytmp", "yT"), writes=["yT"])
                        mk.dve(lambda e: e.tensor_tensor(out=Mtmp[:], in0=Mst[:], in1=psSv[:, 4:8, :], op=ALU.add),
                               reads=R("psS", "Mst"), writes=["Mtmp"])
                        mk.dve(lambda e: e.tensor_tensor(out=Mst[:], in0=Mtmp[:], in1=Pend[:, :, c:c + 1].to_broadcast([128, 4, 64]), op=ALU.mult),
                               reads=R("Mtmp", "Pend"), writes=["Mst"])
                        mk.act(lambda e: e.copy(out=Mbf[:], in_=Mst[:]), reads=R("Mst"), writes=["Mbf"])
                        if pos == cps - 1:
                            if kind == "aux":
                                mk.act(lambda e: e.copy(out=endst[:, aux_idx, :, :], in_=Mst[:]), reads=R("Mst"), writes=["end%d" % aux_idx])
                            elif kind == "prompt":
                                for q in range(4):
                                    mm_group(psT[0:64, q * 128:(q + 1) * 128], "psT", [(Mst[:, q, :], ident, ["Mst", "cst", "SCR"])])
                                mk.act(lambda e: e.copy(out=nsb[0:64, seq, di, :, :], in_=psT[0:64, :].rearrange("p (a n) -> p a n", a=4)),
                                       reads=R("psT"), writes=["nsb"])
                    stages.append(st_ym)
                    return stages

                prev = None
                for idx_c in range(len(order) + 1):
                    A, bufsA = ([], None)
                    if idx_c < len(order):
                        A, bufsA = tseries_stages(order[idx_c])
                    Bs = []
                    if prev is not None:
                        Bs = chain_stages(prev[0], prev[1])
                    for i in range(max(len(A), len(Bs))):
                        if i < len(A):
                            KTC[0] += 1
                            if KTC[0] <= KT:
                                A[i]()
                        if i < len(Bs):
                            KTC[0] += 1
                            if KTC[0] <= KT:
                                Bs[i]()
                    prev = (order[idx_c], bufsA) if idx_c < len(order) else None
            if not full:
                return
            tap("yT_" + kind, yT.rearrange("p q n -> p (q n)"), 4 * N, R("yT"))
            tap("rkv_" + kind, rkv.rearrange("p q n -> p (q n)"), 12 * N, R(*["rkv%d" % i for i in range(12)]))
            barrier()
            MARK[kind] = len(mk.ops)
            pool.off = base_d
            bv = pool.f32(4 * N).rearrange("p (q n) -> p q n", q=4)
            tA = pool.f32(2 * N).rearrange("p (q n) -> p q n", q=2)
            tB = pool.f32(2 * N).rearrange("p (q n) -> p q n", q=2)
            for pr in range(4):
                b = pr % 2
                mk.dve(lambda e, pr=pr, b=b: e.tensor_scalar(out=tA[:, b, :], in0=asum[:, pr, :], scalar1=cnum(5), scalar2=ccol("ka", pr), op0=ALU.subtract, op1=ALU.mult),
                       reads=R("asum%d" % pr, "cols", "tA%d" % b), writes=["tA%d" % b])
                mk.dve(lambda e, pr=pr, b=b: e.scalar_tensor_tensor(out=tA[:, b, :], in0=tA[:, b, :], scalar=2.0, in1=rkv[:, 4 + pr, :], op0=ALU.add, op1=ALU.mult),
                       reads=R("tA%d" % b, "rkv%d" % (4 + pr)), writes=["tA%d" % b])
                mk.dve(lambda e, pr=pr, b=b: e.scalar_tensor_tensor(out=tA[:, b, :], in0=tA[:, b, :], scalar=ccol("rk", pr), in1=rkv[:, pr, :], op0=ALU.mult, op1=ALU.mult),
                       reads=R("tA%d" % b, "rkv%d" % pr, "cols"), writes=["tA%d" % b])
                mm_group(psA[:, b, :], "psA%d" % b, [(bones, tA[:, b, :], ["cst", "tA%d" % b, "SCR"])])
                mk.dve(lambda e, pr=pr, b=b: e.tensor_tensor(out=bv[:, pr, :], in0=psA[:, b, :], in1=rkv[:, 8 + pr, :], op=ALU.mult),
                       reads=R("psA%d" % b, "rkv%d" % (8 + pr)), writes=["bv%d" % pr])
                mm_group(psB[:, b, :], KB(b), [(bones64, yT[:, pr, :], ["cst", "yT", "SCR"])])
                mk.act(lambda e, b=b: e.copy(out=tB[:, b, :], in_=psB[:, b, :]), reads=R("tB%d" % b, *KB(b)), writes=["tB%d" % b])
                mk.dve(lambda e, pr=pr, b=b: e.tensor_tensor(out=yT[:, pr, :], in0=yT[:, pr, :], in1=tB[:, b, :], op=ALU.subtract), reads=R("tB%d" % b, "yT"), writes=["yT"])
                mk.act(lambda e, pr=pr, b=b: e.activation(out=tA[:, b, :], in_=yT[:, pr, :], func=AF.Square), reads=R("yT", "tA%d" % b), writes=["tA%d" % b])
                mm_group(psA[:, b, :], "psA%d" % b, [(bones64, tA[:, b, :], ["cst", "tA%d" % b, "SCR"])])
                mk.act(lambda e, b=b: e.activation(out=tB[:, b, :], in_=psA[:, b, :], func=AF.Ln, bias=cnum(2), scale=1.0), reads=R("cols", "tB%d" % b, "psA%d" % b), writes=["tB%d" % b])
                mk.act(lambda e, b=b: e.activation(out=tB[:, b, :], in_=tB[:, b, :], func=AF.Exp, scale=-0.5), reads=R("tB%d" % b), writes=["tB%d" % b])
                mk.dve(lambda e, pr=pr, b=b: e.scalar_tensor_tensor(out=yT[:, pr, :], in0=yT[:, pr, :], scalar=ccol("gng", pr), in1=tB[:, b, :], op0=ALU.mult, op1=ALU.mult),
                       reads=R("tB%d" % b, "yT", "cols"), writes=["yT"])
                mk.dve(lambda e, pr=pr: e.scalar_tensor_tensor(out=yT[:, pr, :], in0=yT[:, pr, :], scalar=ccol("gnb", pr), in1=bv[:, pr, :], op0=ALU.add, op1=ALU.add),
                       reads=R("bv%d" % pr, "yT", "cols"), writes=["yT"])
            tap("yn_" + kind, yT.rearrange("p q n -> p (q n)"), 4 * N, R("yT"))
            barrier()
            pool.off = 2048
            yA = pool.f32(8 * N).rearrange("p (q n) -> p q n", q=8)
            pool.off = 12288
            yB = pool.f32(8 * N).rearrange("p (q n) -> p q n", q=8)
            tA2 = pool.f32(2 * N).rearrange("p (q n) -> p q n", q=2)
            tB2 = pool.f32(2 * N).rearrange("p (q n) -> p q n", q=2)
            yaT = pool.bf16(4 * N).rearrange("p (q n) -> p q n", q=4)
            ybT = pool.bf16(4 * N).rearrange("p (q n) -> p q n", q=4)
            cbT = pool.bf16(4 * N).rearrange("p (q n) -> p q n", q=4)
            ccT = pool.f32(4 * N).rearrange("p (q n) -> p q n", q=4)
            mgT = pool.bf16(8 * N).rearrange("p (q n) -> p q n", q=8)
            rl = 64 if kind == "own" else L
            r3 = lambda ap: ap.rearrange("p (r n) -> p r n", n=rl)

            def branch(wmat, srcT, srckey, dst, dstkey):
                for half in range(2):
                    slw, keyw = load_slab([(lambda t: t[:, 0:2048].rearrange("p (k n) -> p k n", k=4),
                                            wmat[:, half * 512:(half + 1) * 512].rearrange("(k p) n -> p k n", p=128))])
                    slwv = slw[:, 0:2048].rearrange("p (k n) -> p k n", k=4)
                    for tt in range(4):
                        o = half * 4 + tt
                        b = tt % 2
                        mm_group(psB[:, b, :], KB(b), [(slwv[:, k, tt * 128:(tt + 1) * 128], srcT[:, k, :], [keyw, srckey % k, "SCR"]) for k in range(4)])
                        evac(dst[:, o, :], psB[:, b, :], R(*KB(b)), [dstkey % o])

            for s in range(3, 11):
                sl, key = load_slab([(lambda t: t[:, 0:4096].rearrange("p (k n) -> p k n", k=8),
                                     w_in[:, s * 512:(s + 1) * 512].rearrange("(k p) n -> p k n", p=128))])
                slv = sl[:, 0:4096].rearrange("p (k n) -> p k n", k=8)
                for tt in range(4):
                    b = tt % 2
                    mm_group(psA[:, b, :], "psA%d" % b,
                             [(slv[:, k, tt * 128:(tt + 1) * 128], hT[:, k, :], [key, "hT%d" % k, "SCR"]) for k in range(8)])
                    pa = psA[:, b, :]
                    pk = "psA%d" % b
                    tb = tt % 2
                    if s == 3:
                        mk.act(lambda e, pa=pa, tb=tb: e.activation(out=tA2[:, tb, :], in_=pa, func=AF.Sigmoid), reads=R(pk, "tA2%d" % tb), writes=["tA2%d" % tb])
                        mk.dve(lambda e, tt=tt, tb=tb: e.tensor_tensor(out=yaT[:, tt, :], in0=yT[:, tt, :], in1=tA2[:, tb, :], op=ALU.mult),
                               reads=R("yT", "tA2%d" % tb), writes=["yaT%d" % tt])
                    elif s == 4:
                        mk.act(lambda e, tt=tt, pa=pa: e.copy(out=cbT[:, tt, :], in_=pa), reads=R(pk), writes=["cbT%d" % tt])
                    elif s == 5:
                        mk.act(lambda e, tt=tt, pa=pa: e.copy(out=ccT[:, tt, :], in_=pa), reads=R(pk), writes=["ccT%d" % tt])
                    elif s == 6:
                        u = tA2[:, tb, :]
                        uk = "tA2%d" % tb
                        acc = tB2[:, tb, :]
                        ak = "tB2%d" % tb
                        mk.dve(lambda e, tt=tt, pa=pa, u=u: e.tensor_tensor(out=u, in0=ccT[:, tt, :], in1=pa, op=ALU.mult), reads=R(pk, "ccT%d" % tt, uk), writes=[uk])
                        mk.dve(lambda e, tt=tt, u=u, acc=acc: e.tensor_scalar(out=acc, in0=u, scalar1=ccol("cw", 4 + tt), scalar2=ccol("cb", tt), op0=ALU.mult, op1=ALU.add),
                               reads=R(uk, "cols", ak), writes=[ak])
                        mk.dve(lambda e, tt=tt, u=u, acc=acc: e.scalar_tensor_tensor(out=r3(acc)[:, :, 1:rl], in0=r3(u)[:, :, 0:rl - 1], scalar=ccol("cw", tt),
                                                                                    in1=r3(acc)[:, :, 1:rl], op0=ALU.mult, op1=ALU.add), reads=R(uk, ak, "cols"), writes=[ak])
                        mk.dve(lambda e, tt=tt, u=u, acc=acc: e.scalar_tensor_tensor(out=r3(acc)[:, :, 0:rl - 1], in0=r3(u)[:, :, 1:rl], scalar=ccol("cw", 8 + tt),
                                                                                    in1=r3(acc)[:, :, 0:rl - 1], op0=ALU.mult, op1=ALU.add), reads=R(uk, ak, "cols"), writes=[ak])
                        mk.dve(lambda e, tt=tt, acc=acc: e.tensor_tensor(out=ybT[:, tt, :], in0=cbT[:, tt, :], in1=acc, op=ALU.mult), reads=R(ak, "cbT%d" % tt), writes=["ybT%d" % tt])
                    else:
                        gi = (s - 7) * 4 + tt
                        sg = tA2[:, tb, :]
                        sk = "tA2%d" % tb
                        mk.act(lambda e, pa=pa, sg=sg: e.activation(out=sg, in_=pa, func=AF.Sigmoid), reads=R(pk, sk), writes=[sk])
                        if gi < 8:
                            mk.dve(lambda e, gi=gi, sg=sg: e.tensor_tensor(out=yA[:, gi, :], in0=yA[:, gi, :], in1=sg, op=ALU.mult), reads=R(sk, "yA%d" % gi), writes=["yA%d" % gi])
                        else:
                            g2 = gi - 8
                            mk.dve(lambda e, g2=g2, sg=sg: e.tensor_tensor(out=sg, in0=sg, in1=yB[:, g2, :], op=ALU.mult), reads=R(sk, "yB%d" % g2), writes=[sk])
                            mk.dve(lambda e, g2=g2, sg=sg: e.tensor_tensor(out=mgT[:, g2, :], in0=yA[:, g2, :], in1=sg, op=ALU.add), reads=R(sk, "yA%d" % g2), writes=["mgT%d" % g2])
                if s == 3:
                    barrier()
                    branch(wba, yaT, "yaT%d", yA, "yA%d")
                if s == 6:
                    branch(wbb, ybT, "ybT%d", yB, "yB%d")
            oT = pool.f32(8 * N).rearrange("p (k n) -> p k n", k=8)
            pool.off = 6144
            sq = pool.bf16(8 * N).rearrange("p (k n) -> p k n", k=8)
            tmp = pool.f32(2 * N).rearrange("p (k n) -> p k n", k=2)
            for half in range(2):
                sl, key = load_slab([(lambda t: t[:, 0:4096].rearrange("p (k n) -> p k n", k=8),
                                     wout[:, half * 512:(half + 1) * 512].rearrange("(k p) n -> p k n", p=128))])
                slv = sl[:, 0:4096].rearrange("p (k n) -> p k n", k=8)
                for tt in range(4):
                    i = half * 4 + tt
                    b = tt % 2
                    mm_group(psA[:, b, :], "psA%d" % b, [(slv[:, k, tt * 128:(tt + 1) * 128], mgT[:, k, :], [key, "mgT%d" % k, "SCR"]) for k in range(8)])
                    evac(oT[:, i, :], psA[:, b, :], R("psA%d" % b), ["oT"])
            tap("mo_" + kind, oT.rearrange("p q n -> p (q n)"), 8 * N, R("oT"))
            postnorm_residual(j, 3, oT, sq, tmp)

        stt = sb("stt", [128, 5, 4, 64])
        nsb = sb("nsb", [64, 2, 2, 4, 128])
        mk.dma("sp", "c0", lambda e: e.dma_start(out=stt[:], in_=std.rearrange("a q p v -> p a q v")), writes=["stt"])

        def load_x(g):
            pool = Pool()
            xin = pool.f32(4 * D).rearrange("p (t n) -> p t n", t=4)
            for tt in range(4):
                mk.dma("sp", "xin", lambda e, tt=tt: e.dma_start(out=xin[:, tt, :], in_=xg[g, tt * 128:(tt + 1) * 128, :]), reads=["SCR"], writes=["xin%d" % tt])
            for k in range(8):
                b = k % 2
                for tt in range(4):
                    mm_group(psA[:, b, tt * 128:(tt + 1) * 128], "psA%d" % b, [(xin[:, tt, k * 128:(k + 1) * 128], ident, ["xin%d" % tt, "cst", "SCR"])])
                evac(xT[:, k, :], psA[:, b, :], R("psA%d" % b), ["xT"])

        def store_y(gout):
            pool = Pool()
            yo = pool.f32(4 * D).rearrange("p (t n) -> p t n", t=4)
            for tt in range(4):
                for k in range(8):
                    b = k % 2
                    mm_group(psA[:, b, 0:128], "psA%d" % b, [(xT[:, k, tt * 128:(tt + 1) * 128], ident, ["xT", "cst", "SCR"])])
                    evac(yo[:, tt, k * 128:(k + 1) * 128], psA[:, b, 0:128], R("psA%d" % b), ["yo%d" % tt])
                mk.dma("sp", "yout", lambda e, tt=tt: e.dma_start(out=yout[gout * N + tt * 128:gout * N + (tt + 1) * 128, :], in_=yo[:, tt, :]), reads=R("yo%d" % tt))

        def ffn_phase(j, ai, bi, gi, w13, w2):
            pool = Pool()
            hT = pool.bf16(8 * N).rearrange("p (k n) -> p k n", k=8)
            sq = pool.bf16(8 * N).rearrange("p (k n) -> p k n", k=8)
            hid = pool.bf16(22 * N).rearrange("p (k n) -> p k n", k=22)
            oT = pool.f32(8 * N).rearrange("p (k n) -> p k n", k=8)
            tmp = pool.f32(2 * N).rearrange("p (k n) -> p k n", k=2)
            sgt = pool.f32(2 * N).rearrange("p (k n) -> p k n", k=2)
            prenorm(j, ai, bi, hT, sq, tmp)
            ffn(w13, w2, hT, hid, oT, sgt)
            postnorm_residual(j, gi, oT, sq, tmp)

        def mixer_phase(kind, j, aux_idx):
            pool = Pool()
            hT = pool.bf16(8 * N).rearrange("p (k n) -> p k n", k=8)
            tail = Pool()
            tail.off = SCR - (8 * N // 2 + 2 * N)
            sq = tail.bf16(8 * N).rearrange("p (k n) -> p k n", k=8)
            tmp = tail.f32(2 * N).rearrange("p (k n) -> p k n", k=2)
            prenorm(j, 2, 3, hT, sq, tmp)
            barrier()
            mixer(kind, j, hT, aux_idx)

        def chain_prep_aux(a):
            cc_ = COLS["coef"] + (a - 1)
            mk.dve(lambda e: e.tensor_scalar(out=hbA[:], in0=hlast[:, a - 1, :], scalar1=cols[:, cc_:cc_ + 1], scalar2=None, op0=ALU.mult),
                   reads=["hlast%d" % (a - 1), "cols"], writes=["hbA"])

        def chain_prep_own():
            c2 = COLS["coef"] + 2
            mk.dve(lambda e: e.tensor_scalar(out=hbF[:], in0=hlast[:, 0, :], scalar1=cols[:, c2:c2 + 1], scalar2=None, op0=ALU.mult),
                   reads=["hlast0", "cols"], writes=["hbF"])
            for a in (1, 2):
                mk.dve(lambda e, a=a: e.scalar_tensor_tensor(out=hbF[:], in0=hlast[:, a, :], scalar=cols[:, c2 + a:c2 + a + 1], in1=hbF[:], op0=ALU.mult, op1=ALU.add),
                       reads=["hlast%d" % a, "cols", "hbF"], writes=["hbF"])
            mk.dve(lambda e: e.tensor_scalar(out=hbB[:], in0=hlast[:, 2, :], scalar1=cols[:, c2 + 3:c2 + 4], scalar2=None, op0=ALU.mult),
                   reads=["hlast2", "cols"], writes=["hbB"])

        for a in range(3):
            if go():
                barrier()
                load_x(2 + a)
            if go():
                barrier()
                ffn_phase(1, 0, 0, 1, f1w13, f1w2)
            if go():
                barrier()
                if a > 0:
                    chain_prep_aux(a)
                mixer_phase("aux", 1, a)
        for (g, kind, j) in [(1, "own", 1), (0, "prompt", 0)]:
            if go():
                barrier()
                load_x(g)
            if go():
                barrier()
                ffn_phase(j, 0, 0, 1, f1w13, f1w2)
            if go():
                barrier()
                if kind == "own":
                    chain_prep_own()
                mixer_phase(kind, j, None)
            if go():
                barrier()
                ffn_phase(j, 4, 6, 5, f2w13, f2w2)
            if go():
                barrier()
                store_y(1 if kind == "own" else 0)
        if dbg_spec is not None:
            barrier()
            dbg_spec(mk, nc, locals())
        for sq_ in range(2):
            for dd in range(2):
                mk.dma("sp", "yout", lambda e, sq_=sq_, dd=dd: e.dma_start(out=nsout[sq_, dd].rearrange("(q e) v k -> v q e k", e=2),
                                                                           in_=nsb[:, sq_, dd, :, :].rearrange("p q (e k) -> p q e k", e=2)), reads=["nsb"])
        stats = mk.emit()
    return nc, stats


_CACHE = {}


def _prep_inputs(inp):
    f = lambda a: np.ascontiguousarray(np.asarray(a, np.float32))
    x_prompt, x_sample = f(inp["x_prompt"]), f(inp["x_sample"])
    c, state, c_ctx = f(inp["c"]), f(inp["state_rwkv"]), f(inp["c_ctx"])
    mu = f(inp["mu_shift"])[0]
    w1 = [np.concatenate([f(inp["decay_w1"])[0, d], f(inp["iclr_a1"])[0, d]], axis=1) for d in range(2)]
    w2 = [np.concatenate([f(inp["decay_w2"])[0, d], f(inp["iclr_a2"])[0, d]], axis=0) for d in range(2)]
    dw0, ia0 = f(inp["decay_w0"])[0], f(inp["iclr_a0"])[0]
    shared = {
        "w_mod": f(inp["w_mod"])[0], "f1w13": f(inp["ffn1_w13"])[0], "f1w2": f(inp["ffn1_w2"])[0],
        "f2w13": f(inp["ffn2_w13"])[0], "f2w2": f(inp["ffn2_w2"])[0], "w_in": f(inp["w_in"])[0],
        "wba": f(inp["w_branch_a"])[0], "wbb": f(inp["w_branch_b"])[0], "wout": f(inp["w_out"])[0],
        "consts": _make_consts(),
    }
    in_maps = []
    for core in range(8):
        b, s = core // 4, core % 4
        if s == 0:
            aux = [(3, 1), (2, 1), (1, 1)]
            cont = (1.0, 1.0)
            selF = (0.0, 0.0, 0.0)
            selB = 1.0
            init = ["F", "0", "B", "0", "0"]
        elif s == 1:
            aux = [(0, 0), (3, 1), (2, 1)]
            cont = (0.0, 1.0)
            selF = (1.0, 0.0, 0.0)
            selB = 1.0
            init = ["0", "0", "F", "B", "0"]
        elif s == 2:
            aux = [(0, 0), (1, 0), (3, 1)]
            cont = (1.0, 0.0)
            selF = (0.0, 1.0, 0.0)
            selB = 1.0
            init = ["0", "0", "F", "0", "B"]
        else:
            aux = [(0, 0), (1, 0), (2, 0)]
            cont = (1.0, 1.0)
            selF = (0.0, 0.0, 1.0)
            selB = 0.0
            init = ["0", "B", "F", "0", "0"]
        xgr = np.empty((5, N, D), np.float32)
        xgr[0] = x_prompt[2 * core:2 * core + 2].reshape(N, D)
        xgr[1] = x_sample[b, s * N:(s + 1) * N]
        for a, (seg, dr) in enumerate(aux):
            xs = x_sample[b, seg * N:(seg + 1) * N]
            xgr[2 + a] = xs[::-1] if dr == 1 else xs
        dsl = [0, 1] + [dr for (_, dr) in aux]
        cols = np.zeros((128, NCOL), np.float32)
        cvs = [c_ctx, c[b]]
        for k in range(8):
            for j in range(2):
                cols[:, COLS["cv"] + k * 2 + j] = cvs[j][k * 128:(k + 1) * 128]
        cols[:, COLS["bmod"]:COLS["bmod"] + 72] = _colize(inp["b_mod"][0])
        cols[:, COLS["ng"]:COLS["ng"] + 48] = _colize(np.asarray(inp["norm_g"][0]).reshape(-1))
        for ds, dr in enumerate(dsl):
            for x in range(2):
                cols[:, COLS["mu"] + ds * 16 + x * 8:COLS["mu"] + ds * 16 + x * 8 + 8] = _colize(mu[dr, x])
            cols[:, COLS["w0"] + ds * 4:COLS["w0"] + ds * 4 + 4] = _colize(dw0[dr])
            cols[:, COLS["a0"] + ds * 4:COLS["a0"] + ds * 4 + 4] = _colize(ia0[dr])
        cols[:, COLS["kk"]:COLS["kk"] + 4] = _colize(inp["k_k"][0])
        cols[:, COLS["ka"]:COLS["ka"] + 4] = _colize(inp["k_a"][0])
        cols[:, COLS["rk"]:COLS["rk"] + 4] = _colize(np.asarray(inp["r_k"][0]).reshape(-1))
        cols[:, COLS["gng"]:COLS["gng"] + 4] = _colize(inp["gn_gain"][0])
        cols[:, COLS["gnb"]:COLS["gnb"] + 4] = _colize(inp["gn_bias"][0])
        cols[:, COLS["cw"]:COLS["cw"] + 12] = _colize(np.asarray(inp["conv_w"][0]).reshape(-1))
        cols[:, COLS["cb"]:COLS["cb"] + 4] = _colize(inp["conv_b"][0])
        cols[:, COLS["coef"]:COLS["coef"] + 6] = np.array([cont[0], cont[1], selF[0], selF[1], selF[2], selB], np.float32)[None, :]
        cols[:, COLS["num"]:COLS["num"] + 6] = np.array([1e-6, 1e-12, 64e-5, 0.0, 1.0, 2.0], np.float32)[None, :]
        def mlay(d):
            S = state[b, 0, d]
            return np.ascontiguousarray(S.transpose(0, 2, 1).reshape(4, 128, 64))
        stv = np.zeros((5, 4, 128, 64), np.float32)
        for i, t in enumerate(init):
            if t == "F":
                stv[i] = mlay(0)
            elif t == "B":
                stv[i] = mlay(1)
        m = dict(shared)
        m.update({"xg": xgr, "cols": cols, "st": stv,
                  "w1c": np.ascontiguousarray(np.stack([w1[d] for d in dsl])),
                  "w2c": np.ascontiguousarray(np.stack([w2[d] for d in dsl]))})
        in_maps.append(m)
    return in_maps


def kernel(**inputs):
    if "nc" not in _CACHE:
        _CACHE["nc"], _CACHE["stats"] = build_program(int(os.environ.get("KLIMIT", str(10 ** 9))))
    nc = _CACHE["nc"]
    in_maps = _prep_inputs(inputs)
    res = run_bass_kernel_spmd(nc, in_maps, core_ids=list(range(8)))
    y_prompt = np.empty((16, 256, D), np.float32)
    y_sample = np.empty((2, 2048, D), np.float32)
    new_state = np.empty((16, 1, 2, 8, 64, 64), np.float32)
    for core in range(8):
        r = res.results[core]
        b, s = core // 4, core % 4
        y = np.asarray(r["y"], np.float32)
        y_prompt[2 * core:2 * core + 2] = y[0:N].reshape(2, 256, D)
        y_sample[b, s * N:(s + 1) * N] = y[N:2 * N]
        new_state[2 * core:2 * core + 2, 0] = np.asarray(r["ns"], np.float32)
    return (y_prompt, y_sample, new_state)
```

```python
import contextlib
import numpy as np
import concourse.bass as bass
import concourse.mybir as mybir
from concourse.bass_utils import run_bass_kernel_spmd

F32 = mybir.dt.float32
BF16 = mybir.dt.bfloat16
ALU = mybir.AluOpType
AF = mybir.ActivationFunctionType

D = 1024
DFF = 2816
import os
SUB = int(os.environ.get('KSUB', '99'))
KT = int(os.environ.get('KT', '1000000'))
KTC = [0]
MARK = {}
TAPS = [t for t in os.environ.get('KTAPS', '').split(',') if t]
KV = int(os.environ.get('KV', '0'))
N = 512
C = 64
NCH = N // C
C0 = float(np.exp(-0.5))


class _Op:
    __slots__ = ("idx", "eng", "fn", "deps", "chan", "chanpos", "needs_inc", "inc_count", "engpos")

    def __init__(self, idx, eng, fn, deps, chan):
        self.idx = idx
        self.eng = eng
        self.fn = fn
        self.deps = deps
        self.chan = chan
        self.chanpos = None
        self.needs_inc = False
        self.inc_count = None
        self.engpos = None


class MK:
    ENGS = ("pe", "act", "dve", "pool", "sp")

    def __init__(self, nc):
        self.nc = nc
        self.ops = []
        self.last_writer = {}
        self.readers = {}
        self.chan_count = {}

    def add(self, eng, fn, reads=(), writes=(), chan=None):
        idx = len(self.ops)
        deps = set()
        writes = list(writes)
        if chan is not None:
            writes.append(("__chan__", chan))
        for r in reads:
            w = self.last_writer.get(r)
            if w is not None:
                deps.add(w)
        for w in writes:
            lw = self.last_writer.get(w)
            if lw is not None:
                deps.add(lw)
            deps.update(self.readers.get(w, ()))
        op = _Op(idx, eng, fn, deps, chan)
        if chan is not None:
            op.chanpos = self.chan_count.get(chan, 0)
            self.chan_count[chan] = op.chanpos + 1
        self.ops.append(op)
        for r in reads:
            self.readers.setdefault(r, []).append(idx)
        for w in writes:
            self.last_writer[w] = idx
            self.readers[w] = []
        return idx

    def pe(self, fn, reads=(), writes=()):
        return self.add("pe", fn, reads, writes)

    def act(self, fn, reads=(), writes=()):
        return self.add("act", fn, reads, writes)

    def dve(self, fn, reads=(), writes=()):
        return self.add("dve", fn, reads, writes)

    def pool(self, fn, reads=(), writes=()):
        return self.add("pool", fn, reads, writes)

    def dma(self, eng, chan, fn, reads=(), writes=()):
        return self.add(eng, fn, reads, writes, chan=chan)

    def emit(self):
        nc = self.nc
        ops = self.ops
        per_eng = {e: [] for e in self.ENGS}
        for op in ops:
            op.engpos = len(per_eng[op.eng])
            per_eng[op.eng].append(op)

        def need_sem(op, d):
            if d.chan is not None:
                return True
            if d.eng != op.eng:
                return True
            if op.eng == "pe":
                return False
            return (op.engpos - d.engpos) <= 2

        for op in ops:
            latest = {}
            keep = set()
            for di in op.deps:
                d = ops[di]
                if d.chan is not None:
                    keep.add(di)
                else:
                    cur = latest.get(d.eng)
                    if cur is None or ops[cur].engpos < d.engpos:
                        latest[d.eng] = di
            keep.update(latest.values())
            op.deps = keep
        for op in ops:
            for di in op.deps:
                d = ops[di]
                if d.chan is None and need_sem(op, d):
                    d.needs_inc = True
        cnt = {e: 0 for e in self.ENGS}
        for op in ops:
            if op.chan is None and op.needs_inc:
                cnt[op.eng] += 1
                op.inc_count = cnt[op.eng]
        chans = sorted(self.chan_count.keys())
        with contextlib.ExitStack() as st:
            esem = {e: st.enter_context(nc.semaphore("s_" + e)) for e in self.ENGS}
            csem = {c: st.enter_context(nc.semaphore("c_" + str(c))) for c in chans}
            block = st.enter_context(nc.Block())

            def run_engine(ename, eobj):
                waited = {}

                def wait(key, sem, val):
                    if waited.get(key, 0) >= val:
                        return
                    waited[key] = val
                    eobj.wait_ge(sem, val)

                for op in per_eng[ename]:
                    for di in sorted(op.deps):
                        d = ops[di]
                        if not need_sem(op, d):
                            continue
                        if d.chan is not None:
                            wait(("c", d.chan), csem[d.chan], 16 * (d.chanpos + 1))
                        else:
                            wait(("e", d.eng), esem[d.eng], d.inc_count)
                    ins = op.fn(eobj)
                    if op.chan is not None:
                        ins.then_inc(csem[op.chan], 16)
                    elif op.needs_inc:
                        ins.then_inc(esem[op.eng], 1)
                if ename == "sp":
                    for c in chans:
                        wait(("c", c), csem[c], 16 * self.chan_count[c])

            @block.tensor
            def _(e):
                run_engine("pe", e)

            @block.scalar
            def _(e):
                run_engine("act", e)

            @block.vector
            def _(e):
                run_engine("dve", e)

            @block.gpsimd
            def _(e):
                run_engine("pool", e)

            @block.sync
            def _(e):
                run_engine("sp", e)
        return {e: len(v) for e, v in per_eng.items()}


def _colize(v):
    v = np.asarray(v, np.float32).reshape(-1, 128)
    return np.ascontiguousarray(v.T)


COLS = {}
_off = 0
for _name, _w in [("cv", 16), ("bmod", 72), ("ng", 48), ("mu", 80), ("w0", 20), ("a0", 20), ("kk", 4), ("ka", 4),
                  ("rk", 4), ("gng", 4), ("gnb", 4), ("cw", 12), ("cb", 4), ("coef", 8), ("num", 8)]:
    COLS[_name] = _off
    _off += _w
NCOL = _off

CONSTS = {}
_off = 0
for _name, _w in [("ident", 128), ("bones", 128), ("maskG", 512), ("maskZ", 128), ("cmask", 512), ("id64", 64), ("bones64", 128)]:
    CONSTS[_name] = _off
    _off += _w
NCONST = _off


def _make_consts():
    cst = np.zeros((128, NCONST), np.float32)
    cst[:, CONSTS["ident"]:CONSTS["ident"] + 128] = np.eye(128, dtype=np.float32)
    bo = np.zeros((128, 128), np.float32)
    bo[:64, :64] = 1
    bo[64:, 64:] = 1
    cst[:, CONSTS["bones"]:CONSTS["bones"] + 128] = bo
    cst[:, CONSTS["bones64"]:CONSTS["bones64"] + 128] = bo / 64.0
    s = (np.arange(128) % 64)[:, None]
    t = np.arange(64)[None, :]
    mg = np.zeros((128, 2, 256), np.float32)
    for blk in range(2):
        mg[:, 0, blk * 128:blk * 128 + 64] = (t > s)
        mg[:, 0, blk * 128 + 64:blk * 128 + 128] = (t >= s)
        mg[:, 1, blk * 128:blk * 128 + 64] = (t < s)
        mg[:, 1, blk * 128 + 64:blk * 128 + 128] = (t <= s)
    cst[:, CONSTS["maskG"]:CONSTS["maskG"] + 512] = mg.reshape(128, 512)
    mz = np.zeros((128, 2, 64), np.float32)
    mz[:, 0, :] = (t < s)
    mz[:, 1, :] = (t > s)
    cst[:, CONSTS["maskZ"]:CONSTS["maskZ"] + 128] = mz.reshape(128, 128)
    cm = np.ones((128, 512), np.float32)
    cm[:, ::64] = 0
    cst[:, CONSTS["cmask"]:CONSTS["cmask"] + 512] = cm
    i64 = np.zeros((128, 64), np.float32)
    i64[np.arange(128), np.arange(128) % 64] = 1
    cst[:, CONSTS["id64"]:CONSTS["id64"] + 64] = i64
    return cst


def build_program(limit=10 ** 9, dbg_spec=None, mlimit=10 ** 9):
    nc = bass.Bass("TRN2", target_bir_lowering=False)
    stage = [0]

    def go():
        stage[0] += 1
        return stage[0] <= limit
    mstage = [0]

    def mgo():
        mstage[0] += 1
        return mstage[0] <= mlimit

    def din(name, shape):
        return nc.dram_tensor(name, list(shape), F32, kind="ExternalInput").ap()

    xg = din("xg", [5, N, D])
    colsd = din("cols", [128, NCOL])
    cstd = din("consts", [128, NCONST])
    std = din("st", [5, 4, 128, 64])
    w_mod = din("w_mod", [D, 9 * D])
    f1w13 = din("f1w13", [D, 2 * DFF])
    f1w2 = din("f1w2", [DFF, D])
    f2w13 = din("f2w13", [D, 2 * DFF])
    f2w2 = din("f2w2", [DFF, D])
    w_in = din("w_in", [D, 5632])
    w1c = din("w1c", [5, D, 128])
    w2c = din("w2c", [5, 128, 512])
    wba = din("wba", [512, D])
    wbb = din("wbb", [512, D])
    wout = din("wout", [D, D])
    yout = nc.dram_tensor("y", [2 * N, D], F32, kind="ExternalOutput").ap()
    nsout = nc.dram_tensor("ns", [2, 2, 8, 64, 64], F32, kind="ExternalOutput").ap()

    with contextlib.ExitStack() as stk:
        def sb(name, shape, dt=F32):
            return stk.enter_context(nc.sbuf_tensor(name, list(shape), dt))

        def ps(name, shape):
            return stk.enter_context(nc.psum_tensor(name, list(shape), F32))

        mk = MK(nc)
        cols = sb("cols_t", [128, NCOL])
        cst = sb("cst_t", [128, NCONST])
        onesb = sb("onesb", [128, 128], BF16)
        identb = sb("identb", [128, 128], BF16)
        scb = sb("scb", [128, 8, 2], BF16)
        modT = sb("modT", [128, 72, 2])
        mods = sb("mods", [128, 2, 6, 8])
        xT = sb("xT", [128, 8, N])
        rstd = sb("rstd", [128, N])
        lnt = sb("lnt", [128, N])
        bdum = sb("bdum", [128, 1])
        NSLOT = 3
        slots = [sb("slot%d" % i, [128, 4096], BF16) for i in range(NSLOT)]
        hlast = sb("hlast", [128, 3, 8])
        hbF = sb("hbF", [128, 8])
        hbB = sb("hbB", [128, 8])
        hbA = sb("hbA", [128, 8])
        endst = sb("endst", [128, 3, 4, 64])
        Mst = sb("Mst", [128, 4, 64])
        Mtmp = sb("Mtmp", [128, 4, 64])
        Mbf = sb("Mbf", [128, 4, 64], BF16)
        SCR = 30720
        scr = sb("scr", [128, SCR])

        class Pool:
            def __init__(self):
                self.off = 0

            def f32(self, n):
                a = scr[:, self.off:self.off + n]
                self.off += n
                assert self.off <= SCR, self.off
                return a

            def bf16(self, n):
                m = (n + 1) // 2
                a = scr[:, self.off:self.off + m].bitcast(BF16)
                self.off += m
                assert self.off <= SCR, self.off
                return a

        psA = ps("psA", [128, 2, 512])
        psB = ps("psB", [128, 2, 512])
        psS = ps("psS", [128, 512])
        psC = ps("psC", [128, 2, 512])
        psT = ps("psT", [128, 512])

        def tap(name, ap2d, width, keys):
            if name not in TAPS:
                return
            dt_ = nc.dram_tensor("dbg_" + name, [128, width], F32, kind="ExternalOutput").ap()
            mk.dma("sp", "yout", lambda e: e.dma_start(out=dt_, in_=ap2d), reads=keys)

        cnum = lambda i: cols[:, COLS["num"] + i:COLS["num"] + i + 1]
        ccol = lambda name, i: cols[:, COLS[name] + i:COLS[name] + i + 1]
        ident = cst[:, CONSTS["ident"]:CONSTS["ident"] + 128]
        bones = cst[:, CONSTS["bones"]:CONSTS["bones"] + 128]
        bones64 = cst[:, CONSTS["bones64"]:CONSTS["bones64"] + 128]
        id64 = cst[:, CONSTS["id64"]:CONSTS["id64"] + 64]
        cmask = cst[:, CONSTS["cmask"]:CONSTS["cmask"] + 512]

        def maskG(d):
            o = CONSTS["maskG"] + d * 256
            return cst[:, o:o + 256]

        def maskZ(d):
            o = CONSTS["maskZ"] + d * 64
            return cst[:, o:o + 64]

        evac_rr = [0]

        def evac(out, in_, reads, writes):
            evac_rr[0] ^= 1
            if evac_rr[0]:
                mk.act(lambda e: e.copy(out=out, in_=in_), reads=reads, writes=writes)
            else:
                mk.dve(lambda e: e.tensor_copy(out=out, in_=in_), reads=reads, writes=writes)

        slot_rr = [0]

        def load_slab(pieces):
            s = slot_rr[0] % NSLOT
            slot_rr[0] += 1
            sl = slots[s]
            key = "W%d" % s
            for i, (vf, dap) in enumerate(pieces):
                mk.dma("pool", "w%d" % s, lambda e, vf=vf, dap=dap: e.dma_start(out=vf(sl), in_=dap), writes=[key])
            return sl, key

        def mm_group(out_ap, out_key, terms):
            n = len(terms)
            for i, (l, r, keys) in enumerate(terms):
                mk.pe(lambda e, l=l, r=r, i=i: e.matmul(out_ap, lhsT=l, rhs=r, start=(i == 0), stop=(i == n - 1)),
                      reads=keys, writes=(out_key if isinstance(out_key, list) else [out_key]))

        barrier_n = [0]

        def barrier():
            barrier_n[0] += 1
            mk.dve(lambda e: e.memset(bdum[:], 0.0), reads=["bdum"], writes=["SCR"])

        R = lambda *k: ["SCR"] + list(k)

        mk.dma("sp", "c0", lambda e: e.dma_start(out=cols[:], in_=colsd), writes=["cols"])
        mk.dma("sp", "c0", lambda e: e.dma_start(out=cst[:], in_=cstd), writes=["cst"])
        mk.dve(lambda e: e.memset(onesb[:], 1.0 / 1024.0), writes=["onesb"])
        mk.dve(lambda e: e.tensor_copy(out=identb[:], in_=cst[:, CONSTS["ident"]:CONSTS["ident"] + 128]), reads=["cst"], writes=["cst2"])
        cv0 = COLS["cv"]
        mk.act(lambda e: e.activation(out=scb[:].rearrange("p k j -> p (k j)"), in_=cols[:, cv0:cv0 + 16], func=AF.Silu),
               reads=["cols"], writes=["scb"])
        for s in range(18):
            sl, key = load_slab([(lambda t: t[:, 0:4096].rearrange("p (k n) -> p k n", k=8),
                                 w_mod[:, s * 512:(s + 1) * 512].rearrange("(k p) n -> p k n", p=128))])
            slv = sl[:, 0:4096].rearrange("p (k n) -> p k n", k=8)
            for tt in range(4):
                mm_group(psS[:, tt * 2:tt * 2 + 2], "psS",
                         [(slv[:, k, tt * 128:(tt + 1) * 128], scb[:, k, :], [key, "scb"]) for k in range(8)])
            for j in range(2):
                b0 = COLS["bmod"] + s * 4
                mk.dve(lambda e, s=s, j=j, b0=b0: e.tensor_tensor(
                    out=modT[:, s * 4:s * 4 + 4, j], in0=psS[:, 0:8].rearrange("p (t j) -> p t j", j=2)[:, :, j],
                    in1=cols[:, b0:b0 + 4], op=ALU.add), reads=["psS", "cols"], writes=["modT"])
        ng = lambda i: cols[:, COLS["ng"] + i * 8:COLS["ng"] + i * 8 + 8]
        m_ = lambda i, j: modT[:, i * 8:(i + 1) * 8, j]
        for j in range(2):
            for (dst, mi, gi, half) in [(0, 1, 0, None), (1, 2, 1, 0.5), (2, 4, 2, None), (3, 5, 3, 1.0), (4, 7, 4, None), (5, 8, 5, 0.5)]:
                if half is None:
                    mk.dve(lambda e, j=j, dst=dst, mi=mi, gi=gi: e.scalar_tensor_tensor(
                        out=mods[:, j, dst, :], in0=m_(mi, j), scalar=1.0, in1=ng(gi), op0=ALU.add, op1=ALU.mult),
                        reads=["modT", "cols"], writes=["mods"])
                else:
                    mk.dve(lambda e, j=j, dst=dst, mi=mi, gi=gi, half=half: e.scalar_tensor_tensor(
                        out=mods[:, j, dst, :], in0=m_(mi, j), scalar=half, in1=ng(gi), op0=ALU.mult, op1=ALU.mult),
                        reads=["modT", "cols"], writes=["mods"])

        def rms_rstd(src, src_key, sq, eps_idx):
            for k in range(8):
                if k % 2 == 0:
                    mk.act(lambda e, k=k: e.activation(out=sq[:, k, :], in_=src[:, k, :], func=AF.Square),
                           reads=R(src_key), writes=["sq%d" % k])
                else:
                    mk.dve(lambda e, k=k: e.tensor_tensor(out=sq[:, k, :], in0=src[:, k, :], in1=src[:, k, :], op=ALU.mult),
                           reads=R(src_key), writes=["sq%d" % k])
            mm_group(psS[:], "psS", [(onesb[:], sq[:, k, :], ["onesb", "sq%d" % k, "SCR"]) for k in range(8)])
            mk.act(lambda e: e.activation(out=lnt[:], in_=psS[:], func=AF.Ln, bias=cnum(eps_idx), scale=1.0),
                   reads=["psS", "cols"], writes=["lnt"])
            mk.act(lambda e: e.activation(out=rstd[:], in_=lnt[:], func=AF.Exp, scale=-0.5), reads=["lnt"], writes=["rstd"])

        def prenorm(j, ai, bi, hT, sq, tmp):
            rms_rstd(xT, "xT", sq, 0)
            for k in range(8):
                mk.dve(lambda e, k=k: e.scalar_tensor_tensor(out=tmp[:, k % 2, :], in0=xT[:, k, :], scalar=mods[:, j, ai, k:k + 1],
                                                             in1=rstd[:], op0=ALU.mult, op1=ALU.mult),
                       reads=R("xT", "mods", "rstd"), writes=["ptmp%d" % (k % 2)])
                mk.act(lambda e, k=k: e.activation(out=hT[:, k, :], in_=tmp[:, k % 2, :], func=AF.Identity,
                                                   bias=modT[:, bi * 8 + k, j:j + 1], scale=1.0),
                       reads=R("ptmp%d" % (k % 2), "modT"), writes=["hT%d" % k])

        def postnorm_residual(j, gi, oT, sq, tmp):
            rms_rstd(oT, "oT", sq, 0)
            for k in range(8):
                mk.dve(lambda e, k=k: e.scalar_tensor_tensor(out=tmp[:, k % 2, :], in0=oT[:, k, :], scalar=mods[:, j, gi, k:k + 1],
                                                             in1=rstd[:], op0=ALU.mult, op1=ALU.mult),
                       reads=R("oT", "mods", "rstd"), writes=["ptmp%d" % (k % 2)])
                mk.dve(lambda e, k=k: e.tensor_tensor(out=xT[:, k, :], in0=xT[:, k, :], in1=tmp[:, k % 2, :], op=ALU.add),
                       reads=R("ptmp%d" % (k % 2), "xT"), writes=["xT"])

        def ffn(w13, w2, hT, hid, oT, sgt):
            for s in range(11):
                sl, key = load_slab([
                    (lambda t: t[:, 0:4096].rearrange("p (k n) -> p k n", k=8)[:, :, 0:256],
                     w13[:, s * 256:(s + 1) * 256].rearrange("(k p) n -> p k n", p=128)),
                    (lambda t: t[:, 0:4096].rearrange("p (k n) -> p k n", k=8)[:, :, 256:512],
                     w13[:, DFF + s * 256:DFF + (s + 1) * 256].rearrange("(k p) n -> p k n", p=128))])
                slv = sl[:, 0:4096].rearrange("p (k n) -> p k n", k=8)
                for jj in range(2):
                    jt = 2 * s + jj
                    b = jt % 2
                    mm_group(psA[:, b, :], "psA%d" % b,
                             [(slv[:, k, jj * 128:(jj + 1) * 128], hT[:, k, :], [key, "hT%d" % k, "SCR"]) for k in range(8)])
                    mm_group(psB[:, b, :], KB(b),
                             [(slv[:, k, 256 + jj * 128:256 + (jj + 1) * 128], hT[:, k, :], [key, "hT%d" % k, "SCR"]) for k in range(8)])
                    mk.act(lambda e, b=b: e.activation(out=sgt[:, b, :], in_=psA[:, b, :], func=AF.Silu),
                           reads=R("psA%d" % b), writes=["sgt%d" % b])
                    mk.dve(lambda e, b=b, jt=jt: e.tensor_tensor(out=hid[:, jt, :], in0=sgt[:, b, :], in1=psB[:, b, :], op=ALU.mult),
                           reads=R("sgt%d" % b, *KB(b)), writes=["hid%d" % jt])
            for i in range(8):
                sl, key = load_slab([(lambda t: t[:, 0:2816].rearrange("p (j n) -> p j n", j=22),
                                     w2[:, i * 128:(i + 1) * 128].rearrange("(j p) n -> p j n", p=128))])
                slv = sl[:, 0:2816].rearrange("p (j n) -> p j n", j=22)
                b = i % 2
                mm_group(psA[:, b, :], "psA%d" % b, [(slv[:, jt, :], hid[:, jt, :], [key, "hid%d" % jt, "SCR"]) for jt in range(22)])
                evac(oT[:, i, :], psA[:, b, :], R("psA%d" % b), ["oT"])

        def KB(b):
            return ["psB0", "psB0b"] if b == 0 else ["psB1"]

        def mixer(kind, j, hT, aux_idx):
            pool = Pool()
            pool.off = 2048
            full = kind != "aux"
            nseq = 2 if kind == "prompt" else 1
            L = N // nseq
            cps = NCH // nseq
            rkv = pool.f32(12 * N).rearrange("p (t n) -> p t n", t=12)
            kk = pool.f32(4 * N).rearrange("p (t n) -> p t n", t=4)
            yT = pool.f32(4 * N).rearrange("p (t n) -> p t n", t=4)
            asum = pool.f32(4 * N).rearrange("p (t n) -> p t n", t=4)
            lh = pool.bf16(2 * N).rearrange("p (d n) -> p d n", d=2)
            w2b = pool.bf16(2 * 512).rearrange("p (d n) -> p d n", d=2)
            vbf = pool.bf16(4 * N).rearrange("p (t n) -> p t n", t=4)
            base_d = pool.off
            w1b = pool.bf16(2 * 1024).rearrange("p (d k n) -> p d k n", d=2, k=8)
            w1s = pool.bf16(2 * 1024).rearrange("p (d k n) -> p d k n", d=2, k=8)
            hk = ["hT%d" % k for k in range(8)]
            for s in range(3):
                sl, key = load_slab([(lambda t: t[:, 0:4096].rearrange("p (k n) -> p k n", k=8),
                                     w_in[:, s * 512:(s + 1) * 512].rearrange("(k p) n -> p k n", p=128))])
                slv = sl[:, 0:4096].rearrange("p (k n) -> p k n", k=8)
                for tt in range(4):
                    b = tt % 2
                    mm_group(psA[:, b, :], "psA%d" % b,
                             [(slv[:, k, tt * 128:(tt + 1) * 128], hT[:, k, :], [key, "hT%d" % k, "SCR"]) for k in range(8)])
                    evac(rkv[:, s * 4 + tt, :], psA[:, b, :], R("psA%d" % b), ["rkv%d" % (s * 4 + tt)])
            for pr in range(4):
                mk.act(lambda e, pr=pr: e.copy(out=vbf[:, pr, :], in_=rkv[:, 8 + pr, :]), reads=R("rkv%d" % (8 + pr)), writes=["vbf%d" % pr])
            if not mgo():
                return
            sqk = pool.f32(2 * N).rearrange("p (b n) -> p b n", b=2)
            for pr in range(4):
                b = pr % 2
                mk.dve(lambda e, pr=pr: e.tensor_scalar(out=kk[:, pr, :], in0=rkv[:, 4 + pr, :], scalar1=ccol("kk", pr), scalar2=None,
                                                        op0=ALU.mult), reads=R("rkv%d" % (4 + pr), "cols"), writes=["kk%d" % pr])
                mk.dve(lambda e, pr=pr, b=b: e.tensor_tensor(out=sqk[:, b, :], in0=kk[:, pr, :], in1=kk[:, pr, :], op=ALU.mult),
                       reads=R("kk%d" % pr), writes=["sqk%d" % b])
                mm_group(psB[:, b, :], KB(b), [(bones, sqk[:, b, :], ["cst", "sqk%d" % b, "SCR"])])
                mk.act(lambda e, b=b: e.activation(out=sqk[:, b, :], in_=psB[:, b, :], func=AF.Ln, bias=cnum(1), scale=1.0),
                       reads=R("cols", *KB(b)), writes=["sqk%d" % b])
                mk.act(lambda e, b=b: e.activation(out=sqk[:, b, :], in_=sqk[:, b, :], func=AF.Exp, scale=-0.5),
                       reads=R("sqk%d" % b), writes=["sqk%d" % b])
                mk.dve(lambda e, pr=pr, b=b: e.tensor_tensor(out=kk[:, pr, :], in0=kk[:, pr, :], in1=sqk[:, b, :], op=ALU.mult),
                       reads=R("sqk%d" % b, "kk%d" % pr), writes=["kk%d" % pr])
            if not mgo():
                return
            dslots = [(0, 0), (1, 1)] if full else [(0, 2 + aux_idx)]
            sh = pool.bf16(8 * N).rearrange("p (k n) -> p k n", k=8)
            for di, (dt_, ds) in enumerate(dslots):
                mk.dma("pool", "wl", lambda e, di=di, ds=ds: e.dma_start(out=w1b[:, di, :, :], in_=w1c[ds].rearrange("(k p) n -> p k n", p=128)),
                       reads=["SCR"], writes=["w1b%d" % di])
                mk.dma("pool", "wl", lambda e, di=di, ds=ds: e.dma_start(out=w2b[:, di, :], in_=w2c[ds]), reads=["SCR"], writes=["w2b%d" % di])
                for x in range(2):
                    for k in range(8):
                        mc = COLS["mu"] + ds * 16 + x * 8 + k
                        mk.dve(lambda e, di=di, x=x, k=k, mc=mc: e.tensor_scalar(
                            out=w1s[:, di, k, x * 64:(x + 1) * 64], in0=w1b[:, di, k, x * 64:(x + 1) * 64],
                            scalar1=cols[:, mc:mc + 1], scalar2=None, op0=ALU.mult),
                            reads=R("w1b%d" % di, "cols"), writes=["w1s%d" % di])
                if dt_ == 0:
                    mk.dve(lambda e: e.tensor_tensor(out=sh[:, :, 1:N], in0=hT[:, :, 0:N - 1], in1=hT[:, :, 1:N], op=ALU.subtract),
                           reads=R(*hk), writes=["sh"])
                    for sq_ in range(nseq):
                        col = sq_ * L
                        if kind == "prompt" or (kind == "aux" and aux_idx == 0):
                            mk.dve(lambda e, col=col: e.tensor_scalar(out=sh[:, :, col], in0=hT[:, :, col], scalar1=-1.0, scalar2=None,
                                                                      op0=ALU.mult), reads=R("sh", *hk), writes=["sh"])
                        else:
                            hb = hbF if kind == "own" else hbA
                            hbk = "hbF" if kind == "own" else "hbA"
                            mk.dve(lambda e, col=col, hb=hb: e.tensor_tensor(out=sh[:, :, col], in0=hb[:, :], in1=hT[:, :, col], op=ALU.subtract),
                                   reads=R("sh", hbk, *hk), writes=["sh"])
                else:
                    mk.dve(lambda e: e.tensor_tensor(out=sh[:, :, 0:N - 1], in0=hT[:, :, 1:N], in1=hT[:, :, 0:N - 1], op=ALU.subtract),
                           reads=R(*hk), writes=["sh"])
                    for sq_ in range(nseq):
                        col = sq_ * L + L - 1
                        if kind == "prompt":
                            mk.dve(lambda e, col=col: e.tensor_scalar(out=sh[:, :, col], in0=hT[:, :, col], scalar1=-1.0, scalar2=None,
                                                                      op0=ALU.mult), reads=R("sh", *hk), writes=["sh"])
                        else:
                            mk.dve(lambda e, col=col: e.tensor_tensor(out=sh[:, :, col], in0=hbB[:, :], in1=hT[:, :, col], op=ALU.subtract),
                                   reads=R("sh", "hbB", *hk), writes=["sh"])
                b = di % 2
                mm_group(psB[:, b, :], KB(b),
                         [(w1b[:, di, k, :], hT[:, k, :], ["w1b%d" % di, "hT%d" % k, "SCR"]) for k in range(8)] +
                         [(w1s[:, di, k, :], sh[:, k, :], ["w1s%d" % di, "sh", "SCR"]) for k in range(8)])
                mk.act(lambda e, di=di, b=b: e.activation(out=lh[0:64, di, :], in_=psB[0:64, b, :], func=AF.Tanh),
                       reads=R(*KB(b)), writes=["lh%d" % di])
                mk.act(lambda e, di=di, b=b: e.copy(out=lh[64:128, di, :], in_=psB[64:128, b, :]),
                       reads=R(*KB(b)), writes=["lh%d" % di])
            if kind == "aux":
                mk.dve(lambda e: e.tensor_copy(out=hlast[:, aux_idx, :], in_=hT[:, :, N - 1]), reads=R(*hk), writes=["hlast%d" % aux_idx])

            if not mgo():
                return
            v3 = lambda ap: ap.rearrange("p (c n) -> p c n", c=8)
            psG = psA[:].rearrange("p a (h n) -> p (a h) n", h=2)
            psZ = psB[:].rearrange("p a (h n) -> p (a h) n", h=8)
            psZv = lambda a, hh: psZ[:, a * 4 + hh, :]
            ZK = ["psB0", "psB0b", "psB1"]
            psTv = psT[:].rearrange("p (a h n) -> p a h n", a=2, h=4)
            psCv = psC[:].rearrange("p a (h n) -> p a h n", h=8)
            psSv = psS[:].rearrange("p (h n) -> p h n", h=8)
            for di, (dt_, ds) in enumerate(dslots):
                order = list(range(NCH)) if dt_ == 0 else list(range(NCH - 1, -1, -1))
                barrier()
                pool.off = base_d
                AR = pool.bf16(4 * 8 * 128).rearrange("p (q c n) -> p q c n", q=4, c=8)
                BK = pool.bf16(4 * 8 * 128).rearrange("p (q c n) -> p q c n", q=4, c=8)
                Pend = pool.f32(32).rearrange("p (q c) -> p q c", q=4)
                base_t = pool.off
                sw = pool.f32(2 * N).rearrange("p (q n) -> p q n", q=2)
                av = pool.f32(2 * N).rearrange("p (q n) -> p q n", q=2)
                cs = pool.f32(2 * N).rearrange("p (q n) -> p q n", q=2)
                Lx = pool.f32(2 * N).rearrange("p (q n) -> p q n", q=2)
                Ep = pool.f32(2 * N).rearrange("p (q n) -> p q n", q=2)
                t1 = pool.f32(2 * N).rearrange("p (q n) -> p q n", q=2)
                for hp in range(2):
                    for ql in range(2):
                        pr = 2 * hp + ql
                        b = ql
                        mm_group(psA[:, b, :], "psA%d" % b, [(w2b[0:64, di, pr * 128:(pr + 1) * 128], lh[0:64, di, :], ["w2b%d" % di, "lh%d" % di, "SCR"])])
                        mm_group(psB[:, b, :], KB(b), [(w2b[64:128, di, pr * 128:(pr + 1) * 128], lh[64:128, di, :], ["w2b%d" % di, "lh%d" % di, "SCR"])])
                        mk.act(lambda e, pr=pr, ql=ql, b=b, ds=ds: e.activation(out=sw[:, ql, :], in_=psA[:, b, :], func=AF.Sigmoid,
                                                                               bias=ccol("w0", ds * 4 + pr), scale=1.0),
                               reads=R("psA%d" % b, "cols"), writes=["sw%d" % ql])
                        mk.act(lambda e, pr=pr, ql=ql, b=b, ds=ds: e.activation(out=av[:, ql, :], in_=psB[:, b, :], func=AF.Sigmoid,
                                                                               bias=ccol("a0", ds * 4 + pr), scale=1.0),
                               reads=R("cols", *KB(b)), writes=["av%d" % ql])
                        if full:
                            if di == 0:
                                mk.dve(lambda e, pr=pr, ql=ql: e.tensor_copy(out=asum[:, pr, :], in_=av[:, ql, :]), reads=R("av%d" % ql), writes=["asum%d" % pr])
                            else:
                                mk.dve(lambda e, pr=pr, ql=ql: e.tensor_tensor(out=asum[:, pr, :], in0=asum[:, pr, :], in1=av[:, ql, :], op=ALU.add),
                                       reads=R("av%d" % ql, "asum%d" % pr), writes=["asum%d" % pr])
                        mk.dve(lambda e, ql=ql: e.tensor_tensor_scan(out=cs[:, ql, :], data0=cmask, data1=sw[:, ql, :], initial=0.0,
                                                                     op0=ALU.mult, op1=ALU.add), reads=R("sw%d" % ql, "cst"), writes=["cs%d" % ql])
                        if dt_ == 0:
                            mk.dve(lambda e, ql=ql: e.tensor_tensor(out=Lx[:, ql, :], in0=cs[:, ql, :], in1=sw[:, ql, :], op=ALU.subtract),
                                   reads=R("cs%d" % ql, "sw%d" % ql), writes=["Lx%d" % ql])
                        else:
                            mk.dve(lambda e, ql=ql: e.tensor_tensor(out=v3(Lx[:, ql, :]), in0=v3(cs[:, ql, :])[:, :, 63:64].to_broadcast([128, 8, 64]),
                                                                    in1=v3(cs[:, ql, :]), op=ALU.subtract),
                                   reads=R("cs%d" % ql), writes=["Lx%d" % ql])
                            mk.dve(lambda e, ql=ql: e.tensor_tensor(out=cs[:, ql, :], in0=Lx[:, ql, :], in1=sw[:, ql, :], op=ALU.add),
                                   reads=R("Lx%d" % ql, "sw%d" % ql), writes=["cs%d" % ql])
                        mk.act(lambda e, ql=ql: e.activation(out=Ep[:, ql, :], in_=cs[:, ql, :], func=AF.Exp, scale=-C0), reads=R("cs%d" % ql), writes=["Ep%d" % ql])
                        mk.act(lambda e, ql=ql: e.activation(out=Lx[:, ql, :], in_=Lx[:, ql, :], func=AF.Exp, scale=-C0), reads=R("Lx%d" % ql), writes=["Lx%d" % ql])
                        mk.act(lambda e, ql=ql: e.activation(out=cs[:, ql, :], in_=cs[:, ql, :], func=AF.Exp, scale=C0), reads=R("cs%d" % ql, "Ep%d" % ql), writes=["cs%d" % ql])
                        pcol = 63 if dt_ == 0 else 0
                        mk.dve(lambda e, pr=pr, ql=ql, pcol=pcol: e.tensor_copy(out=Pend[:, pr, :], in_=v3(Ep[:, ql, :])[:, :, pcol]), reads=R("Ep%d" % ql), writes=["Pend"])
                        mk.dve(lambda e, pr=pr, ql=ql: e.scalar_tensor_tensor(out=AR[:, pr, :, 0:64], in0=v3(kk[:, pr, :]), scalar=-1.0, in1=v3(Lx[:, ql, :]),
                                                                              op0=ALU.mult, op1=ALU.mult), reads=R("kk%d" % pr, "Lx%d" % ql), writes=["AR%d" % pr])
                        mk.dve(lambda e, pr=pr, ql=ql: e.tensor_tensor(out=AR[:, pr, :, 64:128], in0=v3(rkv[:, pr, :]), in1=v3(Ep[:, ql, :]), op=ALU.mult),
                               reads=R("rkv%d" % pr, "Ep%d" % ql), writes=["AR%d" % pr])
                        mk.dve(lambda e, pr=pr, ql=ql: e.tensor_tensor(out=t1[:, ql, :], in0=kk[:, pr, :], in1=av[:, ql, :], op=ALU.mult),
                               reads=R("kk%d" % pr, "av%d" % ql), writes=["t1%d" % ql])
                        mk.dve(lambda e, pr=pr, ql=ql: e.tensor_tensor(out=BK[:, pr, :, 0:64], in0=v3(t1[:, ql, :]), in1=v3(cs[:, ql, :]), op=ALU.mult),
                               reads=R("t1%d" % ql, "cs%d" % ql), writes=["BK%d" % pr])
                        mk.dve(lambda e, pr=pr, ql=ql: e.tensor_scalar(out=t1[:, ql, :], in0=av[:, ql, :], scalar1=cnum(4), scalar2=ccol("ka", pr),
                                                                       op0=ALU.subtract, op1=ALU.mult), reads=R("av%d" % ql, "cols", "t1%d" % ql), writes=["t1%d" % ql])
                        mk.dve(lambda e, pr=pr, ql=ql: e.scalar_tensor_tensor(out=t1[:, ql, :], in0=t1[:, ql, :], scalar=1.0, in1=rkv[:, 4 + pr, :],
                                                                              op0=ALU.add, op1=ALU.mult), reads=R("t1%d" % ql, "rkv%d" % (4 + pr)), writes=["t1%d" % ql])
                        mk.dve(lambda e, pr=pr, ql=ql: e.tensor_tensor(out=BK[:, pr, :, 64:128], in0=v3(t1[:, ql, :]), in1=v3(cs[:, ql, :]), op=ALU.mult),
                               reads=R("t1%d" % ql, "cs%d" % ql), writes=["BK%d" % pr])
                if not mgo():
                    return
                barrier()
                pool.off = base_t
                Gm = [pool.bf16(4 * 256).rearrange("p (h n) -> p h n", h=4) for _ in range(2)]
                ZZ = [pool.bf16(2 * 4 * 64).rearrange("p (a h n) -> p a h n", a=2, h=4) for _ in range(2)]
                Qt = [pool.bf16(4 * 64).rearrange("p (h n) -> p h n", h=4) for _ in range(2)]
                TOK = [pool.bf16(3 * 4 * 64).rearrange("p (a h n) -> p a h n", a=3, h=4) for _ in range(2)]
                Wsb = pool.bf16(4 * 64).rearrange("p (h n) -> p h n", h=4)
                Usb = pool.bf16(4 * 64).rearrange("p (h n) -> p h n", h=4)
                Ytmp = pool.f32(4 * 64).rearrange("p (h n) -> p h n", h=4)
                unit = [0]

                def heads():
                    for q in range(4):
                        for e_ in range(2):
                            yield q, 64 * e_

                def tseries_stages(c, dt_=dt_):
                    u = unit[0] % 2
                    unit[0] += 1
                    G, Z2, Q, TK = Gm[u], ZZ[u], Qt[u], TOK[u]
                    gk, zk, qk, tk = "Gm%d" % u, "ZZ%d" % u, "Q%d" % u, "TOK%d" % u
                    stages = []

                    def st_g():
                        for q, fo in heads():
                            mm_group(psG[fo:fo + 64, q, 0:128], "psA%d" % (q // 2),
                                     [(BK[fo:fo + 64, q, c, 0:64], AR[fo:fo + 64, q, c, :], ["BK%d" % q, "AR%d" % q, "SCR"])])
                            mm_group(psG[fo:fo + 64, q, 128:256], "psA%d" % (q // 2),
                                     [(BK[fo:fo + 64, q, c, 64:128], AR[fo:fo + 64, q, c, :], ["BK%d" % q, "AR%d" % q, "SCR"])])
                            mm_group(psZv(0, q)[fo:fo + 64, :], "psB0",
                                     [(AR[fo:fo + 64, q, c, 0:64], BK[fo:fo + 64, q, c, 0:64], ["BK%d" % q, "AR%d" % q, "SCR"])])
                        mk.dve(lambda e: e.tensor_tensor(out=G[:], in0=psG, in1=maskG(dt_).unsqueeze(1).to_broadcast([128, 4, 256]), op=ALU.mult),
                               reads=R("psA0", "psA1", "cst"), writes=[gk])
                        mk.dve(lambda e: e.tensor_tensor(out=Z2[:, 1, :, :], in0=psZ[:, 0:4, :], in1=maskZ(dt_).unsqueeze(1).to_broadcast([128, 4, 64]),
                                                         op=ALU.mult), reads=R("psB0", "cst"), writes=[zk])
                        mk.act(lambda e: e.copy(out=Z2[:, 0, :, :], in_=G[:, :, 0:64]), reads=R(gk), writes=[zk])
                        mk.dve(lambda e: e.tensor_tensor(out=Q[:], in0=G[:, :, 0:64], in1=id64.unsqueeze(1).to_broadcast([128, 4, 64]), op=ALU.add),
                               reads=R(gk, "cst"), writes=[qk])
                    stages.append(st_g)

                    def mk_burst(lev):
                        def st():
                            for q, fo in heads():
                                idb = identb[fo:fo + 64, fo:fo + 64]
                                if lev <= 4:
                                    mm_group(psZv(1, q)[fo:fo + 64, :], "psB0b", [(Z2[fo:fo + 64, 1, q, :], Z2[fo:fo + 64, 0, q, :], [zk, "SCR"])])
                                mm_group(psZv(2, q)[fo:fo + 64, :], "psB1", [(Z2[fo:fo + 64, 0, q, :], Z2[fo:fo + 64, 1, q, :], [zk, "SCR"])])
                                if lev >= 2:
                                    mm_group(psZv(0, q)[fo:fo + 64, :], "psB0", [(idb, Q[fo:fo + 64, q, :], ["cst", qk, "SCR"]),
                                                                                 (Z2[fo:fo + 64, 1, q, :], Q[fo:fo + 64, q, :], [zk, qk, "SCR"])])
                            if lev >= 2:
                                mk.act(lambda e: e.copy(out=Q[:], in_=psZ[:, 0:4, :]), reads=R("psB0"), writes=[qk])
                            if lev <= 4:
                                mk.act(lambda e: e.copy(out=Z2[:, 0, :, :], in_=psZ[:, 4:8, :]), reads=R("psB0b"), writes=[zk])
                            mk.act(lambda e: e.copy(out=Z2[:, 1, :, :], in_=psZ[:, 8:12, :]), reads=R("psB1"), writes=[zk])
                        return st
                    for lev in range(1, 6):
                        stages.append(mk_burst(lev))

                    def st_last():
                        for q, fo in heads():
                            idb = identb[fo:fo + 64, fo:fo + 64]
                            mm_group(psZv(0, q)[fo:fo + 64, :], "psB0", [(idb, Q[fo:fo + 64, q, :], ["cst", qk, "SCR"]),
                                                                         (Z2[fo:fo + 64, 1, q, :], Q[fo:fo + 64, q, :], [zk, qk, "SCR"])])
                        mk.act(lambda e: e.copy(out=Q[:], in_=psZ[:, 0:4, :]), reads=R("psB0"), writes=[qk])
                        for q, fo in heads():
                            idb = identb[fo:fo + 64, fo:fo + 64]
                            mm_group(psTv[fo:fo + 64, 0, q, :], "psT", [(BK[fo:fo + 64, q, c, 0:64], idb, ["BK%d" % q, "cst", "SCR"])])
                            mm_group(psTv[fo:fo + 64, 1, q, :], "psT", [(BK[fo:fo + 64, q, c, 64:128], idb, ["BK%d" % q, "cst", "SCR"])])
                            mm_group(psCv[fo:fo + 64, 1, 4 + q, :], "psCv", [(vbf[fo:fo + 64, q, c * 64:(c + 1) * 64], idb,
                                                                             ["vbf%d" % q, "cst", "SCR"])])
                        mk.act(lambda e: e.copy(out=TK[:, 0:2, :, :], in_=psTv), reads=R("psT"), writes=[tk])
                        mk.dve(lambda e: e.tensor_copy(out=TK[:, 2, :, :], in_=psCv[:, 1, 4:8, :]), reads=R("psCv"), writes=[tk])
                    stages.append(st_last)
                    return stages, (G, Q, TK, gk, qk, tk)

                def chain_stages(c, bufs, dt_=dt_, di=di, order=order):
                    G, Q, TK, gk, qk, tk = bufs
                    seq = c // cps
                    pos = order.index(c) % cps
                    stages = []

                    def st_w():
                        if pos == 0:
                            if kind == "prompt":
                                mk.dve(lambda e: e.memset(Mst[:], 0.0), reads=R(), writes=["Mst"])
                            elif kind == "aux":
                                if aux_idx == 0:
                                    mk.dve(lambda e: e.tensor_copy(out=Mst[:], in_=stt[:, 2, :, :]), reads=R("stt"), writes=["Mst"])
                                else:
                                    cc_ = COLS["coef"] + (aux_idx - 1)
                                    mk.dve(lambda e: e.scalar_tensor_tensor(out=Mst[:], in0=endst[:, aux_idx - 1, :, :], scalar=cols[:, cc_:cc_ + 1],
                                                                            in1=stt[:, 2 + aux_idx, :, :], op0=ALU.mult, op1=ALU.add),
                                           reads=R("stt", "end%d" % (aux_idx - 1), "cols"), writes=["Mst"])
                            else:
                                if dt_ == 0:
                                    mk.dve(lambda e: e.tensor_copy(out=Mst[:], in_=stt[:, 0, :, :]), reads=R("stt"), writes=["Mst"])
                                    for a in range(3):
                                        cc_ = COLS["coef"] + 2 + a
                                        mk.dve(lambda e, a=a, cc_=cc_: e.scalar_tensor_tensor(out=Mst[:], in0=endst[:, a, :, :], scalar=cols[:, cc_:cc_ + 1],
                                                                                          in1=Mst[:], op0=ALU.mult, op1=ALU.add),
                                               reads=R("end%d" % a, "cols", "Mst"), writes=["Mst"])
                                else:
                                    cc_ = COLS["coef"] + 5
                                    mk.dve(lambda e: e.scalar_tensor_tensor(out=Mst[:], in0=endst[:, 2, :, :], scalar=cols[:, cc_:cc_ + 1],
                                                                            in1=stt[:, 1, :, :], op0=ALU.mult, op1=ALU.add),
                                           reads=R("stt", "end2", "cols"), writes=["Mst"])
                        if pos == 0:
                            mk.act(lambda e: e.copy(out=Mbf[:], in_=Mst[:]), reads=R("Mst"), writes=["Mbf"])
                        for q, fo in heads():
                            mm_group(psCv[fo:fo + 64, 0, q, :], "psC0w",
                                     [(AR[fo:fo + 64, q, c, 0:64], Mbf[fo:fo + 64, q, :], ["AR%d" % q, "Mbf", "SCR"]),
                                      (G[fo:fo + 64, q, 128:192], TK[fo:fo + 64, 2, q, :], [gk, tk, "SCR"])])
                        mk.act(lambda e: e.copy(out=Wsb[:], in_=psCv[:, 0, 0:4, :]), reads=R("psC0w"), writes=["Wsb"])
                    stages.append(st_w)

                    def st_u():
                        for q, fo in heads():
                            mm_group(psCv[fo:fo + 64, 0, 4 + q, :], "psC0u", [(Q[fo:fo + 64, q, :], Wsb[fo:fo + 64, q, :], [qk, "Wsb", "SCR"])])
                        mk.dve(lambda e: e.tensor_copy(out=Usb[:], in_=psCv[:, 0, 4:8, :]), reads=R("psC0u"), writes=["Usb"])
                    stages.append(st_u)

                    def st_ym():
                        for q, fo in heads():
                            if full and KV != 5:
                                mm_group(psSv[fo:fo + 64, q, :], "psS",
                                         [(Mbf[fo:fo + 64, q, :], AR[fo:fo + 64, q, c, 64:128], ["Mbf", "AR%d" % q, "SCR"]),
                                          (Usb[fo:fo + 64, q, :], G[fo:fo + 64, q, 64:128], ["Usb", gk, "SCR"]),
                                          (TK[fo:fo + 64, 2, q, :], G[fo:fo + 64, q, 192:256], [tk, gk, "SCR"])])
                            mm_group(psSv[fo:fo + 64, 4 + q, :], "psS",
                                     [(TK[fo:fo + 64, 0, q, :], Usb[fo:fo + 64, q, :], [tk, "Usb", "SCR"]),
                                      (TK[fo:fo + 64, 1, q, :], TK[fo:fo + 64, 2, q, :], [tk, "SCR"])])
                        if full and KV != 6:
                            ydst = yT[:, :, c * 64:(c + 1) * 64]
                            if di == 0 and KV != 9:
                                mk.dve(lambda e: e.tensor_copy(out=ydst, in_=psSv[:, 0:4, :]), reads=R("psS"), writes=["yT"])
                            elif di == 0 and KV == 8:
                                mk.act(lambda e: e.copy(out=Wsb[:], in_=psSv[:, 0:4, :]), reads=R("psS", "Wsb"), writes=["Wsb"])
                                mk.dve(lambda e: e.tensor_copy(out=ydst, in_=Wsb[:]), reads=R("Wsb"), writes=["yT"])
                            elif di == 0:
                                mk.act(lambda e: e.copy(out=ydst, in_=psSv[:, 0:4, :]), reads=R("psS"), writes=["yT"])
                            else:
                                mk.act(lambda e: e.copy(out=Ytmp[:], in_=psSv[:, 0:4, :]), reads=R("psS", "Ytmp"), writes=["Ytmp"])
                                mk.dve(lambda e: e.tensor_tensor(out=ydst, in0=ydst, in1=Ytmp[:], op=ALU.add), reads=R("Ytmp", "yT"), writes=["yT"])
                        mk.dve(lambda e: e.tensor_tensor(out=Mtmp[:], in0=Mst[:], in1=psSv[:, 4:8, :], op=ALU.add),
                               reads=R("psS", "Mst"), writes=["Mtmp"])
                        mk.dve(lambda e: e.tensor_tensor(out=Mst[:], in0=Mtmp[:], in1=Pend[:, :, c:c + 1].to_broadcast([128, 4, 64]), op=ALU.mult),
                               reads=R("Mtmp", "Pend"), writes=["Mst"])
                        mk.act(lambda e: e.copy(out=Mbf[:], in_=Mst[:]), reads=R("Mst"), writes=["Mbf"])
                        if pos == cps - 1:
                            if kind == "aux":
                                mk.act(lambda e: e.copy(out=endst[:, aux_idx, :, :], in_=Mst[:]), reads=R("Mst"), writes=["end%d" % aux_idx])
                            elif kind == "prompt":
                                for q in range(4):
                                    mm_group(psT[0:64, q * 128:(q + 1) * 128], "psT", [(Mst[:, q, :], ident, ["Mst", "cst", "SCR"])])
                                mk.act(lambda e: e.copy(out=nsb[0:64, seq, di, :, :], in_=psT[0:64, :].rearrange("p (a n) -> p a n", a=4)),
                                       reads=R("psT"), writes=["nsb"])
                    stages.append(st_ym)
                    return stages

                prev = None
                for idx_c in range(len(order) + 1):
                    A, bufsA = ([], None)
                    if idx_c < len(order):
                        A, bufsA = tseries_stages(order[idx_c])
                    Bs = []
                    if prev is not None:
                        Bs = chain_stages(prev[0], prev[1])
                    for i in range(max(len(A), len(Bs))):
                        if i < len(A):
                            KTC[0] += 1
                            if KTC[0] <= KT:
                                A[i]()
                        if i < len(Bs):
                            KTC[0] += 1
                            if KTC[0] <= KT:
                                Bs[i]()
                    prev = (order[idx_c], bufsA) if idx_c < len(order) else None
            if not full:
                return
            tap("yT_" + kind, yT.rearrange("p q n -> p (q n)"), 4 * N, R("yT"))
            tap("rkv_" + kind, rkv.rearrange("p q n -> p (q n)"), 12 * N, R(*["rkv%d" % i for i in range(12)]))
            barrier()
            MARK[kind] = len(mk.ops)
            pool.off = base_d
            bv = pool.f32(4 * N).rearrange("p (q n) -> p q n", q=4)
            tA = pool.f32(2 * N).rearrange("p (q n) -> p q n", q=2)
            tB = pool.f32(2 * N).rearrange("p (q n) -> p q n", q=2)
            for pr in range(4):
                b = pr % 2
                mk.dve(lambda e, pr=pr, b=b: e.tensor_scalar(out=tA[:, b, :], in0=asum[:, pr, :], scalar1=cnum(5), scalar2=ccol("ka", pr), op0=ALU.subtract, op1=ALU.mult),
                       reads=R("asum%d" % pr, "cols", "tA%d" % b), writes=["tA%d" % b])
                mk.dve(lambda e, pr=pr, b=b: e.scalar_tensor_tensor(out=tA[:, b, :], in0=tA[:, b, :], scalar=2.0, in1=rkv[:, 4 + pr, :], op0=ALU.add, op1=ALU.mult),
                       reads=R("tA%d" % b, "rkv%d" % (4 + pr)), writes=["tA%d" % b])
                mk.dve(lambda e, pr=pr, b=b: e.scalar_tensor_tensor(out=tA[:, b, :], in0=tA[:, b, :], scalar=ccol("rk", pr), in1=rkv[:, pr, :], op0=ALU.mult, op1=ALU.mult),
                       reads=R("tA%d" % b, "rkv%d" % pr, "cols"), writes=["tA%d" % b])
                mm_group(psA[:, b, :], "psA%d" % b, [(bones, tA[:, b, :], ["cst", "tA%d" % b, "SCR"])])
                mk.dve(lambda e, pr=pr, b=b: e.tensor_tensor(out=bv[:, pr, :], in0=psA[:, b, :], in1=rkv[:, 8 + pr, :], op=ALU.mult),
                       reads=R("psA%d" % b, "rkv%d" % (8 + pr)), writes=["bv%d" % pr])
                mm_group(psB[:, b, :], KB(b), [(bones64, yT[:, pr, :], ["cst", "yT", "SCR"])])
                mk.act(lambda e, b=b: e.copy(out=tB[:, b, :], in_=psB[:, b, :]), reads=R("tB%d" % b, *KB(b)), writes=["tB%d" % b])
                mk.dve(lambda e, pr=pr, b=b: e.tensor_tensor(out=yT[:, pr, :], in0=yT[:, pr, :], in1=tB[:, b, :], op=ALU.subtract), reads=R("tB%d" % b, "yT"), writes=["yT"])
                mk.act(lambda e, pr=pr, b=b: e.activation(out=tA[:, b, :], in_=yT[:, pr, :], func=AF.Square), reads=R("yT", "tA%d" % b), writes=["tA%d" % b])
                mm_group(psA[:, b, :], "psA%d" % b, [(bones64, tA[:, b, :], ["cst", "tA%d" % b, "SCR"])])
                mk.act(lambda e, b=b: e.activation(out=tB[:, b, :], in_=psA[:, b, :], func=AF.Ln, bias=cnum(2), scale=1.0), reads=R("cols", "tB%d" % b, "psA%d" % b), writes=["tB%d" % b])
                mk.act(lambda e, b=b: e.activation(out=tB[:, b, :], in_=tB[:, b, :], func=AF.Exp, scale=-0.5), reads=R("tB%d" % b), writes=["tB%d" % b])
                mk.dve(lambda e, pr=pr, b=b: e.scalar_tensor_tensor(out=yT[:, pr, :], in0=yT[:, pr, :], scalar=ccol("gng", pr), in1=tB[:, b, :], op0=ALU.mult, op1=ALU.mult),
                       reads=R("tB%d" % b, "yT", "cols"), writes=["yT"])
                mk.dve(lambda e, pr=pr: e.scalar_tensor_tensor(out=yT[:, pr, :], in0=yT[:, pr, :], scalar=ccol("gnb", pr), in1=bv[:, pr, :], op0=ALU.add, op1=ALU.add),
                       reads=R("bv%d" % pr, "yT", "cols"), writes=["yT"])
            tap("yn_" + kind, yT.rearrange("p q n -> p (q n)"), 4 * N, R("yT"))
            barrier()
            pool.off = 2048
            yA = pool.f32(8 * N).rearrange("p (q n) -> p q n", q=8)
            pool.off = 12288
            yB = pool.f32(8 * N).rearrange("p (q n) -> p q n", q=8)
            tA2 = pool.f32(2 * N).rearrange("p (q n) -> p q n", q=2)
            tB2 = pool.f32(2 * N).rearrange("p (q n) -> p q n", q=2)
            yaT = pool.bf16(4 * N).rearrange("p (q n) -> p q n", q=4)
            ybT = pool.bf16(4 * N).rearrange("p (q n) -> p q n", q=4)
            cbT = pool.bf16(4 * N).rearrange("p (q n) -> p q n", q=4)
            ccT = pool.f32(4 * N).rearrange("p (q n) -> p q n", q=4)
            mgT = pool.bf16(8 * N).rearrange("p (q n) -> p q n", q=8)
            rl = 64 if kind == "own" else L
            r3 = lambda ap: ap.rearrange("p (r n) -> p r n", n=rl)

            def branch(wmat, srcT, srckey, dst, dstkey):
                for half in range(2):
                    slw, keyw = load_slab([(lambda t: t[:, 0:2048].rearrange("p (k n) -> p k n", k=4),
                                            wmat[:, half * 512:(half + 1) * 512].rearrange("(k p) n -> p k n", p=128))])
                    slwv = slw[:, 0:2048].rearrange("p (k n) -> p k n", k=4)
                    for tt in range(4):
                        o = half * 4 + tt
                        b = tt % 2
                        mm_group(psB[:, b, :], KB(b), [(slwv[:, k, tt * 128:(tt + 1) * 128], srcT[:, k, :], [keyw, srckey % k, "SCR"]) for k in range(4)])
                        evac(dst[:, o, :], psB[:, b, :], R(*KB(b)), [dstkey % o])

            for s in range(3, 11):
                sl, key = load_slab([(lambda t: t[:, 0:4096].rearrange("p (k n) -> p k n", k=8),
                                     w_in[:, s * 512:(s + 1) * 512].rearrange("(k p) n -> p k n", p=128))])
                slv = sl[:, 0:4096].rearrange("p (k n) -> p k n", k=8)
                for tt in range(4):
                    b = tt % 2
                    mm_group(psA[:, b, :], "psA%d" % b,
                             [(slv[:, k, tt * 128:(tt + 1) * 128], hT[:, k, :], [key, "hT%d" % k, "SCR"]) for k in range(8)])
                    pa = psA[:, b, :]
                    pk = "psA%d" % b
                    tb = tt % 2
                    if s == 3:
                        mk.act(lambda e, pa=pa, tb=tb: e.activation(out=tA2[:, tb, :], in_=pa, func=AF.Sigmoid), reads=R(pk, "tA2%d" % tb), writes=["tA2%d" % tb])
                        mk.dve(lambda e, tt=tt, tb=tb: e.tensor_tensor(out=yaT[:, tt, :], in0=yT[:, tt, :], in1=tA2[:, tb, :], op=ALU.mult),
                               reads=R("yT", "tA2%d" % tb), writes=["yaT%d" % tt])
                    elif s == 4:
                        mk.act(lambda e, tt=tt, pa=pa: e.copy(out=cbT[:, tt, :], in_=pa), reads=R(pk), writes=["cbT%d" % tt])
                    elif s == 5:
                        mk.act(lambda e, tt=tt, pa=pa: e.copy(out=ccT[:, tt, :], in_=pa), reads=R(pk), writes=["ccT%d" % tt])
                    elif s == 6:
                        u = tA2[:, tb, :]
                        uk = "tA2%d" % tb
                        acc = tB2[:, tb, :]
                        ak = "tB2%d" % tb
                        mk.dve(lambda e, tt=tt, pa=pa, u=u: e.tensor_tensor(out=u, in0=ccT[:, tt, :], in1=pa, op=ALU.mult), reads=R(pk, "ccT%d" % tt, uk), writes=[uk])
                        mk.dve(lambda e, tt=tt, u=u, acc=acc: e.tensor_scalar(out=acc, in0=u, scalar1=ccol("cw", 4 + tt), scalar2=ccol("cb", tt), op0=ALU.mult, op1=ALU.add),
                               reads=R(uk, "cols", ak), writes=[ak])
                        mk.dve(lambda e, tt=tt, u=u, acc=acc: e.scalar_tensor_tensor(out=r3(acc)[:, :, 1:rl], in0=r3(u)[:, :, 0:rl - 1], scalar=ccol("cw", tt),
                                                                                    in1=r3(acc)[:, :, 1:rl], op0=ALU.mult, op1=ALU.add), reads=R(uk, ak, "cols"), writes=[ak])
                        mk.dve(lambda e, tt=tt, u=u, acc=acc: e.scalar_tensor_tensor(out=r3(acc)[:, :, 0:rl - 1], in0=r3(u)[:, :, 1:rl], scalar=ccol("cw", 8 + tt),
                                                                                    in1=r3(acc)[:, :, 0:rl - 1], op0=ALU.mult, op1=ALU.add), reads=R(uk, ak, "cols"), writes=[ak])
                        mk.dve(lambda e, tt=tt, acc=acc: e.tensor_tensor(out=ybT[:, tt, :], in0=cbT[:, tt, :], in1=acc, op=ALU.mult), reads=R(ak, "cbT%d" % tt), writes=["ybT%d" % tt])
                    else:
                        gi = (s - 7) * 4 + tt
                        sg = tA2[:, tb, :]
                        sk = "tA2%d" % tb
                        mk.act(lambda e, pa=pa, sg=sg: e.activation(out=sg, in_=pa, func=AF.Sigmoid), reads=R(pk, sk), writes=[sk])
                        if gi < 8:
                            mk.dve(lambda e, gi=gi, sg=sg: e.tensor_tensor(out=yA[:, gi, :], in0=yA[:, gi, :], in1=sg, op=ALU.mult), reads=R(sk, "yA%d" % gi), writes=["yA%d" % gi])
                        else:
                            g2 = gi - 8
                            mk.dve(lambda e, g2=g2, sg=sg: e.tensor_tensor(out=sg, in0=sg, in1=yB[:, g2, :], op=ALU.mult), reads=R(sk, "yB%d" % g2), writes=[sk])
                            mk.dve(lambda e, g2=g2, sg=sg: e.tensor_tensor(out=mgT[:, g2, :], in0=yA[:, g2, :], in1=sg, op=ALU.add), reads=R(sk, "yA%d" % g2), writes=["mgT%d" % g2])
                if s == 3:
                    barrier()
                    branch(wba, yaT, "yaT%d", yA, "yA%d")
                if s == 6:
                    branch(wbb, ybT, "ybT%d", yB, "yB%d")
            oT = pool.f32(8 * N).rearrange("p (k n) -> p k n", k=8)
            pool.off = 6144
            sq = pool.bf16(8 * N).rearrange("p (k n) -> p k n", k=8)
            tmp = pool.f32(2 * N).rearrange("p (k n) -> p k n", k=2)
            for half in range(2):
                sl, key = load_slab([(lambda t: t[:, 0:4096].rearrange("p (k n) -> p k n", k=8),
                                     wout[:, half * 512:(half + 1) * 512].rearrange("(k p) n -> p k n", p=128))])
                slv = sl[:, 0:4096].rearrange("p (k n) -> p k n", k=8)
                for tt in range(4):
                    i = half * 4 + tt
                    b = tt % 2
                    mm_group(psA[:, b, :], "psA%d" % b, [(slv[:, k, tt * 128:(tt + 1) * 128], mgT[:, k, :], [key, "mgT%d" % k, "SCR"]) for k in range(8)])
                    evac(oT[:, i, :], psA[:, b, :], R("psA%d" % b), ["oT"])
            tap("mo_" + kind, oT.rearrange("p q n -> p (q n)"), 8 * N, R("oT"))
            postnorm_residual(j, 3, oT, sq, tmp)

        stt = sb("stt", [128, 5, 4, 64])
        nsb = sb("nsb", [64, 2, 2, 4, 128])
        mk.dma("sp", "c0", lambda e: e.dma_start(out=stt[:], in_=std.rearrange("a q p v -> p a q v")), writes=["stt"])

        def load_x(g):
            pool = Pool()
            xin = pool.f32(4 * D).rearrange("p (t n) -> p t n", t=4)
            for tt in range(4):
                mk.dma("sp", "xin", lambda e, tt=tt: e.dma_start(out=xin[:, tt, :], in_=xg[g, tt * 128:(tt + 1) * 128, :]), reads=["SCR"], writes=["xin%d" % tt])
            for k in range(8):
                b = k % 2
                for tt in range(4):
                    mm_group(psA[:, b, tt * 128:(tt + 1) * 128], "psA%d" % b, [(xin[:, tt, k * 128:(k + 1) * 128], ident, ["xin%d" % tt, "cst", "SCR"])])
                evac(xT[:, k, :], psA[:, b, :], R("psA%d" % b), ["xT"])

        def store_y(gout):
            pool = Pool()
            yo = pool.f32(4 * D).rearrange("p (t n) -> p t n", t=4)
            for tt in range(4):
                for k in range(8):
                    b = k % 2
                    mm_group(psA[:, b, 0:128], "psA%d" % b, [(xT[:, k, tt * 128:(tt + 1) * 128], ident, ["xT", "cst", "SCR"])])
                    evac(yo[:, tt, k * 128:(k + 1) * 128], psA[:, b, 0:128], R("psA%d" % b), ["yo%d" % tt])
                mk.dma("sp", "yout", lambda e, tt=tt: e.dma_start(out=yout[gout * N + tt * 128:gout * N + (tt + 1) * 128, :], in_=yo[:, tt, :]), reads=R("yo%d" % tt))

        def ffn_phase(j, ai, bi, gi, w13, w2):
            pool = Pool()
            hT = pool.bf16(8 * N).rearrange("p (k n) -> p k n", k=8)
            sq = pool.bf16(8 * N).rearrange("p (k n) -> p k n", k=8)
            hid = pool.bf16(22 * N).rearrange("p (k n) -> p k n", k=22)
            oT = pool.f32(8 * N).rearrange("p (k n) -> p k n", k=8)
            tmp = pool.f32(2 * N).rearrange("p (k n) -> p k n", k=2)
            sgt = pool.f32(2 * N).rearrange("p (k n) -> p k n", k=2)
            prenorm(j, ai, bi, hT, sq, tmp)
            ffn(w13, w2, hT, hid, oT, sgt)
            postnorm_residual(j, gi, oT, sq, tmp)

        def mixer_phase(kind, j, aux_idx):
            pool = Pool()
            hT = pool.bf16(8 * N).rearrange("p (k n) -> p k n", k=8)
            tail = Pool()
            tail.off = SCR - (8 * N // 2 + 2 * N)
            sq = tail.bf16(8 * N).rearrange("p (k n) -> p k n", k=8)
            tmp = tail.f32(2 * N).rearrange("p (k n) -> p k n", k=2)
            prenorm(j, 2, 3, hT, sq, tmp)
            barrier()
            mixer(kind, j, hT, aux_idx)

        def chain_prep_aux(a):
            cc_ = COLS["coef"] + (a - 1)
            mk.dve(lambda e: e.tensor_scalar(out=hbA[:], in0=hlast[:, a - 1, :], scalar1=cols[:, cc_:cc_ + 1], scalar2=None, op0=ALU.mult),
                   reads=["hlast%d" % (a - 1), "cols"], writes=["hbA"])

        def chain_prep_own():
            c2 = COLS["coef"] + 2
            mk.dve(lambda e: e.tensor_scalar(out=hbF[:], in0=hlast[:, 0, :], scalar1=cols[:, c2:c2 + 1], scalar2=None, op0=ALU.mult),
                   reads=["hlast0", "cols"], writes=["hbF"])
            for a in (1, 2):
                mk.dve(lambda e, a=a: e.scalar_tensor_tensor(out=hbF[:], in0=hlast[:, a, :], scalar=cols[:, c2 + a:c2 + a + 1], in1=hbF[:], op0=ALU.mult, op1=ALU.add),
                       reads=["hlast%d" % a, "cols", "hbF"], writes=["hbF"])
            mk.dve(lambda e: e.tensor_scalar(out=hbB[:], in0=hlast[:, 2, :], scalar1=cols[:, c2 + 3:c2 + 4], scalar2=None, op0=ALU.mult),
                   reads=["hlast2", "cols"], writes=["hbB"])

        for a in range(3):
            if go():
                barrier()
                load_x(2 + a)
            if go():
                barrier()
                ffn_phase(1, 0, 0, 1, f1w13, f1w2)
            if go():
                barrier()
                if a > 0:
                    chain_prep_aux(a)
                mixer_phase("aux", 1, a)
        for (g, kind, j) in [(1, "own", 1), (0, "prompt", 0)]:
            if go():
                barrier()
                load_x(g)
            if go():
                barrier()
                ffn_phase(j, 0, 0, 1, f1w13, f1w2)
            if go():
                barrier()
                if kind == "own":
                    chain_prep_own()
                mixer_phase(kind, j, None)
            if go():
                barrier()
                ffn_phase(j, 4, 6, 5, f2w13, f2w2)
            if go():
                barrier()
                store_y(1 if kind == "own" else 0)
        if dbg_spec is not None:
            barrier()
            dbg_spec(mk, nc, locals())
        for sq_ in range(2):
            for dd in range(2):
                mk.dma("sp", "yout", lambda e, sq_=sq_, dd=dd: e.dma_start(out=nsout[sq_, dd].rearrange("(q e) v k -> v q e k", e=2),
                                                                           in_=nsb[:, sq_, dd, :, :].rearrange("p q (e k) -> p q e k", e=2)), reads=["nsb"])
        stats = mk.emit()
    return nc, stats


_CACHE = {}


def _prep_inputs(inp):
    f = lambda a: np.ascontiguousarray(np.asarray(a, np.float32))
    x_prompt, x_sample = f(inp["x_prompt"]), f(inp["x_sample"])
    c, state, c_ctx = f(inp["c"]), f(inp["state_rwkv"]), f(inp["c_ctx"])
    mu = f(inp["mu_shift"])[0]
    w1 = [np.concatenate([f(inp["decay_w1"])[0, d], f(inp["iclr_a1"])[0, d]], axis=1) for d in range(2)]
    w2 = [np.concatenate([f(inp["decay_w2"])[0, d], f(inp["iclr_a2"])[0, d]], axis=0) for d in range(2)]
    dw0, ia0 = f(inp["decay_w0"])[0], f(inp["iclr_a0"])[0]
    shared = {
        "w_mod": f(inp["w_mod"])[0], "f1w13": f(inp["ffn1_w13"])[0], "f1w2": f(inp["ffn1_w2"])[0],
        "f2w13": f(inp["ffn2_w13"])[0], "f2w2": f(inp["ffn2_w2"])[0], "w_in": f(inp["w_in"])[0],
        "wba": f(inp["w_branch_a"])[0], "wbb": f(inp["w_branch_b"])[0], "wout": f(inp["w_out"])[0],
        "consts": _make_consts(),
    }
    in_maps = []
    for core in range(8):
        b, s = core // 4, core % 4
        if s == 0:
            aux = [(3, 1), (2, 1), (1, 1)]
            cont = (1.0, 1.0)
            selF = (0.0, 0.0, 0.0)
            selB = 1.0
            init = ["F", "0", "B", "0", "0"]
        elif s == 1:
            aux = [(0, 0), (3, 1), (2, 1)]
            cont = (0.0, 1.0)
            selF = (1.0, 0.0, 0.0)
            selB = 1.0
            init = ["0", "0", "F", "B", "0"]
        elif s == 2:
            aux = [(0, 0), (1, 0), (3, 1)]
            cont = (1.0, 0.0)
            selF = (0.0, 1.0, 0.0)
            selB = 1.0
            init = ["0", "0", "F", "0", "B"]
        else:
            aux = [(0, 0), (1, 0), (2, 0)]
            cont = (1.0, 1.0)
            selF = (0.0, 0.0, 1.0)
            selB = 0.0
            init = ["0", "B", "F", "0", "0"]
        xgr = np.empty((5, N, D), np.float32)
        xgr[0] = x_prompt[2 * core:2 * core + 2].reshape(N, D)
        xgr[1] = x_sample[b, s * N:(s + 1) * N]
        for a, (seg, dr) in enumerate(aux):
            xs = x_sample[b, seg * N:(seg + 1) * N]
            xgr[2 + a] = xs[::-1] if dr == 1 else xs
        dsl = [0, 1] + [dr for (_, dr) in aux]
        cols = np.zeros((128, NCOL), np.float32)
        cvs = [c_ctx, c[b]]
        for k in range(8):
            for j in range(2):
                cols[:, COLS["cv"] + k * 2 + j] = cvs[j][k * 128:(k + 1) * 128]
        cols[:, COLS["bmod"]:COLS["bmod"] + 72] = _colize(inp["b_mod"][0])
        cols[:, COLS["ng"]:COLS["ng"] + 48] = _colize(np.asarray(inp["norm_g"][0]).reshape(-1))
        for ds, dr in enumerate(dsl):
            for x in range(2):
                cols[:, COLS["mu"] + ds * 16 + x * 8:COLS["mu"] + ds * 16 + x * 8 + 8] = _colize(mu[dr, x])
            cols[:, COLS["w0"] + ds * 4:COLS["w0"] + ds * 4 + 4] = _colize(dw0[dr])
            cols[:, COLS["a0"] + ds * 4:COLS["a0"] + ds * 4 + 4] = _colize(ia0[dr])
        cols[:, COLS["kk"]:COLS["kk"] + 4] = _colize(inp["k_k"][0])
        cols[:, COLS["ka"]:COLS["ka"] + 4] = _colize(inp["k_a"][0])
        cols[:, COLS["rk"]:COLS["rk"] + 4] = _colize(np.asarray(inp["r_k"][0]).reshape(-1))
        cols[:, COLS["gng"]:COLS["gng"] + 4] = _colize(inp["gn_gain"][0])
        cols[:, COLS["gnb"]:COLS["gnb"] + 4] = _colize(inp["gn_bias"][0])
        cols[:, COLS["cw"]:COLS["cw"] + 12] = _colize(np.asarray(inp["conv_w"][0]).reshape(-1))
        cols[:, COLS["cb"]:COLS["cb"] + 4] = _colize(inp["conv_b"][0])
        cols[:, COLS["coef"]:COLS["coef"] + 6] = np.array([cont[0], cont[1], selF[0], selF[1], selF[2], selB], np.float32)[None, :]
        cols[:, COLS["num"]:COLS["num"] + 6] = np.array([1e-6, 1e-12, 64e-5, 0.0, 1.0, 2.0], np.float32)[None, :]
        def mlay(d):
            S = state[b, 0, d]
            return np.ascontiguousarray(S.transpose(0, 2, 1).reshape(4, 128, 64))
        stv = np.zeros((5, 4, 128, 64), np.float32)
        for i, t in enumerate(init):
            if t == "F":
                stv[i] = mlay(0)
            elif t == "B":
                stv[i] = mlay(1)
        m = dict(shared)
        m.update({"xg": xgr, "cols": cols, "st": stv,
                  "w1c": np.ascontiguousarray(np.stack([w1[d] for d in dsl])),
                  "w2c": np.ascontiguousarray(np.stack([w2[d] for d in dsl]))})
        in_maps.append(m)
    return in_maps


def kernel(**inputs):
    if "nc" not in _CACHE:
        _CACHE["nc"], _CACHE["stats"] = build_program(int(os.environ.get("KLIMIT", str(10 ** 9))))
    nc = _CACHE["nc"]
    in_maps = _prep_inputs(inputs)
    res = run_bass_kernel_spmd(nc, in_maps, core_ids=list(range(8)))
    y_prompt = np.empty((16, 256, D), np.float32)
    y_sample = np.empty((2, 2048, D), np.float32)
    new_state = np.empty((16, 1, 2, 8, 64, 64), np.float32)
    for core in range(8):
        r = res.results[core]
        b, s = core // 4, core % 4
        y = np.asarray(r["y"], np.float32)
        y_prompt[2 * core:2 * core + 2] = y[0:N].reshape(2, 256, D)
        y_sample[b, s * N:(s + 1) * N] = y[N:2 * N]
        new_state[2 * core:2 * core + 2, 0] = np.asarray(r["ns"], np.float32)
    return (y_prompt, y_sample, new_state)
```

```python
import contextlib
import numpy as np
import concourse.bass as bass
import concourse.mybir as mybir
from concourse.bass_utils import run_bass_kernel_spmd

F32 = mybir.dt.float32
BF16 = mybir.dt.bfloat16
ALU = mybir.AluOpType
AF = mybir.ActivationFunctionType

D = 1024
DFF = 2816
import os
SUB = int(os.environ.get('KSUB', '99'))
KT = int(os.environ.get('KT', '1000000'))
KTC = [0]
MARK = {}
TAPS = [t for t in os.environ.get('KTAPS', '').split(',') if t]
KV = int(os.environ.get('KV', '0'))
N = 512
C = 64
NCH = N // C
C0 = float(np.exp(-0.5))


class _Op:
    __slots__ = ("idx", "eng", "fn", "deps", "chan", "chanpos", "needs_inc", "inc_count", "engpos")

    def __init__(self, idx, eng, fn, deps, chan):
        self.idx = idx
        self.eng = eng
        self.fn = fn
        self.deps = deps
        self.chan = chan
        self.chanpos = None
        self.needs_inc = False
        self.inc_count = None
        self.engpos = None


class MK:
    ENGS = ("pe", "act", "dve", "pool", "sp")

    def __init__(self, nc):
        self.nc = nc
        self.ops = []
        self.last_writer = {}
        self.readers = {}
        self.chan_count = {}

    def add(self, eng, fn, reads=(), writes=(), chan=None):
        idx = len(self.ops)
        deps = set()
        writes = list(writes)
        if chan is not None:
            writes.append(("__chan__", chan))
        for r in reads:
            w = self.last_writer.get(r)
            if w is not None:
                deps.add(w)
        for w in writes:
            lw = self.last_writer.get(w)
            if lw is not None:
                deps.add(lw)
            deps.update(self.readers.get(w, ()))
        op = _Op(idx, eng, fn, deps, chan)
        if chan is not None:
            op.chanpos = self.chan_count.get(chan, 0)
            self.chan_count[chan] = op.chanpos + 1
        self.ops.append(op)
        for r in reads:
            self.readers.setdefault(r, []).append(idx)
        for w in writes:
            self.last_writer[w] = idx
            self.readers[w] = []
        return idx

    def pe(self, fn, reads=(), writes=()):
        return self.add("pe", fn, reads, writes)

    def act(self, fn, reads=(), writes=()):
        return self.add("act", fn, reads, writes)

    def dve(self, fn, reads=(), writes=()):
        return self.add("dve", fn, reads, writes)

    def pool(self, fn, reads=(), writes=()):
        return self.add("pool", fn, reads, writes)

    def dma(self, eng, chan, fn, reads=(), writes=()):
        return self.add(eng, fn, reads, writes, chan=chan)

    def emit(self):
        nc = self.nc
        ops = self.ops
        per_eng = {e: [] for e in self.ENGS}
        for op in ops:
            op.engpos = len(per_eng[op.eng])
            per_eng[op.eng].append(op)

        def need_sem(op, d):
            if d.chan is not None:
                return True
            if d.eng != op.eng:
                return True
            if op.eng == "pe":
                return False
            return (op.engpos - d.engpos) <= 2

        for op in ops:
            latest = {}
            keep = set()
            for di in op.deps:
                d = ops[di]
                if d.chan is not None:
                    keep.add(di)
                else:
                    cur = latest.get(d.eng)
                    if cur is None or ops[cur].engpos < d.engpos:
                        latest[d.eng] = di
            keep.update(latest.values())
            op.deps = keep
        for op in ops:
            for di in op.deps:
                d = ops[di]
                if d.chan is None and need_sem(op, d):
                    d.needs_inc = True
        cnt = {e: 0 for e in self.ENGS}
        for op in ops:
            if op.chan is None and op.needs_inc:
                cnt[op.eng] += 1
                op.inc_count = cnt[op.eng]
        chans = sorted(self.chan_count.keys())
        with contextlib.ExitStack() as st:
            esem = {e: st.enter_context(nc.semaphore("s_" + e)) for e in self.ENGS}
            csem = {c: st.enter_context(nc.semaphore("c_" + str(c))) for c in chans}
            block = st.enter_context(nc.Block())

            def run_engine(ename, eobj):
                waited = {}

                def wait(key, sem, val):
                    if waited.get(key, 0) >= val:
                        return
                    waited[key] = val
                    eobj.wait_ge(sem, val)

                for op in per_eng[ename]:
                    for di in sorted(op.deps):
                        d = ops[di]
                        if not need_sem(op, d):
                            continue
                        if d.chan is not None:
                            wait(("c", d.chan), csem[d.chan], 16 * (d.chanpos + 1))
                        else:
                            wait(("e", d.eng), esem[d.eng], d.inc_count)
                    ins = op.fn(eobj)
                    if op.chan is not None:
                        ins.then_inc(csem[op.chan], 16)
                    elif op.needs_inc:
                        ins.then_inc(esem[op.eng], 1)
                if ename == "sp":
                    for c in chans:
                        wait(("c", c), csem[c], 16 * self.chan_count[c])

            @block.tensor
            def _(e):
                run_engine("pe", e)

            @block.scalar
            def _(e):
                run_engine("act", e)

            @block.vector
            def _(e):
                run_engine("dve", e)

            @block.gpsimd
            def _(e):
                run_engine("pool", e)

            @block.sync
            def _(e):
                run_engine("sp", e)
        return {e: len(v) for e, v in per_eng.items()}


def _colize(v):
    v = np.asarray(v, np.float32).reshape(-1, 128)
    return np.ascontiguousarray(v.T)


COLS = {}
_off = 0
for _name, _w in [("cv", 16), ("bmod", 72), ("ng", 48), ("mu", 80), ("w0", 20), ("a0", 20), ("kk", 4), ("ka", 4),
                  ("rk", 4), ("gng", 4), ("gnb", 4), ("cw", 12), ("cb", 4), ("coef", 8), ("num", 8)]:
    COLS[_name] = _off
    _off += _w
NCOL = _off

CONSTS = {}
_off = 0
for _name, _w in [("ident", 128), ("bones", 128), ("maskG", 512), ("maskZ", 128), ("cmask", 512), ("id64", 64), ("bones64", 128)]:
    CONSTS[_name] = _off
    _off += _w
NCONST = _off


def _make_consts():
    cst = np.zeros((128, NCONST), np.float32)
    cst[:, CONSTS["ident"]:CONSTS["ident"] + 128] = np.eye(128, dtype=np.float32)
    bo = np.zeros((128, 128), np.float32)
    bo[:64, :64] = 1
    bo[64:, 64:] = 1
    cst[:, CONSTS["bones"]:CONSTS["bones"] + 128] = bo
    cst[:, CONSTS["bones64"]:CONSTS["bones64"] + 128] = bo / 64.0
    s = (np.arange(128) % 64)[:, None]
    t = np.arange(64)[None, :]
    mg = np.zeros((128, 2, 256), np.float32)
    for blk in range(2):
        mg[:, 0, blk * 128:blk * 128 + 64] = (t > s)
        mg[:, 0, blk * 128 + 64:blk * 128 + 128] = (t >= s)
        mg[:, 1, blk * 128:blk * 128 + 64] = (t < s)
        mg[:, 1, blk * 128 + 64:blk * 128 + 128] = (t <= s)
    cst[:, CONSTS["maskG"]:CONSTS["maskG"] + 512] = mg.reshape(128, 512)
    mz = np.zeros((128, 2, 64), np.float32)
    mz[:, 0, :] = (t < s)
    mz[:, 1, :] = (t > s)
    cst[:, CONSTS["maskZ"]:CONSTS["maskZ"] + 128] = mz.reshape(128, 128)
    cm = np.ones((128, 512), np.float32)
    cm[:, ::64] = 0
    cst[:, CONSTS["cmask"]:CONSTS["cmask"] + 512] = cm
    i64 = np.zeros((128, 64), np.float32)
    i64[np.arange(128), np.arange(128) % 64] = 1
    cst[:, CONSTS["id64"]:CONSTS["id64"] + 64] = i64
    return cst


def build_program(limit=10 ** 9, dbg_spec=None, mlimit=10 ** 9):
    nc = bass.Bass("TRN2", target_bir_lowering=False)
    stage = [0]

    def go():
        stage[0] += 1
        return stage[0] <= limit
    mstage = [0]

    def mgo():
        mstage[0] += 1
        return mstage[0] <= mlimit

    def din(name, shape):
        return nc.dram_tensor(name, list(shape), F32, kind="ExternalInput").ap()

    xg = din("xg", [5, N, D])
    colsd = din("cols", [128, NCOL])
    cstd = din("consts", [128, NCONST])
    std = din("st", [5, 4, 128, 64])
    w_mod = din("w_mod", [D, 9 * D])
    f1w13 = din("f1w13", [D, 2 * DFF])
    f1w2 = din("f1w2", [DFF, D])
    f2w13 = din("f2w13", [D, 2 * DFF])
    f2w2 = din("f2w2", [DFF, D])
    w_in = din("w_in", [D, 5632])
    w1c = din("w1c", [5, D, 128])
    w2c = din("w2c", [5, 128, 512])
    wba = din("wba", [512, D])
    wbb = din("wbb", [512, D])
    wout = din("wout", [D, D])
    yout = nc.dram_tensor("y", [2 * N, D], F32, kind="ExternalOutput").ap()
    nsout = nc.dram_tensor("ns", [2, 2, 8, 64, 64], F32, kind="ExternalOutput").ap()

    with contextlib.ExitStack() as stk:
        def sb(name, shape, dt=F32):
            return stk.enter_context(nc.sbuf_tensor(name, list(shape), dt))

        def ps(name, shape):
            return stk.enter_context(nc.psum_tensor(name, list(shape), F32))

        mk = MK(nc)
        cols = sb("cols_t", [128, NCOL])
        cst = sb("cst_t", [128, NCONST])
        onesb = sb("onesb", [128, 128], BF16)
        identb = sb("identb", [128, 128], BF16)
        scb = sb("scb", [128, 8, 2], BF16)
        modT = sb("modT", [128, 72, 2])
        mods = sb("mods", [128, 2, 6, 8])
        xT = sb("xT", [128, 8, N])
        rstd = sb("rstd", [128, N])
        lnt = sb("lnt", [128, N])
        bdum = sb("bdum", [128, 1])
        NSLOT = 3
        slots = [sb("slot%d" % i, [128, 4096], BF16) for i in range(NSLOT)]
        hlast = sb("hlast", [128, 3, 8])
        hbF = sb("hbF", [128, 8])
        hbB = sb("hbB", [128, 8])
        hbA = sb("hbA", [128, 8])
        endst = sb("endst", [128, 3, 4, 64])
        Mst = sb("Mst", [128, 4, 64])
        Mtmp = sb("Mtmp", [128, 4, 64])
        Mbf = sb("Mbf", [128, 4, 64], BF16)
        SCR = 30720
        scr = sb("scr", [128, SCR])

        class Pool:
            def __init__(self):
                self.off = 0

            def f32(self, n):
                a = scr[:, self.off:self.off + n]
                self.off += n
                assert self.off <= SCR, self.off
                return a

            def bf16(self, n):
                m = (n + 1) // 2
                a = scr[:, self.off:self.off + m].bitcast(BF16)
                self.off += m
                assert self.off <= SCR, self.off
                return a

        psA = ps("psA", [128, 2, 512])
        psB = ps("psB", [128, 2, 512])
        psS = ps("psS", [128, 512])
        psC = ps("psC", [128, 2, 512])
        psT = ps("psT", [128, 512])

        def tap(name, ap2d, width, keys):
            if name not in TAPS:
                return
            dt_ = nc.dram_tensor("dbg_" + name, [128, width], F32, kind="ExternalOutput").ap()
            mk.dma("sp", "yout", lambda e: e.dma_start(out=dt_, in_=ap2d), reads=keys)

        cnum = lambda i: cols[:, COLS["num"] + i:COLS["num"] + i + 1]
        ccol = lambda name, i: cols[:, COLS[name] + i:COLS[name] + i + 1]
        ident = cst[:, CONSTS["ident"]:CONSTS["ident"] + 128]
        bones = cst[:, CONSTS["bones"]:CONSTS["bones"] + 128]
        bones64 = cst[:, CONSTS["bones64"]:CONSTS["bones64"] + 128]
        id64 = cst[:, CONSTS["id64"]:CONSTS["id64"] + 64]
        cmask = cst[:, CONSTS["cmask"]:CONSTS["cmask"] + 512]

        def maskG(d):
            o = CONSTS["maskG"] + d * 256
            return cst[:, o:o + 256]

        def maskZ(d):
            o = CONSTS["maskZ"] + d * 64
            return cst[:, o:o + 64]

        evac_rr = [0]

        def evac(out, in_, reads, writes):
            evac_rr[0] ^= 1
            if evac_rr[0]:
                mk.act(lambda e: e.copy(out=out, in_=in_), reads=reads, writes=writes)
            else:
                mk.dve(lambda e: e.tensor_copy(out=out, in_=in_), reads=reads, writes=writes)

        slot_rr = [0]

        def load_slab(pieces):
            s = slot_rr[0] % NSLOT
            slot_rr[0] += 1
            sl = slots[s]
            key = "W%d" % s
            for i, (vf, dap) in enumerate(pieces):
                mk.dma("pool", "w%d" % s, lambda e, vf=vf, dap=dap: e.dma_start(out=vf(sl), in_=dap), writes=[key])
            return sl, key

        def mm_group(out_ap, out_key, terms):
            n = len(terms)
            for i, (l, r, keys) in enumerate(terms):
                mk.pe(lambda e, l=l, r=r, i=i: e.matmul(out_ap, lhsT=l, rhs=r, start=(i == 0), stop=(i == n - 1)),
                      reads=keys, writes=(out_key if isinstance(out_key, list) else [out_key]))

        barrier_n = [0]

        def barrier():
            barrier_n[0] += 1
            mk.dve(lambda e: e.memset(bdum[:], 0.0), reads=["bdum"], writes=["SCR"])

        R = lambda *k: ["SCR"] + list(k)

        mk.dma("sp", "c0", lambda e: e.dma_start(out=cols[:], in_=colsd), writes=["cols"])
        mk.dma("sp", "c0", lambda e: e.dma_start(out=cst[:], in_=cstd), writes=["cst"])
        mk.dve(lambda e: e.memset(onesb[:], 1.0 / 1024.0), writes=["onesb"])
        mk.dve(lambda e: e.tensor_copy(out=identb[:], in_=cst[:, CONSTS["ident"]:CONSTS["ident"] + 128]), reads=["cst"], writes=["cst2"])
        cv0 = COLS["cv"]
        mk.act(lambda e: e.activation(out=scb[:].rearrange("p k j -> p (k j)"), in_=cols[:, cv0:cv0 + 16], func=AF.Silu),
               reads=["cols"], writes=["scb"])
        for s in range(18):
            sl, key = load_slab([(lambda t: t[:, 0:4096].rearrange("p (k n) -> p k n", k=8),
                                 w_mod[:, s * 512:(s + 1) * 512].rearrange("(k p) n -> p k n", p=128))])
            slv = sl[:, 0:4096].rearrange("p (k n) -> p k n", k=8)
            for tt in range(4):
                mm_group(psS[:, tt * 2:tt * 2 + 2], "psS",
                         [(slv[:, k, tt * 128:(tt + 1) * 128], scb[:, k, :], [key, "scb"]) for k in range(8)])
            for j in range(2):
                b0 = COLS["bmod"] + s * 4
                mk.dve(lambda e, s=s, j=j, b0=b0: e.tensor_tensor(
                    out=modT[:, s * 4:s * 4 + 4, j], in0=psS[:, 0:8].rearrange("p (t j) -> p t j", j=2)[:, :, j],
                    in1=cols[:, b0:b0 + 4], op=ALU.add), reads=["psS", "cols"], writes=["modT"])
        ng = lambda i: cols[:, COLS["ng"] + i * 8:COLS["ng"] + i * 8 + 8]
        m_ = lambda i, j: modT[:, i * 8:(i + 1) * 8, j]
        for j in range(2):
            for (dst, mi, gi, half) in [(0, 1, 0, None), (1, 2, 1, 0.5), (2, 4, 2, None), (3, 5, 3, 1.0), (4, 7, 4, None), (5, 8, 5, 0.5)]:
                if half is None:
                    mk.dve(lambda e, j=j, dst=dst, mi=mi, gi=gi: e.scalar_tensor_tensor(
                        out=mods[:, j, dst, :], in0=m_(mi, j), scalar=1.0, in1=ng(gi), op0=ALU.add, op1=ALU.mult),
                        reads=["modT", "cols"], writes=["mods"])
                else:
                    mk.dve(lambda e, j=j, dst=dst, mi=mi, gi=gi, half=half: e.scalar_tensor_tensor(
                        out=mods[:, j, dst, :], in0=m_(mi, j), scalar=half, in1=ng(gi), op0=ALU.mult, op1=ALU.mult),
                        reads=["modT", "cols"], writes=["mods"])

        def rms_rstd(src, src_key, sq, eps_idx):
            for k in range(8):
                if k % 2 == 0:
                    mk.act(lambda e, k=k: e.activation(out=sq[:, k, :], in_=src[:, k, :], func=AF.Square),
                           reads=R(src_key), writes=["sq%d" % k])
                else:
                    mk.dve(lambda e, k=k: e.tensor_tensor(out=sq[:, k, :], in0=src[:, k, :], in1=src[:, k, :], op=ALU.mult),
                           reads=R(src_key), writes=["sq%d" % k])
            mm_group(psS[:], "psS", [(onesb[:], sq[:, k, :], ["onesb", "sq%d" % k, "SCR"]) for k in range(8)])
            mk.act(lambda e: e.activation(out=lnt[:], in_=psS[:], func=AF.Ln, bias=cnum(eps_idx), scale=1.0),
                   reads=["psS", "cols"], writes=["lnt"])
            mk.act(lambda e: e.activation(out=rstd[:], in_=lnt[:], func=AF.Exp, scale=-0.5), reads=["lnt"], writes=["rstd"])

        def prenorm(j, ai, bi, hT, sq, tmp):
            rms_rstd(xT, "xT", sq, 0)
            for k in range(8):
                mk.dve(lambda e, k=k: e.scalar_tensor_tensor(out=tmp[:, k % 2, :], in0=xT[:, k, :], scalar=mods[:, j, ai, k:k + 1],
                                                             in1=rstd[:], op0=ALU.mult, op1=ALU.mult),
                       reads=R("xT", "mods", "rstd"), writes=["ptmp%d" % (k % 2)])
                mk.act(lambda e, k=k: e.activation(out=hT[:, k, :], in_=tmp[:, k % 2, :], func=AF.Identity,
                                                   bias=modT[:, bi * 8 + k, j:j + 1], scale=1.0),
                       reads=R("ptmp%d" % (k % 2), "modT"), writes=["hT%d" % k])

        def postnorm_residual(j, gi, oT, sq, tmp):
            rms_rstd(oT, "oT", sq, 0)
            for k in range(8):
                mk.dve(lambda e, k=k: e.scalar_tensor_tensor(out=tmp[:, k % 2, :], in0=oT[:, k, :], scalar=mods[:, j, gi, k:k + 1],
                                                             in1=rstd[:], op0=ALU.mult, op1=ALU.mult),
                       reads=R("oT", "mods", "rstd"), writes=["ptmp%d" % (k % 2)])
                mk.dve(lambda e, k=k: e.tensor_tensor(out=xT[:, k, :], in0=xT[:, k, :], in1=tmp[:, k % 2, :], op=ALU.add),
                       reads=R("ptmp%d" % (k % 2), "xT"), writes=["xT"])

        def ffn(w13, w2, hT, hid, oT, sgt):
            for s in range(11):
                sl, key = load_slab([
                    (lambda t: t[:, 0:4096].rearrange("p (k n) -> p k n", k=8)[:, :, 0:256],
                     w13[:, s * 256:(s + 1) * 256].rearrange("(k p) n -> p k n", p=128)),
                    (lambda t: t[:, 0:4096].rearrange("p (k n) -> p k n", k=8)[:, :, 256:512],
                     w13[:, DFF + s * 256:DFF + (s + 1) * 256].rearrange("(k p) n -> p k n", p=128))])
                slv = sl[:, 0:4096].rearrange("p (k n) -> p k n", k=8)
                for jj in range(2):
                    jt = 2 * s + jj
                    b = jt % 2
                    mm_group(psA[:, b, :], "psA%d" % b,
                             [(slv[:, k, jj * 128:(jj + 1) * 128], hT[:, k, :], [key, "hT%d" % k, "SCR"]) for k in range(8)])
                    mm_group(psB[:, b, :], KB(b),
                             [(slv[:, k, 256 + jj * 128:256 + (jj + 1) * 128], hT[:, k, :], [key, "hT%d" % k, "SCR"]) for k in range(8)])
                    mk.act(lambda e, b=b: e.activation(out=sgt[:, b, :], in_=psA[:, b, :], func=AF.Silu),
                           reads=R("psA%d" % b), writes=["sgt%d" % b])
                    mk.dve(lambda e, b=b, jt=jt: e.tensor_tensor(out=hid[:, jt, :], in0=sgt[:, b, :], in1=psB[:, b, :], op=ALU.mult),
                           reads=R("sgt%d" % b, *KB(b)), writes=["hid%d" % jt])
            for i in range(8):
                sl, key = load_slab([(lambda t: t[:, 0:2816].rearrange("p (j n) -> p j n", j=22),
                                     w2[:, i * 128:(i + 1) * 128].rearrange("(j p) n -> p j n", p=128))])
                slv = sl[:, 0:2816].rearrange("p (j n) -> p j n", j=22)
                b = i % 2
                mm_group(psA[:, b, :], "psA%d" % b, [(slv[:, jt, :], hid[:, jt, :], [key, "hid%d" % jt, "SCR"]) for jt in range(22)])
                evac(oT[:, i, :], psA[:, b, :], R("psA%d" % b), ["oT"])

        def KB(b):
            return ["psB0", "psB0b"] if b == 0 else ["psB1"]

        def mixer(kind, j, hT, aux_idx):
            pool = Pool()
            pool.off = 2048
            full = kind != "aux"
            nseq = 2 if kind == "prompt" else 1
            L = N // nseq
            cps = NCH // nseq
            rkv = pool.f32(12 * N).rearrange("p (t n) -> p t n", t=12)
            kk = pool.f32(4 * N).rearrange("p (t n) -> p t n", t=4)
            yT = pool.f32(4 * N).rearrange("p (t n) -> p t n", t=4)
            asum = pool.f32(4 * N).rearrange("p (t n) -> p t n", t=4)
            lh = pool.bf16(2 * N).rearrange("p (d n) -> p d n", d=2)
            w2b = pool.bf16(2 * 512).rearrange("p (d n) -> p d n", d=2)
            vbf = pool.bf16(4 * N).rearrange("p (t n) -> p t n", t=4)
            base_d = pool.off
            w1b = pool.bf16(2 * 1024).rearrange("p (d k n) -> p d k n", d=2, k=8)
            w1s = pool.bf16(2 * 1024).rearrange("p (d k n) -> p d k n", d=2, k=8)
            hk = ["hT%d" % k for k in range(8)]
            for s in range(3):
                sl, key = load_slab([(lambda t: t[:, 0:4096].rearrange("p (k n) -> p k n", k=8),
                                     w_in[:, s * 512:(s + 1) * 512].rearrange("(k p) n -> p k n", p=128))])
                slv = sl[:, 0:4096].rearrange("p (k n) -> p k n", k=8)
                for tt in range(4):
                    b = tt % 2
                    mm_group(psA[:, b, :], "psA%d" % b,
                             [(slv[:, k, tt * 128:(tt + 1) * 128], hT[:, k, :], [key, "hT%d" % k, "SCR"]) for k in range(8)])
                    evac(rkv[:, s * 4 + tt, :], psA[:, b, :], R("psA%d" % b), ["rkv%d" % (s * 4 + tt)])
            for pr in range(4):
                mk.act(lambda e, pr=pr: e.copy(out=vbf[:, pr, :], in_=rkv[:, 8 + pr, :]), reads=R("rkv%d" % (8 + pr)), writes=["vbf%d" % pr])
            if not mgo():
                return
            sqk = pool.f32(2 * N).rearrange("p (b n) -> p b n", b=2)
            for pr in range(4):
                b = pr % 2
                mk.dve(lambda e, pr=pr: e.tensor_scalar(out=kk[:, pr, :], in0=rkv[:, 4 + pr, :], scalar1=ccol("kk", pr), scalar2=None,
                                                        op0=ALU.mult), reads=R("rkv%d" % (4 + pr), "cols"), writes=["kk%d" % pr])
                mk.dve(lambda e, pr=pr, b=b: e.tensor_tensor(out=sqk[:, b, :], in0=kk[:, pr, :], in1=kk[:, pr, :], op=ALU.mult),
                       reads=R("kk%d" % pr), writes=["sqk%d" % b])
                mm_group(psB[:, b, :], KB(b), [(bones, sqk[:, b, :], ["cst", "sqk%d" % b, "SCR"])])
                mk.act(lambda e, b=b: e.activation(out=sqk[:, b, :], in_=psB[:, b, :], func=AF.Ln, bias=cnum(1), scale=1.0),
                       reads=R("cols", *KB(b)), writes=["sqk%d" % b])
                mk.act(lambda e, b=b: e.activation(out=sqk[:, b, :], in_=sqk[:, b, :], func=AF.Exp, scale=-0.5),
                       reads=R("sqk%d" % b), writes=["sqk%d" % b])
                mk.dve(lambda e, pr=pr, b=b: e.tensor_tensor(out=kk[:, pr, :], in0=kk[:, pr, :], in1=sqk[:, b, :], op=ALU.mult),
                       reads=R("sqk%d" % b, "kk%d" % pr), writes=["kk%d" % pr])
            if not mgo():
                return
            dslots = [(0, 0), (1, 1)] if full else [(0, 2 + aux_idx)]
            sh = pool.bf16(8 * N).rearrange("p (k n) -> p k n", k=8)
            for di, (dt_, ds) in enumerate(dslots):
                mk.dma("pool", "wl", lambda e, di=di, ds=ds: e.dma_start(out=w1b[:, di, :, :], in_=w1c[ds].rearrange("(k p) n -> p k n", p=128)),
                       reads=["SCR"], writes=["w1b%d" % di])
                mk.dma("pool", "wl", lambda e, di=di, ds=ds: e.dma_start(out=w2b[:, di, :], in_=w2c[ds]), reads=["SCR"], writes=["w2b%d" % di])
                for x in range(2):
                    for k in range(8):
                        mc = COLS["mu"] + ds * 16 + x * 8 + k
                        mk.dve(lambda e, di=di, x=x, k=k, mc=mc: e.tensor_scalar(
                            out=w1s[:, di, k, x * 64:(x + 1) * 64], in0=w1b[:, di, k, x * 64:(x + 1) * 64],
                            scalar1=cols[:, mc:mc + 1], scalar2=None, op0=ALU.mult),
                            reads=R("w1b%d" % di, "cols"), writes=["w1s%d" % di])
                if dt_ == 0:
                    mk.dve(lambda e: e.tensor_tensor(out=sh[:, :, 1:N], in0=hT[:, :, 0:N - 1], in1=hT[:, :, 1:N], op=ALU.subtract),
                           reads=R(*hk), writes=["sh"])
                    for sq_ in range(nseq):
                        col = sq_ * L
                        if kind == "prompt" or (kind == "aux" and aux_idx == 0):
                            mk.dve(lambda e, col=col: e.tensor_scalar(out=sh[:, :, col], in0=hT[:, :, col], scalar1=-1.0, scalar2=None,
                                                                      op0=ALU.mult), reads=R("sh", *hk), writes=["sh"])
                        else:
                            hb = hbF if kind == "own" else hbA
                            hbk = "hbF" if kind == "own" else "hbA"
                            mk.dve(lambda e, col=col, hb=hb: e.tensor_tensor(out=sh[:, :, col], in0=hb[:, :], in1=hT[:, :, col], op=ALU.subtract),
                                   reads=R("sh", hbk, *hk), writes=["sh"])
                else:
                    mk.dve(lambda e: e.tensor_tensor(out=sh[:, :, 0:N - 1], in0=hT[:, :, 1:N], in1=hT[:, :, 0:N - 1], op=ALU.subtract),
                           reads=R(*hk), writes=["sh"])
                    for sq_ in range(nseq):
                        col = sq_ * L + L - 1
                        if kind == "prompt":
                            mk.dve(lambda e, col=col: e.tensor_scalar(out=sh[:, :, col], in0=hT[:, :, col], scalar1=-1.0, scalar2=None,
                                                                      op0=ALU.mult), reads=R("sh", *hk), writes=["sh"])
                        else:
                            mk.dve(lambda e, col=col: e.tensor_tensor(out=sh[:, :, col], in0=hbB[:, :], in1=hT[:, :, col], op=ALU.subtract),
                                   reads=R("sh", "hbB", *hk), writes=["sh"])
                b = di % 2
                mm_group(psB[:, b, :], KB(b),
                         [(w1b[:, di, k, :], hT[:, k, :], ["w1b%d" % di, "hT%d" % k, "SCR"]) for k in range(8)] +
                         [(w1s[:, di, k, :], sh[:, k, :], ["w1s%d" % di, "sh", "SCR"]) for k in range(8)])
                mk.act(lambda e, di=di, b=b: e.activation(out=lh[0:64, di, :], in_=psB[0:64, b, :], func=AF.Tanh),
                       reads=R(*KB(b)), writes=["lh%d" % di])
                mk.act(lambda e, di=di, b=b: e.copy(out=lh[64:128, di, :], in_=psB[64:128, b, :]),
                       reads=R(*KB(b)), writes=["lh%d" % di])
            if kind == "aux":
                mk.dve(lambda e: e.tensor_copy(out=hlast[:, aux_idx, :], in_=hT[:, :, N - 1]), reads=R(*hk), writes=["hlast%d" % aux_idx])

            if not mgo():
                return
            v3 = lambda ap: ap.rearrange("p (c n) -> p c n", c=8)
            psG = psA[:].rearrange("p a (h n) -> p (a h) n", h=2)
            psZ = psB[:].rearrange("p a (h n) -> p (a h) n", h=8)
            psZv = lambda a, hh: psZ[:, a * 4 + hh, :]
            ZK = ["psB0", "psB0b", "psB1"]
            psTv = psT[:].rearrange("p (a h n) -> p a h n", a=2, h=4)
            psCv = psC[:].rearrange("p a (h n) -> p a h n", h=8)
            psSv = psS[:].rearrange("p (h n) -> p h n", h=8)
            for di, (dt_, ds) in enumerate(dslots):
                order = list(range(NCH)) if dt_ == 0 else list(range(NCH - 1, -1, -1))
                barrier()
                pool.off = base_d
                AR = pool.bf16(4 * 8 * 128).rearrange("p (q c n) -> p q c n", q=4, c=8)
                BK = pool.bf16(4 * 8 * 128).rearrange("p (q c n) -> p q c n", q=4, c=8)
                Pend = pool.f32(32).rearrange("p (q c) -> p q c", q=4)
                base_t = pool.off
                sw = pool.f32(2 * N).rearrange("p (q n) -> p q n", q=2)
                av = pool.f32(2 * N).rearrange("p (q n) -> p q n", q=2)
                cs = pool.f32(2 * N).rearrange("p (q n) -> p q n", q=2)
                Lx = pool.f32(2 * N).rearrange("p (q n) -> p q n", q=2)
                Ep = pool.f32(2 * N).rearrange("p (q n) -> p q n", q=2)
                t1 = pool.f32(2 * N).rearrange("p (q n) -> p q n", q=2)
                for hp in range(2):
                    for ql in range(2):
                        pr = 2 * hp + ql
                        b = ql
                        mm_group(psA[:, b, :], "psA%d" % b, [(w2b[0:64, di, pr * 128:(pr + 1) * 128], lh[0:64, di, :], ["w2b%d" % di, "lh%d" % di, "SCR"])])
                        mm_group(psB[:, b, :], KB(b), [(w2b[64:128, di, pr * 128:(pr + 1) * 128], lh[64:128, di, :], ["w2b%d" % di, "lh%d" % di, "SCR"])])
                        mk.act(lambda e, pr=pr, ql=ql, b=b, ds=ds: e.activation(out=sw[:, ql, :], in_=psA[:, b, :], func=AF.Sigmoid,
                                                                               bias=ccol("w0", ds * 4 + pr), scale=1.0),
                               reads=R("psA%d" % b, "cols"), writes=["sw%d" % ql])
                        mk.act(lambda e, pr=pr, ql=ql, b=b, ds=ds: e.activation(out=av[:, ql, :], in_=psB[:, b, :], func=AF.Sigmoid,
                                                                               bias=ccol("a0", ds * 4 + pr), scale=1.0),
                               reads=R("cols", *KB(b)), writes=["av%d" % ql])
                        if full:
                            if di == 0:
                                mk.dve(lambda e, pr=pr, ql=ql: e.tensor_copy(out=asum[:, pr, :], in_=av[:, ql, :]), reads=R("av%d" % ql), writes=["asum%d" % pr])
                            else:
                                mk.dve(lambda e, pr=pr, ql=ql: e.tensor_tensor(out=asum[:, pr, :], in0=asum[:, pr, :], in1=av[:, ql, :], op=ALU.add),
                                       reads=R("av%d" % ql, "asum%d" % pr), writes=["asum%d" % pr])
                        mk.dve(lambda e, ql=ql: e.tensor_tensor_scan(out=cs[:, ql, :], data0=cmask, data1=sw[:, ql, :], initial=0.0,
                                                                     op0=ALU.mult, op1=ALU.add), reads=R("sw%d" % ql, "cst"), writes=["cs%d" % ql])
                        if dt_ == 0:
                            mk.dve(lambda e, ql=ql: e.tensor_tensor(out=Lx[:, ql, :], in0=cs[:, ql, :], in1=sw[:, ql, :], op=ALU.subtract),
                                   reads=R("cs%d" % ql, "sw%d" % ql), writes=["Lx%d" % ql])
                        else:
                            mk.dve(lambda e, ql=ql: e.tensor_tensor(out=v3(Lx[:, ql, :]), in0=v3(cs[:, ql, :])[:, :, 63:64].to_broadcast([128, 8, 64]),
                                                                    in1=v3(cs[:, ql, :]), op=ALU.subtract),
                                   reads=R("cs%d" % ql), writes=["Lx%d" % ql])
                            mk.dve(lambda e, ql=ql: e.tensor_tensor(out=cs[:, ql, :], in0=Lx[:, ql, :], in1=sw[:, ql, :], op=ALU.add),
                                   reads=R("Lx%d" % ql, "sw%d" % ql), writes=["cs%d" % ql])
                        mk.act(lambda e, ql=ql: e.activation(out=Ep[:, ql, :], in_=cs[:, ql, :], func=AF.Exp, scale=-C0), reads=R("cs%d" % ql), writes=["Ep%d" % ql])
                        mk.act(lambda e, ql=ql: e.activation(out=Lx[:, ql, :], in_=Lx[:, ql, :], func=AF.Exp, scale=-C0), reads=R("Lx%d" % ql), writes=["Lx%d" % ql])
                        mk.act(lambda e, ql=ql: e.activation(out=cs[:, ql, :], in_=cs[:, ql, :], func=AF.Exp, scale=C0), reads=R("cs%d" % ql, "Ep%d" % ql), writes=["cs%d" % ql])
                        pcol = 63 if dt_ == 0 else 0
                        mk.dve(lambda e, pr=pr, ql=ql, pcol=pcol: e.tensor_copy(out=Pend[:, pr, :], in_=v3(Ep[:, ql, :])[:, :, pcol]), reads=R("Ep%d" % ql), writes=["Pend"])
                        mk.dve(lambda e, pr=pr, ql=ql: e.scalar_tensor_tensor(out=AR[:, pr, :, 0:64], in0=v3(kk[:, pr, :]), scalar=-1.0, in1=v3(Lx[:, ql, :]),
                                                                              op0=ALU.mult, op1=ALU.mult), reads=R("kk%d" % pr, "Lx%d" % ql), writes=["AR%d" % pr])
                        mk.dve(lambda e, pr=pr, ql=ql: e.tensor_tensor(out=AR[:, pr, :, 64:128], in0=v3(rkv[:, pr, :]), in1=v3(Ep[:, ql, :]), op=ALU.mult),
                               reads=R("rkv%d" % pr, "Ep%d" % ql), writes=["AR%d" % pr])
                        mk.dve(lambda e, pr=pr, ql=ql: e.tensor_tensor(out=t1[:, ql, :], in0=kk[:, pr, :], in1=av[:, ql, :], op=ALU.mult),
                               reads=R("kk%d" % pr, "av%d" % ql), writes=["t1%d" % ql])
                        mk.dve(lambda e, pr=pr, ql=ql: e.tensor_tensor(out=BK[:, pr, :, 0:64], in0=v3(t1[:, ql, :]), in1=v3(cs[:, ql, :]), op=ALU.mult),
                               reads=R("t1%d" % ql, "cs%d" % ql), writes=["BK%d" % pr])
                        mk.dve(lambda e, pr=pr, ql=ql: e.tensor_scalar(out=t1[:, ql, :], in0=av[:, ql, :], scalar1=cnum(4), scalar2=ccol("ka", pr),
                                                                       op0=ALU.subtract, op1=ALU.mult), reads=R("av%d" % ql, "cols", "t1%d" % ql), writes=["t1%d" % ql])
                        mk.dve(lambda e, pr=pr, ql=ql: e.scalar_tensor_tensor(out=t1[:, ql, :], in0=t1[:, ql, :], scalar=1.0, in1=rkv[:, 4 + pr, :],
                                                                              op0=ALU.add, op1=ALU.mult), reads=R("t1%d" % ql, "rkv%d" % (4 + pr)), writes=["t1%d" % ql])
                        mk.dve(lambda e, pr=pr, ql=ql: e.tensor_tensor(out=BK[:, pr, :, 64:128], in0=v3(t1[:, ql, :]), in1=v3(cs[:, ql, :]), op=ALU.mult),
                               reads=R("t1%d" % ql, "cs%d" % ql), writes=["BK%d" % pr])
                if not mgo():
                    return
                barrier()
                pool.off = base_t
                Gm = [pool.bf16(4 * 256).rearrange("p (h n) -> p h n", h=4) for _ in range(2)]
                ZQb = [[pool.bf16(4 * 2 * 64).rearrange("p (h a n) -> p h a n", h=4, a=2) for _ in range(2)] for _ in range(2)]
                ZTb = [[pool.bf16(4 * 64).rearrange("p (h n) -> p h n", h=4) for _ in range(2)] for _ in range(2)]
                Qt = [pool.bf16(4 * 64).rearrange("p (h n) -> p h n", h=4) for _ in range(2)]
                TOK = [pool.bf16(3 * 4 * 64).rearrange("p (a h n) -> p a h n", a=3, h=4) for _ in range(2)]
                Wsb = pool.bf16(4 * 64).rearrange("p (h n) -> p h n", h=4)
                Usb = pool.bf16(4 * 64).rearrange("p (h n) -> p h n", h=4)
                Ytmp = pool.f32(4 * 64).rearrange("p (h n) -> p h n", h=4)
                unit = [0]

                def heads():
                    for q in range(4):
                        for e_ in range(2):
                            yield q, 64 * e_

                def tseries_stages(c, dt_=dt_):
                    u = unit[0] % 2
                    unit[0] += 1
                    G, Q, TK = Gm[u], Qt[u], TOK[u]
                    ZQ, ZT = ZQb[u], ZTb[u]
                    gk, zk, qk, tk = "Gm%d" % u, "ZZ%d" % u, "Q%d" % u, "TOK%d" % u
                    stages = []
                    psZa = psB[:, 0, :].rearrange("p (h n) -> p h n", h=4)
                    psZb = psB[:, 1, :].rearrange("p (h n) -> p h n", h=8)
                    KA = ["psB0", "psB0b"]

                    def st_g():
                        for q, fo in heads():
                            mm_group(psG[fo:fo + 64, q, 0:128], "psA%d" % (q // 2),
                                     [(BK[fo:fo + 64, q, c, 0:64], AR[fo:fo + 64, q, c, :], ["BK%d" % q, "AR%d" % q, "SCR"])])
                            mm_group(psG[fo:fo + 64, q, 128:256], "psA%d" % (q // 2),
                                     [(BK[fo:fo + 64, q, c, 64:128], AR[fo:fo + 64, q, c, :], ["BK%d" % q, "AR%d" % q, "SCR"])])
                            mm_group(psZb[fo:fo + 64, q, :], "psB1",
                                     [(AR[fo:fo + 64, q, c, 0:64], BK[fo:fo + 64, q, c, 0:64], ["BK%d" % q, "AR%d" % q, "SCR"])])
                        mk.dve(lambda e: e.tensor_tensor(out=G[:], in0=psG, in1=maskG(dt_).unsqueeze(1).to_broadcast([128, 4, 256]), op=ALU.mult),
                               reads=R("psA0", "psA1", "cst"), writes=[gk])
                        mk.dve(lambda e: e.tensor_tensor(out=ZT[0][:], in0=psZb[:, 0:4, :], in1=maskZ(dt_).unsqueeze(1).to_broadcast([128, 4, 64]),
                                                         op=ALU.mult), reads=R("psB1", "cst"), writes=[zk + "t0"])
                        mk.act(lambda e: e.copy(out=ZQ[0][:, :, 0, :], in_=G[:, :, 0:64]), reads=R(gk), writes=[zk + "q0"])
                        mk.act(lambda e: e.copy(out=ZQ[0][:, :, 1, :], in_=id64.unsqueeze(1).to_broadcast([128, 4, 64])), reads=R("cst"), writes=[zk + "q0"])
                    stages.append(st_g)

                    def mk_burst(lev):
                        cur, nxt = (lev - 1) % 2, lev % 2

                        def st():
                            for q, fo in heads():
                                if lev <= 4:
                                    mm_group(psZa[fo:fo + 64, q, :], KA, [(ZT[cur][fo:fo + 64, q, :], ZQ[cur][fo:fo + 64, q, :, :].rearrange("p a n -> p (a n)"),
                                                                          [zk + "t%d" % cur, zk + "q%d" % cur, "SCR"])])
                                else:
                                    mm_group(psZa[fo:fo + 64, q, 64:128], KA, [(ZT[cur][fo:fo + 64, q, :], ZQ[cur][fo:fo + 64, q, 1, :],
                                                                               [zk + "t%d" % cur, zk + "q%d" % cur, "SCR"])])
                                mm_group(psZb[fo:fo + 64, q, :], "psB1", [(ZQ[cur][fo:fo + 64, q, 0, :], ZT[cur][fo:fo + 64, q, :],
                                                                          [zk + "t%d" % cur, zk + "q%d" % cur, "SCR"])])
                            if lev <= 4:
                                mk.act(lambda e: e.copy(out=ZQ[nxt][:, :, 0, :], in_=psZa[:, :, 0:64]), reads=R(*KA), writes=[zk + "q%d" % nxt])
                            mk.dve(lambda e: e.tensor_tensor(out=ZQ[nxt][:, :, 1, :], in0=ZQ[cur][:, :, 1, :], in1=psZa[:, :, 64:128], op=ALU.add),
                                   reads=R(zk + "q%d" % cur, *KA), writes=[zk + "q%d" % nxt])
                            mk.act(lambda e: e.copy(out=ZT[nxt][:], in_=psZb[:, 0:4, :]), reads=R("psB1"), writes=[zk + "t%d" % nxt])
                        return st
                    for lev in range(1, 6):
                        stages.append(mk_burst(lev))

                    def st_last():
                        cur = 1
                        for q, fo in heads():
                            mm_group(psZa[fo:fo + 64, q, 64:128], KA, [(ZT[cur][fo:fo + 64, q, :], ZQ[cur][fo:fo + 64, q, 1, :],
                                                                       [zk + "t%d" % cur, zk + "q%d" % cur, "SCR"])])
                        mk.dve(lambda e: e.tensor_tensor(out=Q[:], in0=ZQ[cur][:, :, 1, :], in1=psZa[:, :, 64:128], op=ALU.add),
                               reads=R(zk + "q%d" % cur, *KA), writes=[qk])
                        for q, fo in heads():
                            idb = identb[fo:fo + 64, fo:fo + 64]
                            mm_group(psTv[fo:fo + 64, 0, q, :], "psT", [(BK[fo:fo + 64, q, c, 0:64], idb, ["BK%d" % q, "cst", "SCR"])])
                            mm_group(psTv[fo:fo + 64, 1, q, :], "psT", [(BK[fo:fo + 64, q, c, 64:128], idb, ["BK%d" % q, "cst", "SCR"])])
                            mm_group(psCv[fo:fo + 64, 1, 4 + q, :], "psCv", [(vbf[fo:fo + 64, q, c * 64:(c + 1) * 64], idb,
                                                                             ["vbf%d" % q, "cst", "SCR"])])
                        mk.act(lambda e: e.copy(out=TK[:, 0:2, :, :], in_=psTv), reads=R("psT"), writes=[tk])
                        mk.dve(lambda e: e.tensor_copy(out=TK[:, 2, :, :], in_=psCv[:, 1, 4:8, :]), reads=R("psCv"), writes=[tk])
                    stages.append(st_last)
                    return stages, (G, Q, TK, gk, qk, tk)

                def chain_stages(c, bufs, dt_=dt_, di=di, order=order):
                    G, Q, TK, gk, qk, tk = bufs
                    seq = c // cps
                    pos = order.index(c) % cps
                    stages = []

                    def st_w():
                        if pos == 0:
                            if kind == "prompt":
                                mk.dve(lambda e: e.memset(Mst[:], 0.0), reads=R(), writes=["Mst"])
                            elif kind == "aux":
                                if aux_idx == 0:
                                    mk.dve(lambda e: e.tensor_copy(out=Mst[:], in_=stt[:, 2, :, :]), reads=R("stt"), writes=["Mst"])
                                else:
                                    cc_ = COLS["coef"] + (aux_idx - 1)
                                    mk.dve(lambda e: e.scalar_tensor_tensor(out=Mst[:], in0=endst[:, aux_idx - 1, :, :], scalar=cols[:, cc_:cc_ + 1],
                                                                            in1=stt[:, 2 + aux_idx, :, :], op0=ALU.mult, op1=ALU.add),
                                           reads=R("stt", "end%d" % (aux_idx - 1), "cols"), writes=["Mst"])
                            else:
                                if dt_ == 0:
                                    mk.dve(lambda e: e.tensor_copy(out=Mst[:], in_=stt[:, 0, :, :]), reads=R("stt"), writes=["Mst"])
                                    for a in range(3):
                                        cc_ = COLS["coef"] + 2 + a
                                        mk.dve(lambda e, a=a, cc_=cc_: e.scalar_tensor_tensor(out=Mst[:], in0=endst[:, a, :, :], scalar=cols[:, cc_:cc_ + 1],
                                                                                          in1=Mst[:], op0=ALU.mult, op1=ALU.add),
                                               reads=R("end%d" % a, "cols", "Mst"), writes=["Mst"])
                                else:
                                    cc_ = COLS["coef"] + 5
                                    mk.dve(lambda e: e.scalar_tensor_tensor(out=Mst[:], in0=endst[:, 2, :, :], scalar=cols[:, cc_:cc_ + 1],
                                                                            in1=stt[:, 1, :, :], op0=ALU.mult, op1=ALU.add),
                                           reads=R("stt", "end2", "cols"), writes=["Mst"])
                        if pos == 0:
                            mk.act(lambda e: e.copy(out=Mbf[:], in_=Mst[:]), reads=R("Mst"), writes=["Mbf"])
                        for q, fo in heads():
                            mm_group(psCv[fo:fo + 64, 0, q, :], "psC0w",
                                     [(AR[fo:fo + 64, q, c, 0:64], Mbf[fo:fo + 64, q, :], ["AR%d" % q, "Mbf", "SCR"]),
                                      (G[fo:fo + 64, q, 128:192], TK[fo:fo + 64, 2, q, :], [gk, tk, "SCR"])])
                        mk.act(lambda e: e.copy(out=Wsb[:], in_=psCv[:, 0, 0:4, :]), reads=R("psC0w"), writes=["Wsb"])
                    stages.append(st_w)

                    def st_u():
                        for q, fo in heads():
                            mm_group(psCv[fo:fo + 64, 0, 4 + q, :], "psC0u", [(Q[fo:fo + 64, q, :], Wsb[fo:fo + 64, q, :], [qk, "Wsb", "SCR"])])
                        mk.dve(lambda e: e.tensor_copy(out=Usb[:], in_=psCv[:, 0, 4:8, :]), reads=R("psC0u"), writes=["Usb"])
                    stages.append(st_u)

                    def st_ym():
                        for q, fo in heads():
                            if full and KV != 5:
                                mm_group(psSv[fo:fo + 64, q, :], "psS",
                                         [(Mbf[fo:fo + 64, q, :], AR[fo:fo + 64, q, c, 64:128], ["Mbf", "AR%d" % q, "SCR"]),
                                          (Usb[fo:fo + 64, q, :], G[fo:fo + 64, q, 64:128], ["Usb", gk, "SCR"]),
                                          (TK[fo:fo + 64, 2, q, :], G[fo:fo + 64, q, 192:256], [tk, gk, "SCR"])])
                            mm_group(psSv[fo:fo + 64, 4 + q, :], "psS",
                                     [(TK[fo:fo + 64, 0, q, :], Usb[fo:fo + 64, q, :], [tk, "Usb", "SCR"]),
                                      (TK[fo:fo + 64, 1, q, :], TK[fo:fo + 64, 2, q, :], [tk, "SCR"])])
                        if full and KV != 6:
                            ydst = yT[:, :, c * 64:(c + 1) * 64]
                            if di == 0 and KV != 9:
                                mk.dve(lambda e: e.tensor_copy(out=ydst, in_=psSv[:, 0:4, :]), reads=R("psS"), writes=["yT"])
                            elif di == 0 and KV == 8:
                                mk.act(lambda e: e.copy(out=Wsb[:], in_=psSv[:, 0:4, :]), reads=R("psS", "Wsb"), writes=["Wsb"])
                                mk.dve(lambda e: e.tensor_copy(out=ydst, in_=Wsb[:]), reads=R("Wsb"), writes=["yT"])
                            elif di == 0:
                                mk.act(lambda e: e.copy(out=ydst, in_=psSv[:, 0:4, :]), reads=R("psS"), writes=["yT"])
                            else:
                                mk.act(lambda e: e.copy(out=Ytmp[:], in_=psSv[:, 0:4, :]), reads=R("psS", "Ytmp"), writes=["Ytmp"])
                                mk.dve(lambda e: e.tensor_tensor(out=ydst, in0=ydst, in1=Ytmp[:], op=ALU.add), reads=R("Ytmp", "yT"), writes=["yT"])
                        mk.dve(lambda e: e.tensor_tensor(out=Mtmp[:], in0=Mst[:], in1=psSv[:, 4:8, :], op=ALU.add),
                               reads=R("psS", "Mst"), writes=["Mtmp"])
                        mk.dve(lambda e: e.tensor_tensor(out=Mst[:], in0=Mtmp[:], in1=Pend[:, :, c:c + 1].to_broadcast([128, 4, 64]), op=ALU.mult),
                               reads=R("Mtmp", "Pend"), writes=["Mst"])
                        mk.act(lambda e: e.copy(out=Mbf[:], in_=Mst[:]), reads=R("Mst"), writes=["Mbf"])
                        if pos == cps - 1:
                            if kind == "aux":
                                mk.act(lambda e: e.copy(out=endst[:, aux_idx, :, :], in_=Mst[:]), reads=R("Mst"), writes=["end%d" % aux_idx])
                            elif kind == "prompt":
                                for q in range(4):
                                    mm_group(psT[0:64, q * 128:(q + 1) * 128], "psT", [(Mst[:, q, :], ident, ["Mst", "cst", "SCR"])])
                                mk.act(lambda e: e.copy(out=nsb[0:64, seq, di, :, :], in_=psT[0:64, :].rearrange("p (a n) -> p a n", a=4)),
                                       reads=R("psT"), writes=["nsb"])
                    stages.append(st_ym)
                    return stages

                prev = None
                for idx_c in range(len(order) + 1):
                    A, bufsA = ([], None)
                    if idx_c < len(order):
                        A, bufsA = tseries_stages(order[idx_c])
                    Bs = []
                    if prev is not None:
                        Bs = chain_stages(prev[0], prev[1])
                    for i in range(max(len(A), len(Bs))):
                        if i < len(A):
                            KTC[0] += 1
                            if KTC[0] <= KT:
                                A[i]()
                        if i < len(Bs):
                            KTC[0] += 1
                            if KTC[0] <= KT:
                                Bs[i]()
                    prev = (order[idx_c], bufsA) if idx_c < len(order) else None
            if not full:
                return
            tap("yT_" + kind, yT.rearrange("p q n -> p (q n)"), 4 * N, R("yT"))
            tap("rkv_" + kind, rkv.rearrange("p q n -> p (q n)"), 12 * N, R(*["rkv%d" % i for i in range(12)]))
            barrier()
            MARK[kind] = len(mk.ops)
            pool.off = base_d
            bv = pool.f32(4 * N).rearrange("p (q n) -> p q n", q=4)
            tA = pool.f32(2 * N).rearrange("p (q n) -> p q n", q=2)
            tB = pool.f32(2 * N).rearrange("p (q n) -> p q n", q=2)
            for pr in range(4):
                b = pr % 2
                mk.dve(lambda e, pr=pr, b=b: e.tensor_scalar(out=tA[:, b, :], in0=asum[:, pr, :], scalar1=cnum(5), scalar2=ccol("ka", pr), op0=ALU.subtract, op1=ALU.mult),
                       reads=R("asum%d" % pr, "cols", "tA%d" % b), writes=["tA%d" % b])
                mk.dve(lambda e, pr=pr, b=b: e.scalar_tensor_tensor(out=tA[:, b, :], in0=tA[:, b, :], scalar=2.0, in1=rkv[:, 4 + pr, :], op0=ALU.add, op1=ALU.mult),
                       reads=R("tA%d" % b, "rkv%d" % (4 + pr)), writes=["tA%d" % b])
                mk.dve(lambda e, pr=pr, b=b: e.scalar_tensor_tensor(out=tA[:, b, :], in0=tA[:, b, :], scalar=ccol("rk", pr), in1=rkv[:, pr, :], op0=ALU.mult, op1=ALU.mult),
                       reads=R("tA%d" % b, "rkv%d" % pr, "cols"), writes=["tA%d" % b])
                mm_group(psA[:, b, :], "psA%d" % b, [(bones, tA[:, b, :], ["cst", "tA%d" % b, "SCR"])])
                mk.dve(lambda e, pr=pr, b=b: e.tensor_tensor(out=bv[:, pr, :], in0=psA[:, b, :], in1=rkv[:, 8 + pr, :], op=ALU.mult),
                       reads=R("psA%d" % b, "rkv%d" % (8 + pr)), writes=["bv%d" % pr])
                mm_group(psB[:, b, :], KB(b), [(bones64, yT[:, pr, :], ["cst", "yT", "SCR"])])
                mk.act(lambda e, b=b: e.copy(out=tB[:, b, :], in_=psB[:, b, :]), reads=R("tB%d" % b, *KB(b)), writes=["tB%d" % b])
                mk.dve(lambda e, pr=pr, b=b: e.tensor_tensor(out=yT[:, pr, :], in0=yT[:, pr, :], in1=tB[:, b, :], op=ALU.subtract), reads=R("tB%d" % b, "yT"), writes=["yT"])
                mk.act(lambda e, pr=pr, b=b: e.activation(out=tA[:, b, :], in_=yT[:, pr, :], func=AF.Square), reads=R("yT", "tA%d" % b), writes=["tA%d" % b])
                mm_group(psA[:, b, :], "psA%d" % b, [(bones64, tA[:, b, :], ["cst", "tA%d" % b, "SCR"])])
                mk.act(lambda e, b=b: e.activation(out=tB[:, b, :], in_=psA[:, b, :], func=AF.Ln, bias=cnum(2), scale=1.0), reads=R("cols", "tB%d" % b, "psA%d" % b), writes=["tB%d" % b])
                mk.act(lambda e, b=b: e.activation(out=tB[:, b, :], in_=tB[:, b, :], func=AF.Exp, scale=-0.5), reads=R("tB%d" % b), writes=["tB%d" % b])
                mk.dve(lambda e, pr=pr, b=b: e.scalar_tensor_tensor(out=yT[:, pr, :], in0=yT[:, pr, :], scalar=ccol("gng", pr), in1=tB[:, b, :], op0=ALU.mult, op1=ALU.mult),
                       reads=R("tB%d" % b, "yT", "cols"), writes=["yT"])
                mk.dve(lambda e, pr=pr: e.scalar_tensor_tensor(out=yT[:, pr, :], in0=yT[:, pr, :], scalar=ccol("gnb", pr), in1=bv[:, pr, :], op0=ALU.add, op1=ALU.add),
                       reads=R("bv%d" % pr, "yT", "cols"), writes=["yT"])
            tap("yn_" + kind, yT.rearrange("p q n -> p (q n)"), 4 * N, R("yT"))
            barrier()
            pool.off = 2048
            yA = pool.f32(8 * N).rearrange("p (q n) -> p q n", q=8)
            pool.off = 12288
            yB = pool.f32(8 * N).rearrange("p (q n) -> p q n", q=8)
            tA2 = pool.f32(2 * N).rearrange("p (q n) -> p q n", q=2)
            tB2 = pool.f32(2 * N).rearrange("p (q n) -> p q n", q=2)
            yaT = pool.bf16(4 * N).rearrange("p (q n) -> p q n", q=4)
            ybT = pool.bf16(4 * N).rearrange("p (q n) -> p q n", q=4)
            cbT = pool.bf16(4 * N).rearrange("p (q n) -> p q n", q=4)
            ccT = pool.f32(4 * N).rearrange("p (q n) -> p q n", q=4)
            mgT = pool.bf16(8 * N).rearrange("p (q n) -> p q n", q=8)
            rl = 64 if kind == "own" else L
            r3 = lambda ap: ap.rearrange("p (r n) -> p r n", n=rl)

            def branch(wmat, srcT, srckey, dst, dstkey):
                for half in range(2):
                    slw, keyw = load_slab([(lambda t: t[:, 0:2048].rearrange("p (k n) -> p k n", k=4),
                                            wmat[:, half * 512:(half + 1) * 512].rearrange("(k p) n -> p k n", p=128))])
                    slwv = slw[:, 0:2048].rearrange("p (k n) -> p k n", k=4)
                    for tt in range(4):
                        o = half * 4 + tt
                        b = tt % 2
                        mm_group(psB[:, b, :], KB(b), [(slwv[:, k, tt * 128:(tt + 1) * 128], srcT[:, k, :], [keyw, srckey % k, "SCR"]) for k in range(4)])
                        evac(dst[:, o, :], psB[:, b, :], R(*KB(b)), [dstkey % o])

            for s in range(3, 11):
                sl, key = load_slab([(lambda t: t[:, 0:4096].rearrange("p (k n) -> p k n", k=8),
                                     w_in[:, s * 512:(s + 1) * 512].rearrange("(k p) n -> p k n", p=128))])
                slv = sl[:, 0:4096].rearrange("p (k n) -> p k n", k=8)
                for tt in range(4):
                    b = tt % 2
                    mm_group(psA[:, b, :], "psA%d" % b,
                             [(slv[:, k, tt * 128:(tt + 1) * 128], hT[:, k, :], [key, "hT%d" % k, "SCR"]) for k in range(8)])
                    pa = psA[:, b, :]
                    pk = "psA%d" % b
                    tb = tt % 2
                    if s == 3:
                        mk.act(lambda e, pa=pa, tb=tb: e.activation(out=tA2[:, tb, :], in_=pa, func=AF.Sigmoid), reads=R(pk, "tA2%d" % tb), writes=["tA2%d" % tb])
                        mk.dve(lambda e, tt=tt, tb=tb: e.tensor_tensor(out=yaT[:, tt, :], in0=yT[:, tt, :], in1=tA2[:, tb, :], op=ALU.mult),
                               reads=R("yT", "tA2%d" % tb), writes=["yaT%d" % tt])
                    elif s == 4:
                        mk.act(lambda e, tt=tt, pa=pa: e.copy(out=cbT[:, tt, :], in_=pa), reads=R(pk), writes=["cbT%d" % tt])
                    elif s == 5:
                        mk.act(lambda e, tt=tt, pa=pa: e.copy(out=ccT[:, tt, :], in_=pa), reads=R(pk), writes=["ccT%d" % tt])
                    elif s == 6:
                        u = tA2[:, tb, :]
                        uk = "tA2%d" % tb
                        acc = tB2[:, tb, :]
                        ak = "tB2%d" % tb
                        mk.dve(lambda e, tt=tt, pa=pa, u=u: e.tensor_tensor(out=u, in0=ccT[:, tt, :], in1=pa, op=ALU.mult), reads=R(pk, "ccT%d" % tt, uk), writes=[uk])
                        mk.dve(lambda e, tt=tt, u=u, acc=acc: e.tensor_scalar(out=acc, in0=u, scalar1=ccol("cw", 4 + tt), scalar2=ccol("cb", tt), op0=ALU.mult, op1=ALU.add),
                               reads=R(uk, "cols", ak), writes=[ak])
                        mk.dve(lambda e, tt=tt, u=u, acc=acc: e.scalar_tensor_tensor(out=r3(acc)[:, :, 1:rl], in0=r3(u)[:, :, 0:rl - 1], scalar=ccol("cw", tt),
                                                                                    in1=r3(acc)[:, :, 1:rl], op0=ALU.mult, op1=ALU.add), reads=R(uk, ak, "cols"), writes=[ak])
                        mk.dve(lambda e, tt=tt, u=u, acc=acc: e.scalar_tensor_tensor(out=r3(acc)[:, :, 0:rl - 1], in0=r3(u)[:, :, 1:rl], scalar=ccol("cw", 8 + tt),
                                                                                    in1=r3(acc)[:, :, 0:rl - 1], op0=ALU.mult, op1=ALU.add), reads=R(uk, ak, "cols"), writes=[ak])
                        mk.dve(lambda e, tt=tt, acc=acc: e.tensor_tensor(out=ybT[:, tt, :], in0=cbT[:, tt, :], in1=acc, op=ALU.mult), reads=R(ak, "cbT%d" % tt), writes=["ybT%d" % tt])
                    else:
                        gi = (s - 7) * 4 + tt
                        sg = tA2[:, tb, :]
                        sk = "tA2%d" % tb
                        mk.act(lambda e, pa=pa, sg=sg: e.activation(out=sg, in_=pa, func=AF.Sigmoid), reads=R(pk, sk), writes=[sk])
                        if gi < 8:
                            mk.dve(lambda e, gi=gi, sg=sg: e.tensor_tensor(out=yA[:, gi, :], in0=yA[:, gi, :], in1=sg, op=ALU.mult), reads=R(sk, "yA%d" % gi), writes=["yA%d" % gi])
                        else:
                            g2 = gi - 8
                            mk.dve(lambda e, g2=g2, sg=sg: e.tensor_tensor(out=sg, in0=sg, in1=yB[:, g2, :], op=ALU.mult), reads=R(sk, "yB%d" % g2), writes=[sk])
                            mk.dve(lambda e, g2=g2, sg=sg: e.tensor_tensor(out=mgT[:, g2, :], in0=yA[:, g2, :], in1=sg, op=ALU.add), reads=R(sk, "yA%d" % g2), writes=["mgT%d" % g2])
                if s == 3:
                    barrier()
                    branch(wba, yaT, "yaT%d", yA, "yA%d")
                if s == 6:
                    branch(wbb, ybT, "ybT%d", yB, "yB%d")
            oT = pool.f32(8 * N).rearrange("p (k n) -> p k n", k=8)
            pool.off = 6144
            sq = pool.bf16(8 * N).rearrange("p (k n) -> p k n", k=8)
            tmp = pool.f32(2 * N).rearrange("p (k n) -> p k n", k=2)
            for half in range(2):
                sl, key = load_slab([(lambda t: t[:, 0:4096].rearrange("p (k n) -> p k n", k=8),
                                     wout[:, half * 512:(half + 1) * 512].rearrange("(k p) n -> p k n", p=128))])
                slv = sl[:, 0:4096].rearrange("p (k n) -> p k n", k=8)
                for tt in range(4):
                    i = half * 4 + tt
                    b = tt % 2
                    mm_group(psA[:, b, :], "psA%d" % b, [(slv[:, k, tt * 128:(tt + 1) * 128], mgT[:, k, :], [key, "mgT%d" % k, "SCR"]) for k in range(8)])
                    evac(oT[:, i, :], psA[:, b, :], R("psA%d" % b), ["oT"])
            tap("mo_" + kind, oT.rearrange("p q n -> p (q n)"), 8 * N, R("oT"))
            postnorm_residual(j, 3, oT, sq, tmp)

        stt = sb("stt", [128, 5, 4, 64])
        nsb = sb("nsb", [64, 2, 2, 4, 128])
        mk.dma("sp", "c0", lambda e: e.dma_start(out=stt[:], in_=std.rearrange("a q p v -> p a q v")), writes=["stt"])

        def load_x(g):
            pool = Pool()
            xin = pool.f32(4 * D).rearrange("p (t n) -> p t n", t=4)
            for tt in range(4):
                mk.dma("sp", "xin", lambda e, tt=tt: e.dma_start(out=xin[:, tt, :], in_=xg[g, tt * 128:(tt + 1) * 128, :]), reads=["SCR"], writes=["xin%d" % tt])
            for k in range(8):
                b = k % 2
                for tt in range(4):
                    mm_group(psA[:, b, tt * 128:(tt + 1) * 128], "psA%d" % b, [(xin[:, tt, k * 128:(k + 1) * 128], ident, ["xin%d" % tt, "cst", "SCR"])])
                evac(xT[:, k, :], psA[:, b, :], R("psA%d" % b), ["xT"])

        def store_y(gout):
            pool = Pool()
            yo = pool.f32(4 * D).rearrange("p (t n) -> p t n", t=4)
            for tt in range(4):
                for k in range(8):
                    b = k % 2
                    mm_group(psA[:, b, 0:128], "psA%d" % b, [(xT[:, k, tt * 128:(tt + 1) * 128], ident, ["xT", "cst", "SCR"])])
                    evac(yo[:, tt, k * 128:(k + 1) * 128], psA[:, b, 0:128], R("psA%d" % b), ["yo%d" % tt])
                mk.dma("sp", "yout", lambda e, tt=tt: e.dma_start(out=yout[gout * N + tt * 128:gout * N + (tt + 1) * 128, :], in_=yo[:, tt, :]), reads=R("yo%d" % tt))

        def ffn_phase(j, ai, bi, gi, w13, w2):
            pool = Pool()
            hT = pool.bf16(8 * N).rearrange("p (k n) -> p k n", k=8)
            sq = pool.bf16(8 * N).rearrange("p (k n) -> p k n", k=8)
            hid = pool.bf16(22 * N).rearrange("p (k n) -> p k n", k=22)
            oT = pool.f32(8 * N).rearrange("p (k n) -> p k n", k=8)
            tmp = pool.f32(2 * N).rearrange("p (k n) -> p k n", k=2)
            sgt = pool.f32(2 * N).rearrange("p (k n) -> p k n", k=2)
            prenorm(j, ai, bi, hT, sq, tmp)
            ffn(w13, w2, hT, hid, oT, sgt)
            postnorm_residual(j, gi, oT, sq, tmp)

        def mixer_phase(kind, j, aux_idx):
            pool = Pool()
            hT = pool.bf16(8 * N).rearrange("p (k n) -> p k n", k=8)
            tail = Pool()
            tail.off = SCR - (8 * N // 2 + 2 * N)
            sq = tail.bf16(8 * N).rearrange("p (k n) -> p k n", k=8)
            tmp = tail.f32(2 * N).rearrange("p (k n) -> p k n", k=2)
            prenorm(j, 2, 3, hT, sq, tmp)
            barrier()
            mixer(kind, j, hT, aux_idx)

        def chain_prep_aux(a):
            cc_ = COLS["coef"] + (a - 1)
            mk.dve(lambda e: e.tensor_scalar(out=hbA[:], in0=hlast[:, a - 1, :], scalar1=cols[:, cc_:cc_ + 1], scalar2=None, op0=ALU.mult),
                   reads=["hlast%d" % (a - 1), "cols"], writes=["hbA"])

        def chain_prep_own():
            c2 = COLS["coef"] + 2
            mk.dve(lambda e: e.tensor_scalar(out=hbF[:], in0=hlast[:, 0, :], scalar1=cols[:, c2:c2 + 1], scalar2=None, op0=ALU.mult),
                   reads=["hlast0", "cols"], writes=["hbF"])
            for a in (1, 2):
                mk.dve(lambda e, a=a: e.scalar_tensor_tensor(out=hbF[:], in0=hlast[:, a, :], scalar=cols[:, c2 + a:c2 + a + 1], in1=hbF[:], op0=ALU.mult, op1=ALU.add),
                       reads=["hlast%d" % a, "cols", "hbF"], writes=["hbF"])
            mk.dve(lambda e: e.tensor_scalar(out=hbB[:], in0=hlast[:, 2, :], scalar1=cols[:, c2 + 3:c2 + 4], scalar2=None, op0=ALU.mult),
                   reads=["hlast2", "cols"], writes=["hbB"])

        for a in range(3):
            if go():
                barrier()
                load_x(2 + a)
            if go():
                barrier()
                ffn_phase(1, 0, 0, 1, f1w13, f1w2)
            if go():
                barrier()
                if a > 0:
                    chain_prep_aux(a)
                mixer_phase("aux", 1, a)
        for (g, kind, j) in [(1, "own", 1), (0, "prompt", 0)]:
            if go():
                barrier()
                load_x(g)
            if go():
                barrier()
                ffn_phase(j, 0, 0, 1, f1w13, f1w2)
            if go():
                barrier()
                if kind == "own":
                    chain_prep_own()
                mixer_phase(kind, j, None)
            if go():
                barrier()
                ffn_phase(j, 4, 6, 5, f2w13, f2w2)
            if go():
                barrier()
                store_y(1 if kind == "own" else 0)
        if dbg_spec is not None:
            barrier()
            dbg_spec(mk, nc, locals())
        for sq_ in range(2):
            for dd in range(2):
                mk.dma("sp", "yout", lambda e, sq_=sq_, dd=dd: e.dma_start(out=nsout[sq_, dd].rearrange("(q e) v k -> v q e k", e=2),
                                                                           in_=nsb[:, sq_, dd, :, :].rearrange("p q (e k) -> p q e k", e=2)), reads=["nsb"])
        stats = mk.emit()
    return nc, stats


_CACHE = {}


def _prep_inputs(inp):
    f = lambda a: np.ascontiguousarray(np.asarray(a, np.float32))
    x_prompt, x_sample = f(inp["x_prompt"]), f(inp["x_sample"])
    c, state, c_ctx = f(inp["c"]), f(inp["state_rwkv"]), f(inp["c_ctx"])
    mu = f(inp["mu_shift"])[0]
    w1 = [np.concatenate([f(inp["decay_w1"])[0, d], f(inp["iclr_a1"])[0, d]], axis=1) for d in range(2)]
    w2 = [np.concatenate([f(inp["decay_w2"])[0, d], f(inp["iclr_a2"])[0, d]], axis=0) for d in range(2)]
    dw0, ia0 = f(inp["decay_w0"])[0], f(inp["iclr_a0"])[0]
    shared = {
        "w_mod": f(inp["w_mod"])[0], "f1w13": f(inp["ffn1_w13"])[0], "f1w2": f(inp["ffn1_w2"])[0],
        "f2w13": f(inp["ffn2_w13"])[0], "f2w2": f(inp["ffn2_w2"])[0], "w_in": f(inp["w_in"])[0],
        "wba": f(inp["w_branch_a"])[0], "wbb": f(inp["w_branch_b"])[0], "wout": f(inp["w_out"])[0],
        "consts": _make_consts(),
    }
    in_maps = []
    for core in range(8):
        b, s = core // 4, core % 4
        if s == 0:
            aux = [(3, 1), (2, 1), (1, 1)]
            cont = (1.0, 1.0)
            selF = (0.0, 0.0, 0.0)
            selB = 1.0
            init = ["F", "0", "B", "0", "0"]
        elif s == 1:
            aux = [(0, 0), (3, 1), (2, 1)]
            cont = (0.0, 1.0)
            selF = (1.0, 0.0, 0.0)
            selB = 1.0
            init = ["0", "0", "F", "B", "0"]
        elif s == 2:
            aux = [(0, 0), (1, 0), (3, 1)]
            cont = (1.0, 0.0)
            selF = (0.0, 1.0, 0.0)
            selB = 1.0
            init = ["0", "0", "F", "0", "B"]
        else:
            aux = [(0, 0), (1, 0), (2, 0)]
            cont = (1.0, 1.0)
            selF = (0.0, 0.0, 1.0)
            selB = 0.0
            init = ["0", "B", "F", "0", "0"]
        xgr = np.empty((5, N, D), np.float32)
        xgr[0] = x_prompt[2 * core:2 * core + 2].reshape(N, D)
        xgr[1] = x_sample[b, s * N:(s + 1) * N]
        for a, (seg, dr) in enumerate(aux):
            xs = x_sample[b, seg * N:(seg + 1) * N]
            xgr[2 + a] = xs[::-1] if dr == 1 else xs
        dsl = [0, 1] + [dr for (_, dr) in aux]
        cols = np.zeros((128, NCOL), np.float32)
        cvs = [c_ctx, c[b]]
        for k in range(8):
            for j in range(2):
                cols[:, COLS["cv"] + k * 2 + j] = cvs[j][k * 128:(k + 1) * 128]
        cols[:, COLS["bmod"]:COLS["bmod"] + 72] = _colize(inp["b_mod"][0])
        cols[:, COLS["ng"]:COLS["ng"] + 48] = _colize(np.asarray(inp["norm_g"][0]).reshape(-1))
        for ds, dr in enumerate(dsl):
            for x in range(2):
                cols[:, COLS["mu"] + ds * 16 + x * 8:COLS["mu"] + ds * 16 + x * 8 + 8] = _colize(mu[dr, x])
            cols[:, COLS["w0"] + ds * 4:COLS["w0"] + ds * 4 + 4] = _colize(dw0[dr])
            cols[:, COLS["a0"] + ds * 4:COLS["a0"] + ds * 4 + 4] = _colize(ia0[dr])
        cols[:, COLS["kk"]:COLS["kk"] + 4] = _colize(inp["k_k"][0])
        cols[:, COLS["ka"]:COLS["ka"] + 4] = _colize(inp["k_a"][0])
        cols[:, COLS["rk"]:COLS["rk"] + 4] = _colize(np.asarray(inp["r_k"][0]).reshape(-1))
        cols[:, COLS["gng"]:COLS["gng"] + 4] = _colize(inp["gn_gain"][0])
        cols[:, COLS["gnb"]:COLS["gnb"] + 4] = _colize(inp["gn_bias"][0])
        cols[:, COLS["cw"]:COLS["cw"] + 12] = _colize(np.asarray(inp["conv_w"][0]).reshape(-1))
        cols[:, COLS["cb"]:COLS["cb"] + 4] = _colize(inp["conv_b"][0])
        cols[:, COLS["coef"]:COLS["coef"] + 6] = np.array([cont[0], cont[1], selF[0], selF[1], selF[2], selB], np.float32)[None, :]
        cols[:, COLS["num"]:COLS["num"] + 6] = np.array([1e-6, 1e-12, 64e-5, 0.0, 1.0, 2.0], np.float32)[None, :]
        def mlay(d):
            S = state[b, 0, d]
            return np.ascontiguousarray(S.transpose(0, 2, 1).reshape(4, 128, 64))
        stv = np.zeros((5, 4, 128, 64), np.float32)
        for i, t in enumerate(init):
            if t == "F":
                stv[i] = mlay(0)
            elif t == "B":
                stv[i] = mlay(1)
        m = dict(shared)
        m.update({"xg": xgr, "cols": cols, "st": stv,
                  "w1c": np.ascontiguousarray(np.stack([w1[d] for d in dsl])),
                  "w2c": np.ascontiguousarray(np.stack([w2[d] for d in dsl]))})
        in_maps.append(m)
    return in_maps


def kernel(**inputs):
    if "nc" not in _CACHE:
        _CACHE["nc"], _CACHE["stats"] = build_program(int(os.environ.get("KLIMIT", str(10 ** 9))))
    nc = _CACHE["nc"]
    in_maps = _prep_inputs(inputs)
    res = run_bass_kernel_spmd(nc, in_maps, core_ids=list(range(8)))
    y_prompt = np.empty((16, 256, D), np.float32)
    y_sample = np.empty((2, 2048, D), np.float32)
    new_state = np.empty((16, 1, 2, 8, 64, 64), np.float32)
    for core in range(8):
        r = res.results[core]
        b, s = core // 4, core % 4
        y = np.asarray(r["y"], np.float32)
        y_prompt[2 * core:2 * core + 2] = y[0:N].reshape(2, 256, D)
        y_sample[b, s * N:(s + 1) * N] = y[N:2 * N]
        new_state[2 * core:2 * core + 2, 0] = np.asarray(r["ns"], np.float32)
    return (y_prompt, y_sample, new_state)
```

```python
import contextlib
import numpy as np
import concourse.bass as bass
import concourse.mybir as mybir
from concourse.bass_utils import run_bass_kernel_spmd

F32 = mybir.dt.float32
BF16 = mybir.dt.bfloat16
ALU = mybir.AluOpType
AF = mybir.ActivationFunctionType

D = 1024
DFF = 2816
import os
SUB = int(os.environ.get('KSUB', '99'))
KT = int(os.environ.get('KT', '1000000'))
KTC = [0]
MARK = {}
TAPS = [t for t in os.environ.get('KTAPS', '').split(',') if t]
KV = int(os.environ.get('KV', '0'))
N = 512
C = 64
NCH = N // C
C0 = float(np.exp(-0.5))


class _Op:
    __slots__ = ("idx", "eng", "fn", "deps", "chan", "chanpos", "needs_inc", "inc_count", "engpos")

    def __init__(self, idx, eng, fn, deps, chan):
        self.idx = idx
        self.eng = eng
        self.fn = fn
        self.deps = deps
        self.chan = chan
        self.chanpos = None
        self.needs_inc = False
        self.inc_count = None
        self.engpos = None


class MK:
    ENGS = ("pe", "act", "dve", "pool", "sp")

    def __init__(self, nc):
        self.nc = nc
        self.ops = []
        self.last_writer = {}
        self.readers = {}
        self.chan_count = {}

    def add(self, eng, fn, reads=(), writes=(), chan=None):
        idx = len(self.ops)
        deps = set()
        writes = list(writes)
        if chan is not None:
            writes.append(("__chan__", chan))
        for r in reads:
            w = self.last_writer.get(r)
            if w is not None:
                deps.add(w)
        for w in writes:
            lw = self.last_writer.get(w)
            if lw is not None:
                deps.add(lw)
            deps.update(self.readers.get(w, ()))
        op = _Op(idx, eng, fn, deps, chan)
        if chan is not None:
            op.chanpos = self.chan_count.get(chan, 0)
            self.chan_count[chan] = op.chanpos + 1
        self.ops.append(op)
        for r in reads:
            self.readers.setdefault(r, []).append(idx)
        for w in writes:
            self.last_writer[w] = idx
            self.readers[w] = []
        return idx

    def pe(self, fn, reads=(), writes=()):
        return self.add("pe", fn, reads, writes)

    def act(self, fn, reads=(), writes=()):
        return self.add("act", fn, reads, writes)

    def dve(self, fn, reads=(), writes=()):
        return self.add("dve", fn, reads, writes)

    def pool(self, fn, reads=(), writes=()):
        return self.add("pool", fn, reads, writes)

    def dma(self, eng, chan, fn, reads=(), writes=()):
        return self.add(eng, fn, reads, writes, chan=chan)

    def emit(self):
        nc = self.nc
        ops = self.ops
        per_eng = {e: [] for e in self.ENGS}
        for op in ops:
            op.engpos = len(per_eng[op.eng])
            per_eng[op.eng].append(op)

        def need_sem(op, d):
            if d.chan is not None:
                return True
            if d.eng != op.eng:
                return True
            if op.eng == "pe":
                return False
            return (op.engpos - d.engpos) <= 2

        for op in ops:
            latest = {}
            keep = set()
            for di in op.deps:
                d = ops[di]
                if d.chan is not None:
                    keep.add(di)
                else:
                    cur = latest.get(d.eng)
                    if cur is None or ops[cur].engpos < d.engpos:
                        latest[d.eng] = di
            keep.update(latest.values())
            op.deps = keep
        for op in ops:
            for di in op.deps:
                d = ops[di]
                if d.chan is None and need_sem(op, d):
                    d.needs_inc = True
        cnt = {e: 0 for e in self.ENGS}
        for op in ops:
            if op.chan is None and op.needs_inc:
                cnt[op.eng] += 1
                op.inc_count = cnt[op.eng]
        chans = sorted(self.chan_count.keys())
        with contextlib.ExitStack() as st:
            esem = {e: st.enter_context(nc.semaphore("s_" + e)) for e in self.ENGS}
            csem = {c: st.enter_context(nc.semaphore("c_" + str(c))) for c in chans}
            block = st.enter_context(nc.Block())

            def run_engine(ename, eobj):
                waited = {}

                def wait(key, sem, val):
                    if waited.get(key, 0) >= val:
                        return
                    waited[key] = val
                    eobj.wait_ge(sem, val)

                for op in per_eng[ename]:
                    for di in sorted(op.deps):
                        d = ops[di]
                        if not need_sem(op, d):
                            continue
                        if d.chan is not None:
                            wait(("c", d.chan), csem[d.chan], 16 * (d.chanpos + 1))
                        else:
                            wait(("e", d.eng), esem[d.eng], d.inc_count)
                    ins = op.fn(eobj)
                    if op.chan is not None:
                        ins.then_inc(csem[op.chan], 16)
                    elif op.needs_inc:
                        ins.then_inc(esem[op.eng], 1)
                if ename == "sp":
                    for c in chans:
                        wait(("c", c), csem[c], 16 * self.chan_count[c])

            @block.tensor
            def _(e):
                run_engine("pe", e)

            @block.scalar
            def _(e):
                run_engine("act", e)

            @block.vector
            def _(e):
                run_engine("dve", e)

            @block.gpsimd
            def _(e):
                run_engine("pool", e)

            @block.sync
            def _(e):
                run_engine("sp", e)
        return {e: len(v) for e, v in per_eng.items()}


def _colize(v):
    v = np.asarray(v, np.float32).reshape(-1, 128)
    return np.ascontiguousarray(v.T)


COLS = {}
_off = 0
for _name, _w in [("cv", 16), ("bmod", 72), ("ng", 48), ("mu", 80), ("w0", 20), ("a0", 20), ("kk", 4), ("ka", 4),
                  ("rk", 4), ("gng", 4), ("gnb", 4), ("cw", 12), ("cb", 4), ("coef", 8), ("num", 8)]:
    COLS[_name] = _off
    _off += _w
NCOL = _off

CONSTS = {}
_off = 0
for _name, _w in [("ident", 128), ("bones", 128), ("maskG", 512), ("maskZ", 128), ("cmask", 512), ("id64", 64), ("bones64", 128)]:
    CONSTS[_name] = _off
    _off += _w
NCONST = _off


def _make_consts():
    cst = np.zeros((128, NCONST), np.float32)
    cst[:, CONSTS["ident"]:CONSTS["ident"] + 128] = np.eye(128, dtype=np.float32)
    bo = np.zeros((128, 128), np.float32)
    bo[:64, :64] = 1
    bo[64:, 64:] = 1
    cst[:, CONSTS["bones"]:CONSTS["bones"] + 128] = bo
    cst[:, CONSTS["bones64"]:CONSTS["bones64"] + 128] = bo / 64.0
    s = (np.arange(128) % 64)[:, None]
    t = np.arange(64)[None, :]
    mg = np.zeros((128, 2, 256), np.float32)
    for blk in range(2):
        mg[:, 0, blk * 128:blk * 128 + 64] = (t > s)
        mg[:, 0, blk * 128 + 64:blk * 128 + 128] = (t >= s)
        mg[:, 1, blk * 128:blk * 128 + 64] = (t < s)
        mg[:, 1, blk * 128 + 64:blk * 128 + 128] = (t <= s)
    cst[:, CONSTS["maskG"]:CONSTS["maskG"] + 512] = mg.reshape(128, 512)
    mz = np.zeros((128, 2, 64), np.float32)
    mz[:, 0, :] = (t < s)
    mz[:, 1, :] = (t > s)
    cst[:, CONSTS["maskZ"]:CONSTS["maskZ"] + 128] = mz.reshape(128, 128)
    cm = np.ones((128, 512), np.float32)
    cm[:, ::64] = 0
    cst[:, CONSTS["cmask"]:CONSTS["cmask"] + 512] = cm
    i64 = np.zeros((128, 64), np.float32)
    i64[np.arange(128), np.arange(128) % 64] = 1
    cst[:, CONSTS["id64"]:CONSTS["id64"] + 64] = i64
    return cst


def build_program(limit=10 ** 9, dbg_spec=None, mlimit=10 ** 9):
    nc = bass.Bass("TRN2", target_bir_lowering=False)
    stage = [0]

    def go():
        stage[0] += 1
        return stage[0] <= limit
    mstage = [0]

    def mgo():
        mstage[0] += 1
        return mstage[0] <= mlimit

    def din(name, shape):
        return nc.dram_tensor(name, list(shape), F32, kind="ExternalInput").ap()

    xg = din("xg", [5, N, D])
    colsd = din("cols", [128, NCOL])
    cstd = din("consts", [128, NCONST])
    std = din("st", [5, 4, 128, 64])
    w_mod = din("w_mod", [D, 9 * D])
    f1w13 = din("f1w13", [D, 2 * DFF])
    f1w2 = din("f1w2", [DFF, D])
    f2w13 = din("f2w13", [D, 2 * DFF])
    f2w2 = din("f2w2", [DFF, D])
    w_in = din("w_in", [D, 5632])
    w1c = din("w1c", [5, D, 128])
    w2c = din("w2c", [5, 128, 512])
    wba = din("wba", [512, D])
    wbb = din("wbb", [512, D])
    wout = din("wout", [D, D])
    yout = nc.dram_tensor("y", [2 * N, D], F32, kind="ExternalOutput").ap()
    nsout = nc.dram_tensor("ns", [2, 2, 8, 64, 64], F32, kind="ExternalOutput").ap()

    with contextlib.ExitStack() as stk:
        def sb(name, shape, dt=F32):
            return stk.enter_context(nc.sbuf_tensor(name, list(shape), dt))

        def ps(name, shape):
            return stk.enter_context(nc.psum_tensor(name, list(shape), F32))

        mk = MK(nc)
        cols = sb("cols_t", [128, NCOL])
        cst = sb("cst_t", [128, NCONST])
        onesb = sb("onesb", [128, 128], BF16)
        identb = sb("identb", [128, 128], BF16)
        scb = sb("scb", [128, 8, 2], BF16)
        modT = sb("modT", [128, 72, 2])
        mods = sb("mods", [128, 2, 6, 8])
        xT = sb("xT", [128, 8, N])
        rstd = sb("rstd", [128, N])
        lnt = sb("lnt", [128, N])
        bdum = sb("bdum", [128, 1])
        NSLOT = 3
        slots = [sb("slot%d" % i, [128, 4096], BF16) for i in range(NSLOT)]
        hlast = sb("hlast", [128, 3, 8])
        hbF = sb("hbF", [128, 8])
        hbB = sb("hbB", [128, 8])
        hbA = sb("hbA", [128, 8])
        endst = sb("endst", [128, 3, 4, 64])
        Mst = sb("Mst", [128, 4, 64])
        Mtmp = sb("Mtmp", [128, 4, 64])
        Mbf = sb("Mbf", [128, 4, 64], BF16)
        SCR = 30720
        scr = sb("scr", [128, SCR])

        class Pool:
            def __init__(self):
                self.off = 0

            def f32(self, n):
                a = scr[:, self.off:self.off + n]
                self.off += n
                assert self.off <= SCR, self.off
                return a

            def bf16(self, n):
                m = (n + 1) // 2
                a = scr[:, self.off:self.off + m].bitcast(BF16)
                self.off += m
                assert self.off <= SCR, self.off
                return a

        psA = ps("psA", [128, 2, 512])
        psB = ps("psB", [128, 2, 512])
        psS = ps("psS", [128, 512])
        psC = ps("psC", [128, 2, 512])
        psT = ps("psT", [128, 512])

        def tap(name, ap2d, width, keys):
            if name not in TAPS:
                return
            dt_ = nc.dram_tensor("dbg_" + name, [128, width], F32, kind="ExternalOutput").ap()
            mk.dma("sp", "yout", lambda e: e.dma_start(out=dt_, in_=ap2d), reads=keys)

        cnum = lambda i: cols[:, COLS["num"] + i:COLS["num"] + i + 1]
        ccol = lambda name, i: cols[:, COLS[name] + i:COLS[name] + i + 1]
        ident = cst[:, CONSTS["ident"]:CONSTS["ident"] + 128]
        bones = cst[:, CONSTS["bones"]:CONSTS["bones"] + 128]
        bones64 = cst[:, CONSTS["bones64"]:CONSTS["bones64"] + 128]
        id64 = cst[:, CONSTS["id64"]:CONSTS["id64"] + 64]
        cmask = cst[:, CONSTS["cmask"]:CONSTS["cmask"] + 512]

        def maskG(d):
            o = CONSTS["maskG"] + d * 256
            return cst[:, o:o + 256]

        def maskZ(d):
            o = CONSTS["maskZ"] + d * 64
            return cst[:, o:o + 64]

        evac_rr = [0]

        def evac(out, in_, reads, writes):
            evac_rr[0] ^= 1
            if evac_rr[0]:
                mk.act(lambda e: e.copy(out=out, in_=in_), reads=reads, writes=writes)
            else:
                mk.dve(lambda e: e.tensor_copy(out=out, in_=in_), reads=reads, writes=writes)

        slot_rr = [0]

        def load_slab(pieces):
            s = slot_rr[0] % NSLOT
            slot_rr[0] += 1
            sl = slots[s]
            key = "W%d" % s
            for i, (vf, dap) in enumerate(pieces):
                mk.dma("pool", "w%d" % s, lambda e, vf=vf, dap=dap: e.dma_start(out=vf(sl), in_=dap), writes=[key])
            return sl, key

        def mm_group(out_ap, out_key, terms):
            n = len(terms)
            for i, (l, r, keys) in enumerate(terms):
                mk.pe(lambda e, l=l, r=r, i=i: e.matmul(out_ap, lhsT=l, rhs=r, start=(i == 0), stop=(i == n - 1)),
                      reads=keys, writes=(out_key if isinstance(out_key, list) else [out_key]))

        barrier_n = [0]

        def barrier():
            barrier_n[0] += 1
            mk.dve(lambda e: e.memset(bdum[:], 0.0), reads=["bdum"], writes=["SCR"])

        R = lambda *k: ["SCR"] + list(k)

        mk.dma("sp", "c0", lambda e: e.dma_start(out=cols[:], in_=colsd), writes=["cols"])
        mk.dma("sp", "c0", lambda e: e.dma_start(out=cst[:], in_=cstd), writes=["cst"])
        mk.dve(lambda e: e.memset(onesb[:], 1.0 / 1024.0), writes=["onesb"])
        mk.dve(lambda e: e.tensor_copy(out=identb[:], in_=cst[:, CONSTS["ident"]:CONSTS["ident"] + 128]), reads=["cst"], writes=["cst2"])
        cv0 = COLS["cv"]
        mk.act(lambda e: e.activation(out=scb[:].rearrange("p k j -> p (k j)"), in_=cols[:, cv0:cv0 + 16], func=AF.Silu),
               reads=["cols"], writes=["scb"])
        for s in range(18):
            sl, key = load_slab([(lambda t: t[:, 0:4096].rearrange("p (k n) -> p k n", k=8),
                                 w_mod[:, s * 512:(s + 1) * 512].rearrange("(k p) n -> p k n", p=128))])
            slv = sl[:, 0:4096].rearrange("p (k n) -> p k n", k=8)
            for tt in range(4):
                mm_group(psS[:, tt * 2:tt * 2 + 2], "psS",
                         [(slv[:, k, tt * 128:(tt + 1) * 128], scb[:, k, :], [key, "scb"]) for k in range(8)])
            for j in range(2):
                b0 = COLS["bmod"] + s * 4
                mk.dve(lambda e, s=s, j=j, b0=b0: e.tensor_tensor(
                    out=modT[:, s * 4:s * 4 + 4, j], in0=psS[:, 0:8].rearrange("p (t j) -> p t j", j=2)[:, :, j],
                    in1=cols[:, b0:b0 + 4], op=ALU.add), reads=["psS", "cols"], writes=["modT"])
        ng = lambda i: cols[:, COLS["ng"] + i * 8:COLS["ng"] + i * 8 + 8]
        m_ = lambda i, j: modT[:, i * 8:(i + 1) * 8, j]
        for j in range(2):
            for (dst, mi, gi, half) in [(0, 1, 0, None), (1, 2, 1, 0.5), (2, 4, 2, None), (3, 5, 3, 1.0), (4, 7, 4, None), (5, 8, 5, 0.5)]:
                if half is None:
                    mk.dve(lambda e, j=j, dst=dst, mi=mi, gi=gi: e.scalar_tensor_tensor(
                        out=mods[:, j, dst, :], in0=m_(mi, j), scalar=1.0, in1=ng(gi), op0=ALU.add, op1=ALU.mult),
                        reads=["modT", "cols"], writes=["mods"])
                else:
                    mk.dve(lambda e, j=j, dst=dst, mi=mi, gi=gi, half=half: e.scalar_tensor_tensor(
                        out=mods[:, j, dst, :], in0=m_(mi, j), scalar=half, in1=ng(gi), op0=ALU.mult, op1=ALU.mult),
                        reads=["modT", "cols"], writes=["mods"])

        def rms_rstd(src, src_key, sq, eps_idx):
            for k in range(8):
                if k % 2 == 0:
                    mk.act(lambda e, k=k: e.activation(out=sq[:, k, :], in_=src[:, k, :], func=AF.Square),
                           reads=R(src_key), writes=["sq%d" % k])
                else:
                    mk.dve(lambda e, k=k: e.tensor_tensor(out=sq[:, k, :], in0=src[:, k, :], in1=src[:, k, :], op=ALU.mult),
                           reads=R(src_key), writes=["sq%d" % k])
            mm_group(psS[:], "psS", [(onesb[:], sq[:, k, :], ["onesb", "sq%d" % k, "SCR"]) for k in range(8)])
            mk.act(lambda e: e.activation(out=lnt[:], in_=psS[:], func=AF.Ln, bias=cnum(eps_idx), scale=1.0),
                   reads=["psS", "cols"], writes=["lnt"])
            mk.act(lambda e: e.activation(out=rstd[:], in_=lnt[:], func=AF.Exp, scale=-0.5), reads=["lnt"], writes=["rstd"])

        def prenorm(j, ai, bi, hT, sq, tmp):
            rms_rstd(xT, "xT", sq, 0)
            for k in range(8):
                mk.dve(lambda e, k=k: e.scalar_tensor_tensor(out=tmp[:, k % 2, :], in0=xT[:, k, :], scalar=mods[:, j, ai, k:k + 1],
                                                             in1=rstd[:], op0=ALU.mult, op1=ALU.mult),
                       reads=R("xT", "mods", "rstd"), writes=["ptmp%d" % (k % 2)])
                mk.act(lambda e, k=k: e.activation(out=hT[:, k, :], in_=tmp[:, k % 2, :], func=AF.Identity,
                                                   bias=modT[:, bi * 8 + k, j:j + 1], scale=1.0),
                       reads=R("ptmp%d" % (k % 2), "modT"), writes=["hT%d" % k])

        def postnorm_residual(j, gi, oT, sq, tmp):
            rms_rstd(oT, "oT", sq, 0)
            for k in range(8):
                mk.dve(lambda e, k=k: e.scalar_tensor_tensor(out=tmp[:, k % 2, :], in0=oT[:, k, :], scalar=mods[:, j, gi, k:k + 1],
                                                             in1=rstd[:], op0=ALU.mult, op1=ALU.mult),
                       reads=R("oT", "mods", "rstd"), writes=["ptmp%d" % (k % 2)])
                mk.dve(lambda e, k=k: e.tensor_tensor(out=xT[:, k, :], in0=xT[:, k, :], in1=tmp[:, k % 2, :], op=ALU.add),
                       reads=R("ptmp%d" % (k % 2), "xT"), writes=["xT"])

        def ffn(w13, w2, hT, hid, oT, sgt):
            v8 = lambda t: t[:, 0:4096].rearrange("p (k n) -> p k n", k=8)
            groups = [(g * 4, 4) for g in range(5)] + [(20, 2)]
            for (j0, nj) in groups:
                w = nj * 128
                slg, keyg = load_slab([(lambda t, w=w: v8(t)[:, :, 0:w], w13[:, j0 * 128:j0 * 128 + w].rearrange("(k p) n -> p k n", p=128))])
                for jj in range(nj):
                    b = jj % 2
                    mm_group(psA[:, b, :], "psA%d" % b,
                             [(v8(slg)[:, k, jj * 128:(jj + 1) * 128], hT[:, k, :], [keyg, "hT%d" % k, "SCR"]) for k in range(8)])
                    mk.act(lambda e, b=b, jj=jj: e.activation(out=sgt[:, jj, :], in_=psA[:, b, :], func=AF.Silu),
                           reads=R("psA%d" % b), writes=["sgt%d" % jj])
                slu, keyu = load_slab([(lambda t, w=w: v8(t)[:, :, 0:w], w13[:, DFF + j0 * 128:DFF + j0 * 128 + w].rearrange("(k p) n -> p k n", p=128))])
                for jj in range(nj):
                    b = jj % 2
                    jt = j0 + jj
                    mm_group(psB[:, b, :], KB(b),
                             [(v8(slu)[:, k, jj * 128:(jj + 1) * 128], hT[:, k, :], [keyu, "hT%d" % k, "SCR"]) for k in range(8)])
                    mk.dve(lambda e, b=b, jj=jj, jt=jt: e.tensor_tensor(out=hid[:, jt, :], in0=sgt[:, jj, :], in1=psB[:, b, :], op=ALU.mult),
                           reads=R("sgt%d" % jj, *KB(b)), writes=["hid%d" % jt])
            banks = [(psA[:, 0, :], ["psA0"]), (psA[:, 1, :], ["psA1"]), (psB[:, 0, :], KB(0)), (psB[:, 1, :], KB(1))]
            jslabs = [(0, 8), (8, 8), (16, 6)]
            for ig in range(2):
                for si, (js, nj) in enumerate(jslabs):
                    sl, key = load_slab([(lambda t, nj=nj: v8(t)[:, 0:nj, :], w2[js * 128:(js + nj) * 128, ig * 512:(ig + 1) * 512].rearrange("(j p) n -> p j n", p=128))])
                    for tt in range(4):
                        pa, pk = banks[tt]
                        for jj in range(nj):
                            jt = js + jj
                            first = (si == 0 and jj == 0)
                            last = (si == len(jslabs) - 1 and jj == nj - 1)
                            mk.pe(lambda e, pa=pa, sl=sl, jj=jj, tt=tt, jt=jt, first=first, last=last: e.matmul(
                                pa, lhsT=v8(sl)[:, jj, tt * 128:(tt + 1) * 128], rhs=hid[:, jt, :], start=first, stop=last),
                                reads=[key, "hid%d" % jt, "SCR"], writes=pk)
                for tt in range(4):
                    pa, pk = banks[tt]
                    evac(oT[:, ig * 4 + tt, :], pa, R(*pk), ["oT"])

        def KB(b):
            return ["psB0", "psB0b"] if b == 0 else ["psB1"]

        def mixer(kind, j, hT, aux_idx):
            pool = Pool()
            pool.off = 2048
            full = kind != "aux"
            nseq = 2 if kind == "prompt" else 1
            L = N // nseq
            cps = NCH // nseq
            rkv = pool.f32(12 * N).rearrange("p (t n) -> p t n", t=12)
            kk = pool.f32(4 * N).rearrange("p (t n) -> p t n", t=4)
            yT = pool.f32(4 * N).rearrange("p (t n) -> p t n", t=4)
            asum = pool.f32(4 * N).rearrange("p (t n) -> p t n", t=4)
            lh = pool.bf16(2 * N).rearrange("p (d n) -> p d n", d=2)
            w2b = pool.bf16(2 * 512).rearrange("p (d n) -> p d n", d=2)
            vbf = pool.bf16(4 * N).rearrange("p (t n) -> p t n", t=4)
            base_d = pool.off
            w1b = pool.bf16(2 * 1024).rearrange("p (d k n) -> p d k n", d=2, k=8)
            w1s = pool.bf16(2 * 1024).rearrange("p (d k n) -> p d k n", d=2, k=8)
            hk = ["hT%d" % k for k in range(8)]
            for s in range(3):
                sl, key = load_slab([(lambda t: t[:, 0:4096].rearrange("p (k n) -> p k n", k=8),
                                     w_in[:, s * 512:(s + 1) * 512].rearrange("(k p) n -> p k n", p=128))])
                slv = sl[:, 0:4096].rearrange("p (k n) -> p k n", k=8)
                for tt in range(4):
                    b = tt % 2
                    mm_group(psA[:, b, :], "psA%d" % b,
                             [(slv[:, k, tt * 128:(tt + 1) * 128], hT[:, k, :], [key, "hT%d" % k, "SCR"]) for k in range(8)])
                    evac(rkv[:, s * 4 + tt, :], psA[:, b, :], R("psA%d" % b), ["rkv%d" % (s * 4 + tt)])
            for pr in range(4):
                mk.act(lambda e, pr=pr: e.copy(out=vbf[:, pr, :], in_=rkv[:, 8 + pr, :]), reads=R("rkv%d" % (8 + pr)), writes=["vbf%d" % pr])
            if not mgo():
                return
            sqk = pool.f32(2 * N).rearrange("p (b n) -> p b n", b=2)
            for pr in range(4):
                b = pr % 2
                mk.dve(lambda e, pr=pr: e.tensor_scalar(out=kk[:, pr, :], in0=rkv[:, 4 + pr, :], scalar1=ccol("kk", pr), scalar2=None,
                                                        op0=ALU.mult), reads=R("rkv%d" % (4 + pr), "cols"), writes=["kk%d" % pr])
                mk.dve(lambda e, pr=pr, b=b: e.tensor_tensor(out=sqk[:, b, :], in0=kk[:, pr, :], in1=kk[:, pr, :], op=ALU.mult),
                       reads=R("kk%d" % pr), writes=["sqk%d" % b])
                mm_group(psB[:, b, :], KB(b), [(bones, sqk[:, b, :], ["cst", "sqk%d" % b, "SCR"])])
                mk.act(lambda e, b=b: e.activation(out=sqk[:, b, :], in_=psB[:, b, :], func=AF.Ln, bias=cnum(1), scale=1.0),
                       reads=R("cols", *KB(b)), writes=["sqk%d" % b])
                mk.act(lambda e, b=b: e.activation(out=sqk[:, b, :], in_=sqk[:, b, :], func=AF.Exp, scale=-0.5),
                       reads=R("sqk%d" % b), writes=["sqk%d" % b])
                mk.dve(lambda e, pr=pr, b=b: e.tensor_tensor(out=kk[:, pr, :], in0=kk[:, pr, :], in1=sqk[:, b, :], op=ALU.mult),
                       reads=R("sqk%d" % b, "kk%d" % pr), writes=["kk%d" % pr])
            if not mgo():
                return
            dslots = [(0, 0), (1, 1)] if full else [(0, 2 + aux_idx)]
            sh = pool.bf16(8 * N).rearrange("p (k n) -> p k n", k=8)
            for di, (dt_, ds) in enumerate(dslots):
                mk.dma("pool", "wl", lambda e, di=di, ds=ds: e.dma_start(out=w1b[:, di, :, :], in_=w1c[ds].rearrange("(k p) n -> p k n", p=128)),
                       reads=["SCR"], writes=["w1b%d" % di])
                mk.dma("pool", "wl", lambda e, di=di, ds=ds: e.dma_start(out=w2b[:, di, :], in_=w2c[ds]), reads=["SCR"], writes=["w2b%d" % di])
                for x in range(2):
                    for k in range(8):
                        mc = COLS["mu"] + ds * 16 + x * 8 + k
                        mk.dve(lambda e, di=di, x=x, k=k, mc=mc: e.tensor_scalar(
                            out=w1s[:, di, k, x * 64:(x + 1) * 64], in0=w1b[:, di, k, x * 64:(x + 1) * 64],
                            scalar1=cols[:, mc:mc + 1], scalar2=None, op0=ALU.mult),
                            reads=R("w1b%d" % di, "cols"), writes=["w1s%d" % di])
                if dt_ == 0:
                    mk.dve(lambda e: e.tensor_tensor(out=sh[:, :, 1:N], in0=hT[:, :, 0:N - 1], in1=hT[:, :, 1:N], op=ALU.subtract),
                           reads=R(*hk), writes=["sh"])
                    for sq_ in range(nseq):
                        col = sq_ * L
                        if kind == "prompt" or (kind == "aux" and aux_idx == 0):
                            mk.dve(lambda e, col=col: e.tensor_scalar(out=sh[:, :, col], in0=hT[:, :, col], scalar1=-1.0, scalar2=None,
                                                                      op0=ALU.mult), reads=R("sh", *hk), writes=["sh"])
                        else:
                            hb = hbF if kind == "own" else hbA
                            hbk = "hbF" if kind == "own" else "hbA"
                            mk.dve(lambda e, col=col, hb=hb: e.tensor_tensor(out=sh[:, :, col], in0=hb[:, :], in1=hT[:, :, col], op=ALU.subtract),
                                   reads=R("sh", hbk, *hk), writes=["sh"])
                else:
                    mk.dve(lambda e: e.tensor_tensor(out=sh[:, :, 0:N - 1], in0=hT[:, :, 1:N], in1=hT[:, :, 0:N - 1], op=ALU.subtract),
                           reads=R(*hk), writes=["sh"])
                    for sq_ in range(nseq):
                        col = sq_ * L + L - 1
                        if kind == "prompt":
                            mk.dve(lambda e, col=col: e.tensor_scalar(out=sh[:, :, col], in0=hT[:, :, col], scalar1=-1.0, scalar2=None,
                                                                      op0=ALU.mult), reads=R("sh", *hk), writes=["sh"])
                        else:
                            mk.dve(lambda e, col=col: e.tensor_tensor(out=sh[:, :, col], in0=hbB[:, :], in1=hT[:, :, col], op=ALU.subtract),
                                   reads=R("sh", "hbB", *hk), writes=["sh"])
                b = di % 2
                mm_group(psB[:, b, :], KB(b),
                         [(w1b[:, di, k, :], hT[:, k, :], ["w1b%d" % di, "hT%d" % k, "SCR"]) for k in range(8)] +
                         [(w1s[:, di, k, :], sh[:, k, :], ["w1s%d" % di, "sh", "SCR"]) for k in range(8)])
                mk.act(lambda e, di=di, b=b: e.activation(out=lh[0:64, di, :], in_=psB[0:64, b, :], func=AF.Tanh),
                       reads=R(*KB(b)), writes=["lh%d" % di])
                mk.act(lambda e, di=di, b=b: e.copy(out=lh[64:128, di, :], in_=psB[64:128, b, :]),
                       reads=R(*KB(b)), writes=["lh%d" % di])
            if kind == "aux":
                mk.dve(lambda e: e.tensor_copy(out=hlast[:, aux_idx, :], in_=hT[:, :, N - 1]), reads=R(*hk), writes=["hlast%d" % aux_idx])

            if not mgo():
                return
            v3 = lambda ap: ap.rearrange("p (c n) -> p c n", c=8)
            psG = psA[:].rearrange("p a (h n) -> p (a h) n", h=2)
            psZ = psB[:].rearrange("p a (h n) -> p (a h) n", h=8)
            psZv = lambda a, hh: psZ[:, a * 4 + hh, :]
            ZK = ["psB0", "psB0b", "psB1"]
            psTv = psT[:].rearrange("p (a h n) -> p a h n", a=2, h=4)
            psCv = psC[:].rearrange("p a (h n) -> p a h n", h=8)
            psSv = psS[:].rearrange("p (h n) -> p h n", h=8)
            for di, (dt_, ds) in enumerate(dslots):
                order = list(range(NCH)) if dt_ == 0 else list(range(NCH - 1, -1, -1))
                barrier()
                pool.off = base_d
                AR = pool.bf16(4 * 8 * 128).rearrange("p (q c n) -> p q c n", q=4, c=8)
                BK = pool.bf16(4 * 8 * 128).rearrange("p (q c n) -> p q c n", q=4, c=8)
                Pend = pool.f32(32).rearrange("p (q c) -> p q c", q=4)
                base_t = pool.off
                sw = pool.f32(2 * N).rearrange("p (q n) -> p q n", q=2)
                av = pool.f32(2 * N).rearrange("p (q n) -> p q n", q=2)
                cs = pool.f32(2 * N).rearrange("p (q n) -> p q n", q=2)
                Lx = pool.f32(2 * N).rearrange("p (q n) -> p q n", q=2)
                Ep = pool.f32(2 * N).rearrange("p (q n) -> p q n", q=2)
                t1 = pool.f32(2 * N).rearrange("p (q n) -> p q n", q=2)
                for hp in range(2):
                    for ql in range(2):
                        pr = 2 * hp + ql
                        b = ql
                        mm_group(psA[:, b, :], "psA%d" % b, [(w2b[0:64, di, pr * 128:(pr + 1) * 128], lh[0:64, di, :], ["w2b%d" % di, "lh%d" % di, "SCR"])])
                        mm_group(psB[:, b, :], KB(b), [(w2b[64:128, di, pr * 128:(pr + 1) * 128], lh[64:128, di, :], ["w2b%d" % di, "lh%d" % di, "SCR"])])
                        mk.act(lambda e, pr=pr, ql=ql, b=b, ds=ds: e.activation(out=sw[:, ql, :], in_=psA[:, b, :], func=AF.Sigmoid,
                                                                               bias=ccol("w0", ds * 4 + pr), scale=1.0),
                               reads=R("psA%d" % b, "cols"), writes=["sw%d" % ql])
                        mk.act(lambda e, pr=pr, ql=ql, b=b, ds=ds: e.activation(out=av[:, ql, :], in_=psB[:, b, :], func=AF.Sigmoid,
                                                                               bias=ccol("a0", ds * 4 + pr), scale=1.0),
                               reads=R("cols", *KB(b)), writes=["av%d" % ql])
                        if full:
                            if di == 0:
                                mk.dve(lambda e, pr=pr, ql=ql: e.tensor_copy(out=asum[:, pr, :], in_=av[:, ql, :]), reads=R("av%d" % ql), writes=["asum%d" % pr])
                            else:
                                mk.dve(lambda e, pr=pr, ql=ql: e.tensor_tensor(out=asum[:, pr, :], in0=asum[:, pr, :], in1=av[:, ql, :], op=ALU.add),
                                       reads=R("av%d" % ql, "asum%d" % pr), writes=["asum%d" % pr])
                        mk.dve(lambda e, ql=ql: e.tensor_tensor_scan(out=cs[:, ql, :], data0=cmask, data1=sw[:, ql, :], initial=0.0,
                                                                     op0=ALU.mult, op1=ALU.add), reads=R("sw%d" % ql, "cst"), writes=["cs%d" % ql])
                        if dt_ == 0:
                            mk.dve(lambda e, ql=ql: e.tensor_tensor(out=Lx[:, ql, :], in0=cs[:, ql, :], in1=sw[:, ql, :], op=ALU.subtract),
                                   reads=R("cs%d" % ql, "sw%d" % ql), writes=["Lx%d" % ql])
                        else:
                            mk.dve(lambda e, ql=ql: e.tensor_tensor(out=v3(Lx[:, ql, :]), in0=v3(cs[:, ql, :])[:, :, 63:64].to_broadcast([128, 8, 64]),
                                                                    in1=v3(cs[:, ql, :]), op=ALU.subtract),
                                   reads=R("cs%d" % ql), writes=["Lx%d" % ql])
                            mk.dve(lambda e, ql=ql: e.tensor_tensor(out=cs[:, ql, :], in0=Lx[:, ql, :], in1=sw[:, ql, :], op=ALU.add),
                                   reads=R("Lx%d" % ql, "sw%d" % ql), writes=["cs%d" % ql])
                        mk.act(lambda e, ql=ql: e.activation(out=Ep[:, ql, :], in_=cs[:, ql, :], func=AF.Exp, scale=-C0), reads=R("cs%d" % ql), writes=["Ep%d" % ql])
                        mk.act(lambda e, ql=ql: e.activation(out=Lx[:, ql, :], in_=Lx[:, ql, :], func=AF.Exp, scale=-C0), reads=R("Lx%d" % ql), writes=["Lx%d" % ql])
                        mk.act(lambda e, ql=ql: e.activation(out=cs[:, ql, :], in_=cs[:, ql, :], func=AF.Exp, scale=C0), reads=R("cs%d" % ql, "Ep%d" % ql), writes=["cs%d" % ql])
                        pcol = 63 if dt_ == 0 else 0
                        mk.dve(lambda e, pr=pr, ql=ql, pcol=pcol: e.tensor_copy(out=Pend[:, pr, :], in_=v3(Ep[:, ql, :])[:, :, pcol]), reads=R("Ep%d" % ql), writes=["Pend"])
                        mk.dve(lambda e, pr=pr, ql=ql: e.scalar_tensor_tensor(out=AR[:, pr, :, 0:64], in0=v3(kk[:, pr, :]), scalar=-1.0, in1=v3(Lx[:, ql, :]),
                                                                              op0=ALU.mult, op1=ALU.mult), reads=R("kk%d" % pr, "Lx%d" % ql), writes=["AR%d" % pr])
                        mk.dve(lambda e, pr=pr, ql=ql: e.tensor_tensor(out=AR[:, pr, :, 64:128], in0=v3(rkv[:, pr, :]), in1=v3(Ep[:, ql, :]), op=ALU.mult),
                               reads=R("rkv%d" % pr, "Ep%d" % ql), writes=["AR%d" % pr])
                        mk.dve(lambda e, pr=pr, ql=ql: e.tensor_tensor(out=t1[:, ql, :], in0=kk[:, pr, :], in1=av[:, ql, :], op=ALU.mult),
                               reads=R("kk%d" % pr, "av%d" % ql), writes=["t1%d" % ql])
                        mk.dve(lambda e, pr=pr, ql=ql: e.tensor_tensor(out=BK[:, pr, :, 0:64], in0=v3(t1[:, ql, :]), in1=v3(cs[:, ql, :]), op=ALU.mult),
                               reads=R("t1%d" % ql, "cs%d" % ql), writes=["BK%d" % pr])
                        mk.dve(lambda e, pr=pr, ql=ql: e.tensor_scalar(out=t1[:, ql, :], in0=av[:, ql, :], scalar1=cnum(4), scalar2=ccol("ka", pr),
                                                                       op0=ALU.subtract, op1=ALU.mult), reads=R("av%d" % ql, "cols", "t1%d" % ql), writes=["t1%d" % ql])
                        mk.dve(lambda e, pr=pr, ql=ql: e.scalar_tensor_tensor(out=t1[:, ql, :], in0=t1[:, ql, :], scalar=1.0, in1=rkv[:, 4 + pr, :],
                                                                              op0=ALU.add, op1=ALU.mult), reads=R("t1%d" % ql, "rkv%d" % (4 + pr)), writes=["t1%d" % ql])
                        mk.dve(lambda e, pr=pr, ql=ql: e.tensor_tensor(out=BK[:, pr, :, 64:128], in0=v3(t1[:, ql, :]), in1=v3(cs[:, ql, :]), op=ALU.mult),
                               reads=R("t1%d" % ql, "cs%d" % ql), writes=["BK%d" % pr])
                if not mgo():
                    return
                barrier()
                pool.off = base_t
                Gm = [pool.bf16(4 * 256).rearrange("p (h n) -> p h n", h=4) for _ in range(2)]
                ZQb = [[pool.bf16(4 * 2 * 64).rearrange("p (h a n) -> p h a n", h=4, a=2) for _ in range(2)] for _ in range(2)]
                ZTb = [[pool.bf16(4 * 64).rearrange("p (h n) -> p h n", h=4) for _ in range(2)] for _ in range(2)]
                Qt = [pool.bf16(4 * 64).rearrange("p (h n) -> p h n", h=4) for _ in range(2)]
                TOK = [pool.bf16(3 * 4 * 64).rearrange("p (a h n) -> p a h n", a=3, h=4) for _ in range(2)]
                Wsb = pool.bf16(4 * 64).rearrange("p (h n) -> p h n", h=4)
                Usb = pool.bf16(4 * 64).rearrange("p (h n) -> p h n", h=4)
                Ytmp = pool.f32(4 * 64).rearrange("p (h n) -> p h n", h=4)
                unit = [0]

                def heads():
                    for q in range(4):
                        for e_ in range(2):
                            yield q, 64 * e_

                def tseries_stages(c, dt_=dt_):
                    u = unit[0] % 2
                    unit[0] += 1
                    G, Q, TK = Gm[u], Qt[u], TOK[u]
                    ZQ, ZT = ZQb[u], ZTb[u]
                    gk, zk, qk, tk = "Gm%d" % u, "ZZ%d" % u, "Q%d" % u, "TOK%d" % u
                    stages = []
                    psZa = psB[:, 0, :].rearrange("p (h n) -> p h n", h=4)
                    psZb = psB[:, 1, :].rearrange("p (h n) -> p h n", h=8)
                    KA = ["psB0", "psB0b"]

                    def st_g():
                        for q, fo in heads():
                            mm_group(psG[fo:fo + 64, q, 0:128], "psA%d" % (q // 2),
                                     [(BK[fo:fo + 64, q, c, 0:64], AR[fo:fo + 64, q, c, :], ["BK%d" % q, "AR%d" % q, "SCR"])])
                            mm_group(psG[fo:fo + 64, q, 128:256], "psA%d" % (q // 2),
                                     [(BK[fo:fo + 64, q, c, 64:128], AR[fo:fo + 64, q, c, :], ["BK%d" % q, "AR%d" % q, "SCR"])])
                            mm_group(psZb[fo:fo + 64, q, :], "psB1",
                                     [(AR[fo:fo + 64, q, c, 0:64], BK[fo:fo + 64, q, c, 0:64], ["BK%d" % q, "AR%d" % q, "SCR"])])
                        mk.dve(lambda e: e.tensor_tensor(out=G[:], in0=psG, in1=maskG(dt_).unsqueeze(1).to_broadcast([128, 4, 256]), op=ALU.mult),
                               reads=R("psA0", "psA1", "cst"), writes=[gk])
                        mk.dve(lambda e: e.tensor_tensor(out=ZT[0][:], in0=psZb[:, 0:4, :], in1=maskZ(dt_).unsqueeze(1).to_broadcast([128, 4, 64]),
                                                         op=ALU.mult), reads=R("psB1", "cst"), writes=[zk + "t0"])
                        mk.act(lambda e: e.copy(out=ZQ[0][:, :, 0, :], in_=G[:, :, 0:64]), reads=R(gk), writes=[zk + "q0"])
                        mk.act(lambda e: e.copy(out=ZQ[0][:, :, 1, :], in_=id64.unsqueeze(1).to_broadcast([128, 4, 64])), reads=R("cst"), writes=[zk + "q0"])
                    stages.append(st_g)

                    def mk_burst(lev):
                        cur, nxt = (lev - 1) % 2, lev % 2

                        def st():
                            for q, fo in heads():
                                if lev <= 4:
                                    mm_group(psZa[fo:fo + 64, q, :], KA, [(ZT[cur][fo:fo + 64, q, :], ZQ[cur][fo:fo + 64, q, :, :].rearrange("p a n -> p (a n)"),
                                                                          [zk + "t%d" % cur, zk + "q%d" % cur, "SCR"])])
                                else:
                                    mm_group(psZa[fo:fo + 64, q, 64:128], KA, [(ZT[cur][fo:fo + 64, q, :], ZQ[cur][fo:fo + 64, q, 1, :],
                                                                               [zk + "t%d" % cur, zk + "q%d" % cur, "SCR"])])
                                mm_group(psZb[fo:fo + 64, q, :], "psB1", [(ZQ[cur][fo:fo + 64, q, 0, :], ZT[cur][fo:fo + 64, q, :],
                                                                          [zk + "t%d" % cur, zk + "q%d" % cur, "SCR"])])
                            if lev <= 4:
                                mk.act(lambda e: e.copy(out=ZQ[nxt][:, :, 0, :], in_=psZa[:, :, 0:64]), reads=R(*KA), writes=[zk + "q%d" % nxt])
                            mk.dve(lambda e: e.tensor_tensor(out=ZQ[nxt][:, :, 1, :], in0=ZQ[cur][:, :, 1, :], in1=psZa[:, :, 64:128], op=ALU.add),
                                   reads=R(zk + "q%d" % cur, *KA), writes=[zk + "q%d" % nxt])
                            mk.act(lambda e: e.copy(out=ZT[nxt][:], in_=psZb[:, 0:4, :]), reads=R("psB1"), writes=[zk + "t%d" % nxt])
                        return st
                    for lev in range(1, 6):
                        stages.append(mk_burst(lev))

                    def st_last():
                        cur = 1
                        for q, fo in heads():
                            mm_group(psZa[fo:fo + 64, q, 64:128], KA, [(ZT[cur][fo:fo + 64, q, :], ZQ[cur][fo:fo + 64, q, 1, :],
                                                                       [zk + "t%d" % cur, zk + "q%d" % cur, "SCR"])])
                        mk.dve(lambda e: e.tensor_tensor(out=Q[:], in0=ZQ[cur][:, :, 1, :], in1=psZa[:, :, 64:128], op=ALU.add),
                               reads=R(zk + "q%d" % cur, *KA), writes=[qk])
                        for q, fo in heads():
                            idb = identb[fo:fo + 64, fo:fo + 64]
                            mm_group(psTv[fo:fo + 64, 0, q, :], "psT", [(BK[fo:fo + 64, q, c, 0:64], idb, ["BK%d" % q, "cst", "SCR"])])
                            mm_group(psTv[fo:fo + 64, 1, q, :], "psT", [(BK[fo:fo + 64, q, c, 64:128], idb, ["BK%d" % q, "cst", "SCR"])])
                            mm_group(psCv[fo:fo + 64, 1, 4 + q, :], "psCv", [(vbf[fo:fo + 64, q, c * 64:(c + 1) * 64], idb,
                                                                             ["vbf%d" % q, "cst", "SCR"])])
                        mk.act(lambda e: e.copy(out=TK[:, 0:2, :, :], in_=psTv), reads=R("psT"), writes=[tk])
                        mk.dve(lambda e: e.tensor_copy(out=TK[:, 2, :, :], in_=psCv[:, 1, 4:8, :]), reads=R("psCv"), writes=[tk])
                    stages.append(st_last)
                    return stages, (G, Q, TK, gk, qk, tk)

                def chain_stages(c, bufs, dt_=dt_, di=di, order=order):
                    G, Q, TK, gk, qk, tk = bufs
                    seq = c // cps
                    pos = order.index(c) % cps
                    stages = []

                    def st_w():
                        if pos == 0:
                            if kind == "prompt":
                                mk.dve(lambda e: e.memset(Mst[:], 0.0), reads=R(), writes=["Mst"])
                            elif kind == "aux":
                                if aux_idx == 0:
                                    mk.dve(lambda e: e.tensor_copy(out=Mst[:], in_=stt[:, 2, :, :]), reads=R("stt"), writes=["Mst"])
                                else:
                                    cc_ = COLS["coef"] + (aux_idx - 1)
                                    mk.dve(lambda e: e.scalar_tensor_tensor(out=Mst[:], in0=endst[:, aux_idx - 1, :, :], scalar=cols[:, cc_:cc_ + 1],
                                                                            in1=stt[:, 2 + aux_idx, :, :], op0=ALU.mult, op1=ALU.add),
                                           reads=R("stt", "end%d" % (aux_idx - 1), "cols"), writes=["Mst"])
                            else:
                                if dt_ == 0:
                                    mk.dve(lambda e: e.tensor_copy(out=Mst[:], in_=stt[:, 0, :, :]), reads=R("stt"), writes=["Mst"])
                                    for a in range(3):
                                        cc_ = COLS["coef"] + 2 + a
                                        mk.dve(lambda e, a=a, cc_=cc_: e.scalar_tensor_tensor(out=Mst[:], in0=endst[:, a, :, :], scalar=cols[:, cc_:cc_ + 1],
                                                                                          in1=Mst[:], op0=ALU.mult, op1=ALU.add),
                                               reads=R("end%d" % a, "cols", "Mst"), writes=["Mst"])
                                else:
                                    cc_ = COLS["coef"] + 5
                                    mk.dve(lambda e: e.scalar_tensor_tensor(out=Mst[:], in0=endst[:, 2, :, :], scalar=cols[:, cc_:cc_ + 1],
                                                                            in1=stt[:, 1, :, :], op0=ALU.mult, op1=ALU.add),
                                           reads=R("stt", "end2", "cols"), writes=["Mst"])
                        if pos == 0:
                            mk.act(lambda e: e.copy(out=Mbf[:], in_=Mst[:]), reads=R("Mst"), writes=["Mbf"])
                        for q, fo in heads():
                            mm_group(psCv[fo:fo + 64, 0, q, :], "psC0w",
                                     [(AR[fo:fo + 64, q, c, 0:64], Mbf[fo:fo + 64, q, :], ["AR%d" % q, "Mbf", "SCR"]),
                                      (G[fo:fo + 64, q, 128:192], TK[fo:fo + 64, 2, q, :], [gk, tk, "SCR"])])
                        mk.act(lambda e: e.copy(out=Wsb[:], in_=psCv[:, 0, 0:4, :]), reads=R("psC0w"), writes=["Wsb"])
                    stages.append(st_w)

                    def st_u():
                        for q, fo in heads():
                            mm_group(psCv[fo:fo + 64, 0, 4 + q, :], "psC0u", [(Q[fo:fo + 64, q, :], Wsb[fo:fo + 64, q, :], [qk, "Wsb", "SCR"])])
                        mk.dve(lambda e: e.tensor_copy(out=Usb[:], in_=psCv[:, 0, 4:8, :]), reads=R("psC0u"), writes=["Usb"])
                    stages.append(st_u)

                    def st_ym():
                        for q, fo in heads():
                            if full and KV != 5:
                                mm_group(psSv[fo:fo + 64, q, :], "psS",
                                         [(Mbf[fo:fo + 64, q, :], AR[fo:fo + 64, q, c, 64:128], ["Mbf", "AR%d" % q, "SCR"]),
                                          (Usb[fo:fo + 64, q, :], G[fo:fo + 64, q, 64:128], ["Usb", gk, "SCR"]),
                                          (TK[fo:fo + 64, 2, q, :], G[fo:fo + 64, q, 192:256], [tk, gk, "SCR"])])
                            mm_group(psSv[fo:fo + 64, 4 + q, :], "psS",
                                     [(TK[fo:fo + 64, 0, q, :], Usb[fo:fo + 64, q, :], [tk, "Usb", "SCR"]),
                                      (TK[fo:fo + 64, 1, q, :], TK[fo:fo + 64, 2, q, :], [tk, "SCR"])])
                        if full and KV != 6:
                            ydst = yT[:, :, c * 64:(c + 1) * 64]
                            if di == 0 and KV != 9:
                                mk.dve(lambda e: e.tensor_copy(out=ydst, in_=psSv[:, 0:4, :]), reads=R("psS"), writes=["yT"])
                            elif di == 0 and KV == 8:
                                mk.act(lambda e: e.copy(out=Wsb[:], in_=psSv[:, 0:4, :]), reads=R("psS", "Wsb"), writes=["Wsb"])
                                mk.dve(lambda e: e.tensor_copy(out=ydst, in_=Wsb[:]), reads=R("Wsb"), writes=["yT"])
                            elif di == 0:
                                mk.act(lambda e: e.copy(out=ydst, in_=psSv[:, 0:4, :]), reads=R("psS"), writes=["yT"])
                            else:
                                mk.act(lambda e: e.copy(out=Ytmp[:], in_=psSv[:, 0:4, :]), reads=R("psS", "Ytmp"), writes=["Ytmp"])
                                mk.dve(lambda e: e.tensor_tensor(out=ydst, in0=ydst, in1=Ytmp[:], op=ALU.add), reads=R("Ytmp", "yT"), writes=["yT"])
                        mk.dve(lambda e: e.tensor_tensor(out=Mtmp[:], in0=Mst[:], in1=psSv[:, 4:8, :], op=ALU.add),
                               reads=R("psS", "Mst"), writes=["Mtmp"])
                        mk.dve(lambda e: e.tensor_tensor(out=Mst[:], in0=Mtmp[:], in1=Pend[:, :, c:c + 1].to_broadcast([128, 4, 64]), op=ALU.mult),
                               reads=R("Mtmp", "Pend"), writes=["Mst"])
                        mk.act(lambda e: e.copy(out=Mbf[:], in_=Mst[:]), reads=R("Mst"), writes=["Mbf"])
                        if pos == cps - 1:
                            if kind == "aux":
                                mk.act(lambda e: e.copy(out=endst[:, aux_idx, :, :], in_=Mst[:]), reads=R("Mst"), writes=["end%d" % aux_idx])
                            elif kind == "prompt":
                                for q in range(4):
                                    mm_group(psT[0:64, q * 128:(q + 1) * 128], "psT", [(Mst[:, q, :], ident, ["Mst", "cst", "SCR"])])
                                mk.act(lambda e: e.copy(out=nsb[0:64, seq, di, :, :], in_=psT[0:64, :].rearrange("p (a n) -> p a n", a=4)),
                                       reads=R("psT"), writes=["nsb"])
                    stages.append(st_ym)
                    return stages

                prev = None
                for idx_c in range(len(order) + 1):
                    A, bufsA = ([], None)
                    if idx_c < len(order):
                        A, bufsA = tseries_stages(order[idx_c])
                    Bs = []
                    if prev is not None:
                        Bs = chain_stages(prev[0], prev[1])
                    for i in range(max(len(A), len(Bs))):
                        if i < len(A):
                            KTC[0] += 1
                            if KTC[0] <= KT:
                                A[i]()
                        if i < len(Bs):
                            KTC[0] += 1
                            if KTC[0] <= KT:
                                Bs[i]()
                    prev = (order[idx_c], bufsA) if idx_c < len(order) else None
            if not full:
                return
            tap("yT_" + kind, yT.rearrange("p q n -> p (q n)"), 4 * N, R("yT"))
            tap("rkv_" + kind, rkv.rearrange("p q n -> p (q n)"), 12 * N, R(*["rkv%d" % i for i in range(12)]))
            barrier()
            MARK[kind] = len(mk.ops)
            pool.off = base_d
            bv = pool.f32(4 * N).rearrange("p (q n) -> p q n", q=4)
            tA = pool.f32(2 * N).rearrange("p (q n) -> p q n", q=2)
            tB = pool.f32(2 * N).rearrange("p (q n) -> p q n", q=2)
            for pr in range(4):
                b = pr % 2
                mk.dve(lambda e, pr=pr, b=b: e.tensor_scalar(out=tA[:, b, :], in0=asum[:, pr, :], scalar1=cnum(5), scalar2=ccol("ka", pr), op0=ALU.subtract, op1=ALU.mult),
                       reads=R("asum%d" % pr, "cols", "tA%d" % b), writes=["tA%d" % b])
                mk.dve(lambda e, pr=pr, b=b: e.scalar_tensor_tensor(out=tA[:, b, :], in0=tA[:, b, :], scalar=2.0, in1=rkv[:, 4 + pr, :], op0=ALU.add, op1=ALU.mult),
                       reads=R("tA%d" % b, "rkv%d" % (4 + pr)), writes=["tA%d" % b])
                mk.dve(lambda e, pr=pr, b=b: e.scalar_tensor_tensor(out=tA[:, b, :], in0=tA[:, b, :], scalar=ccol("rk", pr), in1=rkv[:, pr, :], op0=ALU.mult, op1=ALU.mult),
                       reads=R("tA%d" % b, "rkv%d" % pr, "cols"), writes=["tA%d" % b])
                mm_group(psA[:, b, :], "psA%d" % b, [(bones, tA[:, b, :], ["cst", "tA%d" % b, "SCR"])])
                mk.dve(lambda e, pr=pr, b=b: e.tensor_tensor(out=bv[:, pr, :], in0=psA[:, b, :], in1=rkv[:, 8 + pr, :], op=ALU.mult),
                       reads=R("psA%d" % b, "rkv%d" % (8 + pr)), writes=["bv%d" % pr])
                mm_group(psB[:, b, :], KB(b), [(bones64, yT[:, pr, :], ["cst", "yT", "SCR"])])
                mk.act(lambda e, b=b: e.copy(out=tB[:, b, :], in_=psB[:, b, :]), reads=R("tB%d" % b, *KB(b)), writes=["tB%d" % b])
                mk.dve(lambda e, pr=pr, b=b: e.tensor_tensor(out=yT[:, pr, :], in0=yT[:, pr, :], in1=tB[:, b, :], op=ALU.subtract), reads=R("tB%d" % b, "yT"), writes=["yT"])
                mk.act(lambda e, pr=pr, b=b: e.activation(out=tA[:, b, :], in_=yT[:, pr, :], func=AF.Square), reads=R("yT", "tA%d" % b), writes=["tA%d" % b])
                mm_group(psA[:, b, :], "psA%d" % b, [(bones64, tA[:, b, :], ["cst", "tA%d" % b, "SCR"])])
                mk.act(lambda e, b=b: e.activation(out=tB[:, b, :], in_=psA[:, b, :], func=AF.Ln, bias=cnum(2), scale=1.0), reads=R("cols", "tB%d" % b, "psA%d" % b), writes=["tB%d" % b])
                mk.act(lambda e, b=b: e.activation(out=tB[:, b, :], in_=tB[:, b, :], func=AF.Exp, scale=-0.5), reads=R("tB%d" % b), writes=["tB%d" % b])
                mk.dve(lambda e, pr=pr, b=b: e.scalar_tensor_tensor(out=yT[:, pr, :], in0=yT[:, pr, :], scalar=ccol("gng", pr), in1=tB[:, b, :], op0=ALU.mult, op1=ALU.mult),
                       reads=R("tB%d" % b, "yT", "cols"), writes=["yT"])
                mk.dve(lambda e, pr=pr: e.scalar_tensor_tensor(out=yT[:, pr, :], in0=yT[:, pr, :], scalar=ccol("gnb", pr), in1=bv[:, pr, :], op0=ALU.add, op1=ALU.add),
                       reads=R("bv%d" % pr, "yT", "cols"), writes=["yT"])
            tap("yn_" + kind, yT.rearrange("p q n -> p (q n)"), 4 * N, R("yT"))
            barrier()
            pool.off = 2048
            yA = pool.f32(8 * N).rearrange("p (q n) -> p q n", q=8)
            pool.off = 12288
            yB = pool.f32(8 * N).rearrange("p (q n) -> p q n", q=8)
            tA2 = pool.f32(2 * N).rearrange("p (q n) -> p q n", q=2)
            tB2 = pool.f32(2 * N).rearrange("p (q n) -> p q n", q=2)
            yaT = pool.bf16(4 * N).rearrange("p (q n) -> p q n", q=4)
            ybT = pool.bf16(4 * N).rearrange("p (q n) -> p q n", q=4)
            cbT = pool.bf16(4 * N).rearrange("p (q n) -> p q n", q=4)
            ccT = pool.f32(4 * N).rearrange("p (q n) -> p q n", q=4)
            mgT = pool.bf16(8 * N).rearrange("p (q n) -> p q n", q=8)
            rl = 64 if kind == "own" else L
            r3 = lambda ap: ap.rearrange("p (r n) -> p r n", n=rl)

            def branch(wmat, srcT, srckey, dst, dstkey):
                for half in range(2):
                    slw, keyw = load_slab([(lambda t: t[:, 0:2048].rearrange("p (k n) -> p k n", k=4),
                                            wmat[:, half * 512:(half + 1) * 512].rearrange("(k p) n -> p k n", p=128))])
                    slwv = slw[:, 0:2048].rearrange("p (k n) -> p k n", k=4)
                    for tt in range(4):
                        o = half * 4 + tt
                        b = tt % 2
                        mm_group(psB[:, b, :], KB(b), [(slwv[:, k, tt * 128:(tt + 1) * 128], srcT[:, k, :], [keyw, srckey % k, "SCR"]) for k in range(4)])
                        evac(dst[:, o, :], psB[:, b, :], R(*KB(b)), [dstkey % o])

            for s in range(3, 11):
                sl, key = load_slab([(lambda t: t[:, 0:4096].rearrange("p (k n) -> p k n", k=8),
                                     w_in[:, s * 512:(s + 1) * 512].rearrange("(k p) n -> p k n", p=128))])
                slv = sl[:, 0:4096].rearrange("p (k n) -> p k n", k=8)
                for tt in range(4):
                    b = tt % 2
                    mm_group(psA[:, b, :], "psA%d" % b,
                             [(slv[:, k, tt * 128:(tt + 1) * 128], hT[:, k, :], [key, "hT%d" % k, "SCR"]) for k in range(8)])
                    pa = psA[:, b, :]
                    pk = "psA%d" % b
                    tb = tt % 2
                    if s == 3:
                        mk.act(lambda e, pa=pa, tb=tb: e.activation(out=tA2[:, tb, :], in_=pa, func=AF.Sigmoid), reads=R(pk, "tA2%d" % tb), writes=["tA2%d" % tb])
                        mk.dve(lambda e, tt=tt, tb=tb: e.tensor_tensor(out=yaT[:, tt, :], in0=yT[:, tt, :], in1=tA2[:, tb, :], op=ALU.mult),
                               reads=R("yT", "tA2%d" % tb), writes=["yaT%d" % tt])
                    elif s == 4:
                        mk.act(lambda e, tt=tt, pa=pa: e.copy(out=cbT[:, tt, :], in_=pa), reads=R(pk), writes=["cbT%d" % tt])
                    elif s == 5:
                        mk.act(lambda e, tt=tt, pa=pa: e.copy(out=ccT[:, tt, :], in_=pa), reads=R(pk), writes=["ccT%d" % tt])
                    elif s == 6:
                        u = tA2[:, tb, :]
                        uk = "tA2%d" % tb
                        acc = tB2[:, tb, :]
                        ak = "tB2%d" % tb
                        mk.dve(lambda e, tt=tt, pa=pa, u=u: e.tensor_tensor(out=u, in0=ccT[:, tt, :], in1=pa, op=ALU.mult), reads=R(pk, "ccT%d" % tt, uk), writes=[uk])
                        mk.dve(lambda e, tt=tt, u=u, acc=acc: e.tensor_scalar(out=acc, in0=u, scalar1=ccol("cw", 4 + tt), scalar2=ccol("cb", tt), op0=ALU.mult, op1=ALU.add),
                               reads=R(uk, "cols", ak), writes=[ak])
                        mk.dve(lambda e, tt=tt, u=u, acc=acc: e.scalar_tensor_tensor(out=r3(acc)[:, :, 1:rl], in0=r3(u)[:, :, 0:rl - 1], scalar=ccol("cw", tt),
                                                                                    in1=r3(acc)[:, :, 1:rl], op0=ALU.mult, op1=ALU.add), reads=R(uk, ak, "cols"), writes=[ak])
                        mk.dve(lambda e, tt=tt, u=u, acc=acc: e.scalar_tensor_tensor(out=r3(acc)[:, :, 0:rl - 1], in0=r3(u)[:, :, 1:rl], scalar=ccol("cw", 8 + tt),
                                                                                    in1=r3(acc)[:, :, 0:rl - 1], op0=ALU.mult, op1=ALU.add), reads=R(uk, ak, "cols"), writes=[ak])
                        mk.dve(lambda e, tt=tt, acc=acc: e.tensor_tensor(out=ybT[:, tt, :], in0=cbT[:, tt, :], in1=acc, op=ALU.mult), reads=R(ak, "cbT%d" % tt), writes=["ybT%d" % tt])
                    else:
                        gi = (s - 7) * 4 + tt
                        sg = tA2[:, tb, :]
                        sk = "tA2%d" % tb
                        mk.act(lambda e, pa=pa, sg=sg: e.activation(out=sg, in_=pa, func=AF.Sigmoid), reads=R(pk, sk), writes=[sk])
                        if gi < 8:
                            mk.dve(lambda e, gi=gi, sg=sg: e.tensor_tensor(out=yA[:, gi, :], in0=yA[:, gi, :], in1=sg, op=ALU.mult), reads=R(sk, "yA%d" % gi), writes=["yA%d" % gi])
                        else:
                            g2 = gi - 8
                            mk.dve(lambda e, g2=g2, sg=sg: e.tensor_tensor(out=sg, in0=sg, in1=yB[:, g2, :], op=ALU.mult), reads=R(sk, "yB%d" % g2), writes=[sk])
                            mk.dve(lambda e, g2=g2, sg=sg: e.tensor_tensor(out=mgT[:, g2, :], in0=yA[:, g2, :], in1=sg, op=ALU.add), reads=R(sk, "yA%d" % g2), writes=["mgT%d" % g2])
                if s == 3:
                    barrier()
                    branch(wba, yaT, "yaT%d", yA, "yA%d")
                if s == 6:
                    branch(wbb, ybT, "ybT%d", yB, "yB%d")
            oT = pool.f32(8 * N).rearrange("p (k n) -> p k n", k=8)
            pool.off = 6144
            sq = pool.bf16(8 * N).rearrange("p (k n) -> p k n", k=8)
            tmp = pool.f32(2 * N).rearrange("p (k n) -> p k n", k=2)
            for half in range(2):
                sl, key = load_slab([(lambda t: t[:, 0:4096].rearrange("p (k n) -> p k n", k=8),
                                     wout[:, half * 512:(half + 1) * 512].rearrange("(k p) n -> p k n", p=128))])
                slv = sl[:, 0:4096].rearrange("p (k n) -> p k n", k=8)
                for tt in range(4):
                    i = half * 4 + tt
                    b = tt % 2
                    mm_group(psA[:, b, :], "psA%d" % b, [(slv[:, k, tt * 128:(tt + 1) * 128], mgT[:, k, :], [key, "mgT%d" % k, "SCR"]) for k in range(8)])
                    evac(oT[:, i, :], psA[:, b, :], R("psA%d" % b), ["oT"])
            tap("mo_" + kind, oT.rearrange("p q n -> p (q n)"), 8 * N, R("oT"))
            postnorm_residual(j, 3, oT, sq, tmp)

        stt = sb("stt", [128, 5, 4, 64])
        nsb = sb("nsb", [64, 2, 2, 4, 128])
        mk.dma("sp", "c0", lambda e: e.dma_start(out=stt[:], in_=std.rearrange("a q p v -> p a q v")), writes=["stt"])

        def load_x(g):
            pool = Pool()
            xin = pool.f32(4 * D).rearrange("p (t n) -> p t n", t=4)
            for tt in range(4):
                mk.dma("sp", "xin", lambda e, tt=tt: e.dma_start(out=xin[:, tt, :], in_=xg[g, tt * 128:(tt + 1) * 128, :]), reads=["SCR"], writes=["xin%d" % tt])
            for k in range(8):
                b = k % 2
                for tt in range(4):
                    mm_group(psA[:, b, tt * 128:(tt + 1) * 128], "psA%d" % b, [(xin[:, tt, k * 128:(k + 1) * 128], ident, ["xin%d" % tt, "cst", "SCR"])])
                evac(xT[:, k, :], psA[:, b, :], R("psA%d" % b), ["xT"])

        def store_y(gout):
            pool = Pool()
            yo = pool.f32(4 * D).rearrange("p (t n) -> p t n", t=4)
            for tt in range(4):
                for k in range(8):
                    b = k % 2
                    mm_group(psA[:, b, 0:128], "psA%d" % b, [(xT[:, k, tt * 128:(tt + 1) * 128], ident, ["xT", "cst", "SCR"])])
                    evac(yo[:, tt, k * 128:(k + 1) * 128], psA[:, b, 0:128], R("psA%d" % b), ["yo%d" % tt])
                mk.dma("sp", "yout", lambda e, tt=tt: e.dma_start(out=yout[gout * N + tt * 128:gout * N + (tt + 1) * 128, :], in_=yo[:, tt, :]), reads=R("yo%d" % tt))

        def ffn_phase(j, ai, bi, gi, w13, w2):
            pool = Pool()
            hT = pool.bf16(8 * N).rearrange("p (k n) -> p k n", k=8)
            sq = pool.bf16(8 * N).rearrange("p (k n) -> p k n", k=8)
            hid = pool.bf16(22 * N).rearrange("p (k n) -> p k n", k=22)
            oT = pool.f32(8 * N).rearrange("p (k n) -> p k n", k=8)
            tmp = pool.f32(2 * N).rearrange("p (k n) -> p k n", k=2)
            sgt = pool.f32(4 * N).rearrange("p (k n) -> p k n", k=4)
            prenorm(j, ai, bi, hT, sq, tmp)
            ffn(w13, w2, hT, hid, oT, sgt)
            postnorm_residual(j, gi, oT, sq, tmp)

        def mixer_phase(kind, j, aux_idx):
            pool = Pool()
            hT = pool.bf16(8 * N).rearrange("p (k n) -> p k n", k=8)
            tail = Pool()
            tail.off = SCR - (8 * N // 2 + 2 * N)
            sq = tail.bf16(8 * N).rearrange("p (k n) -> p k n", k=8)
            tmp = tail.f32(2 * N).rearrange("p (k n) -> p k n", k=2)
            prenorm(j, 2, 3, hT, sq, tmp)
            barrier()
            mixer(kind, j, hT, aux_idx)

        def chain_prep_aux(a):
            cc_ = COLS["coef"] + (a - 1)
            mk.dve(lambda e: e.tensor_scalar(out=hbA[:], in0=hlast[:, a - 1, :], scalar1=cols[:, cc_:cc_ + 1], scalar2=None, op0=ALU.mult),
                   reads=["hlast%d" % (a - 1), "cols"], writes=["hbA"])

        def chain_prep_own():
            c2 = COLS["coef"] + 2
            mk.dve(lambda e: e.tensor_scalar(out=hbF[:], in0=hlast[:, 0, :], scalar1=cols[:, c2:c2 + 1], scalar2=None, op0=ALU.mult),
                   reads=["hlast0", "cols"], writes=["hbF"])
            for a in (1, 2):
                mk.dve(lambda e, a=a: e.scalar_tensor_tensor(out=hbF[:], in0=hlast[:, a, :], scalar=cols[:, c2 + a:c2 + a + 1], in1=hbF[:], op0=ALU.mult, op1=ALU.add),
                       reads=["hlast%d" % a, "cols", "hbF"], writes=["hbF"])
            mk.dve(lambda e: e.tensor_scalar(out=hbB[:], in0=hlast[:, 2, :], scalar1=cols[:, c2 + 3:c2 + 4], scalar2=None, op0=ALU.mult),
                   reads=["hlast2", "cols"], writes=["hbB"])

        for a in range(3):
            if go():
                barrier()
                load_x(2 + a)
            if go():
                barrier()
                ffn_phase(1, 0, 0, 1, f1w13, f1w2)
            if go():
                barrier()
                if a > 0:
                    chain_prep_aux(a)
                mixer_phase("aux", 1, a)
        for (g, kind, j) in [(1, "own", 1), (0, "prompt", 0)]:
            if go():
                barrier()
                load_x(g)
            if go():
                barrier()
                ffn_phase(j, 0, 0, 1, f1w13, f1w2)
            if go():
                barrier()
                if kind == "own":
                    chain_prep_own()
                mixer_phase(kind, j, None)
            if go():
                barrier()
                ffn_phase(j, 4, 6, 5, f2w13, f2w2)
            if go():
                barrier()
                store_y(1 if kind == "own" else 0)
        if dbg_spec is not None:
            barrier()
            dbg_spec(mk, nc, locals())
        for sq_ in range(2):
            for dd in range(2):
                mk.dma("sp", "yout", lambda e, sq_=sq_, dd=dd: e.dma_start(out=nsout[sq_, dd].rearrange("(q e) v k -> v q e k", e=2),
                                                                           in_=nsb[:, sq_, dd, :, :].rearrange("p q (e k) -> p q e k", e=2)), reads=["nsb"])
        stats = mk.emit()
    return nc, stats


_CACHE = {}


def _prep_inputs(inp):
    f = lambda a: np.ascontiguousarray(np.asarray(a, np.float32))
    x_prompt, x_sample = f(inp["x_prompt"]), f(inp["x_sample"])
    c, state, c_ctx = f(inp["c"]), f(inp["state_rwkv"]), f(inp["c_ctx"])
    mu = f(inp["mu_shift"])[0]
    w1 = [np.concatenate([f(inp["decay_w1"])[0, d], f(inp["iclr_a1"])[0, d]], axis=1) for d in range(2)]
    w2 = [np.concatenate([f(inp["decay_w2"])[0, d], f(inp["iclr_a2"])[0, d]], axis=0) for d in range(2)]
    dw0, ia0 = f(inp["decay_w0"])[0], f(inp["iclr_a0"])[0]
    shared = {
        "w_mod": f(inp["w_mod"])[0], "f1w13": f(inp["ffn1_w13"])[0], "f1w2": f(inp["ffn1_w2"])[0],
        "f2w13": f(inp["ffn2_w13"])[0], "f2w2": f(inp["ffn2_w2"])[0], "w_in": f(inp["w_in"])[0],
        "wba": f(inp["w_branch_a"])[0], "wbb": f(inp["w_branch_b"])[0], "wout": f(inp["w_out"])[0],
        "consts": _make_consts(),
    }
    in_maps = []
    for core in range(8):
        b, s = core // 4, core % 4
        if s == 0:
            aux = [(3, 1), (2, 1), (1, 1)]
            cont = (1.0, 1.0)
            selF = (0.0, 0.0, 0.0)
            selB = 1.0
            init = ["F", "0", "B", "0", "0"]
        elif s == 1:
            aux = [(0, 0), (3, 1), (2, 1)]
            cont = (0.0, 1.0)
            selF = (1.0, 0.0, 0.0)
            selB = 1.0
            init = ["0", "0", "F", "B", "0"]
        elif s == 2:
            aux = [(0, 0), (1, 0), (3, 1)]
            cont = (1.0, 0.0)
            selF = (0.0, 1.0, 0.0)
            selB = 1.0
            init = ["0", "0", "F", "0", "B"]
        else:
            aux = [(0, 0), (1, 0), (2, 0)]
            cont = (1.0, 1.0)
            selF = (0.0, 0.0, 1.0)
            selB = 0.0
            init = ["0", "B", "F", "0", "0"]
        xgr = np.empty((5, N, D), np.float32)
        xgr[0] = x_prompt[2 * core:2 * core + 2].reshape(N, D)
        xgr[1] = x_sample[b, s * N:(s + 1) * N]
        for a, (seg, dr) in enumerate(aux):
            xs = x_sample[b, seg * N:(seg + 1) * N]
            xgr[2 + a] = xs[::-1] if dr == 1 else xs
        dsl = [0, 1] + [dr for (_, dr) in aux]
        cols = np.zeros((128, NCOL), np.float32)
        cvs = [c_ctx, c[b]]
        for k in range(8):
            for j in range(2):
                cols[:, COLS["cv"] + k * 2 + j] = cvs[j][k * 128:(k + 1) * 128]
        cols[:, COLS["bmod"]:COLS["bmod"] + 72] = _colize(inp["b_mod"][0])
        cols[:, COLS["ng"]:COLS["ng"] + 48] = _colize(np.asarray(inp["norm_g"][0]).reshape(-1))
        for ds, dr in enumerate(dsl):
            for x in range(2):
                cols[:, COLS["mu"] + ds * 16 + x * 8:COLS["mu"] + ds * 16 + x * 8 + 8] = _colize(mu[dr, x])
            cols[:, COLS["w0"] + ds * 4:COLS["w0"] + ds * 4 + 4] = _colize(dw0[dr])
            cols[:, COLS["a0"] + ds * 4:COLS["a0"] + ds * 4 + 4] = _colize(ia0[dr])
        cols[:, COLS["kk"]:COLS["kk"] + 4] = _colize(inp["k_k"][0])
        cols[:, COLS["ka"]:COLS["ka"] + 4] = _colize(inp["k_a"][0])
        cols[:, COLS["rk"]:COLS["rk"] + 4] = _colize(np.asarray(inp["r_k"][0]).reshape(-1))
        cols[:, COLS["gng"]:COLS["gng"] + 4] = _colize(inp["gn_gain"][0])
        cols[:, COLS["gnb"]:COLS["gnb"] + 4] = _colize(inp["gn_bias"][0])
        cols[:, COLS["cw"]:COLS["cw"] + 12] = _colize(np.asarray(inp["conv_w"][0]).reshape(-1))
        cols[:, COLS["cb"]:COLS["cb"] + 4] = _colize(inp["conv_b"][0])
        cols[:, COLS["coef"]:COLS["coef"] + 6] = np.array([cont[0], cont[1], selF[0], selF[1], selF[2], selB], np.float32)[None, :]
        cols[:, COLS["num"]:COLS["num"] + 6] = np.array([1e-6, 1e-12, 64e-5, 0.0, 1.0, 2.0], np.float32)[None, :]
        def mlay(d):
            S = state[b, 0, d]
            return np.ascontiguousarray(S.transpose(0, 2, 1).reshape(4, 128, 64))
        stv = np.zeros((5, 4, 128, 64), np.float32)
        for i, t in enumerate(init):
            if t == "F":
                stv[i] = mlay(0)
            elif t == "B":
                stv[i] = mlay(1)
        m = dict(shared)
        m.update({"xg": xgr, "cols": cols, "st": stv,
                  "w1c": np.ascontiguousarray(np.stack([w1[d] for d in dsl])),
                  "w2c": np.ascontiguousarray(np.stack([w2[d] for d in dsl]))})
        in_maps.append(m)
    return in_maps


def kernel(**inputs):
    if "nc" not in _CACHE:
        _CACHE["nc"], _CACHE["stats"] = build_program(int(os.environ.get("KLIMIT", str(10 ** 9))))
    nc = _CACHE["nc"]
    in_maps = _prep_inputs(inputs)
    res = run_bass_kernel_spmd(nc, in_maps, core_ids=list(range(8)))
    y_prompt = np.empty((16, 256, D), np.float32)
    y_sample = np.empty((2, 2048, D), np.float32)
    new_state = np.empty((16, 1, 2, 8, 64, 64), np.float32)
    for core in range(8):
        r = res.results[core]
        b, s = core // 4, core % 4
        y = np.asarray(r["y"], np.float32)
        y_prompt[2 * core:2 * core + 2] = y[0:N].reshape(2, 256, D)
        y_sample[b, s * N:(s + 1) * N] = y[N:2 * N]
        new_state[2 * core:2 * core + 2, 0] = np.asarray(r["ns"], np.float32)
    return (y_prompt, y_sample, new_state)
```

```python
import contextlib
import numpy as np
import concourse.bass as bass
import concourse.mybir as mybir
from concourse.bass_utils import run_bass_kernel_spmd

F32 = mybir.dt.float32
BF16 = mybir.dt.bfloat16
ALU = mybir.AluOpType
AF = mybir.ActivationFunctionType

D = 1024
DFF = 2816
import os
SUB = int(os.environ.get('KSUB', '99'))
KT = int(os.environ.get('KT', '1000000'))
KTC = [0]
MARK = {}
TAPS = [t for t in os.environ.get('KTAPS', '').split(',') if t]
KV = int(os.environ.get('KV', '0'))
N = 512
C = 64
NCH = N // C
C0 = float(np.exp(-0.5))


class _Op:
    __slots__ = ("idx", "eng", "fn", "deps", "chan", "chanpos", "needs_inc", "inc_count", "engpos")

    def __init__(self, idx, eng, fn, deps, chan):
        self.idx = idx
        self.eng = eng
        self.fn = fn
        self.deps = deps
        self.chan = chan
        self.chanpos = None
        self.needs_inc = False
        self.inc_count = None
        self.engpos = None


class MK:
    ENGS = ("pe", "act", "dve", "pool", "sp")

    def __init__(self, nc):
        self.nc = nc
        self.ops = []
        self.last_writer = {}
        self.readers = {}
        self.chan_count = {}

    def add(self, eng, fn, reads=(), writes=(), chan=None):
        idx = len(self.ops)
        deps = set()
        writes = list(writes)
        if chan is not None:
            writes.append(("__chan__", chan))
        for r in reads:
            w = self.last_writer.get(r)
            if w is not None:
                deps.add(w)
        for w in writes:
            lw = self.last_writer.get(w)
            if lw is not None:
                deps.add(lw)
            deps.update(self.readers.get(w, ()))
        op = _Op(idx, eng, fn, deps, chan)
        if chan is not None:
            op.chanpos = self.chan_count.get(chan, 0)
            self.chan_count[chan] = op.chanpos + 1
        self.ops.append(op)
        for r in reads:
            self.readers.setdefault(r, []).append(idx)
        for w in writes:
            self.last_writer[w] = idx
            self.readers[w] = []
        return idx

    def pe(self, fn, reads=(), writes=()):
        return self.add("pe", fn, reads, writes)

    def act(self, fn, reads=(), writes=()):
        return self.add("act", fn, reads, writes)

    def dve(self, fn, reads=(), writes=()):
        return self.add("dve", fn, reads, writes)

    def pool(self, fn, reads=(), writes=()):
        return self.add("pool", fn, reads, writes)

    def dma(self, eng, chan, fn, reads=(), writes=()):
        return self.add(eng, fn, reads, writes, chan=chan)

    def emit(self):
        nc = self.nc
        ops = self.ops
        per_eng = {e: [] for e in self.ENGS}
        for op in ops:
            op.engpos = len(per_eng[op.eng])
            per_eng[op.eng].append(op)

        def need_sem(op, d):
            if d.chan is not None:
                return True
            if d.eng != op.eng:
                return True
            if op.eng == "pe":
                return False
            return True

        for op in ops:
            latest = {}
            keep = set()
            for di in op.deps:
                d = ops[di]
                if d.chan is not None:
                    keep.add(di)
                else:
                    cur = latest.get(d.eng)
                    if cur is None or ops[cur].engpos < d.engpos:
                        latest[d.eng] = di
            keep.update(latest.values())
            op.deps = keep
        for op in ops:
            for di in op.deps:
                d = ops[di]
                if d.chan is None and need_sem(op, d):
                    d.needs_inc = True
        cnt = {e: 0 for e in self.ENGS}
        for op in ops:
            if op.chan is None and op.needs_inc:
                cnt[op.eng] += 1
                op.inc_count = cnt[op.eng]
        chans = sorted(self.chan_count.keys())
        with contextlib.ExitStack() as st:
            esem = {e: st.enter_context(nc.semaphore("s_" + e)) for e in self.ENGS}
            csem = {c: st.enter_context(nc.semaphore("c_" + str(c))) for c in chans}
            block = st.enter_context(nc.Block())

            def run_engine(ename, eobj):
                waited = {}

                def wait(key, sem, val):
                    if waited.get(key, 0) >= val:
                        return
                    waited[key] = val
                    eobj.wait_ge(sem, val)

                for op in per_eng[ename]:
                    for di in sorted(op.deps):
                        d = ops[di]
                        if not need_sem(op, d):
                            continue
                        if d.chan is not None:
                            wait(("c", d.chan), csem[d.chan], 16 * (d.chanpos + 1))
                        else:
                            wait(("e", d.eng), esem[d.eng], d.inc_count)
                    ins = op.fn(eobj)
                    if op.chan is not None:
                        ins.then_inc(csem[op.chan], 16)
                    elif op.needs_inc:
                        ins.then_inc(esem[op.eng], 1)
                if ename == "sp":
                    for c in chans:
                        wait(("c", c), csem[c], 16 * self.chan_count[c])

            @block.tensor
            def _(e):
                run_engine("pe", e)

            @block.scalar
            def _(e):
                run_engine("act", e)

            @block.vector
            def _(e):
                run_engine("dve", e)

            @block.gpsimd
            def _(e):
                run_engine("pool", e)

            @block.sync
            def _(e):
                run_engine("sp", e)
        return {e: len(v) for e, v in per_eng.items()}


def _colize(v):
    v = np.asarray(v, np.float32).reshape(-1, 128)
    return np.ascontiguousarray(v.T)


COLS = {}
_off = 0
for _name, _w in [("cv", 16), ("bmod", 72), ("ng", 48), ("mu", 80), ("w0", 20), ("a0", 20), ("kk", 4), ("ka", 4),
                  ("rk", 4), ("gng", 4), ("gnb", 4), ("cw", 12), ("cb", 4), ("coef", 8), ("num", 8)]:
    COLS[_name] = _off
    _off += _w
NCOL = _off

CONSTS = {}
_off = 0
for _name, _w in [("ident", 128), ("bones", 128), ("maskG", 512), ("maskZ", 128), ("cmask", 512), ("id64", 64), ("bones64", 128)]:
    CONSTS[_name] = _off
    _off += _w
NCONST = _off


def _make_consts():
    cst = np.zeros((128, NCONST), np.float32)
    cst[:, CONSTS["ident"]:CONSTS["ident"] + 128] = np.eye(128, dtype=np.float32)
    bo = np.zeros((128, 128), np.float32)
    bo[:64, :64] = 1
    bo[64:, 64:] = 1
    cst[:, CONSTS["bones"]:CONSTS["bones"] + 128] = bo
    cst[:, CONSTS["bones64"]:CONSTS["bones64"] + 128] = bo / 64.0
    s = (np.arange(128) % 64)[:, None]
    t = np.arange(64)[None, :]
    mg = np.zeros((128, 2, 256), np.float32)
    for blk in range(2):
        mg[:, 0, blk * 128:blk * 128 + 64] = (t > s)
        mg[:, 0, blk * 128 + 64:blk * 128 + 128] = (t >= s)
        mg[:, 1, blk * 128:blk * 128 + 64] = (t < s)
        mg[:, 1, blk * 128 + 64:blk * 128 + 128] = (t <= s)
    cst[:, CONSTS["maskG"]:CONSTS["maskG"] + 512] = mg.reshape(128, 512)
    mz = np.zeros((128, 2, 64), np.float32)
    mz[:, 0, :] = (t < s)
    mz[:, 1, :] = (t > s)
    cst[:, CONSTS["maskZ"]:CONSTS["maskZ"] + 128] = mz.reshape(128, 128)
    cm = np.ones((128, 512), np.float32)
    cm[:, ::64] = 0
    cst[:, CONSTS["cmask"]:CONSTS["cmask"] + 512] = cm
    i64 = np.zeros((128, 64), np.float32)
    i64[np.arange(128), np.arange(128) % 64] = 1
    cst[:, CONSTS["id64"]:CONSTS["id64"] + 64] = i64
    return cst


def build_program(limit=10 ** 9, dbg_spec=None, mlimit=10 ** 9):
    nc = bass.Bass("TRN2", target_bir_lowering=False)
    stage = [0]

    def go():
        stage[0] += 1
        return stage[0] <= limit
    mstage = [0]

    def mgo():
        mstage[0] += 1
        return mstage[0] <= mlimit

    def din(name, shape):
        return nc.dram_tensor(name, list(shape), F32, kind="ExternalInput").ap()

    xg = din("xg", [5, N, D])
    colsd = din("cols", [128, NCOL])
    cstd = din("consts", [128, NCONST])
    std = din("st", [5, 4, 128, 64])
    w_mod = din("w_mod", [D, 9 * D])
    f1w13 = din("f1w13", [D, 2 * DFF])
    f1w2 = din("f1w2", [DFF, D])
    f2w13 = din("f2w13", [D, 2 * DFF])
    f2w2 = din("f2w2", [DFF, D])
    w_in = din("w_in", [D, 5632])
    w1c = din("w1c", [5, D, 128])
    w2c = din("w2c", [5, 128, 512])
    wba = din("wba", [512, D])
    wbb = din("wbb", [512, D])
    wout = din("wout", [D, D])
    yout = nc.dram_tensor("y", [2 * N, D], F32, kind="ExternalOutput").ap()
    nsout = nc.dram_tensor("ns", [2, 2, 8, 64, 64], F32, kind="ExternalOutput").ap()

    with contextlib.ExitStack() as stk:
        def sb(name, shape, dt=F32):
            return stk.enter_context(nc.sbuf_tensor(name, list(shape), dt))

        def ps(name, shape):
            return stk.enter_context(nc.psum_tensor(name, list(shape), F32))

        mk = MK(nc)
        cols = sb("cols_t", [128, NCOL])
        cst = sb("cst_t", [128, NCONST])
        onesb = sb("onesb", [128, 128], BF16)
        identb = sb("identb", [128, 128], BF16)
        scb = sb("scb", [128, 8, 2], BF16)
        modT = sb("modT", [128, 72, 2])
        mods = sb("mods", [128, 2, 6, 8])
        xT = sb("xT", [128, 8, N])
        rstd = sb("rstd", [128, N])
        lnt = sb("lnt", [128, N])
        bdum = sb("bdum", [128, 1])
        NSLOT = 3
        slots = [sb("slot%d" % i, [128, 4096], BF16) for i in range(NSLOT)]
        hlast = sb("hlast", [128, 3, 8])
        hbF = sb("hbF", [128, 8])
        hbB = sb("hbB", [128, 8])
        hbA = sb("hbA", [128, 8])
        endst = sb("endst", [128, 3, 4, 64])
        Mst = sb("Mst", [128, 4, 64])
        Mtmp = sb("Mtmp", [128, 4, 64])
        Mbf = sb("Mbf", [128, 4, 64], BF16)
        SCR = 30720
        scr = sb("scr", [128, SCR])

        class Pool:
            def __init__(self):
                self.off = 0

            def f32(self, n):
                a = scr[:, self.off:self.off + n]
                self.off += n
                assert self.off <= SCR, self.off
                return a

            def bf16(self, n):
                m = (n + 1) // 2
                a = scr[:, self.off:self.off + m].bitcast(BF16)
                self.off += m
                assert self.off <= SCR, self.off
                return a

        psA = ps("psA", [128, 2, 512])
        psB = ps("psB", [128, 2, 512])
        psS = ps("psS", [128, 512])
        psC = ps("psC", [128, 2, 512])
        psT = ps("psT", [128, 512])

        def tap(name, ap2d, width, keys):
            if name not in TAPS:
                return
            dt_ = nc.dram_tensor("dbg_" + name, [128, width], F32, kind="ExternalOutput").ap()
            mk.dma("sp", "yout", lambda e: e.dma_start(out=dt_, in_=ap2d), reads=keys)

        cnum = lambda i: cols[:, COLS["num"] + i:COLS["num"] + i + 1]
        ccol = lambda name, i: cols[:, COLS[name] + i:COLS[name] + i + 1]
        ident = cst[:, CONSTS["ident"]:CONSTS["ident"] + 128]
        bones = cst[:, CONSTS["bones"]:CONSTS["bones"] + 128]
        bones64 = cst[:, CONSTS["bones64"]:CONSTS["bones64"] + 128]
        id64 = cst[:, CONSTS["id64"]:CONSTS["id64"] + 64]
        cmask = cst[:, CONSTS["cmask"]:CONSTS["cmask"] + 512]

        def maskG(d):
            o = CONSTS["maskG"] + d * 256
            return cst[:, o:o + 256]

        def maskZ(d):
            o = CONSTS["maskZ"] + d * 64
            return cst[:, o:o + 64]

        evac_rr = [0]

        def evac(out, in_, reads, writes):
            evac_rr[0] ^= 1
            if evac_rr[0]:
                mk.act(lambda e: e.copy(out=out, in_=in_), reads=reads, writes=writes)
            else:
                mk.dve(lambda e: e.tensor_copy(out=out, in_=in_), reads=reads, writes=writes)

        slot_rr = [0]

        def load_slab(pieces):
            s = slot_rr[0] % NSLOT
            slot_rr[0] += 1
            sl = slots[s]
            key = "W%d" % s
            for i, (vf, dap) in enumerate(pieces):
                mk.dma("pool", "w%d" % s, lambda e, vf=vf, dap=dap: e.dma_start(out=vf(sl), in_=dap), writes=[key])
            return sl, key

        def mm_group(out_ap, out_key, terms):
            n = len(terms)
            for i, (l, r, keys) in enumerate(terms):
                mk.pe(lambda e, l=l, r=r, i=i: e.matmul(out_ap, lhsT=l, rhs=r, start=(i == 0), stop=(i == n - 1)),
                      reads=keys, writes=(out_key if isinstance(out_key, list) else [out_key]))

        barrier_n = [0]

        def barrier():
            barrier_n[0] += 1
            mk.dve(lambda e: e.memset(bdum[:], 0.0), reads=["bdum"], writes=["SCR"])

        R = lambda *k: ["SCR"] + list(k)

        mk.dma("sp", "c0", lambda e: e.dma_start(out=cols[:], in_=colsd), writes=["cols"])
        mk.dma("sp", "c0", lambda e: e.dma_start(out=cst[:], in_=cstd), writes=["cst"])
        mk.dve(lambda e: e.memset(onesb[:], 1.0 / 1024.0), writes=["onesb"])
        mk.dve(lambda e: e.tensor_copy(out=identb[:], in_=cst[:, CONSTS["ident"]:CONSTS["ident"] + 128]), reads=["cst"], writes=["cst2"])
        cv0 = COLS["cv"]
        mk.act(lambda e: e.activation(out=scb[:].rearrange("p k j -> p (k j)"), in_=cols[:, cv0:cv0 + 16], func=AF.Silu),
               reads=["cols"], writes=["scb"])
        for s in range(18):
            sl, key = load_slab([(lambda t: t[:, 0:4096].rearrange("p (k n) -> p k n", k=8),
                                 w_mod[:, s * 512:(s + 1) * 512].rearrange("(k p) n -> p k n", p=128))])
            slv = sl[:, 0:4096].rearrange("p (k n) -> p k n", k=8)
            for tt in range(4):
                mm_group(psS[:, tt * 2:tt * 2 + 2], "psS",
                         [(slv[:, k, tt * 128:(tt + 1) * 128], scb[:, k, :], [key, "scb"]) for k in range(8)])
            for j in range(2):
                b0 = COLS["bmod"] + s * 4
                mk.dve(lambda e, s=s, j=j, b0=b0: e.tensor_tensor(
                    out=modT[:, s * 4:s * 4 + 4, j], in0=psS[:, 0:8].rearrange("p (t j) -> p t j", j=2)[:, :, j],
                    in1=cols[:, b0:b0 + 4], op=ALU.add), reads=["psS", "cols"], writes=["modT"])
        ng = lambda i: cols[:, COLS["ng"] + i * 8:COLS["ng"] + i * 8 + 8]
        m_ = lambda i, j: modT[:, i * 8:(i + 1) * 8, j]
        for j in range(2):
            for (dst, mi, gi, half) in [(0, 1, 0, None), (1, 2, 1, 0.5), (2, 4, 2, None), (3, 5, 3, 1.0), (4, 7, 4, None), (5, 8, 5, 0.5)]:
                if half is None:
                    mk.dve(lambda e, j=j, dst=dst, mi=mi, gi=gi: e.scalar_tensor_tensor(
                        out=mods[:, j, dst, :], in0=m_(mi, j), scalar=1.0, in1=ng(gi), op0=ALU.add, op1=ALU.mult),
                        reads=["modT", "cols"], writes=["mods"])
                else:
                    mk.dve(lambda e, j=j, dst=dst, mi=mi, gi=gi, half=half: e.scalar_tensor_tensor(
                        out=mods[:, j, dst, :], in0=m_(mi, j), scalar=half, in1=ng(gi), op0=ALU.mult, op1=ALU.mult),
                        reads=["modT", "cols"], writes=["mods"])

        def rms_rstd(src, src_key, sq, eps_idx):
            for k in range(8):
                if k % 2 == 0:
                    mk.act(lambda e, k=k: e.activation(out=sq[:, k, :], in_=src[:, k, :], func=AF.Square),
                           reads=R(src_key), writes=["sq%d" % k])
                else:
                    mk.dve(lambda e, k=k: e.tensor_tensor(out=sq[:, k, :], in0=src[:, k, :], in1=src[:, k, :], op=ALU.mult),
                           reads=R(src_key), writes=["sq%d" % k])
            mm_group(psS[:], "psS", [(onesb[:], sq[:, k, :], ["onesb", "sq%d" % k, "SCR"]) for k in range(8)])
            mk.act(lambda e: e.activation(out=lnt[:], in_=psS[:], func=AF.Ln, bias=cnum(eps_idx), scale=1.0),
                   reads=["psS", "cols"], writes=["lnt"])
            mk.act(lambda e: e.activation(out=rstd[:], in_=lnt[:], func=AF.Exp, scale=-0.5), reads=["lnt"], writes=["rstd"])

        def prenorm(j, ai, bi, hT, sq, tmp):
            rms_rstd(xT, "xT", sq, 0)
            for k in range(8):
                mk.dve(lambda e, k=k: e.scalar_tensor_tensor(out=tmp[:, k % 2, :], in0=xT[:, k, :], scalar=mods[:, j, ai, k:k + 1],
                                                             in1=rstd[:], op0=ALU.mult, op1=ALU.mult),
                       reads=R("xT", "mods", "rstd"), writes=["ptmp%d" % (k % 2)])
                mk.act(lambda e, k=k: e.activation(out=hT[:, k, :], in_=tmp[:, k % 2, :], func=AF.Identity,
                                                   bias=modT[:, bi * 8 + k, j:j + 1], scale=1.0),
                       reads=R("ptmp%d" % (k % 2), "modT"), writes=["hT%d" % k])

        def postnorm_residual(j, gi, oT, sq, tmp):
            rms_rstd(oT, "oT", sq, 0)
            for k in range(8):
                mk.dve(lambda e, k=k: e.scalar_tensor_tensor(out=tmp[:, k % 2, :], in0=oT[:, k, :], scalar=mods[:, j, gi, k:k + 1],
                                                             in1=rstd[:], op0=ALU.mult, op1=ALU.mult),
                       reads=R("oT", "mods", "rstd"), writes=["ptmp%d" % (k % 2)])
                mk.dve(lambda e, k=k: e.tensor_tensor(out=xT[:, k, :], in0=xT[:, k, :], in1=tmp[:, k % 2, :], op=ALU.add),
                       reads=R("ptmp%d" % (k % 2), "xT"), writes=["xT"])

        def ffn(w13, w2, hT, hid, oT, sgt):
            v8 = lambda t: t[:, 0:4096].rearrange("p (k n) -> p k n", k=8)
            groups = [(g * 4, 4) for g in range(5)] + [(20, 2)]
            for (j0, nj) in groups:
                w = nj * 128
                slg, keyg = load_slab([(lambda t, w=w: v8(t)[:, :, 0:w], w13[:, j0 * 128:j0 * 128 + w].rearrange("(k p) n -> p k n", p=128))])
                for jj in range(nj):
                    b = jj % 2
                    mm_group(psA[:, b, :], "psA%d" % b,
                             [(v8(slg)[:, k, jj * 128:(jj + 1) * 128], hT[:, k, :], [keyg, "hT%d" % k, "SCR"]) for k in range(8)])
                    mk.act(lambda e, b=b, jj=jj: e.activation(out=sgt[:, jj, :], in_=psA[:, b, :], func=AF.Silu),
                           reads=R("psA%d" % b), writes=["sgt%d" % jj])
                slu, keyu = load_slab([(lambda t, w=w: v8(t)[:, :, 0:w], w13[:, DFF + j0 * 128:DFF + j0 * 128 + w].rearrange("(k p) n -> p k n", p=128))])
                for jj in range(nj):
                    b = jj % 2
                    jt = j0 + jj
                    mm_group(psB[:, b, :], KB(b),
                             [(v8(slu)[:, k, jj * 128:(jj + 1) * 128], hT[:, k, :], [keyu, "hT%d" % k, "SCR"]) for k in range(8)])
                    mk.dve(lambda e, b=b, jj=jj, jt=jt: e.tensor_tensor(out=hid[:, jt, :], in0=sgt[:, jj, :], in1=psB[:, b, :], op=ALU.mult),
                           reads=R("sgt%d" % jj, *KB(b)), writes=["hid%d" % jt])
            banks = [(psA[:, 0, :], ["psA0"]), (psA[:, 1, :], ["psA1"]), (psB[:, 0, :], KB(0)), (psB[:, 1, :], KB(1))]
            jslabs = [(0, 8), (8, 8), (16, 6)]
            for ig in range(2):
                for si, (js, nj) in enumerate(jslabs):
                    sl, key = load_slab([(lambda t, nj=nj: v8(t)[:, 0:nj, :], w2[js * 128:(js + nj) * 128, ig * 512:(ig + 1) * 512].rearrange("(j p) n -> p j n", p=128))])
                    for tt in range(4):
                        pa, pk = banks[tt]
                        for jj in range(nj):
                            jt = js + jj
                            first = (si == 0 and jj == 0)
                            last = (si == len(jslabs) - 1 and jj == nj - 1)
                            mk.pe(lambda e, pa=pa, sl=sl, jj=jj, tt=tt, jt=jt, first=first, last=last: e.matmul(
                                pa, lhsT=v8(sl)[:, jj, tt * 128:(tt + 1) * 128], rhs=hid[:, jt, :], start=first, stop=last),
                                reads=[key, "hid%d" % jt, "SCR"], writes=pk)
                for tt in range(4):
                    pa, pk = banks[tt]
                    evac(oT[:, ig * 4 + tt, :], pa, R(*pk), ["oT"])

        def KB(b):
            return ["psB0", "psB0b"] if b == 0 else ["psB1"]

        def mixer(kind, j, hT, aux_idx):
            pool = Pool()
            pool.off = 2048
            full = kind != "aux"
            nseq = 2 if kind == "prompt" else 1
            L = N // nseq
            cps = NCH // nseq
            rkv = pool.f32(12 * N).rearrange("p (t n) -> p t n", t=12)
            kk = pool.f32(4 * N).rearrange("p (t n) -> p t n", t=4)
            yT = pool.f32(4 * N).rearrange("p (t n) -> p t n", t=4)
            asum = pool.f32(4 * N).rearrange("p (t n) -> p t n", t=4)
            lh = pool.bf16(2 * N).rearrange("p (d n) -> p d n", d=2)
            w2b = pool.bf16(2 * 512).rearrange("p (d n) -> p d n", d=2)
            vbf = pool.bf16(4 * N).rearrange("p (t n) -> p t n", t=4)
            base_d = pool.off
            w1b = pool.bf16(2 * 1024).rearrange("p (d k n) -> p d k n", d=2, k=8)
            w1s = pool.bf16(2 * 1024).rearrange("p (d k n) -> p d k n", d=2, k=8)
            hk = ["hT%d" % k for k in range(8)]
            for s in range(3):
                sl, key = load_slab([(lambda t: t[:, 0:4096].rearrange("p (k n) -> p k n", k=8),
                                     w_in[:, s * 512:(s + 1) * 512].rearrange("(k p) n -> p k n", p=128))])
                slv = sl[:, 0:4096].rearrange("p (k n) -> p k n", k=8)
                for tt in range(4):
                    b = tt % 2
                    mm_group(psA[:, b, :], "psA%d" % b,
                             [(slv[:, k, tt * 128:(tt + 1) * 128], hT[:, k, :], [key, "hT%d" % k, "SCR"]) for k in range(8)])
                    evac(rkv[:, s * 4 + tt, :], psA[:, b, :], R("psA%d" % b), ["rkv%d" % (s * 4 + tt)])
            for pr in range(4):
                mk.act(lambda e, pr=pr: e.copy(out=vbf[:, pr, :], in_=rkv[:, 8 + pr, :]), reads=R("rkv%d" % (8 + pr)), writes=["vbf%d" % pr])
            if not mgo():
                return
            sqk = pool.f32(2 * N).rearrange("p (b n) -> p b n", b=2)
            for pr in range(4):
                b = pr % 2
                mk.dve(lambda e, pr=pr: e.tensor_scalar(out=kk[:, pr, :], in0=rkv[:, 4 + pr, :], scalar1=ccol("kk", pr), scalar2=None,
                                                        op0=ALU.mult), reads=R("rkv%d" % (4 + pr), "cols"), writes=["kk%d" % pr])
                mk.dve(lambda e, pr=pr, b=b: e.tensor_tensor(out=sqk[:, b, :], in0=kk[:, pr, :], in1=kk[:, pr, :], op=ALU.mult),
                       reads=R("kk%d" % pr), writes=["sqk%d" % b])
                mm_group(psB[:, b, :], KB(b), [(bones, sqk[:, b, :], ["cst", "sqk%d" % b, "SCR"])])
                mk.act(lambda e, b=b: e.activation(out=sqk[:, b, :], in_=psB[:, b, :], func=AF.Ln, bias=cnum(1), scale=1.0),
                       reads=R("cols", *KB(b)), writes=["sqk%d" % b])
                mk.act(lambda e, b=b: e.activation(out=sqk[:, b, :], in_=sqk[:, b, :], func=AF.Exp, scale=-0.5),
                       reads=R("sqk%d" % b), writes=["sqk%d" % b])
                mk.dve(lambda e, pr=pr, b=b: e.tensor_tensor(out=kk[:, pr, :], in0=kk[:, pr, :], in1=sqk[:, b, :], op=ALU.mult),
                       reads=R("sqk%d" % b, "kk%d" % pr), writes=["kk%d" % pr])
            if not mgo():
                return
            dslots = [(0, 0), (1, 1)] if full else [(0, 2 + aux_idx)]
            sh = pool.bf16(8 * N).rearrange("p (k n) -> p k n", k=8)
            for di, (dt_, ds) in enumerate(dslots):
                mk.dma("pool", "wl", lambda e, di=di, ds=ds: e.dma_start(out=w1b[:, di, :, :], in_=w1c[ds].rearrange("(k p) n -> p k n", p=128)),
                       reads=["SCR"], writes=["w1b%d" % di])
                mk.dma("pool", "wl", lambda e, di=di, ds=ds: e.dma_start(out=w2b[:, di, :], in_=w2c[ds]), reads=["SCR"], writes=["w2b%d" % di])
                for x in range(2):
                    for k in range(8):
                        mc = COLS["mu"] + ds * 16 + x * 8 + k
                        mk.dve(lambda e, di=di, x=x, k=k, mc=mc: e.tensor_scalar(
                            out=w1s[:, di, k, x * 64:(x + 1) * 64], in0=w1b[:, di, k, x * 64:(x + 1) * 64],
                            scalar1=cols[:, mc:mc + 1], scalar2=None, op0=ALU.mult),
                            reads=R("w1b%d" % di, "cols"), writes=["w1s%d" % di])
                if dt_ == 0:
                    mk.dve(lambda e: e.tensor_tensor(out=sh[:, :, 1:N], in0=hT[:, :, 0:N - 1], in1=hT[:, :, 1:N], op=ALU.subtract),
                           reads=R(*hk), writes=["sh"])
                    for sq_ in range(nseq):
                        col = sq_ * L
                        if kind == "prompt" or (kind == "aux" and aux_idx == 0):
                            mk.dve(lambda e, col=col: e.tensor_scalar(out=sh[:, :, col], in0=hT[:, :, col], scalar1=-1.0, scalar2=None,
                                                                      op0=ALU.mult), reads=R("sh", *hk), writes=["sh"])
                        else:
                            hb = hbF if kind == "own" else hbA
                            hbk = "hbF" if kind == "own" else "hbA"
                            mk.dve(lambda e, col=col, hb=hb: e.tensor_tensor(out=sh[:, :, col], in0=hb[:, :], in1=hT[:, :, col], op=ALU.subtract),
                                   reads=R("sh", hbk, *hk), writes=["sh"])
                else:
                    mk.dve(lambda e: e.tensor_tensor(out=sh[:, :, 0:N - 1], in0=hT[:, :, 1:N], in1=hT[:, :, 0:N - 1], op=ALU.subtract),
                           reads=R(*hk), writes=["sh"])
                    for sq_ in range(nseq):
                        col = sq_ * L + L - 1
                        if kind == "prompt":
                            mk.dve(lambda e, col=col: e.tensor_scalar(out=sh[:, :, col], in0=hT[:, :, col], scalar1=-1.0, scalar2=None,
                                                                      op0=ALU.mult), reads=R("sh", *hk), writes=["sh"])
                        else:
                            mk.dve(lambda e, col=col: e.tensor_tensor(out=sh[:, :, col], in0=hbB[:, :], in1=hT[:, :, col], op=ALU.subtract),
                                   reads=R("sh", "hbB", *hk), writes=["sh"])
                b = di % 2
                mm_group(psB[:, b, :], KB(b),
                         [(w1b[:, di, k, :], hT[:, k, :], ["w1b%d" % di, "hT%d" % k, "SCR"]) for k in range(8)] +
                         [(w1s[:, di, k, :], sh[:, k, :], ["w1s%d" % di, "sh", "SCR"]) for k in range(8)])
                mk.act(lambda e, di=di, b=b: e.activation(out=lh[0:64, di, :], in_=psB[0:64, b, :], func=AF.Tanh),
                       reads=R(*KB(b)), writes=["lh%d" % di])
                mk.act(lambda e, di=di, b=b: e.copy(out=lh[64:128, di, :], in_=psB[64:128, b, :]),
                       reads=R(*KB(b)), writes=["lh%d" % di])
            if kind == "aux":
                mk.dve(lambda e: e.tensor_copy(out=hlast[:, aux_idx, :], in_=hT[:, :, N - 1]), reads=R(*hk), writes=["hlast%d" % aux_idx])

            if not mgo():
                return
            v3 = lambda ap: ap.rearrange("p (c n) -> p c n", c=8)
            psG = psA[:].rearrange("p a (h n) -> p (a h) n", h=2)
            psZ = psB[:].rearrange("p a (h n) -> p (a h) n", h=8)
            psZv = lambda a, hh: psZ[:, a * 4 + hh, :]
            ZK = ["psB0", "psB0b", "psB1"]
            psTv = psT[:].rearrange("p (a h n) -> p a h n", a=2, h=4)
            psCv = psC[:].rearrange("p a (h n) -> p a h n", h=8)
            psSv = psS[:].rearrange("p (h n) -> p h n", h=8)
            for di, (dt_, ds) in enumerate(dslots):
                order = list(range(NCH)) if dt_ == 0 else list(range(NCH - 1, -1, -1))
                barrier()
                pool.off = base_d
                AR = pool.bf16(4 * 8 * 128).rearrange("p (q c n) -> p q c n", q=4, c=8)
                BK = pool.bf16(4 * 8 * 128).rearrange("p (q c n) -> p q c n", q=4, c=8)
                Pend = pool.f32(32).rearrange("p (q c) -> p q c", q=4)
                base_t = pool.off
                sw = pool.f32(2 * N).rearrange("p (q n) -> p q n", q=2)
                av = pool.f32(2 * N).rearrange("p (q n) -> p q n", q=2)
                cs = pool.f32(2 * N).rearrange("p (q n) -> p q n", q=2)
                Lx = pool.f32(2 * N).rearrange("p (q n) -> p q n", q=2)
                Ep = pool.f32(2 * N).rearrange("p (q n) -> p q n", q=2)
                t1 = pool.f32(2 * N).rearrange("p (q n) -> p q n", q=2)
                for hp in range(2):
                    for ql in range(2):
                        pr = 2 * hp + ql
                        b = ql
                        mm_group(psA[:, b, :], "psA%d" % b, [(w2b[0:64, di, pr * 128:(pr + 1) * 128], lh[0:64, di, :], ["w2b%d" % di, "lh%d" % di, "SCR"])])
                        mm_group(psB[:, b, :], KB(b), [(w2b[64:128, di, pr * 128:(pr + 1) * 128], lh[64:128, di, :], ["w2b%d" % di, "lh%d" % di, "SCR"])])
                        mk.act(lambda e, pr=pr, ql=ql, b=b, ds=ds: e.activation(out=sw[:, ql, :], in_=psA[:, b, :], func=AF.Sigmoid,
                                                                               bias=ccol("w0", ds * 4 + pr), scale=1.0),
                               reads=R("psA%d" % b, "cols"), writes=["sw%d" % ql])
                        mk.act(lambda e, pr=pr, ql=ql, b=b, ds=ds: e.activation(out=av[:, ql, :], in_=psB[:, b, :], func=AF.Sigmoid,
                                                                               bias=ccol("a0", ds * 4 + pr), scale=1.0),
                               reads=R("cols", *KB(b)), writes=["av%d" % ql])
                        if full:
                            if di == 0:
                                mk.dve(lambda e, pr=pr, ql=ql: e.tensor_copy(out=asum[:, pr, :], in_=av[:, ql, :]), reads=R("av%d" % ql), writes=["asum%d" % pr])
                            else:
                                mk.dve(lambda e, pr=pr, ql=ql: e.tensor_tensor(out=asum[:, pr, :], in0=asum[:, pr, :], in1=av[:, ql, :], op=ALU.add),
                                       reads=R("av%d" % ql, "asum%d" % pr), writes=["asum%d" % pr])
                        mk.dve(lambda e, ql=ql: e.tensor_tensor_scan(out=cs[:, ql, :], data0=cmask, data1=sw[:, ql, :], initial=0.0,
                                                                     op0=ALU.mult, op1=ALU.add), reads=R("sw%d" % ql, "cst"), writes=["cs%d" % ql])
                        if dt_ == 0:
                            mk.dve(lambda e, ql=ql: e.tensor_tensor(out=Lx[:, ql, :], in0=cs[:, ql, :], in1=sw[:, ql, :], op=ALU.subtract),
                                   reads=R("cs%d" % ql, "sw%d" % ql), writes=["Lx%d" % ql])
                        else:
                            mk.dve(lambda e, ql=ql: e.tensor_tensor(out=v3(Lx[:, ql, :]), in0=v3(cs[:, ql, :])[:, :, 63:64].to_broadcast([128, 8, 64]),
                                                                    in1=v3(cs[:, ql, :]), op=ALU.subtract),
                                   reads=R("cs%d" % ql), writes=["Lx%d" % ql])
                            mk.dve(lambda e, ql=ql: e.tensor_tensor(out=cs[:, ql, :], in0=Lx[:, ql, :], in1=sw[:, ql, :], op=ALU.add),
                                   reads=R("Lx%d" % ql, "sw%d" % ql), writes=["cs%d" % ql])
                        mk.act(lambda e, ql=ql: e.activation(out=Ep[:, ql, :], in_=cs[:, ql, :], func=AF.Exp, scale=-C0), reads=R("cs%d" % ql), writes=["Ep%d" % ql])
                        mk.act(lambda e, ql=ql: e.activation(out=Lx[:, ql, :], in_=Lx[:, ql, :], func=AF.Exp, scale=-C0), reads=R("Lx%d" % ql), writes=["Lx%d" % ql])
                        mk.act(lambda e, ql=ql: e.activation(out=cs[:, ql, :], in_=cs[:, ql, :], func=AF.Exp, scale=C0), reads=R("cs%d" % ql, "Ep%d" % ql), writes=["cs%d" % ql])
                        pcol = 63 if dt_ == 0 else 0
                        mk.dve(lambda e, pr=pr, ql=ql, pcol=pcol: e.tensor_copy(out=Pend[:, pr, :], in_=v3(Ep[:, ql, :])[:, :, pcol]), reads=R("Ep%d" % ql), writes=["Pend"])
                        mk.dve(lambda e, pr=pr, ql=ql: e.scalar_tensor_tensor(out=AR[:, pr, :, 0:64], in0=v3(kk[:, pr, :]), scalar=-1.0, in1=v3(Lx[:, ql, :]),
                                                                              op0=ALU.mult, op1=ALU.mult), reads=R("kk%d" % pr, "Lx%d" % ql), writes=["AR%d" % pr])
                        mk.dve(lambda e, pr=pr, ql=ql: e.tensor_tensor(out=AR[:, pr, :, 64:128], in0=v3(rkv[:, pr, :]), in1=v3(Ep[:, ql, :]), op=ALU.mult),
                               reads=R("rkv%d" % pr, "Ep%d" % ql), writes=["AR%d" % pr])
                        mk.dve(lambda e, pr=pr, ql=ql: e.tensor_tensor(out=t1[:, ql, :], in0=kk[:, pr, :], in1=av[:, ql, :], op=ALU.mult),
                               reads=R("kk%d" % pr, "av%d" % ql), writes=["t1%d" % ql])
                        mk.dve(lambda e, pr=pr, ql=ql: e.tensor_tensor(out=BK[:, pr, :, 0:64], in0=v3(t1[:, ql, :]), in1=v3(cs[:, ql, :]), op=ALU.mult),
                               reads=R("t1%d" % ql, "cs%d" % ql), writes=["BK%d" % pr])
                        mk.dve(lambda e, pr=pr, ql=ql: e.tensor_scalar(out=t1[:, ql, :], in0=av[:, ql, :], scalar1=cnum(4), scalar2=ccol("ka", pr),
                                                                       op0=ALU.subtract, op1=ALU.mult), reads=R("av%d" % ql, "cols", "t1%d" % ql), writes=["t1%d" % ql])
                        mk.dve(lambda e, pr=pr, ql=ql: e.scalar_tensor_tensor(out=t1[:, ql, :], in0=t1[:, ql, :], scalar=1.0, in1=rkv[:, 4 + pr, :],
                                                                              op0=ALU.add, op1=ALU.mult), reads=R("t1%d" % ql, "rkv%d" % (4 + pr)), writes=["t1%d" % ql])
                        mk.dve(lambda e, pr=pr, ql=ql: e.tensor_tensor(out=BK[:, pr, :, 64:128], in0=v3(t1[:, ql, :]), in1=v3(cs[:, ql, :]), op=ALU.mult),
                               reads=R("t1%d" % ql, "cs%d" % ql), writes=["BK%d" % pr])
                if not mgo():
                    return
                barrier()
                pool.off = base_t
                Gm = [pool.bf16(4 * 256).rearrange("p (h n) -> p h n", h=4) for _ in range(2)]
                ZQb = [[pool.bf16(4 * 2 * 64).rearrange("p (h a n) -> p h a n", h=4, a=2) for _ in range(2)] for _ in range(2)]
                ZTb = [[pool.bf16(4 * 64).rearrange("p (h n) -> p h n", h=4) for _ in range(2)] for _ in range(2)]
                Qt = [pool.bf16(4 * 64).rearrange("p (h n) -> p h n", h=4) for _ in range(2)]
                TOK = [pool.bf16(3 * 4 * 64).rearrange("p (a h n) -> p a h n", a=3, h=4) for _ in range(2)]
                Wsb = pool.bf16(4 * 64).rearrange("p (h n) -> p h n", h=4)
                Usb = pool.bf16(4 * 64).rearrange("p (h n) -> p h n", h=4)
                Ytmp = pool.f32(4 * 64).rearrange("p (h n) -> p h n", h=4)
                unit = [0]

                def heads():
                    for q in range(4):
                        for e_ in range(2):
                            yield q, 64 * e_

                def tseries_stages(c, dt_=dt_):
                    u = unit[0] % 2
                    unit[0] += 1
                    G, Q, TK = Gm[u], Qt[u], TOK[u]
                    ZQ, ZT = ZQb[u], ZTb[u]
                    gk, zk, qk, tk = "Gm%d" % u, "ZZ%d" % u, "Q%d" % u, "TOK%d" % u
                    stages = []
                    psZa = psB[:, 0, :].rearrange("p (h n) -> p h n", h=4)
                    psZb = psB[:, 1, :].rearrange("p (h n) -> p h n", h=8)
                    KA = ["psB0", "psB0b"]

                    def st_g():
                        for q, fo in heads():
                            mm_group(psG[fo:fo + 64, q, 0:128], "psA%d" % (q // 2),
                                     [(BK[fo:fo + 64, q, c, 0:64], AR[fo:fo + 64, q, c, :], ["BK%d" % q, "AR%d" % q, "SCR"])])
                            mm_group(psG[fo:fo + 64, q, 128:256], "psA%d" % (q // 2),
                                     [(BK[fo:fo + 64, q, c, 64:128], AR[fo:fo + 64, q, c, :], ["BK%d" % q, "AR%d" % q, "SCR"])])
                            mm_group(psZb[fo:fo + 64, q, :], "psB1",
                                     [(AR[fo:fo + 64, q, c, 0:64], BK[fo:fo + 64, q, c, 0:64], ["BK%d" % q, "AR%d" % q, "SCR"])])
                        mk.dve(lambda e: e.tensor_tensor(out=G[:], in0=psG, in1=maskG(dt_).unsqueeze(1).to_broadcast([128, 4, 256]), op=ALU.mult),
                               reads=R("psA0", "psA1", "cst"), writes=[gk])
                        mk.dve(lambda e: e.tensor_tensor(out=ZT[0][:], in0=psZb[:, 0:4, :], in1=maskZ(dt_).unsqueeze(1).to_broadcast([128, 4, 64]),
                                                         op=ALU.mult), reads=R("psB1", "cst"), writes=[zk + "t0"])
                        mk.act(lambda e: e.copy(out=ZQ[0][:, :, 0, :], in_=G[:, :, 0:64]), reads=R(gk), writes=[zk + "q0"])
                        mk.act(lambda e: e.copy(out=ZQ[0][:, :, 1, :], in_=id64.unsqueeze(1).to_broadcast([128, 4, 64])), reads=R("cst"), writes=[zk + "q0"])
                    stages.append(st_g)

                    def mk_burst(lev):
                        cur, nxt = (lev - 1) % 2, lev % 2

                        def st():
                            for q, fo in heads():
                                if lev <= 4:
                                    mm_group(psZa[fo:fo + 64, q, :], KA, [(ZT[cur][fo:fo + 64, q, :], ZQ[cur][fo:fo + 64, q, :, :].rearrange("p a n -> p (a n)"),
                                                                          [zk + "t%d" % cur, zk + "q%d" % cur, "SCR"])])
                                else:
                                    mm_group(psZa[fo:fo + 64, q, 64:128], KA, [(ZT[cur][fo:fo + 64, q, :], ZQ[cur][fo:fo + 64, q, 1, :],
                                                                               [zk + "t%d" % cur, zk + "q%d" % cur, "SCR"])])
                                mm_group(psZb[fo:fo + 64, q, :], "psB1", [(ZQ[cur][fo:fo + 64, q, 0, :], ZT[cur][fo:fo + 64, q, :],
                                                                          [zk + "t%d" % cur, zk + "q%d" % cur, "SCR"])])
                            if lev <= 4:
                                mk.act(lambda e: e.copy(out=ZQ[nxt][:, :, 0, :], in_=psZa[:, :, 0:64]), reads=R(*KA), writes=[zk + "q%d" % nxt])
                            mk.dve(lambda e: e.tensor_tensor(out=ZQ[nxt][:, :, 1, :], in0=ZQ[cur][:, :, 1, :], in1=psZa[:, :, 64:128], op=ALU.add),
                                   reads=R(zk + "q%d" % cur, *KA), writes=[zk + "q%d" % nxt])
                            mk.act(lambda e: e.copy(out=ZT[nxt][:], in_=psZb[:, 0:4, :]), reads=R("psB1"), writes=[zk + "t%d" % nxt])
                        return st
                    for lev in range(1, 6):
                        stages.append(mk_burst(lev))

                    def st_last():
                        cur = 1
                        for q, fo in heads():
                            mm_group(psZa[fo:fo + 64, q, 64:128], KA, [(ZT[cur][fo:fo + 64, q, :], ZQ[cur][fo:fo + 64, q, 1, :],
                                                                       [zk + "t%d" % cur, zk + "q%d" % cur, "SCR"])])
                        mk.dve(lambda e: e.tensor_tensor(out=Q[:], in0=ZQ[cur][:, :, 1, :], in1=psZa[:, :, 64:128], op=ALU.add),
                               reads=R(zk + "q%d" % cur, *KA), writes=[qk])
                        for q, fo in heads():
                            idb = identb[fo:fo + 64, fo:fo + 64]
                            mm_group(psTv[fo:fo + 64, 0, q, :], "psT", [(BK[fo:fo + 64, q, c, 0:64], idb, ["BK%d" % q, "cst", "SCR"])])
                            mm_group(psTv[fo:fo + 64, 1, q, :], "psT", [(BK[fo:fo + 64, q, c, 64:128], idb, ["BK%d" % q, "cst", "SCR"])])
                            mm_group(psCv[fo:fo + 64, 1, 4 + q, :], "psCv", [(vbf[fo:fo + 64, q, c * 64:(c + 1) * 64], idb,
                                                                             ["vbf%d" % q, "cst", "SCR"])])
                        mk.act(lambda e: e.copy(out=TK[:, 0:2, :, :], in_=psTv), reads=R("psT"), writes=[tk])
                        mk.dve(lambda e: e.tensor_copy(out=TK[:, 2, :, :], in_=psCv[:, 1, 4:8, :]), reads=R("psCv"), writes=[tk])
                    stages.append(st_last)
                    return stages, (G, Q, TK, gk, qk, tk)

                def chain_stages(c, bufs, dt_=dt_, di=di, order=order):
                    G, Q, TK, gk, qk, tk = bufs
                    seq = c // cps
                    pos = order.index(c) % cps
                    stages = []

                    def st_w():
                        if pos == 0:
                            if kind == "prompt":
                                mk.dve(lambda e: e.memset(Mst[:], 0.0), reads=R(), writes=["Mst"])
                            elif kind == "aux":
                                if aux_idx == 0:
                                    mk.dve(lambda e: e.tensor_copy(out=Mst[:], in_=stt[:, 2, :, :]), reads=R("stt"), writes=["Mst"])
                                else:
                                    cc_ = COLS["coef"] + (aux_idx - 1)
                                    mk.dve(lambda e: e.scalar_tensor_tensor(out=Mst[:], in0=endst[:, aux_idx - 1, :, :], scalar=cols[:, cc_:cc_ + 1],
                                                                            in1=stt[:, 2 + aux_idx, :, :], op0=ALU.mult, op1=ALU.add),
                                           reads=R("stt", "end%d" % (aux_idx - 1), "cols"), writes=["Mst"])
                            else:
                                if dt_ == 0:
                                    mk.dve(lambda e: e.tensor_copy(out=Mst[:], in_=stt[:, 0, :, :]), reads=R("stt"), writes=["Mst"])
                                    for a in range(3):
                                        cc_ = COLS["coef"] + 2 + a
                                        mk.dve(lambda e, a=a, cc_=cc_: e.scalar_tensor_tensor(out=Mst[:], in0=endst[:, a, :, :], scalar=cols[:, cc_:cc_ + 1],
                                                                                          in1=Mst[:], op0=ALU.mult, op1=ALU.add),
                                               reads=R("end%d" % a, "cols", "Mst"), writes=["Mst"])
                                else:
                                    cc_ = COLS["coef"] + 5
                                    mk.dve(lambda e: e.scalar_tensor_tensor(out=Mst[:], in0=endst[:, 2, :, :], scalar=cols[:, cc_:cc_ + 1],
                                                                            in1=stt[:, 1, :, :], op0=ALU.mult, op1=ALU.add),
                                           reads=R("stt", "end2", "cols"), writes=["Mst"])
                        if pos == 0:
                            mk.act(lambda e: e.copy(out=Mbf[:], in_=Mst[:]), reads=R("Mst"), writes=["Mbf"])
                        for q, fo in heads():
                            mm_group(psCv[fo:fo + 64, 0, q, :], "psC0w",
                                     [(AR[fo:fo + 64, q, c, 0:64], Mbf[fo:fo + 64, q, :], ["AR%d" % q, "Mbf", "SCR"]),
                                      (G[fo:fo + 64, q, 128:192], TK[fo:fo + 64, 2, q, :], [gk, tk, "SCR"])])
                        mk.act(lambda e: e.copy(out=Wsb[:], in_=psCv[:, 0, 0:4, :]), reads=R("psC0w"), writes=["Wsb"])
                    stages.append(st_w)

                    def st_u():
                        for q, fo in heads():
                            mm_group(psCv[fo:fo + 64, 0, 4 + q, :], "psC0u", [(Q[fo:fo + 64, q, :], Wsb[fo:fo + 64, q, :], [qk, "Wsb", "SCR"])])
                        mk.dve(lambda e: e.tensor_copy(out=Usb[:], in_=psCv[:, 0, 4:8, :]), reads=R("psC0u"), writes=["Usb"])
                    stages.append(st_u)

                    def st_ym():
                        for q, fo in heads():
                            if full and KV != 5:
                                mm_group(psSv[fo:fo + 64, q, :], "psS",
                                         [(Mbf[fo:fo + 64, q, :], AR[fo:fo + 64, q, c, 64:128], ["Mbf", "AR%d" % q, "SCR"]),
                                          (Usb[fo:fo + 64, q, :], G[fo:fo + 64, q, 64:128], ["Usb", gk, "SCR"]),
                                          (TK[fo:fo + 64, 2, q, :], G[fo:fo + 64, q, 192:256], [tk, gk, "SCR"])])
                            mm_group(psSv[fo:fo + 64, 4 + q, :], "psS",
                                     [(TK[fo:fo + 64, 0, q, :], Usb[fo:fo + 64, q, :], [tk, "Usb", "SCR"]),
                                      (TK[fo:fo + 64, 1, q, :], TK[fo:fo + 64, 2, q, :], [tk, "SCR"])])
                        if full and KV != 6:
                            ydst = yT[:, :, c * 64:(c + 1) * 64]
                            if di == 0 and KV != 9:
                                mk.dve(lambda e: e.tensor_copy(out=ydst, in_=psSv[:, 0:4, :]), reads=R("psS"), writes=["yT"])
                            elif di == 0 and KV == 8:
                                mk.act(lambda e: e.copy(out=Wsb[:], in_=psSv[:, 0:4, :]), reads=R("psS", "Wsb"), writes=["Wsb"])
                                mk.dve(lambda e: e.tensor_copy(out=ydst, in_=Wsb[:]), reads=R("Wsb"), writes=["yT"])
                            elif di == 0:
                                mk.act(lambda e: e.copy(out=ydst, in_=psSv[:, 0:4, :]), reads=R("psS"), writes=["yT"])
                            else:
                                mk.act(lambda e: e.copy(out=Ytmp[:], in_=psSv[:, 0:4, :]), reads=R("psS", "Ytmp"), writes=["Ytmp"])
                                mk.dve(lambda e: e.tensor_tensor(out=ydst, in0=ydst, in1=Ytmp[:], op=ALU.add), reads=R("Ytmp", "yT"), writes=["yT"])
                        mk.dve(lambda e: e.tensor_tensor(out=Mtmp[:], in0=Mst[:], in1=psSv[:, 4:8, :], op=ALU.add),
                               reads=R("psS", "Mst"), writes=["Mtmp"])
                        mk.dve(lambda e: e.tensor_tensor(out=Mst[:], in0=Mtmp[:], in1=Pend[:, :, c:c + 1].to_broadcast([128, 4, 64]), op=ALU.mult),
                               reads=R("Mtmp", "Pend"), writes=["Mst"])
                        mk.act(lambda e: e.copy(out=Mbf[:], in_=Mst[:]), reads=R("Mst"), writes=["Mbf"])
                        if pos == cps - 1:
                            if kind == "aux":
                                mk.act(lambda e: e.copy(out=endst[:, aux_idx, :, :], in_=Mst[:]), reads=R("Mst"), writes=["end%d" % aux_idx])
                            elif kind == "prompt":
                                for q in range(4):
                                    mm_group(psT[0:64, q * 128:(q + 1) * 128], "psT", [(Mst[:, q, :], ident, ["Mst", "cst", "SCR"])])
                                mk.act(lambda e: e.copy(out=nsb[0:64, seq, di, :, :], in_=psT[0:64, :].rearrange("p (a n) -> p a n", a=4)),
                                       reads=R("psT"), writes=["nsb"])
                    stages.append(st_ym)
                    return stages

                prev = None
                for idx_c in range(len(order) + 1):
                    A, bufsA = ([], None)
                    if idx_c < len(order):
                        A, bufsA = tseries_stages(order[idx_c])
                    Bs = []
                    if prev is not None:
                        Bs = chain_stages(prev[0], prev[1])
                    for i in range(max(len(A), len(Bs))):
                        if i < len(A):
                            KTC[0] += 1
                            if KTC[0] <= KT:
                                A[i]()
                        if i < len(Bs):
                            KTC[0] += 1
                            if KTC[0] <= KT:
                                Bs[i]()
                    prev = (order[idx_c], bufsA) if idx_c < len(order) else None
            if not full:
                return
            tap("yT_" + kind, yT.rearrange("p q n -> p (q n)"), 4 * N, R("yT"))
            tap("rkv_" + kind, rkv.rearrange("p q n -> p (q n)"), 12 * N, R(*["rkv%d" % i for i in range(12)]))
            barrier()
            MARK[kind] = len(mk.ops)
            pool.off = base_d
            bv = pool.f32(4 * N).rearrange("p (q n) -> p q n", q=4)
            tA = pool.f32(2 * N).rearrange("p (q n) -> p q n", q=2)
            tB = pool.f32(2 * N).rearrange("p (q n) -> p q n", q=2)
            for pr in range(4):
                b = pr % 2
                mk.dve(lambda e, pr=pr, b=b: e.tensor_scalar(out=tA[:, b, :], in0=asum[:, pr, :], scalar1=cnum(5), scalar2=ccol("ka", pr), op0=ALU.subtract, op1=ALU.mult),
                       reads=R("asum%d" % pr, "cols", "tA%d" % b), writes=["tA%d" % b])
                mk.dve(lambda e, pr=pr, b=b: e.scalar_tensor_tensor(out=tA[:, b, :], in0=tA[:, b, :], scalar=2.0, in1=rkv[:, 4 + pr, :], op0=ALU.add, op1=ALU.mult),
                       reads=R("tA%d" % b, "rkv%d" % (4 + pr)), writes=["tA%d" % b])
                mk.dve(lambda e, pr=pr, b=b: e.scalar_tensor_tensor(out=tA[:, b, :], in0=tA[:, b, :], scalar=ccol("rk", pr), in1=rkv[:, pr, :], op0=ALU.mult, op1=ALU.mult),
                       reads=R("tA%d" % b, "rkv%d" % pr, "cols"), writes=["tA%d" % b])
                mm_group(psA[:, b, :], "psA%d" % b, [(bones, tA[:, b, :], ["cst", "tA%d" % b, "SCR"])])
                mk.dve(lambda e, pr=pr, b=b: e.tensor_tensor(out=bv[:, pr, :], in0=psA[:, b, :], in1=rkv[:, 8 + pr, :], op=ALU.mult),
                       reads=R("psA%d" % b, "rkv%d" % (8 + pr)), writes=["bv%d" % pr])
                mm_group(psB[:, b, :], KB(b), [(bones64, yT[:, pr, :], ["cst", "yT", "SCR"])])
                mk.act(lambda e, b=b: e.copy(out=tB[:, b, :], in_=psB[:, b, :]), reads=R("tB%d" % b, *KB(b)), writes=["tB%d" % b])
                mk.dve(lambda e, pr=pr, b=b: e.tensor_tensor(out=yT[:, pr, :], in0=yT[:, pr, :], in1=tB[:, b, :], op=ALU.subtract), reads=R("tB%d" % b, "yT"), writes=["yT"])
                mk.act(lambda e, pr=pr, b=b: e.activation(out=tA[:, b, :], in_=yT[:, pr, :], func=AF.Square), reads=R("yT", "tA%d" % b), writes=["tA%d" % b])
                mm_group(psA[:, b, :], "psA%d" % b, [(bones64, tA[:, b, :], ["cst", "tA%d" % b, "SCR"])])
                mk.act(lambda e, b=b: e.activation(out=tB[:, b, :], in_=psA[:, b, :], func=AF.Ln, bias=cnum(2), scale=1.0), reads=R("cols", "tB%d" % b, "psA%d" % b), writes=["tB%d" % b])
                mk.act(lambda e, b=b: e.activation(out=tB[:, b, :], in_=tB[:, b, :], func=AF.Exp, scale=-0.5), reads=R("tB%d" % b), writes=["tB%d" % b])
                mk.dve(lambda e, pr=pr, b=b: e.scalar_tensor_tensor(out=yT[:, pr, :], in0=yT[:, pr, :], scalar=ccol("gng", pr), in1=tB[:, b, :], op0=ALU.mult, op1=ALU.mult),
                       reads=R("tB%d" % b, "yT", "cols"), writes=["yT"])
                mk.dve(lambda e, pr=pr: e.scalar_tensor_tensor(out=yT[:, pr, :], in0=yT[:, pr, :], scalar=ccol("gnb", pr), in1=bv[:, pr, :], op0=ALU.add, op1=ALU.add),
                       reads=R("bv%d" % pr, "yT", "cols"), writes=["yT"])
            tap("yn_" + kind, yT.rearrange("p q n -> p (q n)"), 4 * N, R("yT"))
            barrier()
            pool.off = 2048
            yA = pool.f32(8 * N).rearrange("p (q n) -> p q n", q=8)
            pool.off = 12288
            yB = pool.f32(8 * N).rearrange("p (q n) -> p q n", q=8)
            tA2 = pool.f32(2 * N).rearrange("p (q n) -> p q n", q=2)
            tB2 = pool.f32(2 * N).rearrange("p (q n) -> p q n", q=2)
            yaT = pool.bf16(4 * N).rearrange("p (q n) -> p q n", q=4)
            ybT = pool.bf16(4 * N).rearrange("p (q n) -> p q n", q=4)
            cbT = pool.bf16(4 * N).rearrange("p (q n) -> p q n", q=4)
            ccT = pool.f32(4 * N).rearrange("p (q n) -> p q n", q=4)
            mgT = pool.bf16(8 * N).rearrange("p (q n) -> p q n", q=8)
            rl = 64 if kind == "own" else L
            r3 = lambda ap: ap.rearrange("p (r n) -> p r n", n=rl)

            def branch(wmat, srcT, srckey, dst, dstkey):
                for half in range(2):
                    slw, keyw = load_slab([(lambda t: t[:, 0:2048].rearrange("p (k n) -> p k n", k=4),
                                            wmat[:, half * 512:(half + 1) * 512].rearrange("(k p) n -> p k n", p=128))])
                    slwv = slw[:, 0:2048].rearrange("p (k n) -> p k n", k=4)
                    for tt in range(4):
                        o = half * 4 + tt
                        b = tt % 2
                        mm_group(psB[:, b, :], KB(b), [(slwv[:, k, tt * 128:(tt + 1) * 128], srcT[:, k, :], [keyw, srckey % k, "SCR"]) for k in range(4)])
                        evac(dst[:, o, :], psB[:, b, :], R(*KB(b)), [dstkey % o])

            for s in range(3, 11):
                sl, key = load_slab([(lambda t: t[:, 0:4096].rearrange("p (k n) -> p k n", k=8),
                                     w_in[:, s * 512:(s + 1) * 512].rearrange("(k p) n -> p k n", p=128))])
                slv = sl[:, 0:4096].rearrange("p (k n) -> p k n", k=8)
                for tt in range(4):
                    b = tt % 2
                    mm_group(psA[:, b, :], "psA%d" % b,
                             [(slv[:, k, tt * 128:(tt + 1) * 128], hT[:, k, :], [key, "hT%d" % k, "SCR"]) for k in range(8)])
                    pa = psA[:, b, :]
                    pk = "psA%d" % b
                    tb = tt % 2
                    if s == 3:
                        mk.act(lambda e, pa=pa, tb=tb: e.activation(out=tA2[:, tb, :], in_=pa, func=AF.Sigmoid), reads=R(pk, "tA2%d" % tb), writes=["tA2%d" % tb])
                        mk.dve(lambda e, tt=tt, tb=tb: e.tensor_tensor(out=yaT[:, tt, :], in0=yT[:, tt, :], in1=tA2[:, tb, :], op=ALU.mult),
                               reads=R("yT", "tA2%d" % tb), writes=["yaT%d" % tt])
                    elif s == 4:
                        mk.act(lambda e, tt=tt, pa=pa: e.copy(out=cbT[:, tt, :], in_=pa), reads=R(pk), writes=["cbT%d" % tt])
                    elif s == 5:
                        mk.act(lambda e, tt=tt, pa=pa: e.copy(out=ccT[:, tt, :], in_=pa), reads=R(pk), writes=["ccT%d" % tt])
                    elif s == 6:
                        u = tA2[:, tb, :]
                        uk = "tA2%d" % tb
                        acc = tB2[:, tb, :]
                        ak = "tB2%d" % tb
                        mk.dve(lambda e, tt=tt, pa=pa, u=u: e.tensor_tensor(out=u, in0=ccT[:, tt, :], in1=pa, op=ALU.mult), reads=R(pk, "ccT%d" % tt, uk), writes=[uk])
                        mk.dve(lambda e, tt=tt, u=u, acc=acc: e.tensor_scalar(out=acc, in0=u, scalar1=ccol("cw", 4 + tt), scalar2=ccol("cb", tt), op0=ALU.mult, op1=ALU.add),
                               reads=R(uk, "cols", ak), writes=[ak])
                        mk.dve(lambda e, tt=tt, u=u, acc=acc: e.scalar_tensor_tensor(out=r3(acc)[:, :, 1:rl], in0=r3(u)[:, :, 0:rl - 1], scalar=ccol("cw", tt),
                                                                                    in1=r3(acc)[:, :, 1:rl], op0=ALU.mult, op1=ALU.add), reads=R(uk, ak, "cols"), writes=[ak])
                        mk.dve(lambda e, tt=tt, u=u, acc=acc: e.scalar_tensor_tensor(out=r3(acc)[:, :, 0:rl - 1], in0=r3(u)[:, :, 1:rl], scalar=ccol("cw", 8 + tt),
                                                                                    in1=r3(acc)[:, :, 0:rl - 1], op0=ALU.mult, op1=ALU.add), reads=R(uk, ak, "cols"), writes=[ak])
                        mk.dve(lambda e, tt=tt, acc=acc: e.tensor_tensor(out=ybT[:, tt, :], in0=cbT[:, tt, :], in1=acc, op=ALU.mult), reads=R(ak, "cbT%d" % tt), writes=["ybT%d" % tt])
                    else:
                        gi = (s - 7) * 4 + tt
                        sg = tA2[:, tb, :]
                        sk = "tA2%d" % tb
                        mk.act(lambda e, pa=pa, sg=sg: e.activation(out=sg, in_=pa, func=AF.Sigmoid), reads=R(pk, sk), writes=[sk])
                        if gi < 8:
                            mk.dve(lambda e, gi=gi, sg=sg: e.tensor_tensor(out=yA[:, gi, :], in0=yA[:, gi, :], in1=sg, op=ALU.mult), reads=R(sk, "yA%d" % gi), writes=["yA%d" % gi])
                        else:
                            g2 = gi - 8
                            mk.dve(lambda e, g2=g2, sg=sg: e.tensor_tensor(out=sg, in0=sg, in1=yB[:, g2, :], op=ALU.mult), reads=R(sk, "yB%d" % g2), writes=[sk])
                            mk.dve(lambda e, g2=g2, sg=sg: e.tensor_tensor(out=mgT[:, g2, :], in0=yA[:, g2, :], in1=sg, op=ALU.add), reads=R(sk, "yA%d" % g2), writes=["mgT%d" % g2])
                if s == 3:
                    barrier()
                    branch(wba, yaT, "yaT%d", yA, "yA%d")
                if s == 6:
                    branch(wbb, ybT, "ybT%d", yB, "yB%d")
            oT = pool.f32(8 * N).rearrange("p (k n) -> p k n", k=8)
            pool.off = 6144
            sq = pool.bf16(8 * N).rearrange("p (k n) -> p k n", k=8)
            tmp = pool.f32(2 * N).rearrange("p (k n) -> p k n", k=2)
            for half in range(2):
                sl, key = load_slab([(lambda t: t[:, 0:4096].rearrange("p (k n) -> p k n", k=8),
                                     wout[:, half * 512:(half + 1) * 512].rearrange("(k p) n -> p k n", p=128))])
                slv = sl[:, 0:4096].rearrange("p (k n) -> p k n", k=8)
                for tt in range(4):
                    i = half * 4 + tt
                    b = tt % 2
                    mm_group(psA[:, b, :], "psA%d" % b, [(slv[:, k, tt * 128:(tt + 1) * 128], mgT[:, k, :], [key, "mgT%d" % k, "SCR"]) for k in range(8)])
                    evac(oT[:, i, :], psA[:, b, :], R("psA%d" % b), ["oT"])
            tap("mo_" + kind, oT.rearrange("p q n -> p (q n)"), 8 * N, R("oT"))
            postnorm_residual(j, 3, oT, sq, tmp)

        stt = sb("stt", [128, 5, 4, 64])
        nsb = sb("nsb", [64, 2, 2, 4, 128])
        mk.dma("sp", "c0", lambda e: e.dma_start(out=stt[:], in_=std.rearrange("a q p v -> p a q v")), writes=["stt"])

        def load_x(g):
            pool = Pool()
            xin = pool.f32(4 * D).rearrange("p (t n) -> p t n", t=4)
            for tt in range(4):
                mk.dma("sp", "xin", lambda e, tt=tt: e.dma_start(out=xin[:, tt, :], in_=xg[g, tt * 128:(tt + 1) * 128, :]), reads=["SCR"], writes=["xin%d" % tt])
            for k in range(8):
                b = k % 2
                for tt in range(4):
                    mm_group(psA[:, b, tt * 128:(tt + 1) * 128], "psA%d" % b, [(xin[:, tt, k * 128:(k + 1) * 128], ident, ["xin%d" % tt, "cst", "SCR"])])
                evac(xT[:, k, :], psA[:, b, :], R("psA%d" % b), ["xT"])

        def store_y(gout):
            pool = Pool()
            yo = pool.f32(4 * D).rearrange("p (t n) -> p t n", t=4)
            for tt in range(4):
                for k in range(8):
                    b = k % 2
                    mm_group(psA[:, b, 0:128], "psA%d" % b, [(xT[:, k, tt * 128:(tt + 1) * 128], ident, ["xT", "cst", "SCR"])])
                    evac(yo[:, tt, k * 128:(k + 1) * 128], psA[:, b, 0:128], R("psA%d" % b), ["yo%d" % tt])
                mk.dma("sp", "yout", lambda e, tt=tt: e.dma_start(out=yout[gout * N + tt * 128:gout * N + (tt + 1) * 128, :], in_=yo[:, tt, :]), reads=R("yo%d" % tt))

        def ffn_phase(j, ai, bi, gi, w13, w2):
            pool = Pool()
            hT = pool.bf16(8 * N).rearrange("p (k n) -> p k n", k=8)
            sq = pool.bf16(8 * N).rearrange("p (k n) -> p k n", k=8)
            hid = pool.bf16(22 * N).rearrange("p (k n) -> p k n", k=22)
            oT = pool.f32(8 * N).rearrange("p (k n) -> p k n", k=8)
            tmp = pool.f32(2 * N).rearrange("p (k n) -> p k n", k=2)
            sgt = pool.f32(4 * N).rearrange("p (k n) -> p k n", k=4)
            prenorm(j, ai, bi, hT, sq, tmp)
            ffn(w13, w2, hT, hid, oT, sgt)
            postnorm_residual(j, gi, oT, sq, tmp)

        def mixer_phase(kind, j, aux_idx):
            pool = Pool()
            hT = pool.bf16(8 * N).rearrange("p (k n) -> p k n", k=8)
            tail = Pool()
            tail.off = SCR - (8 * N // 2 + 2 * N)
            sq = tail.bf16(8 * N).rearrange("p (k n) -> p k n", k=8)
            tmp = tail.f32(2 * N).rearrange("p (k n) -> p k n", k=2)
            prenorm(j, 2, 3, hT, sq, tmp)
            barrier()
            mixer(kind, j, hT, aux_idx)

        def chain_prep_aux(a):
            cc_ = COLS["coef"] + (a - 1)
            mk.dve(lambda e: e.tensor_scalar(out=hbA[:], in0=hlast[:, a - 1, :], scalar1=cols[:, cc_:cc_ + 1], scalar2=None, op0=ALU.mult),
                   reads=["hlast%d" % (a - 1), "cols"], writes=["hbA"])

        def chain_prep_own():
            c2 = COLS["coef"] + 2
            mk.dve(lambda e: e.tensor_scalar(out=hbF[:], in0=hlast[:, 0, :], scalar1=cols[:, c2:c2 + 1], scalar2=None, op0=ALU.mult),
                   reads=["hlast0", "cols"], writes=["hbF"])
            for a in (1, 2):
                mk.dve(lambda e, a=a: e.scalar_tensor_tensor(out=hbF[:], in0=hlast[:, a, :], scalar=cols[:, c2 + a:c2 + a + 1], in1=hbF[:], op0=ALU.mult, op1=ALU.add),
                       reads=["hlast%d" % a, "cols", "hbF"], writes=["hbF"])
            mk.dve(lambda e: e.tensor_scalar(out=hbB[:], in0=hlast[:, 2, :], scalar1=cols[:, c2 + 3:c2 + 4], scalar2=None, op0=ALU.mult),
                   reads=["hlast2", "cols"], writes=["hbB"])

        for a in range(3):
            if go():
                barrier()
                load_x(2 + a)
            if go():
                barrier()
                ffn_phase(1, 0, 0, 1, f1w13, f1w2)
            if go():
                barrier()
                if a > 0:
                    chain_prep_aux(a)
                mixer_phase("aux", 1, a)
        for (g, kind, j) in [(1, "own", 1), (0, "prompt", 0)]:
            if go():
                barrier()
                load_x(g)
            if go():
                barrier()
                ffn_phase(j, 0, 0, 1, f1w13, f1w2)
            if go():
                barrier()
                if kind == "own":
                    chain_prep_own()
                mixer_phase(kind, j, None)
            if go():
                barrier()
                ffn_phase(j, 4, 6, 5, f2w13, f2w2)
            if go():
                barrier()
                store_y(1 if kind == "own" else 0)
        if dbg_spec is not None:
            barrier()
            dbg_spec(mk, nc, locals())
        for sq_ in range(2):
            for dd in range(2):
                mk.dma("sp", "yout", lambda e, sq_=sq_, dd=dd: e.dma_start(out=nsout[sq_, dd].rearrange("(q e) v k -> v q e k", e=2),
                                                                           in_=nsb[:, sq_, dd, :, :].rearrange("p q (e k) -> p q e k", e=2)), reads=["nsb"])
        stats = mk.emit()
    return nc, stats


_CACHE = {}


def _prep_inputs(inp):
    f = lambda a: np.ascontiguousarray(np.asarray(a, np.float32))
    x_prompt, x_sample = f(inp["x_prompt"]), f(inp["x_sample"])
    c, state, c_ctx = f(inp["c"]), f(inp["state_rwkv"]), f(inp["c_ctx"])
    mu = f(inp["mu_shift"])[0]
    w1 = [np.concatenate([f(inp["decay_w1"])[0, d], f(inp["iclr_a1"])[0, d]], axis=1) for d in range(2)]
    w2 = [np.concatenate([f(inp["decay_w2"])[0, d], f(inp["iclr_a2"])[0, d]], axis=0) for d in range(2)]
    dw0, ia0 = f(inp["decay_w0"])[0], f(inp["iclr_a0"])[0]
    shared = {
        "w_mod": f(inp["w_mod"])[0], "f1w13": f(inp["ffn1_w13"])[0], "f1w2": f(inp["ffn1_w2"])[0],
        "f2w13": f(inp["ffn2_w13"])[0], "f2w2": f(inp["ffn2_w2"])[0], "w_in": f(inp["w_in"])[0],
        "wba": f(inp["w_branch_a"])[0], "wbb": f(inp["w_branch_b"])[0], "wout": f(inp["w_out"])[0],
        "consts": _make_consts(),
    }
    in_maps = []
    for core in range(8):
        b, s = core // 4, core % 4
        if s == 0:
            aux = [(3, 1), (2, 1), (1, 1)]
            cont = (1.0, 1.0)
            selF = (0.0, 0.0, 0.0)
            selB = 1.0
            init = ["F", "0", "B", "0", "0"]
        elif s == 1:
            aux = [(0, 0), (3, 1), (2, 1)]
            cont = (0.0, 1.0)
            selF = (1.0, 0.0, 0.0)
            selB = 1.0
            init = ["0", "0", "F", "B", "0"]
        elif s == 2:
            aux = [(0, 0), (1, 0), (3, 1)]
            cont = (1.0, 0.0)
            selF = (0.0, 1.0, 0.0)
            selB = 1.0
            init = ["0", "0", "F", "0", "B"]
        else:
            aux = [(0, 0), (1, 0), (2, 0)]
            cont = (1.0, 1.0)
            selF = (0.0, 0.0, 1.0)
            selB = 0.0
            init = ["0", "B", "F", "0", "0"]
        xgr = np.empty((5, N, D), np.float32)
        xgr[0] = x_prompt[2 * core:2 * core + 2].reshape(N, D)
        xgr[1] = x_sample[b, s * N:(s + 1) * N]
        for a, (seg, dr) in enumerate(aux):
            xs = x_sample[b, seg * N:(seg + 1) * N]
            xgr[2 + a] = xs[::-1] if dr == 1 else xs
        dsl = [0, 1] + [dr for (_, dr) in aux]
        cols = np.zeros((128, NCOL), np.float32)
        cvs = [c_ctx, c[b]]
        for k in range(8):
            for j in range(2):
                cols[:, COLS["cv"] + k * 2 + j] = cvs[j][k * 128:(k + 1) * 128]
        cols[:, COLS["bmod"]:COLS["bmod"] + 72] = _colize(inp["b_mod"][0])
        cols[:, COLS["ng"]:COLS["ng"] + 48] = _colize(np.asarray(inp["norm_g"][0]).reshape(-1))
        for ds, dr in enumerate(dsl):
            for x in range(2):
                cols[:, COLS["mu"] + ds * 16 + x * 8:COLS["mu"] + ds * 16 + x * 8 + 8] = _colize(mu[dr, x])
            cols[:, COLS["w0"] + ds * 4:COLS["w0"] + ds * 4 + 4] = _colize(dw0[dr])
            cols[:, COLS["a0"] + ds * 4:COLS["a0"] + ds * 4 + 4] = _colize(ia0[dr])
        cols[:, COLS["kk"]:COLS["kk"] + 4] = _colize(inp["k_k"][0])
        cols[:, COLS["ka"]:COLS["ka"] + 4] = _colize(inp["k_a"][0])
        cols[:, COLS["rk"]:COLS["rk"] + 4] = _colize(np.asarray(inp["r_k"][0]).reshape(-1))
        cols[:, COLS["gng"]:COLS["gng"] + 4] = _colize(inp["gn_gain"][0])
        cols[:, COLS["gnb"]:COLS["gnb"] + 4] = _colize(inp["gn_bias"][0])
        cols[:, COLS["cw"]:COLS["cw"] + 12] = _colize(np.asarray(inp["conv_w"][0]).reshape(-1))
        cols[:, COLS["cb"]:COLS["cb"] + 4] = _colize(inp["conv_b"][0])
        cols[:, COLS["coef"]:COLS["coef"] + 6] = np.array([cont[0], cont[1], selF[0], selF[1], selF[2], selB], np.float32)[None, :]
        cols[:, COLS["num"]:COLS["num"] + 6] = np.array([1e-6, 1e-12, 64e-5, 0.0, 1.0, 2.0], np.float32)[None, :]
        def mlay(d):
            S = state[b, 0, d]
            return np.ascontiguousarray(S.transpose(0, 2, 1).reshape(4, 128, 64))
        stv = np.zeros((5, 4, 128, 64), np.float32)
        for i, t in enumerate(init):
            if t == "F":
                stv[i] = mlay(0)
            elif t == "B":
                stv[i] = mlay(1)
        m = dict(shared)
        m.update({"xg": xgr, "cols": cols, "st": stv,
                  "w1c": np.ascontiguousarray(np.stack([w1[d] for d in dsl])),
                  "w2c": np.ascontiguousarray(np.stack([w2[d] for d in dsl]))})
        in_maps.append(m)
    return in_maps


def kernel(**inputs):
    if "nc" not in _CACHE:
        _CACHE["nc"], _CACHE["stats"] = build_program(int(os.environ.get("KLIMIT", str(10 ** 9))))
    nc = _CACHE["nc"]
    in_maps = _prep_inputs(inputs)
    res = run_bass_kernel_spmd(nc, in_maps, core_ids=list(range(8)))
    y_prompt = np.empty((16, 256, D), np.float32)
    y_sample = np.empty((2, 2048, D), np.float32)
    new_state = np.empty((16, 1, 2, 8, 64, 64), np.float32)
    for core in range(8):
        r = res.results[core]
        b, s = core // 4, core % 4
        y = np.asarray(r["y"], np.float32)
        y_prompt[2 * core:2 * core + 2] = y[0:N].reshape(2, 256, D)
        y_sample[b, s * N:(s + 1) * N] = y[N:2 * N]
        new_state[2 * core:2 * core + 2, 0] = np.asarray(r["ns"], np.float32)
    return (y_prompt, y_sample, new_state)
```
